# Optimizing a Trainium2 kernel written in Bass

```python
import jax, jax.numpy as jnp
from jax import lax
import numpy as np

D_MODEL = 2048
BATCH = 2
SEQ = 16384
DEPTH = 2

N_MIXERS = 2
FOX_HEADS = 16
FOX_HEAD_DIM = D_MODEL // FOX_HEADS
FOX_BLOCK_Q = 128
RET_HEADS = 8
RET_QK_DIM = D_MODEL // RET_HEADS
RET_V_DIM = 2 * D_MODEL // RET_HEADS
RET_CHUNK = 128
RET_ROT_BASE = 10000.0
D_FF = 5632
CONV_WIDTH = 3
NORM_EPS = 1e-6

kernel_name = "fox_retnet_interleaved_convffn_sandwich"


def rmsnorm(x, g):
    xf = x.astype(jnp.float32)
    y = xf * lax.rsqrt(jnp.mean(xf * xf, axis=-1, keepdims=True) + NORM_EPS)
    return (y * g.astype(jnp.float32)).astype(x.dtype)


def fox_attention(h, w_in, b_f, w_o):
    B, S, _ = h.shape
    H, dh, D = FOX_HEADS, FOX_HEAD_DIM, D_MODEL
    proj = h @ w_in
    q = proj[..., :D].reshape(B, S, H, dh).transpose(0, 2, 1, 3)
    k = proj[..., D:2 * D].reshape(B, S, H, dh).transpose(0, 2, 1, 3)
    v = proj[..., 2 * D:3 * D].reshape(B, S, H, dh).transpose(0, 2, 1, 3)
    log_f = jax.nn.log_sigmoid((proj[..., 3 * D:] + b_f).astype(jnp.float32))
    cum = jnp.cumsum(log_f, axis=1).transpose(0, 2, 1)
    scale = dh ** -0.5
    kpos = jnp.arange(S)

    def block(i):
        start = i * FOX_BLOCK_Q
        qb = lax.dynamic_slice_in_dim(q, start, FOX_BLOCK_Q, axis=2)
        cb = lax.dynamic_slice_in_dim(cum, start, FOX_BLOCK_Q, axis=2)
        qpos = start + jnp.arange(FOX_BLOCK_Q)
        logits = (jnp.einsum('bhqd,bhkd->bhqk', qb, k).astype(jnp.float32) * scale
                  + cb[..., :, None] - cum[..., None, :])
        logits = jnp.where(kpos[None, :] <= qpos[:, None], logits, -jnp.inf)
        p = jax.nn.softmax(logits, axis=-1).astype(v.dtype)
        return jnp.einsum('bhqk,bhkd->bhqd', p, v)

    out = lax.map(block, jnp.arange(S // FOX_BLOCK_Q))
    out = out.transpose(1, 0, 3, 2, 4).reshape(B, S, D)
    return out @ w_o


def rotate_every_two(x, sin, cos):
    x1 = x[..., 0::2]
    x2 = x[..., 1::2]
    s = sin[None, :, None, :]
    c = cos[None, :, None, :]
    return jnp.stack([x1 * c - x2 * s, x1 * s + x2 * c], axis=-1).reshape(x.shape)


def retention(h, w_in, w_o):
    B, S, _ = h.shape
    H, dk, dv, D, C = RET_HEADS, RET_QK_DIM, RET_V_DIM, D_MODEL, RET_CHUNK
    NC = S // C
    proj = h @ w_in
    q = proj[..., :D].reshape(B, S, H, dk)
    k = proj[..., D:2 * D].reshape(B, S, H, dk)
    v = proj[..., 2 * D:4 * D].reshape(B, S, H, dv)
    g = proj[..., 4 * D:]

    theta = 1.0 / (RET_ROT_BASE ** jnp.linspace(0.0, 1.0, dk // 2, dtype=jnp.float32))
    ang = jnp.arange(S, dtype=jnp.float32)[:, None] * theta[None, :]
    sin, cos = jnp.sin(ang), jnp.cos(ang)
    q = rotate_every_two(q.astype(jnp.float32), sin, cos)
    k = rotate_every_two(k.astype(jnp.float32), sin, cos) * (dk ** -0.5)
    v = v.astype(jnp.float32)

    def to_chunks(t):
        return t.reshape(B, NC, C, H, t.shape[-1]).transpose(1, 0, 3, 2, 4)

    qc, kc, vc = to_chunks(q), to_chunks(k), to_chunks(v)

    log_gamma = jnp.log1p(-(2.0 ** (-5.0 - jnp.arange(H, dtype=jnp.float32))))
    idx = jnp.arange(C, dtype=jnp.float32)
    rel = idx[:, None] - idx[None, :]
    decay_intra = jnp.where(rel[None] >= 0,
                            jnp.exp(jnp.maximum(rel, 0.0)[None] * log_gamma[:, None, None]),
                            0.0)
    decay_q = jnp.exp((idx + 1.0)[None, :] * log_gamma[:, None])
    decay_k = jnp.exp((C - 1.0 - idx)[None, :] * log_gamma[:, None])
    decay_chunk = jnp.exp(C * log_gamma)

    def step(state, xs):
        qb, kb, vb = xs
        scores = jnp.einsum('bhnd,bhmd->bhnm', qb, kb) * decay_intra[None]
        o = (jnp.einsum('bhnm,bhmv->bhnv', scores, vb)
             + jnp.einsum('bhnd,bhdv->bhnv', qb, state) * decay_q[None, :, :, None])
        new_state = (decay_chunk[None, :, None, None] * state
                     + jnp.einsum('bhmd,bhmv->bhdv', kb * decay_k[None, :, :, None], vb))
        return new_state, o

    state0 = jnp.zeros((B, H, dk, dv), jnp.float32)
    _, o = lax.scan(step, state0, (qc, kc, vc))
    o = o.transpose(1, 0, 3, 2, 4).reshape(B, S, H, dv)
    o = o * lax.rsqrt(jnp.mean(o * o, axis=-1, keepdims=True) + NORM_EPS)
    o = o.reshape(B, S, H * dv).astype(h.dtype) * jax.nn.silu(g)
    return o @ w_o


def conv_gated_mlp(h, w_up, conv_w, conv_b, w_down):
    u = h @ w_up
    u = lax.conv_general_dilated(
        u, conv_w[:, None, :], window_strides=(1,),
        padding=((CONV_WIDTH - 1, 0),),
        dimension_numbers=('NWC', 'WIO', 'NWC'),
        feature_group_count=u.shape[-1]) + conv_b
    a, b = u[..., :D_FF], u[..., D_FF:]
    return (jax.nn.silu(a) * b) @ w_down


def setup_inputs(seed: int = 0) -> dict:
    key = jax.random.key(seed)
    ks = jax.random.split(key, 16)
    D, F = D_MODEL, D_FF
    n_fox = (DEPTH + 1) // 2
    n_ret = DEPTH // 2
    nrm = jax.random.normal
    x = nrm(ks[0], (BATCH, SEQ, D), jnp.float32)
    norm_g = 1.0 + 0.05 * nrm(ks[1], (DEPTH, 4, D), jnp.float32)
    fox_qkv = nrm(ks[2], (n_fox, D, 3 * D), jnp.float32) * D ** -0.5
    fox_f = nrm(ks[3], (n_fox, D, FOX_HEADS), jnp.float32) * (0.1 * D ** -0.5)
    fox_w_in = jnp.concatenate([fox_qkv, fox_f], axis=-1)
    fox_b_f = (jnp.linspace(1.0, 6.0, FOX_HEADS, dtype=jnp.float32)[None, :]
               + 0.1 * nrm(ks[4], (n_fox, FOX_HEADS), jnp.float32))
    fox_w_o = nrm(ks[5], (n_fox, D, D), jnp.float32) * D ** -0.5
    ret_w_in = nrm(ks[6], (n_ret, D, 6 * D), jnp.float32) * D ** -0.5
    ret_w_o = nrm(ks[7], (n_ret, 2 * D, D), jnp.float32) * (2 * D) ** -0.5
    ffn_w_up = nrm(ks[8], (DEPTH, D, 2 * F), jnp.float32) * D ** -0.5
    ffn_conv_w = nrm(ks[9], (DEPTH, CONV_WIDTH, 2 * F), jnp.float32) * CONV_WIDTH ** -0.5
    ffn_conv_b = 0.02 * nrm(ks[10], (DEPTH, 2 * F), jnp.float32)
    ffn_w_down = nrm(ks[11], (DEPTH, F, D), jnp.float32) * F ** -0.5
    return {"x": x, "norm_g": norm_g, "fox_w_in": fox_w_in, "fox_b_f": fox_b_f,
            "fox_w_o": fox_w_o, "ret_w_in": ret_w_in, "ret_w_o": ret_w_o,
            "ffn_w_up": ffn_w_up, "ffn_conv_w": ffn_conv_w, "ffn_conv_b": ffn_conv_b,
            "ffn_w_down": ffn_w_down}


def reference(x, norm_g, fox_w_in, fox_b_f, fox_w_o, ret_w_in, ret_w_o,
              ffn_w_up, ffn_conv_w, ffn_conv_b, ffn_w_down):
    for i in range(DEPTH):
        g = norm_g[i]
        h = rmsnorm(x, g[0])
        j = i // N_MIXERS
        if i % N_MIXERS == 0:
            m = fox_attention(h, fox_w_in[j], fox_b_f[j], fox_w_o[j])
        else:
            m = retention(h, ret_w_in[j], ret_w_o[j])
        x = x + rmsnorm(m, g[1])
        h = rmsnorm(x, g[2])
        f = conv_gated_mlp(h, ffn_w_up[i], ffn_conv_w[i], ffn_conv_b[i], ffn_w_down[i])
        x = x + rmsnorm(f, g[3])
    return x
```

```python
import contextlib
import numpy as np
import concourse.bass as bass
import concourse.mybir as mybir
from concourse.bass_utils import run_bass_kernel_spmd

F32 = mybir.dt.float32
BF16 = mybir.dt.bfloat16
AF = mybir.ActivationFunctionType
ALU = mybir.AluOpType
AX = mybir.AxisListType

D_MODEL = 2048
SEQ = 16384
BATCH = 2
D_FF = 5632
NORM_EPS = 1e-6
NCORES = 8

SAME_ENG_SYNC = True
SEM_GEN = 30000
SEM_DMA_GEN = 1500


class Res:
    __slots__ = ("name", "w", "rs")

    def __init__(self, name=""):
        self.name = name
        self.w = None
        self.rs = []


class _Op:
    __slots__ = ("eng", "fn", "deps", "ev", "dma")


class Sched:
    ENGS = ("pe", "act", "dve", "pool", "sp")

    def __init__(self, nc, ndma=12):
        self.nc = nc
        self.ops = {e: [] for e in self.ENGS}
        self.cnt = {e: 0 for e in self.ENGS}
        self.dman = {e: 0 for e in self.ENGS}
        self.dmahist = {e: [] for e in self.ENGS}
        self.ndma = ndma
        self.all_dma = []

    def op(self, eng, fn, reads=(), writes=(), dma=False):
        o = _Op()
        o.eng = eng
        o.fn = fn
        o.dma = dma
        deps = []
        for r in reads:
            if r.w is not None:
                deps.append(r.w)
        for r in writes:
            if r.w is not None:
                deps.append(r.w)
            deps.extend(r.rs)
        if dma:
            n = self.dman[eng]
            self.dman[eng] = n + 1
            rnd = n // self.ndma
            o.ev = ("d_" + eng, (n % self.ndma, rnd // SEM_DMA_GEN), 16 * (rnd % SEM_DMA_GEN + 1))
            if n >= self.ndma:
                deps.append(self.dmahist[eng][n - self.ndma])
            self.dmahist[eng].append(o)
            self.all_dma.append(o)
        else:
            c = self.cnt[eng]
            self.cnt[eng] = c + 1
            o.ev = ("c_" + eng, c // SEM_GEN, c % SEM_GEN + 1)
        dd = []
        seen = set()
        for d in deps:
            if id(d) in seen:
                continue
            seen.add(id(d))
            if (not d.dma) and d.eng == eng:
                if eng == "pe" or not SAME_ENG_SYNC:
                    continue
            dd.append(d)
        o.deps = dd
        for r in reads:
            r.rs.append(o)
        for r in writes:
            r.w = o
            r.rs = []
        self.ops[eng].append(o)
        return o

    def emit(self):
        nc = self.nc
        with contextlib.ExitStack() as st:
            sems = {}
            for e in self.ENGS:
                for o in self.ops[e]:
                    k = (o.ev[0], o.ev[1])
                    if k not in sems:
                        sems[k] = st.enter_context(nc.semaphore("s%d" % len(sems)))
            block = st.enter_context(nc.Block())

            def run(eng_name, is_last=False):
                def body(eng):
                    waited = {}
                    for o in self.ops[eng_name]:
                        for d in o.deps:
                            k = (d.ev[0], d.ev[1])
                            if waited.get(k, 0) < d.ev[2]:
                                eng.wait_ge(sems[k], d.ev[2])
                                waited[k] = d.ev[2]
                        ins = o.fn(eng)
                        ins.then_inc(sems[(o.ev[0], o.ev[1])], 16 if o.dma else 1)
                    if is_last:
                        fin = {}
                        for o in self.all_dma:
                            k = (o.ev[0], o.ev[1])
                            fin[k] = max(fin.get(k, 0), o.ev[2])
                        for k, v in fin.items():
                            if waited.get(k, 0) < v:
                                eng.wait_ge(sems[k], v)
                return body

            block.tensor(run("pe"))
            block.scalar(run("act"))
            block.vector(run("dve"))
            block.gpsimd(run("pool"))
            block.sync(run("sp", True))


class Pool:
    def __init__(self, kb, name, shape, dt, n, views=None):
        if views is not None:
            self.tiles = list(views)
            n = len(views)
        else:
            self.tiles = [kb.sb("%s%d" % (name, i), shape, dt) for i in range(n)]
        self.res = [Res("%s%d" % (name, i)) for i in range(n)]
        self.i = 0

    def next(self):
        i = self.i % len(self.tiles)
        self.i += 1
        return self.tiles[i], self.res[i]


class KB:
    def __init__(self):
        self.nc = bass.Bass("TRN2", target_bir_lowering=False)
        self.S = Sched(self.nc)
        self.st = contextlib.ExitStack()
        self.ps = []
        self.psr = []
        self.psi = 0

    def sb(self, name, shape, dt):
        return self.st.enter_context(self.nc.sbuf_tensor(name, list(shape), dt))

    def alloc_psum(self, n=8):
        for i in range(n):
            self.ps.append(self.st.enter_context(self.nc.psum_tensor("ps%d" % i, [128, 512], F32)))
            self.psr.append(Res("ps%d" % i))

    def bank(self, lo=0, hi=8):
        i = lo + self.psi % (hi - lo)
        self.psi += 1
        return self.ps[i], self.psr[i]

    def din(self, name, shape, dt=F32):
        return self.nc.dram_tensor(name, list(shape), dt, kind="ExternalInput").ap()

    def dout(self, name, shape, dt=F32):
        return self.nc.dram_tensor(name, list(shape), dt, kind="ExternalOutput").ap()

    def dtmp(self, name, shape, dt=F32):
        return self.nc.dram_tensor(name, list(shape), dt, kind="Internal").ap()

    def finish(self):
        self.S.emit()
        self.st.close()
        return self.nc

    def dma(self, q, out, in_, r, w):
        return self.S.op(q, lambda e: e.dma_start(out=out, in_=in_), r, w, dma=True)

    def act(self, out, in_, func, r, w, **kw):
        return self.S.op("act", lambda e: e.activation(out=out, in_=in_, func=func, **kw), r, w)

    def mm(self, out, lhsT, rhs, start, stop, r, w):
        return self.S.op("pe", lambda e: e.matmul(out, lhsT=lhsT, rhs=rhs, start=start, stop=stop,
                                                  skip_group_check=True), r, w)

    def tr(self, out, in_, ident, r, w):
        return self.S.op("pe", lambda e: e.transpose(out=out, in_=in_, identity=ident), r, w)

    def v(self, eng, meth, r, w, **kw):
        return self.S.op(eng, lambda e: getattr(e, meth)(**kw), r, w)


def emit_rstd(kb, x_ap, r_x, ncols, P=128):
    junk, r_j = kb.junk.next()
    st, r_s = kb.small.next()
    kb.act(junk[0:P, 0:ncols], x_ap, AF.Square, [r_x], [r_j])
    kb.v("dve", "reduce_sum", [r_j], [r_s], out=st[0:P, 0:1], in_=junk[0:P, 0:ncols], axis=AX.X)
    kb.v("dve", "tensor_scalar", [r_s], [r_s], out=st[0:P, 1:2], in0=st[0:P, 0:1], scalar1=1.0 / ncols,
         scalar2=NORM_EPS, op0=ALU.mult, op1=ALU.add)
    kb.act(st[0:P, 2:3], st[0:P, 1:2], AF.Sqrt, [r_s], [r_s])
    kb.v("dve", "reciprocal", [r_s], [r_s], out=st[0:P, 3:4], in_=st[0:P, 2:3])
    return st[0:P, 3:4], r_s


def emit_norm_T(kb, x_tile, r_x, G, r_G, hT, r_hT, col0, KC, cp_eng="act"):
    rstd, r_s = emit_rstd(kb, x_tile[:, 0:KC * 128], r_x, KC * 128)
    hn, r_hn = kb.hn.next()
    kb.v("dve", "scalar_tensor_tensor", [r_x, r_s, r_G], [r_hn], out=hn[:, 0:KC * 128], in0=x_tile[:, 0:KC * 128],
         scalar=rstd, in1=G[:, 0:KC * 128], op0=ALU.mult, op1=ALU.mult)
    emit_T(kb, hn, r_hn, hT, r_hT, col0, KC, cp_eng)


def emit_T(kb, hn, r_hn, hT, r_hT, col0, KC, cp_eng="act"):
    for k0 in range(0, KC, 8):
        n = min(8, KC - k0)
        ps, r_ps = kb.bank()
        psb = ps[:].bitcast(BF16)
        for j in range(n):
            kb.tr(psb[:, j * 128:(j + 1) * 128], hn[:, (k0 + j) * 128:(k0 + j + 1) * 128], kb.identb[:],
                  [r_hn, kb.r_ident], [r_ps])
        src = psb[:, 0:n * 128].rearrange("p (k c) -> p k c", c=128)
        dst = hT[:, k0:k0 + n, col0:col0 + 128]
        if cp_eng == "act":
            kb.act(dst, src, AF.Copy, [r_ps], [r_hT])
        else:
            kb.v(cp_eng, "tensor_copy", [r_ps], [r_hT], out=dst, in_=src)


def emit_post(kb, f_ap, r_f, x_ap, r_x, G, r_G, out_tile, r_out, ncols=D_MODEL):
    rstd, r_s = emit_rstd(kb, f_ap, r_f, ncols)
    kb.v("dve", "scalar_tensor_tensor", [r_f, r_s, r_G], [r_out], out=out_tile, in0=f_ap, scalar=rstd, in1=G[:, 0:ncols],
         op0=ALU.mult, op1=ALU.mult)
    kb.v("pool", "tensor_tensor", [r_out, r_x], [r_out], out=out_tile, in0=out_tile, in1=x_ap, op=ALU.add)


def setup_common(kb, ident_in):
    kb.alloc_psum(8)
    idf = kb.sb("idf", [128, 128], F32)
    kb.identb = kb.sb("identb", [128, 128], BF16)
    kb.r_ident = Res("ident")
    r_idf = Res("idf")
    kb.dma("sp", idf[:], ident_in, [], [r_idf])
    kb.v("dve", "tensor_copy", [r_idf], [kb.r_ident], out=kb.identb[:], in_=idf[:])
    kb.small = Pool(kb, "small", [128, 4], F32, 6)


_cast_rr = [0]


def emit_cast(kb, out, in_, r, w):
    i = _cast_rr[0] % 3
    _cast_rr[0] += 1
    if i == 0:
        kb.v("dve", "tensor_copy", r, w, out=out, in_=in_)
    elif i == 1:
        kb.act(out, in_, AF.Copy, r, w)
    else:
        kb.v("pool", "tensor_copy", r, w, out=out, in_=in_)


TB = 512
NPAIR = D_FF // 128
KC = D_MODEL // 128


def build_ffn(T):
    kb = KB()
    NB = T // TB
    xm = kb.din("xm", [T, D_MODEL])
    xh = kb.din("xh", [128, D_MODEL])
    wup = kb.din("wup", [NPAIR, 128, 2 * KC * 128])
    wdn = kb.din("wdn", [4, 128, NPAIR * 512])
    G2d = kb.din("G2", [128, D_MODEL])
    G3d = kb.din("G3", [128, D_MODEL])
    cwd = kb.din("cw", [128, 2 * NPAIR * 3])
    cbd = kb.din("cb", [128, 2 * NPAIR])
    identd = kb.din("ident", [128, 128])
    xo = kb.dout("xo", [T, D_MODEL])
    wub = kb.dtmp("wub", [NPAIR, 128, 2 * KC * 128], BF16)
    wdb = kb.dtmp("wdb", [4, 128, NPAIR * 512], BF16)

    setup_common(kb, identd)
    kb.junk = Pool(kb, "junk", [128, D_MODEL], F32, 1)
    kb.hn = Pool(kb, "hn", [128, D_MODEL], BF16, 2)
    xin = Pool(kb, "xin", [128, D_MODEL], F32, 2)
    G2 = kb.sb("G2s", [128, D_MODEL], F32); r_G2 = Res()
    G3 = kb.sb("G3s", [128, D_MODEL], F32); r_G3 = Res()
    cw = kb.sb("cws", [128, 2, NPAIR, 3], F32); r_cw = Res()
    cb = kb.sb("cbs", [128, 2, NPAIR], F32); r_cb = Res()
    kb.dma("sp", G2[:], G2d, [], [r_G2])
    kb.dma("sp", G3[:], G3d, [], [r_G3])
    kb.dma("sp", cw[:], cwd.rearrange("p (h i t) -> p h i t", h=2, i=NPAIR), [], [r_cw])
    kb.dma("sp", cb[:], cbd.rearrange("p (h i) -> p h i", h=2), [], [r_cb])

    hT = kb.sb("hT", [128, KC, TB], BF16); r_hT = Res()
    gT = kb.sb("gT", [128, NPAIR, TB], BF16); r_gT = [Res() for _ in range(NPAIR)]
    ft = kb.sb("ft", [128, TB // 128, D_MODEL], F32); r_ft = [Res() for _ in range(TB // 128)]
    wupp = Pool(kb, "wupp", [128, 2, KC, 128], BF16, 2)
    wdp = Pool(kb, "wdp", [128, 4, 512], BF16, 3)
    ub = Pool(kb, "ub", [128, TB + 2], F32, 4)
    tmp = Pool(kb, "tmp", [128, TB], F32, 5)
    carry = kb.sb("carry", [128, 2 * NPAIR, 2], F32); r_carry = [Res() for _ in range(2 * NPAIR)]
    hTh = kb.sb("hTh", [128, KC, 128], BF16); r_hTh = Res()
    r_wub = [Res() for _ in range(NPAIR)]
    r_wdb = [Res() for _ in range(4)]

    for i in range(NPAIR):
        for hf in range(2):
            st_, r_st = xin.next()
            sb_, r_sb = kb.hn.next()
            sl = slice(hf * KC * 128, (hf + 1) * KC * 128)
            kb.dma("sp", st_[:], wup[i, :, sl], [], [r_st])
            emit_cast(kb, sb_[:], st_[:], [r_st], [r_sb])
            kb.dma("pool", wub[i, :, sl], sb_[:], [r_sb], [r_wub[i]])
    for dq in range(4):
        for g in range(NPAIR * 512 // D_MODEL):
            st_, r_st = xin.next()
            sb_, r_sb = kb.hn.next()
            sl = slice(g * D_MODEL, (g + 1) * D_MODEL)
            kb.dma("sp", st_[:], wdn[dq, :, sl], [], [r_st])
            emit_cast(kb, sb_[:], st_[:], [r_st], [r_sb])
            kb.dma("pool", wdb[dq, :, sl], sb_[:], [r_sb], [r_wdb[dq]])

    xt, r_xt = xin.next()
    kb.dma("sp", xt[:], xh, [], [r_xt])
    emit_norm_T(kb, xt, r_xt, G2, r_G2, hTh, r_hTh, 0, KC)
    psc, r_psc = kb.bank()
    for i in range(NPAIR):
        wt, r_wt = wupp.next()
        kb.dma("sp", wt[:], wub[i].rearrange("p (h k c) -> p h k c", h=2, k=KC), [r_wub[i]], [r_wt])
        for hf in range(2):
            ch = hf * NPAIR + i
            for k in range(KC):
                kb.mm(psc[:, ch * 2:ch * 2 + 2], wt[:, hf, k, :], hTh[:, k, 126:128], k == 0, k == KC - 1,
                      [r_wt, r_hTh], [r_psc])
    kb.v("dve", "tensor_copy", [r_psc], r_carry, out=carry[:].rearrange("p c t -> p (c t)"), in_=psc[:, 0:4 * NPAIR])

    for b in range(NB):
        t0 = b * TB
        for tt in range(TB // 128):
            xt, r_xt = xin.next()
            kb.dma("sp", xt[:], xm[t0 + tt * 128:t0 + (tt + 1) * 128, :], [], [r_xt])
            emit_norm_T(kb, xt, r_xt, G2, r_G2, hT, r_hT, tt * 128, KC)
        for i in range(NPAIR):
            wt, r_wt = wupp.next()
            kb.dma("sp", wt[:], wub[i].rearrange("p (h k c) -> p h k c", h=2, k=KC), [r_wub[i]], [r_wt])
            cv = []
            for hf in range(2):
                ch = hf * NPAIR + i
                ps, r_ps = kb.bank()
                for k in range(KC):
                    kb.mm(ps[:, 0:TB], wt[:, hf, k, :], hT[:, k, :], k == 0, k == KC - 1, [r_wt, r_hT], [r_ps])
                u, r_u = ub.next()
                kb.act(u[:, 2:TB + 2], ps[:, 0:TB], AF.Copy, [r_ps], [r_u])
                kb.v("dve", "tensor_copy", [r_carry[ch]], [r_u], out=u[:, 0:2], in_=carry[:, ch, :])
                kb.v("dve", "tensor_copy", [r_u], [r_carry[ch]], out=carry[:, ch, :], in_=u[:, TB:TB + 2])
                t1, r_t1 = tmp.next()
                kb.act(t1[:], u[:, 2:TB + 2], AF.Identity, [r_u, r_cw, r_cb], [r_t1], scale=cw[:, hf, i, 2:3],
                       bias=cb[:, hf, i:i + 1])
                kb.v("dve", "scalar_tensor_tensor", [r_u, r_t1, r_cw], [r_t1], out=t1[:], in0=u[:, 1:TB + 1],
                     scalar=cw[:, hf, i, 1:2], in1=t1[:], op0=ALU.mult, op1=ALU.add)
                kb.v("dve", "scalar_tensor_tensor", [r_u, r_t1, r_cw], [r_t1], out=t1[:], in0=u[:, 0:TB],
                     scalar=cw[:, hf, i, 0:1], in1=t1[:], op0=ALU.mult, op1=ALU.add)
                cv.append((t1, r_t1))
            (ta, r_ta), (tb_, r_tb) = cv
            sa, r_sa = tmp.next()
            kb.act(sa[:], ta[:], AF.Silu, [r_ta], [r_sa])
            kb.v("dve", "tensor_tensor", [r_sa, r_tb], [r_gT[i]], out=gT[:, i, :], in0=sa[:], in1=tb_[:], op=ALU.mult)
        NTT = TB // 128
        for dq in range(4):
            banks = [kb.bank() for _ in range(NTT)]
            for g in range(NPAIR // 4):
                wd, r_wd = wdp.next()
                kb.dma("sp", wd[:], wdb[dq, :, g * 2048:(g + 1) * 2048].rearrange("p (f c) -> p f c", c=512),
                       [r_wdb[dq]], [r_wd])
                for j in range(4):
                    fc = g * 4 + j
                    for tt in range(NTT):
                        kb.mm(banks[tt][0][:, 0:512], gT[:, fc, tt * 128:(tt + 1) * 128], wd[:, j, :], fc == 0,
                              fc == NPAIR - 1, [r_gT[fc], r_wd], [banks[tt][1]])
            for tt in range(NTT):
                kb.act(ft[:, tt, dq * 512:(dq + 1) * 512], banks[tt][0][:, 0:512], AF.Copy, [banks[tt][1]], [r_ft[tt]])
        for tt in range(NTT):
            xt, r_xt = xin.next()
            rows = slice(t0 + tt * 128, t0 + (tt + 1) * 128)
            kb.dma("sp", xt[:], xm[rows, :], [], [r_xt])
            emit_post(kb, ft[:, tt, :], r_ft[tt], xt[:], r_xt, G3, r_G3, ft[:, tt, :], r_ft[tt])
            kb.dma("pool", xo[rows, :], ft[:, tt, :], [r_ft[tt]], [Res()])
    return kb.finish()


def ffn_host_layouts(w_up, conv_w, conv_b, w_down, g2, g3):
    D, F = D_MODEL, D_FF
    w = w_up.reshape(KC, 128, 2, NPAIR, 128)
    wup = np.ascontiguousarray(w.transpose(3, 1, 2, 0, 4)).reshape(NPAIR, 128, 2 * KC * 128)
    w = w_down.reshape(NPAIR, 128, 4, 512)
    wdn = np.ascontiguousarray(w.transpose(2, 1, 0, 3)).reshape(4, 128, NPAIR * 512)
    c = conv_w.reshape(3, 2, NPAIR, 128)
    cw = np.ascontiguousarray(c.transpose(3, 1, 2, 0)).reshape(128, 2 * NPAIR * 3)
    c = conv_b.reshape(2, NPAIR, 128)
    cb = np.ascontiguousarray(c.transpose(2, 0, 1)).reshape(128, 2 * NPAIR)
    G2 = np.ascontiguousarray(np.broadcast_to(g2[None, :], (128, D)))
    G3 = np.ascontiguousarray(np.broadcast_to(g3[None, :], (128, D)))
    return dict(wup=wup, wdn=wdn, cw=cw, cb=cb, G2=G2, G3=G3, ident=np.eye(128, dtype=np.float32))


_NC_CACHE = {}


def run_ffn(xmid, lay):
    T = SEQ * BATCH // NCORES
    if ("ffn", T) not in _NC_CACHE:
        _NC_CACHE[("ffn", T)] = build_ffn(T)
    nc = _NC_CACHE[("ffn", T)]
    flat = xmid.reshape(BATCH * SEQ, D_MODEL)
    maps = []
    for c in range(NCORES):
        r0 = c * T
        xh = np.zeros((128, D_MODEL), np.float32)
        if r0 % SEQ != 0:
            xh[126:128] = flat[r0 - 2:r0]
        m = dict(lay)
        m["xm"] = flat[r0:r0 + T]
        m["xh"] = xh
        maps.append(m)
    res = run_bass_kernel_spmd(nc, maps, core_ids=list(range(NCORES)))
    return np.concatenate([res.results[c]["xo"] for c in range(NCORES)], axis=0).reshape(BATCH, SEQ, D_MODEL)


DH = 128
HPC = 4
QB = 512


def build_fox(S):
    kb = KB()
    NB = S // QB
    NKB = S // 128
    xb = kb.din("xb", [S, D_MODEL])
    wqkd = kb.din("wqk", [128, KC * 8 * 128])
    wvd = kb.din("wv", [128, KC * 512])
    wfd = kb.din("wf", [128, KC * HPC])
    bfd = kb.din("bf", [HPC, 1])
    G0d = kb.din("G0", [128, D_MODEL])
    identd = kb.din("ident", [128, 128])
    nmd = kb.din("negmask", [128, 128])
    att = kb.dout("att", [S, HPC * DH])
    qTd = kb.dtmp("qTd", [HPC, 128, S], BF16)
    kTd = kb.dtmp("kTd", [HPC, 128, S], BF16)
    V1d = kb.dtmp("V1d", [HPC, S, DH + 1], BF16)
    Fsd = kb.dtmp("Fsd", [3, HPC, S], BF16)
    r_qTd = [Res() for _ in range(HPC)]
    r_kTd = [Res() for _ in range(HPC)]
    r_V1d = [Res() for _ in range(HPC)]
    r_Fsd = Res()

    setup_common(kb, identd)
    kb.junk = Pool(kb, "junk", [128, D_MODEL], F32, 1)
    kb.hn = Pool(kb, "hn", [128, D_MODEL], BF16, 2)
    A2 = kb.sb("A2", [128, 16384], BF16)
    xin = Pool(kb, "xin", None, None, 2, views=[A2[:, 8192:12288].bitcast(F32), A2[:, 12288:16384].bitcast(F32)])
    G0 = kb.sb("G0s", [128, D_MODEL], F32); r_G0 = Res()
    kb.dma("sp", G0[:], G0d, [], [r_G0])
    nmf = kb.sb("nmf", [128, 128], F32); r_nmf = Res()
    negmask = kb.sb("negmask_b", [128, 128], BF16); r_nm = Res()
    kb.dma("sp", nmf[:], nmd, [], [r_nmf])
    kb.v("dve", "tensor_copy", [r_nmf], [r_nm], out=negmask[:], in_=nmf[:])
    bft = kb.sb("bft", [HPC, 2], F32); r_bf = Res()
    kb.dma("sp", bft[:, 0:1], bfd, [], [r_bf])
    kb.v("dve", "tensor_scalar", [r_bf], [r_bf], out=bft[:, 1:2], in0=bft[:, 0:1], scalar1=-1.0, scalar2=None, op0=ALU.mult)
    ones4 = kb.sb("ones4", [HPC, QB], F32); r_ones = Res()
    kb.v("dve", "memset", [], [r_ones], ap=ones4[:], constant=1.0)

    wbig = kb.sb("wbig", [128, KC * 1024 + KC * 512], BF16); r_w = Res()
    wqk = wbig[:, 0:KC * 1024].rearrange("p (k c) -> p k c", k=KC)
    wv = wbig[:, KC * 1024:KC * 1536].rearrange("p (k c) -> p k c", k=KC)
    wf = kb.sb("wf_s", [128, KC, HPC], BF16)
    wqk_flat = wbig[:, 0:KC * 1024]
    for g in range(KC * 1024 // 2048):
        st_, r_st = xin.next()
        kb.dma("sp", st_[:], wqkd[:, g * 2048:(g + 1) * 2048], [], [r_st])
        emit_cast(kb, wqk_flat[:, g * 2048:(g + 1) * 2048], st_[:], [r_st], [r_w])
    wv_flat = wbig[:, KC * 1024:KC * 1536]
    for g in range(KC * 512 // 2048):
        st_, r_st = xin.next()
        kb.dma("sp", st_[:], wvd[:, g * 2048:(g + 1) * 2048], [], [r_st])
        emit_cast(kb, wv_flat[:, g * 2048:(g + 1) * 2048], st_[:], [r_st], [r_w])
    st_, r_st = xin.next()
    kb.dma("sp", st_[:, 0:KC * HPC], wfd, [], [r_st])
    kb.v("dve", "tensor_copy", [r_st], [r_w], out=wf[:].rearrange("p k c -> p (k c)"), in_=st_[:, 0:KC * HPC])

    hT = A2[:, 0:8192].rearrange("p (k c) -> p k c", k=KC); r_hT = Res()
    stq = Pool(kb, "stq", [128, QB], BF16, 4)
    vst = Pool(kb, "vst", [128, HPC, DH + 1], BF16, 3)
    for t_, r_ in zip(vst.tiles, vst.res):
        kb.v("dve", "memset", [], [r_], ap=t_[:], constant=1.0)
    fe = Pool(kb, "fe", [HPC, QB], F32, 2)
    Fb = Pool(kb, "Fb", [HPC, QB], F32, 2)
    fr = Pool(kb, "fr", [HPC, QB], F32, 4)
    fsb = Pool(kb, "fsb", [HPC, QB], BF16, 6)
    scale = float(DH) ** -0.5

    prevF = None
    for b in range(NB):
        t0 = b * QB
        for tt in range(QB // 128):
            xt, r_xt = xin.next()
            kb.dma("sp", xt[:], xb[t0 + tt * 128:t0 + (tt + 1) * 128, :], [], [r_xt])
            emit_norm_T(kb, xt, r_xt, G0, r_G0, hT, r_hT, tt * 128, KC)
        for j in range(8):
            ps, r_ps = kb.bank()
            for k in range(KC):
                kb.mm(ps[:, 0:QB], wqk[:, k, j * 128:(j + 1) * 128], hT[:, k, :], k == 0, k == KC - 1, [r_w, r_hT], [r_ps])
            s_, r_s = stq.next()
            h = j % HPC
            if j < HPC:
                kb.act(s_[:], ps[:, 0:QB], AF.Copy, [r_ps], [r_s], scale=scale)
                kb.dma("pool", qTd[h, :, t0:t0 + QB], s_[:], [r_s], [r_qTd[h]])
            else:
                kb.v("dve", "tensor_copy", [r_ps], [r_s], out=s_[:], in_=ps[:, 0:QB])
                kb.dma("pool", kTd[h, :, t0:t0 + QB], s_[:], [r_s], [r_kTd[h]])
        for tt in range(QB // 128):
            ps, r_ps = kb.bank()
            for k in range(KC):
                kb.mm(ps[:, 0:512], hT[:, k, tt * 128:(tt + 1) * 128], wv[:, k, :], k == 0, k == KC - 1, [r_w, r_hT], [r_ps])
            v_, r_v = vst.next()
            src = ps[:, 0:512].rearrange("p (h c) -> p h c", h=HPC)
            if tt % 2 == 0:
                kb.act(v_[:, :, 0:DH], src, AF.Copy, [r_ps], [r_v])
            else:
                kb.v("dve", "tensor_copy", [r_ps], [r_v], out=v_[:, :, 0:DH], in_=src)
            rows = slice(t0 + tt * 128, t0 + (tt + 1) * 128)
            kb.dma("pool", V1d.rearrange("h t c -> t h c")[rows, :, :], v_[:], [r_v], r_V1d)
        ps, r_ps = kb.bank()
        for k in range(KC):
            kb.mm(ps[0:HPC, 0:QB], wf[:, k, :], hT[:, k, :], k == 0, k == KC - 1, [r_w, r_hT], [r_ps])
        e_, r_e = fe.next()
        kb.act(e_[:], ps[0:HPC, 0:QB], AF.Exp, [r_ps, r_bf], [r_e], scale=-1.0, bias=bft[:, 1:2])
        kb.act(e_[:], e_[:], AF.Ln, [r_e], [r_e], bias=1.0)
        F_, r_F = Fb.next()
        init = 0.0 if prevF is None else prevF[0][:, QB - 1:QB]
        rd = [r_e, r_ones] + ([] if prevF is None else [prevF[1]])
        kb.v("dve", "tensor_tensor_scan", rd, [r_F], out=F_[:], data0=ones4[:], data1=e_[:], initial=init,
             op0=ALU.mult, op1=ALU.subtract)
        prevF = (F_, r_F)
        cur, r_cur = F_, r_F
        for i in range(3):
            fb_, r_fb = fsb.next()
            kb.v("dve", "tensor_copy", [r_cur], [r_fb], out=fb_[:], in_=cur[:])
            kb.dma("pool", Fsd[i, :, t0:t0 + QB], fb_[:], [r_fb], [r_Fsd])
            if i < 2:
                ff, r_ff = fr.next()
                kb.v("dve", "tensor_copy", [r_fb], [r_ff], out=ff[:], in_=fb_[:])
                nr, r_nr = fr.next()
                kb.v("dve", "tensor_tensor", [r_cur, r_ff], [r_nr], out=nr[:], in0=cur[:], in1=ff[:], op=ALU.subtract)
                cur, r_cur = nr, r_nr

    assert S <= 16384
    kT = A2[:, 0:S]; r_kT = Res()
    kt_first = [r_hT, xin.res[0], xin.res[1]]
    V1 = kb.sb("V1", [128, NKB, DH + 1], BF16); r_V1 = Res()
    KF = wbig[0:6, 0:S]; r_KF = Res()
    qTb = Pool(kb, "qTb", [128, QB], BF16, 2)
    QFb = Pool(kb, "QFb", [6, QB], BF16, 2)
    for t_, r_ in zip(QFb.tiles, QFb.res):
        kb.v("dve", "memset", [], [r_], ap=t_[:], constant=-1.0)
    PT = Pool(kb, "PT", [128, QB], BF16, 4)
    ost = Pool(kb, "ost", [128, 4, DH], F32, 2)
    first = True
    for h in range(HPC):
        nsp = 4
        for i in range(nsp):
            cs = slice(i * S // nsp, (i + 1) * S // nsp)
            kb.dma("sp", kT[:, cs], kTd[h, :, cs], [r_kTd[h]], [r_kT] + kt_first)
            kt_first = []
        nvp = max(1, NKB // 8)
        for i in range(nvp):
            ks = slice(i * NKB // nvp, (i + 1) * NKB // nvp)
            kb.dma("sp", V1[:, ks, :], V1d[h].rearrange("(n p) c -> p n c", p=128)[:, ks, :], r_V1d, [r_V1])
        kb.v("dve", "memset", [], [r_KF, r_w] if first else [r_KF], ap=KF, constant=1.0)
        first = False
        kb.dma("sp", wbig[3:6, 0:S], Fsd[:, h, :], [r_Fsd], [r_KF])
        for Q in range(NB):
            q_, r_q = qTb.next()
            kb.dma("sp", q_[:], qTd[h, :, Q * QB:(Q + 1) * QB], [r_qTd[h]], [r_q])
            qf, r_qf = QFb.next()
            kb.dma("sp", qf[0:3, :], Fsd[:, h, Q * QB:(Q + 1) * QB], [r_Fsd], [r_qf])
            acc = [(kb.ps[4 + j], kb.psr[4 + j]) for j in range(4)]
            nkb = 4 * Q + 4
            pend = None
            for kbi in range(nkb + 1):
                if kbi < nkb:
                    j0 = max(0, kbi - 4 * Q)
                    c0 = j0 * 128
                    ps, r_ps = kb.bank(0, 3)
                    diag = kbi >= 4 * Q
                    kb.mm(ps[:, c0:QB], kT[:, kbi * 128:(kbi + 1) * 128], q_[:, c0:QB], True, False, [r_kT, r_q], [r_ps])
                    kb.mm(ps[:, c0:QB], KF[:, kbi * 128:(kbi + 1) * 128], qf[:, c0:QB], False, not diag, [r_KF, r_qf], [r_ps])
                    if diag:
                        kb.mm(ps[:, c0:c0 + 128], kb.identb[:], negmask[:], False, True, [kb.r_ident, r_nm], [r_ps])
                    p_, r_p = PT.next()
                    kb.act(p_[:, c0:QB], ps[:, c0:QB], AF.Exp, [r_ps], [r_p])
                    new = (kbi, j0, p_, r_p)
                else:
                    new = None
                if pend is not None:
                    pk, pj0, pp, r_pp = pend
                    for j in range(pj0, 4):
                        kb.mm(acc[j][0][:, 0:DH + 1], pp[:, j * 128:(j + 1) * 128], V1[:, pk, :], pk == 0, pk == 4 * Q + j,
                              [r_pp, r_V1], [acc[j][1]])
                pend = new
            o_, r_o = ost.next()
            for j in range(4):
                sm, r_sm = kb.small.next()
                kb.v("dve", "reciprocal", [acc[j][1]], [r_sm], out=sm[:, 0:1], in_=acc[j][0][:, DH:DH + 1])
                kb.act(o_[:, j, :], acc[j][0][:, 0:DH], AF.Copy, [acc[j][1], r_sm], [r_o], scale=sm[:, 0:1])
            kb.dma("pool", att.rearrange("(n p) c -> p n c", p=128)[:, Q * 4:Q * 4 + 4, h * DH:(h + 1) * DH], o_[:], [r_o], [Res()])
    return kb.finish()


def fox_host_layouts(w_in, b_f, g0, m):
    D = D_MODEL
    cols_q = w_in[:, (HPC * m) * DH:(HPC * m + HPC) * DH]
    cols_k = w_in[:, D + (HPC * m) * DH:D + (HPC * m + HPC) * DH]
    wqk = np.concatenate([cols_q, cols_k], axis=1).reshape(KC, 128, 8 * 128)
    wqk = np.ascontiguousarray(wqk.transpose(1, 0, 2)).reshape(128, KC * 1024)
    wv = w_in[:, 2 * D + HPC * m * DH:2 * D + (HPC * m + HPC) * DH].reshape(KC, 128, 512)
    wv = np.ascontiguousarray(wv.transpose(1, 0, 2)).reshape(128, KC * 512)
    wf = w_in[:, 3 * D + HPC * m:3 * D + HPC * m + HPC].reshape(KC, 128, HPC)
    wf = np.ascontiguousarray(wf.transpose(1, 0, 2)).reshape(128, KC * HPC)
    bf = np.ascontiguousarray(b_f[HPC * m:HPC * m + HPC].reshape(HPC, 1))
    G0 = np.ascontiguousarray(np.broadcast_to(g0[None, :], (128, D)))
    idx = np.arange(128)
    negmask = np.where(idx[:, None] <= idx[None, :], 0.0, -30000.0).astype(np.float32)
    return dict(wqk=wqk, wv=wv, wf=wf, bf=bf, G0=G0, ident=np.eye(128, dtype=np.float32), negmask=negmask)


def build_wo(T, KCI):
    kb = KB()
    TBW = TB if KCI <= 16 else 256
    NB = T // TBW
    aT = kb.din("aT", [KCI * 128, T])
    wd = kb.din("w", [128, KCI * D_MODEL])
    xr = kb.din("xr", [T, D_MODEL])
    Gd = kb.din("G", [128, D_MODEL])
    identd = kb.din("ident", [128, 128])
    xo = kb.dout("xo", [T, D_MODEL])
    setup_common(kb, identd)
    kb.junk = Pool(kb, "junk", [128, D_MODEL], F32, 1)
    xin = Pool(kb, "xin", [128, D_MODEL], F32, 3 if KCI <= 16 else 2)
    G = kb.sb("Gs", [128, D_MODEL], F32); r_G = Res()
    kb.dma("sp", G[:], Gd, [], [r_G])
    w = kb.sb("wres", [128, KCI, D_MODEL], BF16); r_w = Res()
    for k in range(KCI):
        st_, r_st = xin.next()
        kb.dma("sp", st_[:], wd[:, k * D_MODEL:(k + 1) * D_MODEL], [], [r_st])
        emit_cast(kb, w[:, k, :], st_[:], [r_st], [r_w])
    aTb = kb.sb("aTb", [128, KCI, TBW], BF16); r_a = Res()
    ft = Pool(kb, "ft", [128, D_MODEL], F32, 2)
    for b in range(NB):
        t0 = b * TBW
        KG = 2048 // TBW
        for k0 in range(0, KCI, KG):
            st_, r_st = xin.next()
            kb.dma("sp", st_[:].rearrange("p (k c) -> p k c", k=KG),
                   aT[k0 * 128:(k0 + KG) * 128, t0:t0 + TBW].rearrange("(k p) c -> p k c", p=128), [], [r_st])
            emit_cast(kb, aTb[:, k0:k0 + KG, :], st_[:].rearrange("p (k c) -> p k c", k=KG), [r_st], [r_a])
        for tt in range(TBW // 128):
            f_, r_f = ft.next()
            for nq in range(4):
                ps, r_ps = kb.bank()
                for k in range(KCI):
                    kb.mm(ps[:, 0:512], aTb[:, k, tt * 128:(tt + 1) * 128], w[:, k, nq * 512:(nq + 1) * 512], k == 0, k == KCI - 1,
                          [r_a, r_w], [r_ps])
                if nq % 2 == 0:
                    kb.act(f_[:, nq * 512:(nq + 1) * 512], ps[:, 0:512], AF.Copy, [r_ps], [r_f])
                else:
                    kb.v("dve", "tensor_copy", [r_ps], [r_f], out=f_[:, nq * 512:(nq + 1) * 512], in_=ps[:, 0:512])
            xt, r_xt = xin.next()
            rows = slice(t0 + tt * 128, t0 + (tt + 1) * 128)
            kb.dma("sp", xt[:], xr[rows, :], [], [r_xt])
            emit_post(kb, f_[:], r_f, xt[:], r_xt, G, r_G, f_[:], r_f)
            kb.dma("pool", xo[rows, :], f_[:], [r_f], [Res()])
    return kb.finish()


def wo_host_layout(w, g):
    KCI = w.shape[0] // 128
    wl = np.ascontiguousarray(w.reshape(KCI, 128, D_MODEL).transpose(1, 0, 2)).reshape(128, KCI * D_MODEL)
    G = np.ascontiguousarray(np.broadcast_to(g[None, :], (128, D_MODEL)))
    return dict(w=wl, G=G, ident=np.eye(128, dtype=np.float32))


RH = 8
DK_ = 256
DV_ = 512
RB = 1024
NS = 24


def ret_gammas():
    return [1.0 - 2.0 ** (-5.0 - h) for h in range(RH)]


def build_ret(T, full):
    kb = KB()
    NBLK = T // RB
    NCH = T // 128
    xr = kb.din("xr", [T, D_MODEL])
    wind = kb.din("win", [NS, 128, KC * 512])
    Gd = kb.din("G", [128, D_MODEL])
    identd = kb.din("ident", [128, 128])
    cosd = kb.din("cos2", [T, 256])
    sind = kb.din("sin2", [T, 256])
    DKd = kb.din("DK", [128, D_MODEL])
    if full:
        Mpd = kb.din("Mp", [128, RH * 128])
        DQd = kb.din("DQ", [128, RH])
        coefd = kb.din("coef", [128, 3 * RH])
        Lprev = kb.din("Lprev", [3, RH, 2, 128, DV_])
        og = kb.dout("og", [T, RH * DV_])
    else:
        Lout = kb.dout("L", [RH, 2, 128, DV_])
    wib = kb.dtmp("wib", [NS, 128, KC * 512], BF16)
    qd = kb.dtmp("qd", [T, D_MODEL], BF16)
    kd = kb.dtmp("kd", [T, D_MODEL], BF16)
    vd = kb.dtmp("vd", [T, 2 * D_MODEL], BF16)
    sgd = kb.dtmp("sgd", [T, 2 * D_MODEL], BF16)
    r_wib = [Res() for _ in range(NS)]
    r_qd, r_kd, r_vd, r_sgd = Res(), Res(), Res(), Res()
    slices = list(range(NS)) if full else list(range(4, 16))

    setup_common(kb, identd)
    kb.junk = Pool(kb, "junk", [128, D_MODEL], F32, 1)
    kb.hn = Pool(kb, "hn", [128, D_MODEL], BF16, 2)
    xin = Pool(kb, "xin", [128, D_MODEL], F32, 2)
    G = kb.sb("Gs", [128, D_MODEL], F32); r_G = Res()
    kb.dma("sp", G[:], Gd, [], [r_G])
    DK = kb.sb("DKs", [128, D_MODEL], F32); r_DK = Res()
    kb.dma("sp", DK[:], DKd, [], [r_DK])
    A1 = kb.sb("A1", [128, KC * RB], BF16); r_A1 = Res()
    hT = A1[:].rearrange("p (k c) -> p k c", k=KC)
    wpool = Pool(kb, "wsl", [128, KC * 512], BF16, 2)
    cs = kb.sb("cs", [128, 2, RB // 128, 256], F32); r_cs = Res()
    rt = Pool(kb, "rt", [128, 256], F32, 6)
    so = Pool(kb, "so", [128, 512], BF16, 4)

    for ns in slices:
        for g in range(KC * 512 // D_MODEL):
            st_, r_st = xin.next()
            sb_, r_sb = kb.hn.next()
            sl = slice(g * D_MODEL, (g + 1) * D_MODEL)
            kb.dma("sp", st_[:], wind[ns, :, sl], [], [r_st])
            emit_cast(kb, sb_[:], st_[:], [r_st], [r_sb])
            kb.dma("pool", wib[ns, :, sl], sb_[:], [r_sb], [r_wib[ns]])

    for b in range(NBLK):
        t0 = b * RB
        for tt in range(RB // 128):
            xt, r_xt = xin.next()
            kb.dma("sp", xt[:], xr[t0 + tt * 128:t0 + (tt + 1) * 128, :], [], [r_xt])
            emit_norm_T(kb, xt, r_xt, G, r_G, hT, r_A1, tt * 128, KC)
        kb.dma("sp", cs[:, 0, :, :], cosd[t0:t0 + RB, :].rearrange("(n p) c -> p n c", p=128), [], [r_cs])
        kb.dma("sp", cs[:, 1, :, :], sind[t0:t0 + RB, :].rearrange("(n p) c -> p n c", p=128), [], [r_cs])
        for ns in slices:
            wt, r_wt = wpool.next()
            wv_ = wt[:].rearrange("p (k c) -> p k c", k=KC)
            kb.dma("sp", wt[:], wib[ns], [r_wib[ns]], [r_wt])
            for tt in range(RB // 128):
                ps, r_ps = kb.bank()
                for k in range(KC):
                    kb.mm(ps[:, 0:512], hT[:, k, tt * 128:(tt + 1) * 128], wv_[:, k, :], k == 0, k == KC - 1, [r_A1, r_wt], [r_ps])
                o_, r_o = so.next()
                rows = slice(t0 + tt * 128, t0 + (tt + 1) * 128)
                if ns < 8:
                    psv = ps[:, 0:512].rearrange("p (i t) -> p i t", t=2)
                    ov = o_[:].rearrange("p (i t) -> p i t", t=2)
                    c_ = cs[:, 0, tt, :]
                    s_ = cs[:, 1, tt, :]
                    t1, r_t1 = rt.next(); t2, r_t2 = rt.next()
                    kb.v("dve", "tensor_tensor", [r_ps, r_cs], [r_t1], out=t1[:], in0=psv[:, :, 0], in1=c_, op=ALU.mult)
                    kb.v("dve", "tensor_tensor", [r_ps, r_cs], [r_t2], out=t2[:], in0=psv[:, :, 1], in1=s_, op=ALU.mult)
                    kb.v("pool", "tensor_tensor", [r_t1, r_t2], [r_o], out=ov[:, :, 0], in0=t1[:], in1=t2[:], op=ALU.subtract)
                    t3, r_t3 = rt.next(); t4, r_t4 = rt.next()
                    kb.v("dve", "tensor_tensor", [r_ps, r_cs], [r_t3], out=t3[:], in0=psv[:, :, 0], in1=s_, op=ALU.mult)
                    kb.v("dve", "tensor_tensor", [r_ps, r_cs], [r_t4], out=t4[:], in0=psv[:, :, 1], in1=c_, op=ALU.mult)
                    kb.v("pool", "tensor_tensor", [r_t3, r_t4], [r_o], out=ov[:, :, 1], in0=t3[:], in1=t4[:], op=ALU.add)
                    if ns < 4:
                        kb.dma("pool", qd[rows, ns * 512:(ns + 1) * 512], o_[:], [r_o], [r_qd])
                    else:
                        kb.dma("pool", kd[rows, (ns - 4) * 512:(ns - 3) * 512], o_[:], [r_o], [r_kd])
                elif ns < 16:
                    kb.act(o_[:], ps[:, 0:512], AF.Copy, [r_ps], [r_o])
                    kb.dma("pool", vd[rows, (ns - 8) * 512:(ns - 7) * 512], o_[:], [r_o], [r_vd])
                else:
                    kb.act(o_[:], ps[:, 0:512], AF.Silu, [r_ps], [r_o])
                    kb.dma("pool", sgd[rows, (ns - 16) * 512:(ns - 15) * 512], o_[:], [r_o], [r_sgd])

    gam = ret_gammas()
    Sv = A1[:].bitcast(F32).rearrange("p (h d v) -> p h d v", h=RH, d=2)
    Sbf = wpool.tiles[0][:].rearrange("p (h d v) -> p h d v", h=RH, d=2)
    qkT = wpool.tiles[1][:].rearrange("p (b s j c) -> p b s j c", b=2, s=2, j=16)
    r_S = [Res() for _ in range(RH)]
    r_Sbf = [Res() for _ in range(RH)]
    r_qkT = [Res(), Res()]
    qk_tiles = [t[:].bitcast(BF16) for t in xin.tiles]
    r_qk = xin.res
    kdec = Pool(kb, "kdec", [128, D_MODEL], BF16, 2)
    vh = Pool(kb, "vh", [128, DV_], BF16, 4)
    barrier_w = [r_A1, wpool.res[0], wpool.res[1]] + r_S + r_Sbf + r_qkT
    kb.v("dve", "memset", [], barrier_w, ap=A1[:].bitcast(F32), constant=0.0)
    if full:
        Mp = kb.sb("Mps", [128, RH, 128], F32); r_Mp = Res()
        kb.dma("sp", Mp[:], Mpd.rearrange("p (h n) -> p h n", h=RH), [], [r_Mp])
        DQ = kb.sb("DQs", [128, RH], F32); r_DQ = Res()
        kb.dma("sp", DQ[:], DQd, [], [r_DQ])
        coef = kb.sb("coefs", [128, 3 * RH], F32); r_coef = Res()
        kb.dma("sp", coef[:], coefd, [], [r_coef])
        sgh = Pool(kb, "sgh", [128, DV_], BF16, 3)
        oh = Pool(kb, "oh", [128, DV_], F32, 3)
        ogp = Pool(kb, "ogp", [128, DV_], F32, 3)
        sT = Pool(kb, "sT", [128, 128], BF16, 3)
        for i in range(3):
            for h in range(RH):
                for d in range(2):
                    l_, r_l = oh.next()
                    kb.dma("sp", l_[:], Lprev[i, h, d], [], [r_l])
                    kb.v("dve", "scalar_tensor_tensor", [r_l, r_coef, r_S[h]], [r_S[h]], out=Sv[:, h, d, :], in0=l_[:],
                         scalar=coef[:, i * RH + h:i * RH + h + 1], in1=Sv[:, h, d, :], op0=ALU.mult, op1=ALU.add)
        for h in range(RH):
            kb.act(Sbf[:, h, :, :], Sv[:, h, :, :], AF.Copy, [r_S[h]], [r_Sbf[h]])
    for c in range(NCH):
        rows = slice(c * 128, (c + 1) * 128)
        bi = c % 2
        qk = qk_tiles[bi]
        r_q = r_qk[bi]
        if full:
            kb.dma("sp", qk[:, 0:D_MODEL], qd[rows, :], [r_qd], [r_q])
        kb.dma("sp", qk[:, D_MODEL:2 * D_MODEL], kd[rows, :], [r_kd], [r_q])
        kd_, r_kdec = kdec.next()
        kb.v("dve", "tensor_tensor", [r_q, r_DK], [r_kdec], out=kd_[:], in0=qk[:, D_MODEL:2 * D_MODEL], in1=DK[:], op=ALU.mult)
        if full:
            for s in range(2):
                for half in range(2):
                    ps, r_ps = kb.bank()
                    psb = ps[:].bitcast(BF16)
                    for j in range(8):
                        col = s * D_MODEL + (half * 8 + j) * 128
                        kb.tr(psb[:, j * 128:(j + 1) * 128], qk[:, col:col + 128], kb.identb[:], [r_q, kb.r_ident], [r_ps])
                    src = psb[:, 0:1024].rearrange("p (j c) -> p j c", c=128)
                    dst = qkT[:, bi, s, half * 8:half * 8 + 8, :]
                    if half == 0:
                        kb.act(dst, src, AF.Copy, [r_ps], [r_qkT[bi]])
                    else:
                        kb.v("dve", "tensor_copy", [r_ps], [r_qkT[bi]], out=dst, in_=src)
        for h in range(RH):
            v_, r_v = vh.next()
            kb.dma("sp", v_[:], vd[rows, h * DV_:(h + 1) * DV_], [r_vd], [r_v])
            if full:
                g_, r_g = sgh.next()
                kb.dma("sp", g_[:], sgd[rows, h * DV_:(h + 1) * DV_], [r_sgd], [r_g])
                ps, r_ps = kb.bank()
                for d in range(2):
                    kb.mm(ps[:, 0:128], qkT[:, bi, 1, h * 2 + d, :], qkT[:, bi, 0, h * 2 + d, :], d == 0, d == 1, [r_qkT[bi]], [r_ps])
                st_, r_st = sT.next()
                kb.v("dve", "tensor_tensor", [r_ps, r_Mp], [r_st], out=st_[:], in0=ps[:, 0:128], in1=Mp[:, h, :], op=ALU.mult)
                po, r_po = kb.bank()
                kb.mm(po[:, 0:DV_], st_[:], v_[:], True, False, [r_st, r_v], [r_po])
                for d in range(2):
                    kb.mm(po[:, 0:DV_], qkT[:, bi, 0, h * 2 + d, :], Sbf[:, h, d, :], False, d == 1, [r_qkT[bi], r_Sbf[h]], [r_po])
                o_, r_o = oh.next()
                kb.act(o_[:], po[:, 0:DV_], AF.Copy, [r_po, r_DQ], [r_o], scale=DQ[:, h:h + 1])
                rstd, r_rs = emit_rstd(kb, o_[:], r_o, DV_)
                og_, r_og = ogp.next()
                kb.v("dve", "scalar_tensor_tensor", [r_o, r_rs, r_g], [r_og], out=og_[:], in0=o_[:], scalar=rstd, in1=g_[:],
                     op0=ALU.mult, op1=ALU.mult)
                kb.dma("pool", og[rows, h * DV_:(h + 1) * DV_], og_[:], [r_og], [Res()])
            for d in range(2):
                pS, r_pS = kb.bank()
                kb.mm(pS[:, 0:DV_], kd_[:, h * DK_ + d * 128:h * DK_ + (d + 1) * 128], v_[:], True, True, [r_kdec, r_v], [r_pS])
                kb.v("dve", "scalar_tensor_tensor", [r_pS, r_S[h]], [r_S[h]], out=Sv[:, h, d, :], in0=Sv[:, h, d, :],
                     scalar=float(gam[h] ** 128), in1=pS[:, 0:DV_], op0=ALU.mult, op1=ALU.add)
            if full:
                kb.act(Sbf[:, h, :, :], Sv[:, h, :, :], AF.Copy, [r_S[h]], [r_Sbf[h]])
    if not full:
        for h in range(RH):
            for d in range(2):
                kb.dma("pool", Lout[h, d], Sv[:, h, d, :], [r_S[h]], [Res()])
    return kb.finish()


def ret_host_layouts(w_in, g0):
    w = w_in.reshape(KC, 128, NS, 512)
    win = np.ascontiguousarray(w.transpose(2, 1, 0, 3)).reshape(NS, 128, KC * 512)
    G = np.ascontiguousarray(np.broadcast_to(g0[None, :], (128, D_MODEL)))
    return dict(win=win, G=G, ident=np.eye(128, dtype=np.float32))


def ret_const_tables(pos0, T, j):
    theta = (1.0 / (10000.0 ** np.linspace(0.0, 1.0, DK_ // 2, dtype=np.float32))).astype(np.float32)
    ang = (np.arange(pos0, pos0 + T, dtype=np.float32)[:, None] * theta[None, :]).astype(np.float32)
    cos = np.cos(ang).astype(np.float32)
    sin = np.sin(ang).astype(np.float32)
    cos2 = np.ascontiguousarray(np.concatenate([cos, cos], axis=1))
    sin2 = np.ascontiguousarray(np.concatenate([sin, sin], axis=1))
    gam = np.array(ret_gammas(), dtype=np.float64)
    lg = np.log1p(-(2.0 ** (-5.0 - np.arange(RH, dtype=np.float64))))
    m = np.arange(128, dtype=np.float64)
    ks = DK_ ** -0.5
    DK = np.exp((127.0 - m)[:, None] * lg[None, :]) * ks
    DK = np.ascontiguousarray(np.repeat(DK, DK_, axis=1)).astype(np.float32)
    DQ = np.exp((m + 1.0)[:, None] * lg[None, :]).astype(np.float32)
    Mp = np.exp(-(m + 1.0)[:, None, None] * lg[None, :, None]) * ks
    Mp = Mp * (m[None, None, :] >= m[:, None, None])
    Mp = np.ascontiguousarray(Mp.reshape(128, RH * 128)).astype(np.float32)
    coef = np.zeros((3, RH), np.float64)
    for i in range(3):
        if i < j:
            coef[i] = np.exp(T * (j - 1 - i) * lg)
    coef = np.ascontiguousarray(np.broadcast_to(coef.reshape(1, 3 * RH), (128, 3 * RH))).astype(np.float32)
    return dict(cos2=cos2, sin2=sin2, DK=DK, DQ=DQ, Mp=Mp, coef=coef)


def _get(key, fn):
    if key not in _NC_CACHE:
        _NC_CACHE[key] = fn()
    return _NC_CACHE[key]


def _run(nc, maps):
    return run_bass_kernel_spmd(nc, maps, core_ids=list(range(NCORES))).results


def run_wo(a, xres, w, g):
    T = SEQ * BATCH // NCORES
    KCI = w.shape[0] // 128
    nc = _get(("wo", T, KCI), lambda: build_wo(T, KCI))
    lay = wo_host_layout(w, g)
    af = a.reshape(BATCH * SEQ, KCI * 128)
    xf = xres.reshape(BATCH * SEQ, D_MODEL)
    maps = []
    for c in range(NCORES):
        m = dict(lay)
        m["aT"] = np.ascontiguousarray(af[c * T:(c + 1) * T].T)
        m["xr"] = xf[c * T:(c + 1) * T]
        maps.append(m)
    res = _run(nc, maps)
    return np.concatenate([res[c]["xo"] for c in range(NCORES)], axis=0).reshape(BATCH, SEQ, D_MODEL)


def run_fox(x, w_in, b_f, g0):
    nc = _get(("fox", SEQ), lambda: build_fox(SEQ))
    G = NCORES // BATCH
    maps = []
    for c in range(NCORES):
        m = dict(fox_host_layouts(w_in, b_f, g0, c % G))
        m["xb"] = x[c // G]
        maps.append(m)
    res = _run(nc, maps)
    att = np.empty((BATCH, SEQ, D_MODEL), np.float32)
    for c in range(NCORES):
        att[c // G, :, (c % G) * HPC * DH:(c % G + 1) * HPC * DH] = res[c]["att"]
    return att


def run_ret(x1, w_in, g0):
    T = SEQ * BATCH // NCORES
    G = NCORES // BATCH
    lay = ret_host_layouts(w_in, g0)
    xf = x1.reshape(BATCH * SEQ, D_MODEL)
    nc1 = _get(("ret", T, False), lambda: build_ret(T, False))
    nc2 = _get(("ret", T, True), lambda: build_ret(T, True))
    consts = [ret_const_tables((c % G) * T, T, c % G) for c in range(NCORES)]
    maps = []
    for c in range(NCORES):
        m = dict(lay)
        for k_ in ("cos2", "sin2", "DK"):
            m[k_] = consts[c][k_]
        m["xr"] = xf[c * T:(c + 1) * T]
        maps.append(m)
    res = _run(nc1, maps)
    L = [res[c]["L"] for c in range(NCORES)]
    maps = []
    for c in range(NCORES):
        m = dict(lay)
        m.update(consts[c])
        m["xr"] = xf[c * T:(c + 1) * T]
        b = c // G
        m["Lprev"] = np.ascontiguousarray(np.stack([L[b * G + i] for i in range(3)]))
        maps.append(m)
    res = _run(nc2, maps)
    return np.concatenate([res[c]["og"] for c in range(NCORES)], axis=0).reshape(BATCH, SEQ, 2 * D_MODEL)


def kernel(x, norm_g, fox_w_in, fox_b_f, fox_w_o, ret_w_in, ret_w_o,
           ffn_w_up, ffn_conv_w, ffn_conv_b, ffn_w_down):
    f = lambda a: np.ascontiguousarray(np.asarray(a, dtype=np.float32))
    x, norm_g = f(x), f(norm_g)
    fox_w_in, fox_b_f, fox_w_o = f(fox_w_in), f(fox_b_f), f(fox_w_o)
    ret_w_in, ret_w_o = f(ret_w_in), f(ret_w_o)
    ffn_w_up, ffn_conv_w, ffn_conv_b, ffn_w_down = f(ffn_w_up), f(ffn_conv_w), f(ffn_conv_b), f(ffn_w_down)
    att = run_fox(x, fox_w_in[0], fox_b_f[0], norm_g[0, 0])
    xm = run_wo(att, x, fox_w_o[0], norm_g[0, 1])
    del att
    x1 = run_ffn(xm, ffn_host_layouts(ffn_w_up[0], ffn_conv_w[0], ffn_conv_b[0], ffn_w_down[0], norm_g[0, 2], norm_g[0, 3]))
    og = run_ret(x1, ret_w_in[0], norm_g[1, 0])
    xm = run_wo(og, x1, ret_w_o[0], norm_g[1, 1])
    del og
    out = run_ffn(xm, ffn_host_layouts(ffn_w_up[1], ffn_conv_w[1], ffn_conv_b[1], ffn_w_down[1], norm_g[1, 2], norm_g[1, 3]))
    return out.astype(np.float32)
```

```python
import contextlib
import numpy as np
import concourse.bass as bass
import concourse.mybir as mybir
from concourse.bass_utils import run_bass_kernel_spmd

F32 = mybir.dt.float32
BF16 = mybir.dt.bfloat16
AF = mybir.ActivationFunctionType
ALU = mybir.AluOpType
AX = mybir.AxisListType

D_MODEL = 2048
SEQ = 16384
BATCH = 2
D_FF = 5632
NORM_EPS = 1e-6
NCORES = 8

SAME_ENG_SYNC = True
SEM_GEN = 30000
SEM_DMA_GEN = 1500


class Res:
    __slots__ = ("name", "w", "rs")

    def __init__(self, name=""):
        self.name = name
        self.w = None
        self.rs = []


class _Op:
    __slots__ = ("eng", "fn", "deps", "ev", "dma", "inc")


class Sched:
    ENGS = ("pe", "act", "dve", "pool", "sp")

    def __init__(self, nc, ndma=10):
        self.nc = nc
        self.ops = {e: [] for e in self.ENGS}
        self.cnt = {e: 0 for e in self.ENGS}
        self.dman = {e: 0 for e in self.ENGS}
        self.dmahist = {e: [] for e in self.ENGS}
        self.ndma = ndma
        self.ncc = 0
        self.sems = {}
        self.semstack = contextlib.ExitStack()
        self.waited = {e: {} for e in self.ENGS}
        self.fin = {}

    def op(self, eng, fn, reads=(), writes=(), dma=False, inc=None):
        o = _Op()
        o.eng = eng
        o.fn = fn
        o.dma = dma
        o.inc = inc
        deps = []
        for r in reads:
            if r.w is not None:
                deps.append(r.w)
        for r in writes:
            if r.w is not None:
                deps.append(r.w)
            deps.extend(r.rs)
        if inc is not None:
            o.ev = ("cc", self.ncc, inc)
            self.ncc += 1
        elif dma:
            n = self.dman[eng]
            self.dman[eng] = n + 1
            rnd = n // self.ndma
            o.ev = ("d_" + eng, (n % self.ndma, rnd // SEM_DMA_GEN), 16 * (rnd % SEM_DMA_GEN + 1))
            if n >= self.ndma:
                deps.append(self.dmahist[eng][n - self.ndma])
            self.dmahist[eng].append(o)
        else:
            c = self.cnt[eng]
            self.cnt[eng] = c + 1
            o.ev = ("c_" + eng, c // SEM_GEN, c % SEM_GEN + 1)
        dd = []
        seen = set()
        for d in deps:
            if id(d) in seen:
                continue
            seen.add(id(d))
            if (not d.dma) and d.eng == eng:
                if eng == "pe" or not SAME_ENG_SYNC:
                    continue
            dd.append(d)
        o.deps = dd
        for r in reads:
            r.rs.append(o)
        for r in writes:
            r.w = o
            r.rs = []
        self.ops[eng].append(o)
        return o

    def flush(self):
        nc = self.nc
        for e in self.ENGS:
            for o in self.ops[e]:
                k = (o.ev[0], o.ev[1])
                if k not in self.sems:
                    self.sems[k] = self.semstack.enter_context(nc.semaphore("s%d" % len(self.sems)))
                self.fin[k] = max(self.fin.get(k, 0), o.ev[2])
        sems = self.sems
        fin = dict(self.fin)
        ops = self.ops
        self.ops = {e: [] for e in self.ENGS}
        with nc.Block() as block:
            def run(eng_name):
                def body(eng):
                    waited = self.waited[eng_name]
                    for o in ops[eng_name]:
                        for d in o.deps:
                            k = (d.ev[0], d.ev[1])
                            if waited.get(k, 0) < d.ev[2]:
                                eng.wait_ge(sems[k], d.ev[2])
                                waited[k] = d.ev[2]
                        ins = o.fn(eng)
                        ins.then_inc(sems[(o.ev[0], o.ev[1])], o.inc if o.inc is not None else (16 if o.dma else 1))
                    for k, v in fin.items():
                        if waited.get(k, 0) < v:
                            eng.wait_ge(sems[k], v)
                            waited[k] = v
                return body

            block.tensor(run("pe"))
            block.scalar(run("act"))
            block.vector(run("dve"))
            block.gpsimd(run("pool"))
            block.sync(run("sp"))


class Pool:
    def __init__(self, kb, name, shape, dt, n, views=None):
        if views is not None:
            self.tiles = list(views)
            n = len(views)
        else:
            self.tiles = [kb.sb("%s%d" % (name, i), shape, dt) for i in range(n)]
        self.res = [Res("%s%d" % (name, i)) for i in range(n)]
        self.i = 0

    def next(self):
        i = self.i % len(self.tiles)
        self.i += 1
        return self.tiles[i], self.res[i]


class KB:
    def __init__(self):
        self.nc = bass.Bass("TRN2", target_bir_lowering=False)
        self.S = Sched(self.nc)
        self.st = None
        self.phase = 0

    def begin(self, ident_in):
        self.phase += 1
        self.st = contextlib.ExitStack()
        self.ps = []
        self.psr = []
        self.psi = 0
        setup_common(self, ident_in)

    def end(self):
        self.S.flush()
        self.st.close()
        self.st = None

    def sb(self, name, shape, dt):
        return self.st.enter_context(self.nc.sbuf_tensor("p%d_%s" % (self.phase, name), list(shape), dt))

    def alloc_psum(self, n=8):
        for i in range(n):
            self.ps.append(self.st.enter_context(self.nc.psum_tensor("p%d_ps%d" % (self.phase, i), [128, 512], F32)))
            self.psr.append(Res("ps%d" % i))

    def bank(self, lo=0, hi=8):
        i = lo + self.psi % (hi - lo)
        self.psi += 1
        return self.ps[i], self.psr[i]

    def din(self, name, shape, dt=F32):
        return self.nc.dram_tensor(name, list(shape), dt, kind="ExternalInput").ap()

    def dout(self, name, shape, dt=F32):
        return self.nc.dram_tensor(name, list(shape), dt, kind="ExternalOutput").ap()

    def dtmp(self, name, shape, dt=F32):
        return self.nc.dram_tensor(name, list(shape), dt, kind="Internal").ap()

    def collective(self, kind, groups, in_ap, out_ap, r, w, flush=True):
        if not hasattr(self, "r_cc"):
            self.r_cc = Res("cc")
        self.S.op("pool", lambda e: e.collective_compute(kind, ALU.bypass, replica_groups=groups, ins=[in_ap], outs=[out_ap]),
                  r, list(w) + [self.r_cc], dma=True, inc=1)
        if flush:
            self.S.flush()

    def dma(self, q, out, in_, r, w):
        return self.S.op(q, lambda e: e.dma_start(out=out, in_=in_), r, w, dma=True)

    def act(self, out, in_, func, r, w, **kw):
        return self.S.op("act", lambda e: e.activation(out=out, in_=in_, func=func, **kw), r, w)

    def mm(self, out, lhsT, rhs, start, stop, r, w):
        return self.S.op("pe", lambda e: e.matmul(out, lhsT=lhsT, rhs=rhs, start=start, stop=stop,
                                                  skip_group_check=True), r, w)

    def tr(self, out, in_, ident, r, w):
        return self.S.op("pe", lambda e: e.transpose(out=out, in_=in_, identity=ident), r, w)

    def v(self, eng, meth, r, w, **kw):
        return self.S.op(eng, lambda e: getattr(e, meth)(**kw), r, w)


def emit_rstd(kb, x_ap, r_x, ncols, P=128):
    junk, r_j = kb.junk.next()
    st, r_s = kb.small.next()
    kb.act(junk[0:P, 0:ncols], x_ap, AF.Square, [r_x], [r_j])
    kb.v("dve", "reduce_sum", [r_j], [r_s], out=st[0:P, 0:1], in_=junk[0:P, 0:ncols], axis=AX.X)
    kb.v("dve", "tensor_scalar", [r_s], [r_s], out=st[0:P, 1:2], in0=st[0:P, 0:1], scalar1=1.0 / ncols,
         scalar2=NORM_EPS, op0=ALU.mult, op1=ALU.add)
    kb.act(st[0:P, 2:3], st[0:P, 1:2], AF.Sqrt, [r_s], [r_s])
    kb.v("dve", "reciprocal", [r_s], [r_s], out=st[0:P, 3:4], in_=st[0:P, 2:3])
    return st[0:P, 3:4], r_s


def emit_norm_T(kb, x_tile, r_x, G, r_G, hT, r_hT, col0, KC, cp_eng="act"):
    rstd, r_s = emit_rstd(kb, x_tile[:, 0:KC * 128], r_x, KC * 128)
    hn, r_hn = kb.hn.next()
    kb.v("dve", "scalar_tensor_tensor", [r_x, r_s, r_G], [r_hn], out=hn[:, 0:KC * 128], in0=x_tile[:, 0:KC * 128],
         scalar=rstd, in1=G[:, 0:KC * 128], op0=ALU.mult, op1=ALU.mult)
    emit_T(kb, hn, r_hn, hT, r_hT, col0, KC, cp_eng)


def emit_T(kb, hn, r_hn, hT, r_hT, col0, KC, cp_eng="act"):
    for k0 in range(0, KC, 8):
        n = min(8, KC - k0)
        ps, r_ps = kb.bank()
        psb = ps[:].bitcast(BF16)
        for j in range(n):
            kb.tr(psb[:, j * 128:(j + 1) * 128], hn[:, (k0 + j) * 128:(k0 + j + 1) * 128], kb.identb[:],
                  [r_hn, kb.r_ident], [r_ps])
        src = psb[:, 0:n * 128].rearrange("p (k c) -> p k c", c=128)
        dst = hT[:, k0:k0 + n, col0:col0 + 128]
        if cp_eng == "act":
            kb.act(dst, src, AF.Copy, [r_ps], [r_hT])
        else:
            kb.v(cp_eng, "tensor_copy", [r_ps], [r_hT], out=dst, in_=src)


def emit_post(kb, f_ap, r_f, x_ap, r_x, G, r_G, out_tile, r_out, ncols=D_MODEL):
    rstd, r_s = emit_rstd(kb, f_ap, r_f, ncols)
    kb.v("dve", "scalar_tensor_tensor", [r_f, r_s, r_G], [r_out], out=out_tile, in0=f_ap, scalar=rstd, in1=G[:, 0:ncols],
         op0=ALU.mult, op1=ALU.mult)
    kb.v("pool", "tensor_tensor", [r_out, r_x], [r_out], out=out_tile, in0=out_tile, in1=x_ap, op=ALU.add)


def setup_common(kb, ident_in):
    kb.alloc_psum(8)
    idf = kb.sb("idf", [128, 128], F32)
    kb.identb = kb.sb("identb", [128, 128], BF16)
    kb.r_ident = Res("ident")
    r_idf = Res("idf")
    kb.dma("sp", idf[:], ident_in, [], [r_idf])
    kb.v("dve", "tensor_copy", [r_idf], [kb.r_ident], out=kb.identb[:], in_=idf[:])
    kb.small = Pool(kb, "small", [128, 4], F32, 6)


_cast_rr = [0]


def emit_cast(kb, out, in_, r, w):
    i = _cast_rr[0] % 3
    _cast_rr[0] += 1
    if i == 0:
        kb.v("dve", "tensor_copy", r, w, out=out, in_=in_)
    elif i == 1:
        kb.act(out, in_, AF.Copy, r, w)
    else:
        kb.v("pool", "tensor_copy", r, w, out=out, in_=in_)


TB = 512
NPAIR = D_FF // 128
KC = D_MODEL // 128


def ffn_inputs(kb, pfx):
    return dict(wup=kb.din(pfx + "wup", [NPAIR, 128, 2 * KC * 128]), wdn=kb.din(pfx + "wdn", [4, 128, NPAIR * 512]),
                G2=kb.din(pfx + "G2", [128, D_MODEL]), G3=kb.din(pfx + "G3", [128, D_MODEL]),
                cw=kb.din(pfx + "cw", [128, 2 * NPAIR * 3]), cb=kb.din(pfx + "cb", [128, 2 * NPAIR]))


def emit_ffn(kb, T, pfx, io, xm, r_xm, halo_all, r_halo, sel4d, identd, xo, r_xo):
    NB = T // TB
    wup, wdn, G2d, G3d, cwd, cbd = io["wup"], io["wdn"], io["G2"], io["G3"], io["cw"], io["cb"]
    wub = kb.dtmp(pfx + "wub", [NPAIR, 128, 2 * KC * 128], BF16)
    wdb = kb.dtmp(pfx + "wdb", [4, 128, NPAIR * 512], BF16)

    kb.begin(identd)
    kb.junk = Pool(kb, "junk", [128, D_MODEL], F32, 1)
    kb.hn = Pool(kb, "hn", [128, D_MODEL], BF16, 2)
    xin = Pool(kb, "xin", [128, D_MODEL], F32, 2)
    G2 = kb.sb("G2s", [128, D_MODEL], F32); r_G2 = Res()
    G3 = kb.sb("G3s", [128, D_MODEL], F32); r_G3 = Res()
    cw = kb.sb("cws", [128, 2, NPAIR, 3], F32); r_cw = Res()
    cb = kb.sb("cbs", [128, 2, NPAIR], F32); r_cb = Res()
    kb.dma("sp", G2[:], G2d, [], [r_G2])
    kb.dma("sp", G3[:], G3d, [], [r_G3])
    kb.dma("sp", cw[:], cwd.rearrange("p (h i t) -> p h i t", h=2, i=NPAIR), [], [r_cw])
    kb.dma("sp", cb[:], cbd.rearrange("p (h i) -> p h i", h=2), [], [r_cb])

    hT = kb.sb("hT", [128, KC, TB], BF16); r_hT = Res()
    gT = kb.sb("gT", [128, NPAIR, TB], BF16); r_gT = [Res() for _ in range(NPAIR)]
    ft = kb.sb("ft", [128, TB // 128, D_MODEL], F32); r_ft = [Res() for _ in range(TB // 128)]
    wupp = Pool(kb, "wupp", [128, 2, KC, 128], BF16, 2)
    wdp = Pool(kb, "wdp", [128, 4, 512], BF16, 3)
    ub = Pool(kb, "ub", [128, TB + 2], F32, 4)
    tmp = Pool(kb, "tmp", [128, TB], F32, 5)
    carry = kb.sb("carry", [128, 2 * NPAIR, 2], F32); r_carry = [Res() for _ in range(2 * NPAIR)]
    hTh = kb.sb("hTh", [128, KC, 128], BF16); r_hTh = Res()
    r_wub = [Res() for _ in range(NPAIR)]
    r_wdb = [Res() for _ in range(4)]

    for i in range(NPAIR):
        for hf in range(2):
            st_, r_st = xin.next()
            sb_, r_sb = kb.hn.next()
            sl = slice(hf * KC * 128, (hf + 1) * KC * 128)
            kb.dma("sp", st_[:], wup[i, :, sl], [], [r_st])
            emit_cast(kb, sb_[:], st_[:], [r_st], [r_sb])
            kb.dma("pool", wub[i, :, sl], sb_[:], [r_sb], [r_wub[i]])
    for dq in range(4):
        for g in range(NPAIR * 512 // D_MODEL):
            st_, r_st = xin.next()
            sb_, r_sb = kb.hn.next()
            sl = slice(g * D_MODEL, (g + 1) * D_MODEL)
            kb.dma("sp", st_[:], wdn[dq, :, sl], [], [r_st])
            emit_cast(kb, sb_[:], st_[:], [r_st], [r_sb])
            kb.dma("pool", wdb[dq, :, sl], sb_[:], [r_sb], [r_wdb[dq]])

    sel4 = kb.sb("sel4", [128, 4], F32); r_sel = Res()
    kb.dma("sp", sel4[:], sel4d, [], [r_sel])
    xt, r_xt = xin.next()
    kb.v("dve", "memset", [], [r_xt], ap=xt[:], constant=0.0)
    for i in range(4):
        ct, r_ct = ft[:, i, :], r_ft[i]
        kb.v("pool", "memset", [], [r_ct], ap=ct, constant=0.0)
        kb.dma("sp", ft[126:128, i, :], halo_all[2 * i:2 * i + 2, :], [r_halo], [r_ct])
        kb.v("dve", "scalar_tensor_tensor", [r_ct, r_sel, r_xt], [r_xt], out=xt[:], in0=ct, scalar=sel4[:, i:i + 1],
             in1=xt[:], op0=ALU.mult, op1=ALU.add)
    emit_norm_T(kb, xt, r_xt, G2, r_G2, hTh, r_hTh, 0, KC)
    psc, r_psc = kb.bank()
    for i in range(NPAIR):
        wt, r_wt = wupp.next()
        kb.dma("sp", wt[:], wub[i].rearrange("p (h k c) -> p h k c", h=2, k=KC), [r_wub[i]], [r_wt])
        for hf in range(2):
            ch = hf * NPAIR + i
            for k in range(KC):
                kb.mm(psc[:, ch * 2:ch * 2 + 2], wt[:, hf, k, :], hTh[:, k, 126:128], k == 0, k == KC - 1,
                      [r_wt, r_hTh], [r_psc])
    kb.v("dve", "tensor_copy", [r_psc], r_carry, out=carry[:].rearrange("p c t -> p (c t)"), in_=psc[:, 0:4 * NPAIR])

    for b in range(NB):
        t0 = b * TB
        for tt in range(TB // 128):
            xt, r_xt = xin.next()
            kb.dma("sp", xt[:], xm[t0 + tt * 128:t0 + (tt + 1) * 128, :], [r_xm], [r_xt])
            emit_norm_T(kb, xt, r_xt, G2, r_G2, hT, r_hT, tt * 128, KC)
        for i in range(NPAIR):
            wt, r_wt = wupp.next()
            kb.dma("sp", wt[:], wub[i].rearrange("p (h k c) -> p h k c", h=2, k=KC), [r_wub[i]], [r_wt])
            cv = []
            for hf in range(2):
                ch = hf * NPAIR + i
                ps, r_ps = kb.bank()
                for k in range(KC):
                    kb.mm(ps[:, 0:TB], wt[:, hf, k, :], hT[:, k, :], k == 0, k == KC - 1, [r_wt, r_hT], [r_ps])
                u, r_u = ub.next()
                kb.act(u[:, 2:TB + 2], ps[:, 0:TB], AF.Copy, [r_ps], [r_u])
                kb.v("dve", "tensor_copy", [r_carry[ch]], [r_u], out=u[:, 0:2], in_=carry[:, ch, :])
                kb.v("dve", "tensor_copy", [r_u], [r_carry[ch]], out=carry[:, ch, :], in_=u[:, TB:TB + 2])
                t1, r_t1 = tmp.next()
                kb.act(t1[:], u[:, 2:TB + 2], AF.Identity, [r_u, r_cw, r_cb], [r_t1], scale=cw[:, hf, i, 2:3],
                       bias=cb[:, hf, i:i + 1])
                kb.v("dve", "scalar_tensor_tensor", [r_u, r_t1, r_cw], [r_t1], out=t1[:], in0=u[:, 1:TB + 1],
                     scalar=cw[:, hf, i, 1:2], in1=t1[:], op0=ALU.mult, op1=ALU.add)
                kb.v("dve", "scalar_tensor_tensor", [r_u, r_t1, r_cw], [r_t1], out=t1[:], in0=u[:, 0:TB],
                     scalar=cw[:, hf, i, 0:1], in1=t1[:], op0=ALU.mult, op1=ALU.add)
                cv.append((t1, r_t1))
            (ta, r_ta), (tb_, r_tb) = cv
            sa, r_sa = tmp.next()
            kb.act(sa[:], ta[:], AF.Silu, [r_ta], [r_sa])
            kb.v("dve", "tensor_tensor", [r_sa, r_tb], [r_gT[i]], out=gT[:, i, :], in0=sa[:], in1=tb_[:], op=ALU.mult)
        NTT = TB // 128
        for dq in range(4):
            banks = [kb.bank() for _ in range(NTT)]
            for g in range(NPAIR // 4):
                wd, r_wd = wdp.next()
                kb.dma("sp", wd[:], wdb[dq, :, g * 2048:(g + 1) * 2048].rearrange("p (f c) -> p f c", c=512),
                       [r_wdb[dq]], [r_wd])
                for j in range(4):
                    fc = g * 4 + j
                    for tt in range(NTT):
                        kb.mm(banks[tt][0][:, 0:512], gT[:, fc, tt * 128:(tt + 1) * 128], wd[:, j, :], fc == 0,
                              fc == NPAIR - 1, [r_gT[fc], r_wd], [banks[tt][1]])
            for tt in range(NTT):
                kb.act(ft[:, tt, dq * 512:(dq + 1) * 512], banks[tt][0][:, 0:512], AF.Copy, [banks[tt][1]], [r_ft[tt]])
        for tt in range(NTT):
            xt, r_xt = xin.next()
            rows = slice(t0 + tt * 128, t0 + (tt + 1) * 128)
            kb.dma("sp", xt[:], xm[rows, :], [r_xm], [r_xt])
            emit_post(kb, ft[:, tt, :], r_ft[tt], xt[:], r_xt, G3, r_G3, ft[:, tt, :], r_ft[tt])
            kb.dma("pool", xo[rows, :], ft[:, tt, :], [r_ft[tt]], [r_xo])
    kb.end()


def ffn_host_layouts(w_up, conv_w, conv_b, w_down, g2, g3):
    D, F = D_MODEL, D_FF
    w = w_up.reshape(KC, 128, 2, NPAIR, 128)
    wup = np.ascontiguousarray(w.transpose(3, 1, 2, 0, 4)).reshape(NPAIR, 128, 2 * KC * 128)
    w = w_down.reshape(NPAIR, 128, 4, 512)
    wdn = np.ascontiguousarray(w.transpose(2, 1, 0, 3)).reshape(4, 128, NPAIR * 512)
    c = conv_w.reshape(3, 2, NPAIR, 128)
    cw = np.ascontiguousarray(c.transpose(3, 1, 2, 0)).reshape(128, 2 * NPAIR * 3)
    c = conv_b.reshape(2, NPAIR, 128)
    cb = np.ascontiguousarray(c.transpose(2, 0, 1)).reshape(128, 2 * NPAIR)
    G2 = np.ascontiguousarray(np.broadcast_to(g2[None, :], (128, D)))
    G3 = np.ascontiguousarray(np.broadcast_to(g3[None, :], (128, D)))
    return dict(wup=wup, wdn=wdn, cw=cw, cb=cb, G2=G2, G3=G3, ident=np.eye(128, dtype=np.float32))


DH = 128
HPC = 4
QB = 512


def fox_inputs(kb):
    return dict(wqk=kb.din("wqk", [128, KC * 8 * 128]), wv=kb.din("wv", [128, KC * 512]), wf=kb.din("wf", [128, KC * HPC]),
                bf=kb.din("bf", [HPC, 1]), G0=kb.din("G0", [128, D_MODEL]), negmask=kb.din("negmask", [128, 128]))


def emit_fox(kb, S, io, xb, identd, att, r_att):
    NB = S // QB
    NKB = S // 128
    wqkd, wvd, wfd, bfd, G0d, nmd = io["wqk"], io["wv"], io["wf"], io["bf"], io["G0"], io["negmask"]
    qTd = kb.dtmp("qTd", [HPC, 128, S], BF16)
    kTd = kb.dtmp("kTd", [HPC, 128, S], BF16)
    V1d = kb.dtmp("V1d", [HPC, S, DH + 1], BF16)
    Fsd = kb.dtmp("Fsd", [3, HPC, S], BF16)
    r_qTd = [Res() for _ in range(HPC)]
    r_kTd = [Res() for _ in range(HPC)]
    r_V1d = [Res() for _ in range(HPC)]
    r_Fsd = Res()

    kb.begin(identd)
    kb.junk = Pool(kb, "junk", [128, D_MODEL], F32, 1)
    kb.hn = Pool(kb, "hn", [128, D_MODEL], BF16, 2)
    A2 = kb.sb("A2", [128, 16384], BF16)
    xin = Pool(kb, "xin", None, None, 2, views=[A2[:, 8192:12288].bitcast(F32), A2[:, 12288:16384].bitcast(F32)])
    G0 = kb.sb("G0s", [128, D_MODEL], F32); r_G0 = Res()
    kb.dma("sp", G0[:], G0d, [], [r_G0])
    nmf = kb.sb("nmf", [128, 128], F32); r_nmf = Res()
    negmask = kb.sb("negmask_b", [128, 128], BF16); r_nm = Res()
    kb.dma("sp", nmf[:], nmd, [], [r_nmf])
    kb.v("dve", "tensor_copy", [r_nmf], [r_nm], out=negmask[:], in_=nmf[:])
    bft = kb.sb("bft", [HPC, 2], F32); r_bf = Res()
    kb.dma("sp", bft[:, 0:1], bfd, [], [r_bf])
    kb.v("dve", "tensor_scalar", [r_bf], [r_bf], out=bft[:, 1:2], in0=bft[:, 0:1], scalar1=-1.0, scalar2=None, op0=ALU.mult)
    ones4 = kb.sb("ones4", [HPC, QB], F32); r_ones = Res()
    kb.v("dve", "memset", [], [r_ones], ap=ones4[:], constant=1.0)

    wbig = kb.sb("wbig", [128, KC * 1024 + KC * 512], BF16); r_w = Res()
    wqk = wbig[:, 0:KC * 1024].rearrange("p (k c) -> p k c", k=KC)
    wv = wbig[:, KC * 1024:KC * 1536].rearrange("p (k c) -> p k c", k=KC)
    wf = kb.sb("wf_s", [128, KC, HPC], BF16)
    wqk_flat = wbig[:, 0:KC * 1024]
    for g in range(KC * 1024 // 2048):
        st_, r_st = xin.next()
        kb.dma("sp", st_[:], wqkd[:, g * 2048:(g + 1) * 2048], [], [r_st])
        emit_cast(kb, wqk_flat[:, g * 2048:(g + 1) * 2048], st_[:], [r_st], [r_w])
    wv_flat = wbig[:, KC * 1024:KC * 1536]
    for g in range(KC * 512 // 2048):
        st_, r_st = xin.next()
        kb.dma("sp", st_[:], wvd[:, g * 2048:(g + 1) * 2048], [], [r_st])
        emit_cast(kb, wv_flat[:, g * 2048:(g + 1) * 2048], st_[:], [r_st], [r_w])
    st_, r_st = xin.next()
    kb.dma("sp", st_[:, 0:KC * HPC], wfd, [], [r_st])
    kb.v("dve", "tensor_copy", [r_st], [r_w], out=wf[:].rearrange("p k c -> p (k c)"), in_=st_[:, 0:KC * HPC])

    hT = A2[:, 0:8192].rearrange("p (k c) -> p k c", k=KC); r_hT = Res()
    stq = Pool(kb, "stq", [128, QB], BF16, 4)
    vst = Pool(kb, "vst", [128, HPC, DH + 1], BF16, 3)
    for t_, r_ in zip(vst.tiles, vst.res):
        kb.v("dve", "memset", [], [r_], ap=t_[:], constant=1.0)
    fe = Pool(kb, "fe", [HPC, QB], F32, 2)
    Fb = Pool(kb, "Fb", [HPC, QB], F32, 2)
    fr = Pool(kb, "fr", [HPC, QB], F32, 4)
    fsb = Pool(kb, "fsb", [HPC, QB], BF16, 6)
    scale = float(DH) ** -0.5

    prevF = None
    for b in range(NB):
        t0 = b * QB
        for tt in range(QB // 128):
            xt, r_xt = xin.next()
            kb.dma("sp", xt[:], xb[t0 + tt * 128:t0 + (tt + 1) * 128, :], [], [r_xt])
            emit_norm_T(kb, xt, r_xt, G0, r_G0, hT, r_hT, tt * 128, KC)
        for j in range(8):
            ps, r_ps = kb.bank()
            for k in range(KC):
                kb.mm(ps[:, 0:QB], wqk[:, k, j * 128:(j + 1) * 128], hT[:, k, :], k == 0, k == KC - 1, [r_w, r_hT], [r_ps])
            s_, r_s = stq.next()
            h = j % HPC
            if j < HPC:
                kb.act(s_[:], ps[:, 0:QB], AF.Copy, [r_ps], [r_s], scale=scale)
                kb.dma("pool", qTd[h, :, t0:t0 + QB], s_[:], [r_s], [r_qTd[h]])
            else:
                kb.v("dve", "tensor_copy", [r_ps], [r_s], out=s_[:], in_=ps[:, 0:QB])
                kb.dma("pool", kTd[h, :, t0:t0 + QB], s_[:], [r_s], [r_kTd[h]])
        for tt in range(QB // 128):
            ps, r_ps = kb.bank()
            for k in range(KC):
                kb.mm(ps[:, 0:512], hT[:, k, tt * 128:(tt + 1) * 128], wv[:, k, :], k == 0, k == KC - 1, [r_w, r_hT], [r_ps])
            v_, r_v = vst.next()
            src = ps[:, 0:512].rearrange("p (h c) -> p h c", h=HPC)
            if tt % 2 == 0:
                kb.act(v_[:, :, 0:DH], src, AF.Copy, [r_ps], [r_v])
            else:
                kb.v("dve", "tensor_copy", [r_ps], [r_v], out=v_[:, :, 0:DH], in_=src)
            rows = slice(t0 + tt * 128, t0 + (tt + 1) * 128)
            kb.dma("pool", V1d.rearrange("h t c -> t h c")[rows, :, :], v_[:], [r_v], r_V1d)
        ps, r_ps = kb.bank()
        for k in range(KC):
            kb.mm(ps[0:HPC, 0:QB], wf[:, k, :], hT[:, k, :], k == 0, k == KC - 1, [r_w, r_hT], [r_ps])
        e_, r_e = fe.next()
        kb.act(e_[:], ps[0:HPC, 0:QB], AF.Exp, [r_ps, r_bf], [r_e], scale=-1.0, bias=bft[:, 1:2])
        kb.act(e_[:], e_[:], AF.Ln, [r_e], [r_e], bias=1.0)
        F_, r_F = Fb.next()
        init = 0.0 if prevF is None else prevF[0][:, QB - 1:QB]
        rd = [r_e, r_ones] + ([] if prevF is None else [prevF[1]])
        kb.v("dve", "tensor_tensor_scan", rd, [r_F], out=F_[:], data0=ones4[:], data1=e_[:], initial=init,
             op0=ALU.mult, op1=ALU.subtract)
        prevF = (F_, r_F)
        cur, r_cur = F_, r_F
        for i in range(3):
            fb_, r_fb = fsb.next()
            kb.v("dve", "tensor_copy", [r_cur], [r_fb], out=fb_[:], in_=cur[:])
            kb.dma("pool", Fsd[i, :, t0:t0 + QB], fb_[:], [r_fb], [r_Fsd])
            if i < 2:
                ff, r_ff = fr.next()
                kb.v("dve", "tensor_copy", [r_fb], [r_ff], out=ff[:], in_=fb_[:])
                nr, r_nr = fr.next()
                kb.v("dve", "tensor_tensor", [r_cur, r_ff], [r_nr], out=nr[:], in0=cur[:], in1=ff[:], op=ALU.subtract)
                cur, r_cur = nr, r_nr

    assert S <= 16384
    kT = A2[:, 0:S]; r_kT = Res()
    kt_first = [r_hT, xin.res[0], xin.res[1]]
    V1 = kb.sb("V1", [128, NKB, DH + 1], BF16); r_V1 = Res()
    KF = wbig[0:6, 0:S]; r_KF = Res()
    qTb = Pool(kb, "qTb", [128, QB], BF16, 2)
    QFb = Pool(kb, "QFb", [6, QB], BF16, 2)
    for t_, r_ in zip(QFb.tiles, QFb.res):
        kb.v("dve", "memset", [], [r_], ap=t_[:], constant=-1.0)
    PT = Pool(kb, "PT", [128, QB], BF16, 4)
    ost = Pool(kb, "ost", [128, 4, DH], BF16, 2)
    first = True
    for h in range(HPC):
        nsp = 4
        for i in range(nsp):
            cs = slice(i * S // nsp, (i + 1) * S // nsp)
            kb.dma("sp", kT[:, cs], kTd[h, :, cs], [r_kTd[h]], [r_kT] + kt_first)
            kt_first = []
        nvp = max(1, NKB // 8)
        for i in range(nvp):
            ks = slice(i * NKB // nvp, (i + 1) * NKB // nvp)
            kb.dma("sp", V1[:, ks, :], V1d[h].rearrange("(n p) c -> p n c", p=128)[:, ks, :], r_V1d, [r_V1])
        kb.v("dve", "memset", [], [r_KF, r_w] if first else [r_KF], ap=KF, constant=1.0)
        first = False
        kb.dma("sp", wbig[3:6, 0:S], Fsd[:, h, :], [r_Fsd], [r_KF])
        for Q in range(NB):
            q_, r_q = qTb.next()
            kb.dma("sp", q_[:], qTd[h, :, Q * QB:(Q + 1) * QB], [r_qTd[h]], [r_q])
            qf, r_qf = QFb.next()
            kb.dma("sp", qf[0:3, :], Fsd[:, h, Q * QB:(Q + 1) * QB], [r_Fsd], [r_qf])
            acc = [(kb.ps[4 + j], kb.psr[4 + j]) for j in range(4)]
            nkb = 4 * Q + 4
            pend = None
            for kbi in range(nkb + 1):
                if kbi < nkb:
                    j0 = max(0, kbi - 4 * Q)
                    c0 = j0 * 128
                    ps, r_ps = kb.bank(0, 3)
                    diag = kbi >= 4 * Q
                    kb.mm(ps[:, c0:QB], kT[:, kbi * 128:(kbi + 1) * 128], q_[:, c0:QB], True, False, [r_kT, r_q], [r_ps])
                    kb.mm(ps[:, c0:QB], KF[:, kbi * 128:(kbi + 1) * 128], qf[:, c0:QB], False, not diag, [r_KF, r_qf], [r_ps])
                    if diag:
                        kb.mm(ps[:, c0:c0 + 128], kb.identb[:], negmask[:], False, True, [kb.r_ident, r_nm], [r_ps])
                    p_, r_p = PT.next()
                    kb.act(p_[:, c0:QB], ps[:, c0:QB], AF.Exp, [r_ps], [r_p])
                    new = (kbi, j0, p_, r_p)
                else:
                    new = None
                if pend is not None:
                    pk, pj0, pp, r_pp = pend
                    for j in range(pj0, 4):
                        kb.mm(acc[j][0][:, 0:DH + 1], pp[:, j * 128:(j + 1) * 128], V1[:, pk, :], pk == 0, pk == 4 * Q + j,
                              [r_pp, r_V1], [acc[j][1]])
                pend = new
            o_, r_o = ost.next()
            for j in range(4):
                sm, r_sm = kb.small.next()
                kb.v("dve", "reciprocal", [acc[j][1]], [r_sm], out=sm[:, 0:1], in_=acc[j][0][:, DH:DH + 1])
                kb.act(o_[:, j, :], acc[j][0][:, 0:DH], AF.Copy, [acc[j][1], r_sm], [r_o], scale=sm[:, 0:1])
            kb.dma("pool", att.rearrange("(n p) c -> p n c", p=128)[:, Q * 4:Q * 4 + 4, h * DH:(h + 1) * DH], o_[:], [r_o], [r_att])
    kb.end()


def fox_host_layouts(w_in, b_f, g0, m):
    D = D_MODEL
    cols_q = w_in[:, (HPC * m) * DH:(HPC * m + HPC) * DH]
    cols_k = w_in[:, D + (HPC * m) * DH:D + (HPC * m + HPC) * DH]
    wqk = np.concatenate([cols_q, cols_k], axis=1).reshape(KC, 128, 8 * 128)
    wqk = np.ascontiguousarray(wqk.transpose(1, 0, 2)).reshape(128, KC * 1024)
    wv = w_in[:, 2 * D + HPC * m * DH:2 * D + (HPC * m + HPC) * DH].reshape(KC, 128, 512)
    wv = np.ascontiguousarray(wv.transpose(1, 0, 2)).reshape(128, KC * 512)
    wf = w_in[:, 3 * D + HPC * m:3 * D + HPC * m + HPC].reshape(KC, 128, HPC)
    wf = np.ascontiguousarray(wf.transpose(1, 0, 2)).reshape(128, KC * HPC)
    bf = np.ascontiguousarray(b_f[HPC * m:HPC * m + HPC].reshape(HPC, 1))
    G0 = np.ascontiguousarray(np.broadcast_to(g0[None, :], (128, D)))
    idx = np.arange(128)
    negmask = np.where(idx[:, None] <= idx[None, :], 0.0, -30000.0).astype(np.float32)
    return dict(wqk=wqk, wv=wv, wf=wf, bf=bf, G0=G0, ident=np.eye(128, dtype=np.float32), negmask=negmask)


def wo_inputs(kb, pfx, KCI):
    return dict(w=kb.din(pfx + "w", [128, KCI * D_MODEL]), G=kb.din(pfx + "G", [128, D_MODEL]))


def emit_wo(kb, T, KCI, io, identd, src_fn, r_src, sel4d, xr, r_xr, xo, r_xo):
    TBW = 512 if KCI <= 16 else 128
    NB = T // TBW
    wd, Gd = io["w"], io["G"]
    kb.begin(identd)
    kb.junk = Pool(kb, "junk", [128, D_MODEL], F32, 1)
    xin = Pool(kb, "xin", [128, D_MODEL], F32, 2)
    G = kb.sb("Gs", [128, D_MODEL], F32); r_G = Res()
    kb.dma("sp", G[:], Gd, [], [r_G])
    sel4 = kb.sb("sel4", [128, 4], F32); r_sel = Res()
    kb.dma("sp", sel4[:], sel4d, [], [r_sel])
    w = kb.sb("wres", [128, KCI, D_MODEL], BF16); r_w = Res()
    for k in range(KCI):
        st_, r_st = xin.next()
        kb.dma("sp", st_[:], wd[:, k * D_MODEL:(k + 1) * D_MODEL], [], [r_st])
        emit_cast(kb, w[:, k, :], st_[:], [r_st], [r_w])
    aTb = kb.sb("aTb", [128, KCI, TBW], BF16); r_a = Res()
    cand = Pool(kb, "cand", [128, KCI * 128], BF16, 2)
    hn = Pool(kb, "hnw", [128, KCI * 128], BF16, 2) if KCI <= 16 else None
    ft = Pool(kb, "ft", [128, D_MODEL], F32, 2)
    for b in range(NB):
        t0 = b * TBW
        for tt in range(TBW // 128):
            rows = slice(t0 + tt * 128, t0 + (tt + 1) * 128)
            srcs = src_fn(rows)
            if len(srcs) > 1:
                h_, r_h = hn.next()
            for i, (sap, view) in enumerate(srcs):
                c_, r_c = cand.next()
                kb.dma("sp", view(c_), sap, [r_src], [r_c])
                if len(srcs) == 1:
                    h_, r_h = c_, r_c
                elif i == 0:
                    kb.v("dve", "tensor_scalar", [r_c, r_sel], [r_h], out=h_[:], in0=c_[:], scalar1=sel4[:, 0:1], scalar2=None,
                         op0=ALU.mult)
                else:
                    kb.v("dve", "scalar_tensor_tensor", [r_c, r_sel, r_h], [r_h], out=h_[:], in0=c_[:], scalar=sel4[:, i:i + 1],
                         in1=h_[:], op0=ALU.mult, op1=ALU.add)
            emit_T(kb, h_, r_h, aTb, r_a, tt * 128, KCI)
        for tt in range(TBW // 128):
            f_, r_f = ft.next()
            for nq in range(4):
                ps, r_ps = kb.bank()
                for k in range(KCI):
                    kb.mm(ps[:, 0:512], aTb[:, k, tt * 128:(tt + 1) * 128], w[:, k, nq * 512:(nq + 1) * 512], k == 0, k == KCI - 1,
                          [r_a, r_w], [r_ps])
                if nq % 2 == 0:
                    kb.act(f_[:, nq * 512:(nq + 1) * 512], ps[:, 0:512], AF.Copy, [r_ps], [r_f])
                else:
                    kb.v("dve", "tensor_copy", [r_ps], [r_f], out=f_[:, nq * 512:(nq + 1) * 512], in_=ps[:, 0:512])
            xt, r_xt = xin.next()
            rows = slice(t0 + tt * 128, t0 + (tt + 1) * 128)
            kb.dma("sp", xt[:], xr[rows, :], [r_xr], [r_xt])
            emit_post(kb, f_[:], r_f, xt[:], r_xt, G, r_G, f_[:], r_f)
            kb.dma("pool", xo[rows, :], f_[:], [r_f], [r_xo])
    kb.end()


def wo_host_layout(w, g):
    KCI = w.shape[0] // 128
    wl = np.ascontiguousarray(w.reshape(KCI, 128, D_MODEL).transpose(1, 0, 2)).reshape(128, KCI * D_MODEL)
    G = np.ascontiguousarray(np.broadcast_to(g[None, :], (128, D_MODEL)))
    return dict(w=wl, G=G, ident=np.eye(128, dtype=np.float32))


RH = 8
DK_ = 256
DV_ = 512
RB = 1024
NS = 24


def ret_gammas():
    return [1.0 - 2.0 ** (-5.0 - h) for h in range(RH)]


def ret_inputs(kb, T):
    return dict(win=kb.din("r_win", [NS, 128, KC * 512]), G=kb.din("r_G", [128, D_MODEL]), cos2=kb.din("cos2", [T, 256]),
                sin2=kb.din("sin2", [T, 256]), DK=kb.din("DK", [128, D_MODEL]), Mp=kb.din("Mp", [128, RH * 128]),
                DQ=kb.din("DQ", [128, RH]), coef=kb.din("coef", [128, 3 * RH]))


def emit_ret(kb, T, full, io, identd, xr, r_xr, Lout=None, r_L=None, Lprev=None, og=None, r_og_d=None):
    NBLK = T // RB
    NCH = T // 128
    wind, Gd, cosd, sind, DKd = io["win"], io["G"], io["cos2"], io["sin2"], io["DK"]
    Mpd, DQd, coefd = io["Mp"], io["DQ"], io["coef"]
    sfx = "f" if full else "s"
    wib = kb.dtmp("wib" + sfx, [NS, 128, KC * 512], BF16)
    qd = kb.dtmp("qd" + sfx, [T, D_MODEL], BF16)
    kd = kb.dtmp("kd" + sfx, [T, D_MODEL], BF16)
    vd = kb.dtmp("vd" + sfx, [T, 2 * D_MODEL], BF16)
    sgd = kb.dtmp("sgd" + sfx, [T, 2 * D_MODEL], BF16)
    r_wib = [Res() for _ in range(NS)]
    r_qd, r_kd, r_vd, r_sgd = Res(), Res(), Res(), Res()
    slices = list(range(NS)) if full else list(range(4, 16))

    kb.begin(identd)
    kb.junk = Pool(kb, "junk", [128, D_MODEL], F32, 1)
    kb.hn = Pool(kb, "hn", [128, D_MODEL], BF16, 2)
    xin = Pool(kb, "xin", [128, D_MODEL], F32, 2)
    G = kb.sb("Gs", [128, D_MODEL], F32); r_G = Res()
    kb.dma("sp", G[:], Gd, [], [r_G])
    DK = kb.sb("DKs", [128, D_MODEL], F32); r_DK = Res()
    kb.dma("sp", DK[:], DKd, [], [r_DK])
    A1 = kb.sb("A1", [128, KC * RB], BF16); r_A1 = Res()
    hT = A1[:].rearrange("p (k c) -> p k c", k=KC)
    wpool = Pool(kb, "wsl", [128, KC * 512], BF16, 2)
    cs = kb.sb("cs", [128, 2, RB // 128, 256], F32); r_cs = Res()
    rt = Pool(kb, "rt", [128, 256], F32, 6)
    so = Pool(kb, "so", [128, 512], BF16, 4)

    for ns in slices:
        for g in range(KC * 512 // D_MODEL):
            st_, r_st = xin.next()
            sb_, r_sb = kb.hn.next()
            sl = slice(g * D_MODEL, (g + 1) * D_MODEL)
            kb.dma("sp", st_[:], wind[ns, :, sl], [], [r_st])
            emit_cast(kb, sb_[:], st_[:], [r_st], [r_sb])
            kb.dma("pool", wib[ns, :, sl], sb_[:], [r_sb], [r_wib[ns]])

    for b in range(NBLK):
        t0 = b * RB
        for tt in range(RB // 128):
            xt, r_xt = xin.next()
            kb.dma("sp", xt[:], xr[t0 + tt * 128:t0 + (tt + 1) * 128, :], [r_xr], [r_xt])
            emit_norm_T(kb, xt, r_xt, G, r_G, hT, r_A1, tt * 128, KC)
        kb.dma("sp", cs[:, 0, :, :], cosd[t0:t0 + RB, :].rearrange("(n p) c -> p n c", p=128), [], [r_cs])
        kb.dma("sp", cs[:, 1, :, :], sind[t0:t0 + RB, :].rearrange("(n p) c -> p n c", p=128), [], [r_cs])
        for ns in slices:
            wt, r_wt = wpool.next()
            wv_ = wt[:].rearrange("p (k c) -> p k c", k=KC)
            kb.dma("sp", wt[:], wib[ns], [r_wib[ns]], [r_wt])
            for tt in range(RB // 128):
                ps, r_ps = kb.bank()
                for k in range(KC):
                    kb.mm(ps[:, 0:512], hT[:, k, tt * 128:(tt + 1) * 128], wv_[:, k, :], k == 0, k == KC - 1, [r_A1, r_wt], [r_ps])
                o_, r_o = so.next()
                rows = slice(t0 + tt * 128, t0 + (tt + 1) * 128)
                if ns < 8:
                    psv = ps[:, 0:512].rearrange("p (i t) -> p i t", t=2)
                    ov = o_[:].rearrange("p (i t) -> p i t", t=2)
                    c_ = cs[:, 0, tt, :]
                    s_ = cs[:, 1, tt, :]
                    t1, r_t1 = rt.next(); t2, r_t2 = rt.next()
                    kb.v("dve", "tensor_tensor", [r_ps, r_cs], [r_t1], out=t1[:], in0=psv[:, :, 0], in1=c_, op=ALU.mult)
                    kb.v("dve", "tensor_tensor", [r_ps, r_cs], [r_t2], out=t2[:], in0=psv[:, :, 1], in1=s_, op=ALU.mult)
                    kb.v("pool", "tensor_tensor", [r_t1, r_t2], [r_o], out=ov[:, :, 0], in0=t1[:], in1=t2[:], op=ALU.subtract)
                    t3, r_t3 = rt.next(); t4, r_t4 = rt.next()
                    kb.v("dve", "tensor_tensor", [r_ps, r_cs], [r_t3], out=t3[:], in0=psv[:, :, 0], in1=s_, op=ALU.mult)
                    kb.v("dve", "tensor_tensor", [r_ps, r_cs], [r_t4], out=t4[:], in0=psv[:, :, 1], in1=c_, op=ALU.mult)
                    kb.v("pool", "tensor_tensor", [r_t3, r_t4], [r_o], out=ov[:, :, 1], in0=t3[:], in1=t4[:], op=ALU.add)
                    if ns < 4:
                        kb.dma("pool", qd[rows, ns * 512:(ns + 1) * 512], o_[:], [r_o], [r_qd])
                    else:
                        kb.dma("pool", kd[rows, (ns - 4) * 512:(ns - 3) * 512], o_[:], [r_o], [r_kd])
                elif ns < 16:
                    kb.act(o_[:], ps[:, 0:512], AF.Copy, [r_ps], [r_o])
                    kb.dma("pool", vd[rows, (ns - 8) * 512:(ns - 7) * 512], o_[:], [r_o], [r_vd])
                else:
                    kb.act(o_[:], ps[:, 0:512], AF.Silu, [r_ps], [r_o])
                    kb.dma("pool", sgd[rows, (ns - 16) * 512:(ns - 15) * 512], o_[:], [r_o], [r_sgd])

    gam = ret_gammas()
    Sv = A1[:].bitcast(F32).rearrange("p (h d v) -> p h d v", h=RH, d=2)
    Sbf = wpool.tiles[0][:].rearrange("p (h d v) -> p h d v", h=RH, d=2)
    qkT = wpool.tiles[1][:].rearrange("p (b s j c) -> p b s j c", b=2, s=2, j=16)
    r_S = [Res() for _ in range(RH)]
    r_Sbf = [Res() for _ in range(RH)]
    r_qkT = [Res(), Res()]
    qk_tiles = [t[:].bitcast(BF16) for t in xin.tiles]
    r_qk = xin.res
    kdec = Pool(kb, "kdec", [128, D_MODEL], BF16, 2)
    vh = Pool(kb, "vh", [128, DV_], BF16, 4)
    barrier_w = [r_A1, wpool.res[0], wpool.res[1]] + r_S + r_Sbf + r_qkT
    kb.v("dve", "memset", [], barrier_w, ap=A1[:].bitcast(F32), constant=0.0)
    if full:
        Mp = kb.sb("Mps", [128, RH, 128], F32); r_Mp = Res()
        kb.dma("sp", Mp[:], Mpd.rearrange("p (h n) -> p h n", h=RH), [], [r_Mp])
        DQ = kb.sb("DQs", [128, RH], F32); r_DQ = Res()
        kb.dma("sp", DQ[:], DQd, [], [r_DQ])
        coef = kb.sb("coefs", [128, 3 * RH], F32); r_coef = Res()
        kb.dma("sp", coef[:], coefd, [], [r_coef])
        sgh = Pool(kb, "sgh", [128, DV_], BF16, 3)
        oh = Pool(kb, "oh", [128, DV_], F32, 3)
        ogp = Pool(kb, "ogp", [128, DV_], BF16, 3)
        sT = Pool(kb, "sT", [128, 128], BF16, 3)
        for i in range(3):
            for h in range(RH):
                for d in range(2):
                    l_, r_l = oh.next()
                    kb.dma("sp", l_[:], Lprev[i, h, d], [r_L], [r_l])
                    kb.v("dve", "scalar_tensor_tensor", [r_l, r_coef, r_S[h]], [r_S[h]], out=Sv[:, h, d, :], in0=l_[:],
                         scalar=coef[:, i * RH + h:i * RH + h + 1], in1=Sv[:, h, d, :], op0=ALU.mult, op1=ALU.add)
        for h in range(RH):
            kb.act(Sbf[:, h, :, :], Sv[:, h, :, :], AF.Copy, [r_S[h]], [r_Sbf[h]])
    for c in range(NCH):
        rows = slice(c * 128, (c + 1) * 128)
        bi = c % 2
        qk = qk_tiles[bi]
        r_q = r_qk[bi]
        if full:
            kb.dma("sp", qk[:, 0:D_MODEL], qd[rows, :], [r_qd], [r_q])
        kb.dma("sp", qk[:, D_MODEL:2 * D_MODEL], kd[rows, :], [r_kd], [r_q])
        kd_, r_kdec = kdec.next()
        kb.v("dve", "tensor_tensor", [r_q, r_DK], [r_kdec], out=kd_[:], in0=qk[:, D_MODEL:2 * D_MODEL], in1=DK[:], op=ALU.mult)
        if full:
            for s in range(2):
                for half in range(2):
                    ps, r_ps = kb.bank()
                    psb = ps[:].bitcast(BF16)
                    for j in range(8):
                        col = s * D_MODEL + (half * 8 + j) * 128
                        kb.tr(psb[:, j * 128:(j + 1) * 128], qk[:, col:col + 128], kb.identb[:], [r_q, kb.r_ident], [r_ps])
                    src = psb[:, 0:1024].rearrange("p (j c) -> p j c", c=128)
                    dst = qkT[:, bi, s, half * 8:half * 8 + 8, :]
                    if half == 0:
                        kb.act(dst, src, AF.Copy, [r_ps], [r_qkT[bi]])
                    else:
                        kb.v("dve", "tensor_copy", [r_ps], [r_qkT[bi]], out=dst, in_=src)
        for h in range(RH):
            v_, r_v = vh.next()
            kb.dma("sp", v_[:], vd[rows, h * DV_:(h + 1) * DV_], [r_vd], [r_v])
            if full:
                g_, r_g = sgh.next()
                kb.dma("sp", g_[:], sgd[rows, h * DV_:(h + 1) * DV_], [r_sgd], [r_g])
                ps, r_ps = kb.bank()
                for d in range(2):
                    kb.mm(ps[:, 0:128], qkT[:, bi, 1, h * 2 + d, :], qkT[:, bi, 0, h * 2 + d, :], d == 0, d == 1, [r_qkT[bi]], [r_ps])
                st_, r_st = sT.next()
                kb.v("dve", "tensor_tensor", [r_ps, r_Mp], [r_st], out=st_[:], in0=ps[:, 0:128], in1=Mp[:, h, :], op=ALU.mult)
                po, r_po = kb.bank()
                kb.mm(po[:, 0:DV_], st_[:], v_[:], True, False, [r_st, r_v], [r_po])
                for d in range(2):
                    kb.mm(po[:, 0:DV_], qkT[:, bi, 0, h * 2 + d, :], Sbf[:, h, d, :], False, d == 1, [r_qkT[bi], r_Sbf[h]], [r_po])
                o_, r_o = oh.next()
                kb.act(o_[:], po[:, 0:DV_], AF.Copy, [r_po, r_DQ], [r_o], scale=DQ[:, h:h + 1])
                rstd, r_rs = emit_rstd(kb, o_[:], r_o, DV_)
                og_, r_og = ogp.next()
                kb.v("dve", "scalar_tensor_tensor", [r_o, r_rs, r_g], [r_og], out=og_[:], in0=o_[:], scalar=rstd, in1=g_[:],
                     op0=ALU.mult, op1=ALU.mult)
                kb.dma("pool", og[rows, h * DV_:(h + 1) * DV_], og_[:], [r_og], [r_og_d])
            for d in range(2):
                pS, r_pS = kb.bank()
                kb.mm(pS[:, 0:DV_], kd_[:, h * DK_ + d * 128:h * DK_ + (d + 1) * 128], v_[:], True, True, [r_kdec, r_v], [r_pS])
                kb.v("dve", "scalar_tensor_tensor", [r_pS, r_S[h]], [r_S[h]], out=Sv[:, h, d, :], in0=Sv[:, h, d, :],
                     scalar=float(gam[h] ** 128), in1=pS[:, 0:DV_], op0=ALU.mult, op1=ALU.add)
            if full:
                kb.act(Sbf[:, h, :, :], Sv[:, h, :, :], AF.Copy, [r_S[h]], [r_Sbf[h]])
    if not full:
        for h in range(RH):
            for d in range(2):
                kb.dma("pool", Lout[h, d], Sv[:, h, d, :], [r_S[h]], [r_L])
    kb.end()


def ret_host_layouts(w_in, g0):
    w = w_in.reshape(KC, 128, NS, 512)
    win = np.ascontiguousarray(w.transpose(2, 1, 0, 3)).reshape(NS, 128, KC * 512)
    G = np.ascontiguousarray(np.broadcast_to(g0[None, :], (128, D_MODEL)))
    return dict(win=win, G=G, ident=np.eye(128, dtype=np.float32))


def ret_const_tables(pos0, T, j):
    theta = (1.0 / (10000.0 ** np.linspace(0.0, 1.0, DK_ // 2, dtype=np.float32))).astype(np.float32)
    ang = (np.arange(pos0, pos0 + T, dtype=np.float32)[:, None] * theta[None, :]).astype(np.float32)
    cos = np.cos(ang).astype(np.float32)
    sin = np.sin(ang).astype(np.float32)
    cos2 = np.ascontiguousarray(np.concatenate([cos, cos], axis=1))
    sin2 = np.ascontiguousarray(np.concatenate([sin, sin], axis=1))
    gam = np.array(ret_gammas(), dtype=np.float64)
    lg = np.log1p(-(2.0 ** (-5.0 - np.arange(RH, dtype=np.float64))))
    m = np.arange(128, dtype=np.float64)
    ks = DK_ ** -0.5
    DK = np.exp((127.0 - m)[:, None] * lg[None, :]) * ks
    DK = np.ascontiguousarray(np.repeat(DK, DK_, axis=1)).astype(np.float32)
    DQ = np.exp((m + 1.0)[:, None] * lg[None, :]).astype(np.float32)
    Mp = np.exp(-(m + 1.0)[:, None, None] * lg[None, :, None]) * ks
    Mp = Mp * (m[None, None, :] >= m[:, None, None])
    Mp = np.ascontiguousarray(Mp.reshape(128, RH * 128)).astype(np.float32)
    coef = np.zeros((3, RH), np.float64)
    for i in range(3):
        if i < j:
            coef[i] = np.exp(T * (j - 1 - i) * lg)
    coef = np.ascontiguousarray(np.broadcast_to(coef.reshape(1, 3 * RH), (128, 3 * RH))).astype(np.float32)
    return dict(cos2=cos2, sin2=sin2, DK=DK, DQ=DQ, Mp=Mp, coef=coef)


GROUPS = [[0, 1, 2, 3], [4, 5, 6, 7]]
TPC = SEQ * BATCH // NCORES


def build_fused():
    kb = KB()
    T, S, D = TPC, SEQ, D_MODEL
    ident = kb.din("ident", [128, 128])
    xb = kb.din("xb", [S, D])
    xs = kb.din("xs", [T, D])
    sel_prev = kb.din("sel_prev", [128, 4])
    sel_own = kb.din("sel_own", [128, 4])
    fio = fox_inputs(kb)
    w0 = wo_inputs(kb, "wo0_", 16)
    f0 = ffn_inputs(kb, "f0_")
    rio = ret_inputs(kb, T)
    w1 = wo_inputs(kb, "wo1_", 32)
    f1 = ffn_inputs(kb, "f1_")
    out = kb.dout("out", [T, D])

    att = kb.dtmp("att", [S, HPC * DH], BF16); r_att = Res()
    emit_fox(kb, S, fio, xb, ident, att, r_att)
    CR = 1024
    r_attg = Res()
    attg = []
    for i in range(S // CR):
        g_ = kb.dtmp("attg%d" % i, [4 * CR, HPC * DH], BF16)
        kb.collective("AllGather", GROUPS, att[i * CR:(i + 1) * CR, :], g_, [r_att], [r_attg], flush=(i == S // CR - 1))
        attg.append(g_.rearrange("(m t) c -> t m c", m=4))

    def src0(rows):
        res = []
        for jj in range(4):
            r0 = jj * T + rows.start
            res.append((attg[r0 // CR][r0 % CR:r0 % CR + 128], lambda c_: c_[:].rearrange("p (m c) -> p m c", m=4)))
        return res

    xm0 = kb.dtmp("xm0", [T, D]); r_xm0 = Res()
    emit_wo(kb, T, 16, w0, ident, src0, r_attg, sel_own, xs, Res(), xm0, r_xm0)
    halo0 = kb.dtmp("halo0", [8, D]); r_h0 = Res()
    kb.collective("AllGather", GROUPS, xm0[T - 2:T, :], halo0, [r_xm0], [r_h0])
    x1 = kb.dtmp("x1", [T, D]); r_x1 = Res()
    emit_ffn(kb, T, "f0_", f0, xm0, r_xm0, halo0, r_h0, sel_prev, ident, x1, r_x1)

    L = kb.dtmp("Lst", [RH, 2, 128, DV_]); r_L = Res()
    emit_ret(kb, T, False, rio, ident, x1, r_x1, Lout=L, r_L=r_L)
    r_La = Res()
    Lall = []
    Lf = L.rearrange("h d p v -> (h d p) v")
    for i in range(RH // 2):
        g_ = kb.dtmp("Lall%d" % i, [4 * 512, DV_])
        kb.collective("AllGather", GROUPS, Lf[i * 512:(i + 1) * 512, :], g_, [r_L], [r_La], flush=(i == RH // 2 - 1))
        Lall.append(g_.rearrange("(i h d p) v -> i h d p v", i=4, h=2, d=2))

    class _LP:
        def __getitem__(self, idx):
            i, h, d = idx
            return Lall[h // 2][i, h % 2, d]

    og = kb.dtmp("og", [T, RH * DV_], BF16); r_og = Res()
    emit_ret(kb, T, True, rio, ident, x1, r_x1, r_L=r_La, Lprev=_LP(), og=og, r_og_d=r_og)

    def src1(rows):
        return [(og[rows, :], lambda c_: c_[:])]

    xm1 = kb.dtmp("xm1", [T, D]); r_xm1 = Res()
    emit_wo(kb, T, 32, w1, ident, src1, r_og, sel_own, x1, r_x1, xm1, r_xm1)
    halo1 = kb.dtmp("halo1", [8, D]); r_h1 = Res()
    kb.collective("AllGather", GROUPS, xm1[T - 2:T, :], halo1, [r_xm1], [r_h1])
    emit_ffn(kb, T, "f1_", f1, xm1, r_xm1, halo1, r_h1, sel_prev, ident, out, Res())
    return kb.nc


_NC = []


def kernel(x, norm_g, fox_w_in, fox_b_f, fox_w_o, ret_w_in, ret_w_o,
           ffn_w_up, ffn_conv_w, ffn_conv_b, ffn_w_down):
    f = lambda a: np.ascontiguousarray(np.asarray(a, dtype=np.float32))
    x, norm_g = f(x), f(norm_g)
    fox_w_in, fox_b_f, fox_w_o = f(fox_w_in), f(fox_b_f), f(fox_w_o)
    ret_w_in, ret_w_o = f(ret_w_in), f(ret_w_o)
    ffn_w_up, ffn_conv_w, ffn_conv_b, ffn_w_down = f(ffn_w_up), f(ffn_conv_w), f(ffn_conv_b), f(ffn_w_down)
    if not _NC:
        _NC.append(build_fused())
    nc = _NC[0]
    T, G = TPC, NCORES // BATCH
    shared = {"ident": np.eye(128, dtype=np.float32)}
    for k_, v_ in wo_host_layout(fox_w_o[0], norm_g[0, 1]).items():
        if k_ != "ident":
            shared["wo0_" + k_] = v_
    for k_, v_ in wo_host_layout(ret_w_o[0], norm_g[1, 1]).items():
        if k_ != "ident":
            shared["wo1_" + k_] = v_
    for l, pfx in ((0, "f0_"), (1, "f1_")):
        lay = ffn_host_layouts(ffn_w_up[l], ffn_conv_w[l], ffn_conv_b[l], ffn_w_down[l], norm_g[l, 2], norm_g[l, 3])
        for k_, v_ in lay.items():
            if k_ != "ident":
                shared[pfx + k_] = v_
    rl = ret_host_layouts(ret_w_in[0], norm_g[1, 0])
    shared["r_win"] = rl["win"]
    shared["r_G"] = rl["G"]
    foxl = [fox_host_layouts(fox_w_in[0], fox_b_f[0], norm_g[0, 0], m) for m in range(G)]
    maps = []
    for c in range(NCORES):
        b, j = c // G, c % G
        m = dict(shared)
        for k_, v_ in foxl[j].items():
            if k_ != "ident":
                m[k_] = v_
        m.update(ret_const_tables(j * T, T, j))
        m["xb"] = x[b]
        m["xs"] = np.ascontiguousarray(x[b, j * T:(j + 1) * T])
        sp = np.zeros((128, 4), np.float32)
        so = np.zeros((128, 4), np.float32)
        if j > 0:
            sp[:, j - 1] = 1.0
        so[:, j] = 1.0
        m["sel_prev"] = sp
        m["sel_own"] = so
        maps.append(m)
    res = run_bass_kernel_spmd(nc, maps, core_ids=list(range(NCORES))).results
    out = np.concatenate([res[c]["out"] for c in range(NCORES)], axis=0).reshape(BATCH, SEQ, D_MODEL)
    return out.astype(np.float32)
```

```python
import contextlib
import numpy as np
import concourse.bass as bass
import concourse.mybir as mybir
from concourse.bass_utils import run_bass_kernel_spmd

F32 = mybir.dt.float32
BF16 = mybir.dt.bfloat16
AF = mybir.ActivationFunctionType
ALU = mybir.AluOpType
AX = mybir.AxisListType

D_MODEL = 2048
SEQ = 16384
BATCH = 2
D_FF = 5632
NORM_EPS = 1e-6
NCORES = 8

SAME_ENG_SYNC = True
SEM_GEN = 30000
SEM_DMA_GEN = 1500


class Res:
    __slots__ = ("name", "w", "rs")

    def __init__(self, name=""):
        self.name = name
        self.w = None
        self.rs = []


class _Op:
    __slots__ = ("eng", "fn", "deps", "ev", "dma", "inc")


class Sched:
    ENGS = ("pe", "act", "dve", "pool", "sp")

    def __init__(self, nc, ndma=10):
        self.nc = nc
        self.ops = {e: [] for e in self.ENGS}
        self.cnt = {e: 0 for e in self.ENGS}
        self.dman = {e: 0 for e in self.ENGS}
        self.dmahist = {e: [] for e in self.ENGS}
        self.ndma = ndma
        self.ncc = 0
        self.sems = {}
        self.semstack = contextlib.ExitStack()
        self.waited = {e: {} for e in self.ENGS}
        self.fin = {}

    def op(self, eng, fn, reads=(), writes=(), dma=False, inc=None):
        o = _Op()
        o.eng = eng
        o.fn = fn
        o.dma = dma
        o.inc = inc
        deps = []
        for r in reads:
            if r.w is not None:
                deps.append(r.w)
        for r in writes:
            if r.w is not None:
                deps.append(r.w)
            deps.extend(r.rs)
        if inc is not None:
            o.ev = ("cc", self.ncc, inc)
            self.ncc += 1
        elif dma:
            n = self.dman[eng]
            self.dman[eng] = n + 1
            rnd = n // self.ndma
            o.ev = ("d_" + eng, (n % self.ndma, rnd // SEM_DMA_GEN), 16 * (rnd % SEM_DMA_GEN + 1))
            if n >= self.ndma:
                deps.append(self.dmahist[eng][n - self.ndma])
            self.dmahist[eng].append(o)
        else:
            c = self.cnt[eng]
            self.cnt[eng] = c + 1
            o.ev = ("c_" + eng, c // SEM_GEN, c % SEM_GEN + 1)
        dd = []
        seen = set()
        for d in deps:
            if id(d) in seen:
                continue
            seen.add(id(d))
            if (not d.dma) and d.eng == eng:
                if eng == "pe" or not SAME_ENG_SYNC:
                    continue
            dd.append(d)
        o.deps = dd
        for r in reads:
            r.rs.append(o)
        for r in writes:
            r.w = o
            r.rs = []
        self.ops[eng].append(o)
        return o

    def flush(self):
        nc = self.nc
        for e in self.ENGS:
            for o in self.ops[e]:
                k = (o.ev[0], o.ev[1])
                if k not in self.sems:
                    self.sems[k] = self.semstack.enter_context(nc.semaphore("s%d" % len(self.sems)))
                self.fin[k] = max(self.fin.get(k, 0), o.ev[2])
        sems = self.sems
        fin = dict(self.fin)
        ops = self.ops
        self.ops = {e: [] for e in self.ENGS}
        with nc.Block() as block:
            def run(eng_name):
                def body(eng):
                    waited = self.waited[eng_name]
                    for o in ops[eng_name]:
                        for d in o.deps:
                            k = (d.ev[0], d.ev[1])
                            if waited.get(k, 0) < d.ev[2]:
                                eng.wait_ge(sems[k], d.ev[2])
                                waited[k] = d.ev[2]
                        ins = o.fn(eng)
                        ins.then_inc(sems[(o.ev[0], o.ev[1])], o.inc if o.inc is not None else (16 if o.dma else 1))
                    for k, v in fin.items():
                        if waited.get(k, 0) < v:
                            eng.wait_ge(sems[k], v)
                            waited[k] = v
                return body

            block.tensor(run("pe"))
            block.scalar(run("act"))
            block.vector(run("dve"))
            block.gpsimd(run("pool"))
            block.sync(run("sp"))


class Pool:
    def __init__(self, kb, name, shape, dt, n, views=None):
        if views is not None:
            self.tiles = list(views)
            n = len(views)
        else:
            self.tiles = [kb.sb("%s%d" % (name, i), shape, dt) for i in range(n)]
        self.res = [Res("%s%d" % (name, i)) for i in range(n)]
        self.i = 0

    def next(self):
        i = self.i % len(self.tiles)
        self.i += 1
        return self.tiles[i], self.res[i]


class KB:
    def __init__(self):
        self.nc = bass.Bass("TRN2", target_bir_lowering=False)
        self.S = Sched(self.nc)
        self.st = None
        self.phase = 0

    def begin(self, ident_in):
        self.phase += 1
        self.st = contextlib.ExitStack()
        self.ps = []
        self.psr = []
        self.psi = 0
        setup_common(self, ident_in)

    def end(self):
        self.S.flush()
        self.st.close()
        self.st = None

    def sb(self, name, shape, dt):
        return self.st.enter_context(self.nc.sbuf_tensor("p%d_%s" % (self.phase, name), list(shape), dt))

    def alloc_psum(self, n=8):
        for i in range(n):
            self.ps.append(self.st.enter_context(self.nc.psum_tensor("p%d_ps%d" % (self.phase, i), [128, 512], F32)))
            self.psr.append(Res("ps%d" % i))

    def bank(self, lo=0, hi=8):
        i = lo + self.psi % (hi - lo)
        self.psi += 1
        return self.ps[i], self.psr[i]

    def din(self, name, shape, dt=F32):
        return self.nc.dram_tensor(name, list(shape), dt, kind="ExternalInput").ap()

    def dout(self, name, shape, dt=F32):
        return self.nc.dram_tensor(name, list(shape), dt, kind="ExternalOutput").ap()

    def dtmp(self, name, shape, dt=F32):
        return self.nc.dram_tensor(name, list(shape), dt, kind="Internal").ap()

    def collective(self, kind, groups, in_ap, out_ap, r, w, flush=True):
        if not hasattr(self, "r_cc"):
            self.r_cc = Res("cc")
        self.S.op("pool", lambda e: e.collective_compute(kind, ALU.bypass, replica_groups=groups, ins=[in_ap], outs=[out_ap]),
                  r, list(w) + [self.r_cc], dma=True, inc=1)
        if flush:
            self.S.flush()

    def dma(self, q, out, in_, r, w):
        return self.S.op(q, lambda e: e.dma_start(out=out, in_=in_), r, w, dma=True)

    def act(self, out, in_, func, r, w, **kw):
        return self.S.op("act", lambda e: e.activation(out=out, in_=in_, func=func, **kw), r, w)

    def mm(self, out, lhsT, rhs, start, stop, r, w):
        return self.S.op("pe", lambda e: e.matmul(out, lhsT=lhsT, rhs=rhs, start=start, stop=stop,
                                                  skip_group_check=True), r, w)

    def tr(self, out, in_, ident, r, w):
        return self.S.op("pe", lambda e: e.transpose(out=out, in_=in_, identity=ident), r, w)

    def v(self, eng, meth, r, w, **kw):
        return self.S.op(eng, lambda e: getattr(e, meth)(**kw), r, w)


def emit_rstd(kb, x_ap, r_x, ncols, P=128):
    junk, r_j = kb.junk.next()
    st, r_s = kb.small.next()
    kb.act(junk[0:P, 0:ncols], x_ap, AF.Square, [r_x], [r_j])
    kb.v("dve", "reduce_sum", [r_j], [r_s], out=st[0:P, 0:1], in_=junk[0:P, 0:ncols], axis=AX.X)
    kb.v("dve", "tensor_scalar", [r_s], [r_s], out=st[0:P, 1:2], in0=st[0:P, 0:1], scalar1=1.0 / ncols,
         scalar2=NORM_EPS, op0=ALU.mult, op1=ALU.add)
    kb.act(st[0:P, 2:3], st[0:P, 1:2], AF.Sqrt, [r_s], [r_s])
    kb.v("dve", "reciprocal", [r_s], [r_s], out=st[0:P, 3:4], in_=st[0:P, 2:3])
    return st[0:P, 3:4], r_s


def emit_norm_T(kb, x_tile, r_x, G, r_G, hT, r_hT, col0, KC, cp_eng="act"):
    rstd, r_s = emit_rstd(kb, x_tile[:, 0:KC * 128], r_x, KC * 128)
    hn, r_hn = kb.hn.next()
    kb.v("dve", "scalar_tensor_tensor", [r_x, r_s, r_G], [r_hn], out=hn[:, 0:KC * 128], in0=x_tile[:, 0:KC * 128],
         scalar=rstd, in1=G[:, 0:KC * 128], op0=ALU.mult, op1=ALU.mult)
    emit_T(kb, hn, r_hn, hT, r_hT, col0, KC, cp_eng)


def emit_T(kb, hn, r_hn, hT, r_hT, col0, KC, cp_eng="act"):
    for k0 in range(0, KC, 8):
        n = min(8, KC - k0)
        ps, r_ps = kb.bank()
        psb = ps[:].bitcast(BF16)
        for j in range(n):
            kb.tr(psb[:, j * 128:(j + 1) * 128], hn[:, (k0 + j) * 128:(k0 + j + 1) * 128], kb.identb[:],
                  [r_hn, kb.r_ident], [r_ps])
        src = psb[:, 0:n * 128].rearrange("p (k c) -> p k c", c=128)
        dst = hT[:, k0:k0 + n, col0:col0 + 128]
        if cp_eng == "act":
            kb.act(dst, src, AF.Copy, [r_ps], [r_hT])
        else:
            kb.v(cp_eng, "tensor_copy", [r_ps], [r_hT], out=dst, in_=src)


def emit_post(kb, f_ap, r_f, x_ap, r_x, G, r_G, out_tile, r_out, ncols=D_MODEL):
    rstd, r_s = emit_rstd(kb, f_ap, r_f, ncols)
    kb.v("dve", "scalar_tensor_tensor", [r_f, r_s, r_G], [r_out], out=out_tile, in0=f_ap, scalar=rstd, in1=G[:, 0:ncols],
         op0=ALU.mult, op1=ALU.mult)
    kb.v("pool", "tensor_tensor", [r_out, r_x], [r_out], out=out_tile, in0=out_tile, in1=x_ap, op=ALU.add)


def setup_common(kb, ident_in):
    kb.alloc_psum(8)
    idf = kb.sb("idf", [128, 128], F32)
    kb.identb = kb.sb("identb", [128, 128], BF16)
    kb.r_ident = Res("ident")
    r_idf = Res("idf")
    kb.dma("sp", idf[:], ident_in, [], [r_idf])
    kb.identf = idf
    kb.r_identf = r_idf
    kb.v("dve", "tensor_copy", [r_idf], [kb.r_ident], out=kb.identb[:], in_=idf[:])
    kb.small = Pool(kb, "small", [128, 4], F32, 6)


_cast_rr = [0]


def emit_cast(kb, out, in_, r, w):
    i = _cast_rr[0] % 3
    _cast_rr[0] += 1
    if i == 0:
        kb.v("dve", "tensor_copy", r, w, out=out, in_=in_)
    elif i == 1:
        kb.act(out, in_, AF.Copy, r, w)
    else:
        kb.v("pool", "tensor_copy", r, w, out=out, in_=in_)


TB = 512
NPAIR = D_FF // 128
KC = D_MODEL // 128


def ffn_inputs(kb, pfx):
    return dict(wup=kb.din(pfx + "wup", [NPAIR, 128, 2 * KC * 128]), wdn=kb.din(pfx + "wdn", [4, 128, NPAIR * 512]),
                G2=kb.din(pfx + "G2", [128, D_MODEL]), G3=kb.din(pfx + "G3", [128, D_MODEL]),
                cw=kb.din(pfx + "cw", [128, 2 * NPAIR * 3]), cb=kb.din(pfx + "cb", [128, 2 * NPAIR]))


class Prep:
    def __init__(self):
        self.steps = []

    def add(self, src_ap, dst_ap, r_dst):
        self.steps.append((src_ap, dst_ap, r_dst))

    def run(self, kb, n, stf, stb):
        for _ in range(n):
            if not self.steps:
                return
            src_ap, dst_ap, r_dst = self.steps.pop(0)
            st_, r_st = stf.next()
            sb_, r_sb = stb.next()
            kb.dma("sp", st_[:], src_ap, [], [r_st])
            kb.v("pool", "tensor_copy", [r_st], [r_sb], out=sb_[:], in_=st_[:])
            kb.dma("pool", dst_ap, sb_[:], [r_sb], [r_dst])


def prep_ffn(kb, prep, pfx, io):
    wup, wdn = io["wup"], io["wdn"]
    wub = kb.dtmp(pfx + "wub", [NPAIR, 128, 2 * KC * 128], BF16)
    wdb = kb.dtmp(pfx + "wdb", [4, 128, NPAIR * 512], BF16)
    r_wub = [Res() for _ in range(NPAIR)]
    r_wdb = [Res() for _ in range(4)]
    for i in range(NPAIR):
        for hf in range(2):
            sl = slice(hf * KC * 128, (hf + 1) * KC * 128)
            prep.add(wup[i, :, sl], wub[i, :, sl], r_wub[i])
    for dq in range(4):
        for g in range(NPAIR * 512 // D_MODEL):
            sl = slice(g * D_MODEL, (g + 1) * D_MODEL)
            prep.add(wdn[dq, :, sl], wdb[dq, :, sl], r_wdb[dq])
    return dict(wub=wub, wdb=wdb, r_wub=r_wub, r_wdb=r_wdb)


def prep_ret(kb, prep, io):
    wind = io["win"]
    wib = kb.dtmp("wib", [NS, 128, KC * 512], BF16)
    r_wib = [Res() for _ in range(NS)]
    for ns in range(NS):
        for g in range(KC * 512 // D_MODEL):
            sl = slice(g * D_MODEL, (g + 1) * D_MODEL)
            prep.add(wind[ns, :, sl], wib[ns, :, sl], r_wib[ns])
    return dict(wib=wib, r_wib=r_wib)


def emit_ffn(kb, T, pfx, io, pw, xm, r_xm, halo_all, r_halo, sel4d, identd, xo, r_xo):
    NB = T // TB
    G2d, G3d, cwd, cbd = io["G2"], io["G3"], io["cw"], io["cb"]
    wub, wdb, r_wub, r_wdb = pw["wub"], pw["wdb"], pw["r_wub"], pw["r_wdb"]

    kb.begin(identd)
    kb.junk = Pool(kb, "junk", [128, D_MODEL], F32, 1)
    kb.hn = Pool(kb, "hn", [128, D_MODEL], BF16, 2)
    xin = Pool(kb, "xin", [128, D_MODEL], F32, 2)
    G2 = kb.sb("G2s", [128, D_MODEL], F32); r_G2 = Res()
    G3 = kb.sb("G3s", [128, D_MODEL], F32); r_G3 = Res()
    cw = kb.sb("cws", [128, 2, NPAIR, 3], F32); r_cw = Res()
    cb = kb.sb("cbs", [128, 2, NPAIR], F32); r_cb = Res()
    kb.dma("sp", G2[:], G2d, [], [r_G2])
    kb.dma("sp", G3[:], G3d, [], [r_G3])
    kb.dma("sp", cw[:], cwd.rearrange("p (h i t) -> p h i t", h=2, i=NPAIR), [], [r_cw])
    kb.dma("sp", cb[:], cbd.rearrange("p (h i) -> p h i", h=2), [], [r_cb])

    hT = kb.sb("hT", [128, KC, TB], BF16); r_hT = Res()
    gT = kb.sb("gT", [128, NPAIR, TB], BF16); r_gT = [Res() for _ in range(NPAIR)]
    ft = kb.sb("ft", [128, TB // 128, D_MODEL], F32); r_ft = [Res() for _ in range(TB // 128)]
    wupp = Pool(kb, "wupp", [128, 2, KC, 128], BF16, 2)
    wdp = Pool(kb, "wdp", [128, 4, 512], BF16, 3)
    ub = Pool(kb, "ub", [128, TB + 2], F32, 4)
    tmp = Pool(kb, "tmp", [128, TB], F32, 5)
    carry = kb.sb("carry", [128, 2 * NPAIR, 2], F32); r_carry = [Res() for _ in range(2 * NPAIR)]
    hTh = kb.sb("hTh", [128, KC, 128], BF16); r_hTh = Res()
    sel4 = kb.sb("sel4", [128, 4], F32); r_sel = Res()
    kb.dma("sp", sel4[:], sel4d, [], [r_sel])
    xt, r_xt = xin.next()
    kb.v("dve", "memset", [], [r_xt], ap=xt[:], constant=0.0)
    for i in range(4):
        ct, r_ct = ft[:, i, :], r_ft[i]
        kb.v("pool", "memset", [], [r_ct], ap=ct, constant=0.0)
        kb.dma("sp", ft[126:128, i, :], halo_all[2 * i:2 * i + 2, :], [r_halo], [r_ct])
        kb.v("dve", "scalar_tensor_tensor", [r_ct, r_sel, r_xt], [r_xt], out=xt[:], in0=ct, scalar=sel4[:, i:i + 1],
             in1=xt[:], op0=ALU.mult, op1=ALU.add)
    emit_norm_T(kb, xt, r_xt, G2, r_G2, hTh, r_hTh, 0, KC)
    psc, r_psc = kb.bank()
    for i in range(NPAIR):
        wt, r_wt = wupp.next()
        kb.dma("sp", wt[:], wub[i].rearrange("p (h k c) -> p h k c", h=2, k=KC), [r_wub[i]], [r_wt])
        for hf in range(2):
            ch = hf * NPAIR + i
            for k in range(KC):
                kb.mm(psc[:, ch * 2:ch * 2 + 2], wt[:, hf, k, :], hTh[:, k, 126:128], k == 0, k == KC - 1,
                      [r_wt, r_hTh], [r_psc])
    kb.v("dve", "tensor_copy", [r_psc], r_carry, out=carry[:].rearrange("p c t -> p (c t)"), in_=psc[:, 0:4 * NPAIR])

    for b in range(NB):
        t0 = b * TB
        for tt in range(TB // 128):
            xt, r_xt = xin.next()
            kb.dma("sp", xt[:], xm[t0 + tt * 128:t0 + (tt + 1) * 128, :], [r_xm], [r_xt])
            emit_norm_T(kb, xt, r_xt, G2, r_G2, hT, r_hT, tt * 128, KC)
        for i in range(NPAIR):
            wt, r_wt = wupp.next()
            kb.dma("sp", wt[:], wub[i].rearrange("p (h k c) -> p h k c", h=2, k=KC), [r_wub[i]], [r_wt])
            cv = []
            for hf in range(2):
                ch = hf * NPAIR + i
                ps, r_ps = kb.bank()
                for k in range(KC):
                    kb.mm(ps[:, 0:TB], wt[:, hf, k, :], hT[:, k, :], k == 0, k == KC - 1, [r_wt, r_hT], [r_ps])
                u, r_u = ub.next()
                kb.act(u[:, 2:TB + 2], ps[:, 0:TB], AF.Copy, [r_ps], [r_u])
                kb.v("dve", "tensor_copy", [r_carry[ch]], [r_u], out=u[:, 0:2], in_=carry[:, ch, :])
                kb.v("dve", "tensor_copy", [r_u], [r_carry[ch]], out=carry[:, ch, :], in_=u[:, TB:TB + 2])
                t1, r_t1 = tmp.next()
                kb.act(t1[:], u[:, 2:TB + 2], AF.Identity, [r_u, r_cw, r_cb], [r_t1], scale=cw[:, hf, i, 2:3],
                       bias=cb[:, hf, i:i + 1])
                kb.v("dve", "scalar_tensor_tensor", [r_u, r_t1, r_cw], [r_t1], out=t1[:], in0=u[:, 1:TB + 1],
                     scalar=cw[:, hf, i, 1:2], in1=t1[:], op0=ALU.mult, op1=ALU.add)
                kb.v("dve", "scalar_tensor_tensor", [r_u, r_t1, r_cw], [r_t1], out=t1[:], in0=u[:, 0:TB],
                     scalar=cw[:, hf, i, 0:1], in1=t1[:], op0=ALU.mult, op1=ALU.add)
                cv.append((t1, r_t1))
            (ta, r_ta), (tb_, r_tb) = cv
            sa, r_sa = tmp.next()
            kb.act(sa[:], ta[:], AF.Silu, [r_ta], [r_sa])
            kb.v("dve", "tensor_tensor", [r_sa, r_tb], [r_gT[i]], out=gT[:, i, :], in0=sa[:], in1=tb_[:], op=ALU.mult)
        NTT = TB // 128
        for dq in range(4):
            banks = [kb.bank() for _ in range(NTT)]
            for g in range(NPAIR // 4):
                wd, r_wd = wdp.next()
                kb.dma("sp", wd[:], wdb[dq, :, g * 2048:(g + 1) * 2048].rearrange("p (f c) -> p f c", c=512),
                       [r_wdb[dq]], [r_wd])
                for j in range(4):
                    fc = g * 4 + j
                    for tt in range(NTT):
                        kb.mm(banks[tt][0][:, 0:512], gT[:, fc, tt * 128:(tt + 1) * 128], wd[:, j, :], fc == 0,
                              fc == NPAIR - 1, [r_gT[fc], r_wd], [banks[tt][1]])
            for tt in range(NTT):
                kb.act(ft[:, tt, dq * 512:(dq + 1) * 512], banks[tt][0][:, 0:512], AF.Copy, [banks[tt][1]], [r_ft[tt]])
        for tt in range(NTT):
            xt, r_xt = xin.next()
            rows = slice(t0 + tt * 128, t0 + (tt + 1) * 128)
            kb.dma("sp", xt[:], xm[rows, :], [r_xm], [r_xt])
            emit_post(kb, ft[:, tt, :], r_ft[tt], xt[:], r_xt, G3, r_G3, ft[:, tt, :], r_ft[tt])
            kb.dma("pool", xo[rows, :], ft[:, tt, :], [r_ft[tt]], [r_xo])
    kb.end()


def ffn_host_layouts(w_up, conv_w, conv_b, w_down, g2, g3):
    D, F = D_MODEL, D_FF
    w = w_up.reshape(KC, 128, 2, NPAIR, 128)
    wup = np.ascontiguousarray(w.transpose(3, 1, 2, 0, 4)).reshape(NPAIR, 128, 2 * KC * 128)
    w = w_down.reshape(NPAIR, 128, 4, 512)
    wdn = np.ascontiguousarray(w.transpose(2, 1, 0, 3)).reshape(4, 128, NPAIR * 512)
    c = conv_w.reshape(3, 2, NPAIR, 128)
    cw = np.ascontiguousarray(c.transpose(3, 1, 2, 0)).reshape(128, 2 * NPAIR * 3)
    c = conv_b.reshape(2, NPAIR, 128)
    cb = np.ascontiguousarray(c.transpose(2, 0, 1)).reshape(128, 2 * NPAIR)
    G2 = np.ascontiguousarray(np.broadcast_to(g2[None, :], (128, D)))
    G3 = np.ascontiguousarray(np.broadcast_to(g3[None, :], (128, D)))
    return dict(wup=wup, wdn=wdn, cw=cw, cb=cb, G2=G2, G3=G3, ident=np.eye(128, dtype=np.float32))


DH = 128
HPC = 4
QB = 512


def fox_inputs(kb):
    return dict(wqk=kb.din("wqk", [128, KC * 8 * 128]), wv=kb.din("wv", [128, KC * 512]), wf=kb.din("wf", [128, KC * HPC]),
                bf=kb.din("bf", [HPC, 1]), G0=kb.din("G0", [128, D_MODEL]), negmask=kb.din("negmask", [128, 128]))


def emit_fox(kb, S, io, xb, identd, att, r_att, prep=None):
    NB = S // QB
    NKB = S // 128
    wqkd, wvd, wfd, bfd, G0d, nmd = io["wqk"], io["wv"], io["wf"], io["bf"], io["G0"], io["negmask"]
    qTd = kb.dtmp("qTd", [HPC, 128, S], BF16)
    kTd = kb.dtmp("kTd", [HPC, 128, S], BF16)
    V1d = kb.dtmp("V1d", [HPC, S, DH + 1], BF16)
    Fsd = kb.dtmp("Fsd", [3, HPC, S], BF16)
    r_qTd = [Res() for _ in range(HPC)]
    r_kTd = [Res() for _ in range(HPC)]
    r_V1d = [Res() for _ in range(HPC)]
    r_Fsd = Res()

    kb.begin(identd)
    kb.junk = Pool(kb, "junk", [128, D_MODEL], F32, 1)
    kb.hn = Pool(kb, "hn", [128, D_MODEL], BF16, 2)
    A2 = kb.sb("A2", [128, 16384], BF16)
    xin = Pool(kb, "xin", None, None, 2, views=[A2[:, 8192:12288].bitcast(F32), A2[:, 12288:16384].bitcast(F32)])
    G0 = kb.sb("G0s", [128, D_MODEL], F32); r_G0 = Res()
    kb.dma("sp", G0[:], G0d, [], [r_G0])
    nmf = kb.sb("nmf", [128, 128], F32); r_nmf = Res()
    negmask = kb.sb("negmask_b", [128, 128], BF16); r_nm = Res()
    kb.dma("sp", nmf[:], nmd, [], [r_nmf])
    kb.v("dve", "tensor_copy", [r_nmf], [r_nm], out=negmask[:], in_=nmf[:])
    bft = kb.sb("bft", [HPC, 2], F32); r_bf = Res()
    kb.dma("sp", bft[:, 0:1], bfd, [], [r_bf])
    kb.v("dve", "tensor_scalar", [r_bf], [r_bf], out=bft[:, 1:2], in0=bft[:, 0:1], scalar1=-1.0, scalar2=None, op0=ALU.mult)
    ones4 = kb.sb("ones4", [HPC, QB], F32); r_ones = Res()
    kb.v("dve", "memset", [], [r_ones], ap=ones4[:], constant=1.0)

    wbig = kb.sb("wbig", [128, KC * 1024 + KC * 512], BF16); r_w = Res()
    wqk = wbig[:, 0:KC * 1024].rearrange("p (k c) -> p k c", k=KC)
    wv = wbig[:, KC * 1024:KC * 1536].rearrange("p (k c) -> p k c", k=KC)
    wf = kb.sb("wf_s", [128, KC, HPC], BF16)
    wqk_flat = wbig[:, 0:KC * 1024]
    for g in range(KC * 1024 // 2048):
        st_, r_st = xin.next()
        kb.dma("sp", st_[:], wqkd[:, g * 2048:(g + 1) * 2048], [], [r_st])
        emit_cast(kb, wqk_flat[:, g * 2048:(g + 1) * 2048], st_[:], [r_st], [r_w])
    wv_flat = wbig[:, KC * 1024:KC * 1536]
    for g in range(KC * 512 // 2048):
        st_, r_st = xin.next()
        kb.dma("sp", st_[:], wvd[:, g * 2048:(g + 1) * 2048], [], [r_st])
        emit_cast(kb, wv_flat[:, g * 2048:(g + 1) * 2048], st_[:], [r_st], [r_w])
    st_, r_st = xin.next()
    kb.dma("sp", st_[:, 0:KC * HPC], wfd, [], [r_st])
    kb.v("dve", "tensor_copy", [r_st], [r_w], out=wf[:].rearrange("p k c -> p (k c)"), in_=st_[:, 0:KC * HPC])

    hT = A2[:, 0:8192].rearrange("p (k c) -> p k c", k=KC); r_hT = Res()
    stq = Pool(kb, "stq", [128, QB], BF16, 4)
    vst = Pool(kb, "vst", [128, HPC, DH + 1], BF16, 3)
    for t_, r_ in zip(vst.tiles, vst.res):
        kb.v("dve", "memset", [], [r_], ap=t_[:], constant=1.0)
    fe = Pool(kb, "fe", [HPC, QB], F32, 1)
    Fb = Pool(kb, "Fb", [HPC, QB], F32, 2)
    fr = Pool(kb, "fr", [HPC, QB], F32, 3)
    fsb = Pool(kb, "fsb", [HPC, QB], BF16, 3)
    scale = float(DH) ** -0.5
    FK = kb.sb("FK", [128, NKB, HPC], F32); r_FK = Res()

    prevF = None
    for b in range(NB):
        t0 = b * QB
        for tt in range(QB // 128):
            xt, r_xt = xin.next()
            kb.dma("sp", xt[:], xb[t0 + tt * 128:t0 + (tt + 1) * 128, :], [], [r_xt])
            emit_norm_T(kb, xt, r_xt, G0, r_G0, hT, r_hT, tt * 128, KC)
        for j in range(8):
            ps, r_ps = kb.bank()
            for k in range(KC):
                kb.mm(ps[:, 0:QB], wqk[:, k, j * 128:(j + 1) * 128], hT[:, k, :], k == 0, k == KC - 1, [r_w, r_hT], [r_ps])
            s_, r_s = stq.next()
            h = j % HPC
            if j < HPC:
                kb.act(s_[:], ps[:, 0:QB], AF.Copy, [r_ps], [r_s], scale=scale)
                kb.dma("pool", qTd[h, :, t0:t0 + QB], s_[:], [r_s], [r_qTd[h]])
            else:
                kb.v("dve", "tensor_copy", [r_ps], [r_s], out=s_[:], in_=ps[:, 0:QB])
                kb.dma("pool", kTd[h, :, t0:t0 + QB], s_[:], [r_s], [r_kTd[h]])
        for tt in range(QB // 128):
            ps, r_ps = kb.bank()
            for k in range(KC):
                kb.mm(ps[:, 0:512], hT[:, k, tt * 128:(tt + 1) * 128], wv[:, k, :], k == 0, k == KC - 1, [r_w, r_hT], [r_ps])
            v_, r_v = vst.next()
            src = ps[:, 0:512].rearrange("p (h c) -> p h c", h=HPC)
            if tt % 2 == 0:
                kb.act(v_[:, :, 0:DH], src, AF.Copy, [r_ps], [r_v])
            else:
                kb.v("dve", "tensor_copy", [r_ps], [r_v], out=v_[:, :, 0:DH], in_=src)
            rows = slice(t0 + tt * 128, t0 + (tt + 1) * 128)
            kb.dma("pool", V1d.rearrange("h t c -> t h c")[rows, :, :], v_[:], [r_v], r_V1d)
        ps, r_ps = kb.bank()
        for k in range(KC):
            kb.mm(ps[0:HPC, 0:QB], wf[:, k, :], hT[:, k, :], k == 0, k == KC - 1, [r_w, r_hT], [r_ps])
        e_, r_e = fe.next()
        kb.act(e_[:], ps[0:HPC, 0:QB], AF.Exp, [r_ps, r_bf], [r_e], scale=-1.0, bias=bft[:, 1:2])
        kb.act(e_[:], e_[:], AF.Ln, [r_e], [r_e], bias=1.0)
        F_, r_F = Fb.next()
        init = 0.0 if prevF is None else prevF[0][:, QB - 1:QB]
        rd = [r_e, r_ones] + ([] if prevF is None else [prevF[1]])
        kb.v("dve", "tensor_tensor_scan", rd, [r_F], out=F_[:], data0=ones4[:], data1=e_[:], initial=init,
             op0=ALU.mult, op1=ALU.subtract)
        prevF = (F_, r_F)
        ps, r_ps = kb.bank()
        for tt in range(QB // 128):
            kb.tr(ps[:, tt * HPC:(tt + 1) * HPC], F_[:, tt * 128:(tt + 1) * 128], kb.identf[0:HPC, 0:HPC], [r_F, kb.r_identf], [r_ps])
        kb.act(FK[:, b * 4:(b + 1) * 4, :], ps[:, 0:4 * HPC].rearrange("p (t h) -> p t h", h=HPC), AF.Copy, [r_ps], [r_FK], scale=-1.0)
        cur, r_cur = F_, r_F
        for i in range(3):
            fb_, r_fb = fsb.next()
            kb.v("dve", "tensor_copy", [r_cur], [r_fb], out=fb_[:], in_=cur[:])
            kb.dma("pool", Fsd[i, :, t0:t0 + QB], fb_[:], [r_fb], [r_Fsd])
            if i < 2:
                ff, r_ff = fr.next()
                kb.v("dve", "tensor_copy", [r_fb], [r_ff], out=ff[:], in_=fb_[:])
                nr, r_nr = fr.next()
                kb.v("dve", "tensor_tensor", [r_cur, r_ff], [r_nr], out=nr[:], in0=cur[:], in1=ff[:], op=ALU.subtract)
                cur, r_cur = nr, r_nr

    assert S <= 16384
    kT = A2[:, 0:S]; r_kT = Res()
    kt_first = [r_hT, xin.res[0], xin.res[1]]
    V1 = kb.sb("V1", [128, NKB, DH + 1], BF16); r_V1 = Res()
    KF = wbig[0:6, 0:S]; r_KF = Res()
    qTb = Pool(kb, "qTb", [128, QB], BF16, 2)
    QFb = Pool(kb, "QFb", [6, QB], BF16, 2)
    for t_, r_ in zip(QFb.tiles, QFb.res):
        kb.v("dve", "memset", [], [r_], ap=t_[:], constant=-1.0)
    PT = Pool(kb, "PT", [128, QB], BF16, 3)
    ost = Pool(kb, "ost", [128, 4, DH], BF16, 2)
    frow = Pool(kb, "frow", [128, QB], F32, 2)
    stmp = Pool(kb, "stmp", [128, QB], F32, 3)
    ones3 = kb.sb("ones3", [3, 128], BF16); r_o3 = Res()
    kb.v("dve", "memset", [], [r_o3], ap=ones3[:], constant=1.0)
    if prep is not None:
        pstf = Pool(kb, "pstf", None, None, 2, views=[kb.junk.tiles[0], kb.sb("pstf1", [128, D_MODEL], F32)])
        pstf.res[0] = kb.junk.res[0]
        pstb = kb.hn
        per_q = -(-len(prep.steps) // max(1, (HPC * NB * 3) // 4))
    first = True
    for h in range(HPC):
        nsp = 4
        for i in range(nsp):
            cs = slice(i * S // nsp, (i + 1) * S // nsp)
            kb.dma("sp", kT[:, cs], kTd[h, :, cs], [r_kTd[h]], [r_kT] + kt_first)
            kt_first = []
        nvp = max(1, NKB // 8)
        for i in range(nvp):
            ks = slice(i * NKB // nvp, (i + 1) * NKB // nvp)
            kb.dma("sp", V1[:, ks, :], V1d[h].rearrange("(n p) c -> p n c", p=128)[:, ks, :], r_V1d, [r_V1])
        first = False
        for Q in range(NB):
            q_, r_q = qTb.next()
            kb.dma("sp", q_[:], qTd[h, :, Q * QB:(Q + 1) * QB], [r_qTd[h]], [r_q])
            qf, r_qf = QFb.next()
            kb.dma("sp", qf[0:3, :], Fsd[:, h, Q * QB:(Q + 1) * QB], [r_Fsd], [r_qf])
            if prep is not None:
                prep.run(kb, per_q, pstf, pstb)
            psF, r_psF = kb.bank(0, 4)
            kb.mm(psF[:, 0:QB], ones3[:], qf[0:3, :], True, True, [r_o3, r_qf], [r_psF])
            fr_, r_fr = frow.next()
            kb.v("dve", "tensor_copy", [r_psF], [r_fr], out=fr_[:], in_=psF[:, 0:QB])
            acc = [(kb.ps[4 + j], kb.psr[4 + j]) for j in range(4)]
            nkb = 4 * Q + 4
            pend = None
            for kbi in range(nkb + 1):
                if kbi < nkb:
                    j0 = max(0, kbi - 4 * Q)
                    c0 = j0 * 128
                    ps, r_ps = kb.bank(0, 4)
                    diag = kbi >= 4 * Q
                    kb.mm(ps[:, c0:QB], kT[:, kbi * 128:(kbi + 1) * 128], q_[:, c0:QB], True, not diag, [r_kT, r_q], [r_ps])
                    if diag:
                        kb.mm(ps[:, c0:c0 + 128], kb.identb[:], negmask[:], False, True, [kb.r_ident, r_nm], [r_ps])
                    t_, r_t = stmp.next()
                    kb.v("dve", "tensor_tensor", [r_ps, r_fr], [r_t], out=t_[:, c0:QB], in0=ps[:, c0:QB], in1=fr_[:, c0:QB], op=ALU.add)
                    p_, r_p = PT.next()
                    kb.act(p_[:, c0:QB], t_[:, c0:QB], AF.Exp, [r_t, r_FK], [r_p], bias=FK[:, kbi, h:h + 1])
                    new = (kbi, j0, p_, r_p)
                else:
                    new = None
                if pend is not None:
                    pk, pj0, pp, r_pp = pend
                    for j in range(pj0, 4):
                        kb.mm(acc[j][0][:, 0:DH + 1], pp[:, j * 128:(j + 1) * 128], V1[:, pk, :], pk == 0, pk == 4 * Q + j,
                              [r_pp, r_V1], [acc[j][1]])
                pend = new
            o_, r_o = ost.next()
            for j in range(4):
                sm, r_sm = kb.small.next()
                kb.v("dve", "reciprocal", [acc[j][1]], [r_sm], out=sm[:, 0:1], in_=acc[j][0][:, DH:DH + 1])
                kb.act(o_[:, j, :], acc[j][0][:, 0:DH], AF.Copy, [acc[j][1], r_sm], [r_o], scale=sm[:, 0:1])
            kb.dma("pool", att.rearrange("(n p) c -> p n c", p=128)[:, Q * 4:Q * 4 + 4, h * DH:(h + 1) * DH], o_[:], [r_o], [r_att])
    if prep is not None:
        prep.run(kb, len(prep.steps), pstf, pstb)
    kb.end()


def fox_host_layouts(w_in, b_f, g0, m):
    D = D_MODEL
    cols_q = w_in[:, (HPC * m) * DH:(HPC * m + HPC) * DH]
    cols_k = w_in[:, D + (HPC * m) * DH:D + (HPC * m + HPC) * DH]
    wqk = np.concatenate([cols_q, cols_k], axis=1).reshape(KC, 128, 8 * 128)
    wqk = np.ascontiguousarray(wqk.transpose(1, 0, 2)).reshape(128, KC * 1024)
    wv = w_in[:, 2 * D + HPC * m * DH:2 * D + (HPC * m + HPC) * DH].reshape(KC, 128, 512)
    wv = np.ascontiguousarray(wv.transpose(1, 0, 2)).reshape(128, KC * 512)
    wf = w_in[:, 3 * D + HPC * m:3 * D + HPC * m + HPC].reshape(KC, 128, HPC)
    wf = np.ascontiguousarray(wf.transpose(1, 0, 2)).reshape(128, KC * HPC)
    bf = np.ascontiguousarray(b_f[HPC * m:HPC * m + HPC].reshape(HPC, 1))
    G0 = np.ascontiguousarray(np.broadcast_to(g0[None, :], (128, D)))
    idx = np.arange(128)
    negmask = np.where(idx[:, None] <= idx[None, :], 0.0, -30000.0).astype(np.float32)
    return dict(wqk=wqk, wv=wv, wf=wf, bf=bf, G0=G0, ident=np.eye(128, dtype=np.float32), negmask=negmask)


def wo_inputs(kb, pfx, KCI):
    return dict(w=kb.din(pfx + "w", [128, KCI * D_MODEL]), G=kb.din(pfx + "G", [128, D_MODEL]))


def emit_wo(kb, T, KCI, io, identd, src_fn, r_src, sel4d, xr, r_xr, xo, r_xo):
    TBW = 512 if KCI <= 16 else 128
    NB = T // TBW
    wd, Gd = io["w"], io["G"]
    kb.begin(identd)
    kb.junk = Pool(kb, "junk", [128, D_MODEL], F32, 1)
    xin = Pool(kb, "xin", [128, D_MODEL], F32, 2)
    G = kb.sb("Gs", [128, D_MODEL], F32); r_G = Res()
    kb.dma("sp", G[:], Gd, [], [r_G])
    sel4 = kb.sb("sel4", [128, 4], F32); r_sel = Res()
    kb.dma("sp", sel4[:], sel4d, [], [r_sel])
    w = kb.sb("wres", [128, KCI, D_MODEL], BF16); r_w = Res()
    for k in range(KCI):
        st_, r_st = xin.next()
        kb.dma("sp", st_[:], wd[:, k * D_MODEL:(k + 1) * D_MODEL], [], [r_st])
        emit_cast(kb, w[:, k, :], st_[:], [r_st], [r_w])
    aTb = kb.sb("aTb", [128, KCI, TBW], BF16); r_a = Res()
    cand = Pool(kb, "cand", [128, KCI * 128], BF16, 2)
    hn = Pool(kb, "hnw", [128, KCI * 128], BF16, 2) if KCI <= 16 else None
    ft = Pool(kb, "ft", [128, D_MODEL], F32, 2)
    for b in range(NB):
        t0 = b * TBW
        for tt in range(TBW // 128):
            rows = slice(t0 + tt * 128, t0 + (tt + 1) * 128)
            srcs = src_fn(rows)
            if len(srcs) > 1:
                h_, r_h = hn.next()
            for i, (sap, view) in enumerate(srcs):
                c_, r_c = cand.next()
                kb.dma("sp", view(c_), sap, [r_src], [r_c])
                if len(srcs) == 1:
                    h_, r_h = c_, r_c
                elif i == 0:
                    kb.v("dve", "tensor_scalar", [r_c, r_sel], [r_h], out=h_[:], in0=c_[:], scalar1=sel4[:, 0:1], scalar2=None,
                         op0=ALU.mult)
                else:
                    kb.v("dve", "scalar_tensor_tensor", [r_c, r_sel, r_h], [r_h], out=h_[:], in0=c_[:], scalar=sel4[:, i:i + 1],
                         in1=h_[:], op0=ALU.mult, op1=ALU.add)
            emit_T(kb, h_, r_h, aTb, r_a, tt * 128, KCI)
        for tt in range(TBW // 128):
            f_, r_f = ft.next()
            for nq in range(4):
                ps, r_ps = kb.bank()
                for k in range(KCI):
                    kb.mm(ps[:, 0:512], aTb[:, k, tt * 128:(tt + 1) * 128], w[:, k, nq * 512:(nq + 1) * 512], k == 0, k == KCI - 1,
                          [r_a, r_w], [r_ps])
                if nq % 2 == 0:
                    kb.act(f_[:, nq * 512:(nq + 1) * 512], ps[:, 0:512], AF.Copy, [r_ps], [r_f])
                else:
                    kb.v("dve", "tensor_copy", [r_ps], [r_f], out=f_[:, nq * 512:(nq + 1) * 512], in_=ps[:, 0:512])
            xt, r_xt = xin.next()
            rows = slice(t0 + tt * 128, t0 + (tt + 1) * 128)
            kb.dma("sp", xt[:], xr[rows, :], [r_xr], [r_xt])
            emit_post(kb, f_[:], r_f, xt[:], r_xt, G, r_G, f_[:], r_f)
            kb.dma("pool", xo[rows, :], f_[:], [r_f], [r_xo])
    kb.end()


def wo_host_layout(w, g):
    KCI = w.shape[0] // 128
    wl = np.ascontiguousarray(w.reshape(KCI, 128, D_MODEL).transpose(1, 0, 2)).reshape(128, KCI * D_MODEL)
    G = np.ascontiguousarray(np.broadcast_to(g[None, :], (128, D_MODEL)))
    return dict(w=wl, G=G, ident=np.eye(128, dtype=np.float32))


RH = 8
DK_ = 256
DV_ = 512
RB = 1024
NS = 24


def ret_gammas():
    return [1.0 - 2.0 ** (-5.0 - h) for h in range(RH)]


def ret_inputs(kb, T):
    return dict(win=kb.din("r_win", [NS, 128, KC * 512]), G=kb.din("r_G", [128, D_MODEL]), cos2=kb.din("cos2", [T, 256]),
                sin2=kb.din("sin2", [T, 256]), DK=kb.din("DK", [128, D_MODEL]), Mp=kb.din("Mp", [128, RH * 128]),
                DQ=kb.din("DQ", [128, RH]), coef=kb.din("coef", [128, 3 * RH]))


def emit_ret(kb, T, full, io, pw, identd, xr, r_xr, Lout=None, r_L=None, Lprev=None, og=None, r_og_d=None):
    NBLK = T // RB
    NCH = T // 128
    wind, Gd, cosd, sind, DKd = io["win"], io["G"], io["cos2"], io["sin2"], io["DK"]
    Mpd, DQd, coefd = io["Mp"], io["DQ"], io["coef"]
    sfx = "f" if full else "s"
    wib, r_wib = pw["wib"], pw["r_wib"]
    qd = kb.dtmp("qd" + sfx, [T, D_MODEL], BF16)
    kd = kb.dtmp("kd" + sfx, [T, D_MODEL], BF16)
    vd = kb.dtmp("vd" + sfx, [T, 2 * D_MODEL], BF16)
    sgd = kb.dtmp("sgd" + sfx, [T, 2 * D_MODEL], BF16)
    r_qd, r_kd, r_vd, r_sgd = Res(), Res(), Res(), Res()
    slices = list(range(NS)) if full else list(range(4, 16))

    kb.begin(identd)
    kb.junk = Pool(kb, "junk", [128, D_MODEL], F32, 1)
    kb.hn = Pool(kb, "hn", [128, D_MODEL], BF16, 2)
    xin = Pool(kb, "xin", [128, D_MODEL], F32, 2)
    G = kb.sb("Gs", [128, D_MODEL], F32); r_G = Res()
    kb.dma("sp", G[:], Gd, [], [r_G])
    DK = kb.sb("DKs", [128, D_MODEL], F32); r_DK = Res()
    kb.dma("sp", DK[:], DKd, [], [r_DK])
    A1 = kb.sb("A1", [128, KC * RB], BF16); r_A1 = Res()
    hT = A1[:].rearrange("p (k c) -> p k c", k=KC)
    wpool = Pool(kb, "wsl", [128, KC * 512], BF16, 2)
    cs = kb.sb("cs", [128, 2, RB // 128, 256], F32); r_cs = Res()
    rt = Pool(kb, "rt", [128, 256], F32, 6)
    so = Pool(kb, "so", [128, 512], BF16, 4)

    for b in range(NBLK):
        t0 = b * RB
        for tt in range(RB // 128):
            xt, r_xt = xin.next()
            kb.dma("sp", xt[:], xr[t0 + tt * 128:t0 + (tt + 1) * 128, :], [r_xr], [r_xt])
            emit_norm_T(kb, xt, r_xt, G, r_G, hT, r_A1, tt * 128, KC)
        kb.dma("sp", cs[:, 0, :, :], cosd[t0:t0 + RB, :].rearrange("(n p) c -> p n c", p=128), [], [r_cs])
        kb.dma("sp", cs[:, 1, :, :], sind[t0:t0 + RB, :].rearrange("(n p) c -> p n c", p=128), [], [r_cs])
        for ns in slices:
            wt, r_wt = wpool.next()
            wv_ = wt[:].rearrange("p (k c) -> p k c", k=KC)
            kb.dma("sp", wt[:], wib[ns], [r_wib[ns]], [r_wt])
            for tt in range(RB // 128):
                ps, r_ps = kb.bank()
                for k in range(KC):
                    kb.mm(ps[:, 0:512], hT[:, k, tt * 128:(tt + 1) * 128], wv_[:, k, :], k == 0, k == KC - 1, [r_A1, r_wt], [r_ps])
                o_, r_o = so.next()
                rows = slice(t0 + tt * 128, t0 + (tt + 1) * 128)
                if ns < 8:
                    psv = ps[:, 0:512].rearrange("p (i t) -> p i t", t=2)
                    ov = o_[:].rearrange("p (i t) -> p i t", t=2)
                    c_ = cs[:, 0, tt, :]
                    s_ = cs[:, 1, tt, :]
                    t1, r_t1 = rt.next(); t2, r_t2 = rt.next()
                    kb.v("dve", "tensor_tensor", [r_ps, r_cs], [r_t1], out=t1[:], in0=psv[:, :, 0], in1=c_, op=ALU.mult)
                    kb.v("dve", "tensor_tensor", [r_ps, r_cs], [r_t2], out=t2[:], in0=psv[:, :, 1], in1=s_, op=ALU.mult)
                    kb.v("pool", "tensor_tensor", [r_t1, r_t2], [r_o], out=ov[:, :, 0], in0=t1[:], in1=t2[:], op=ALU.subtract)
                    t3, r_t3 = rt.next(); t4, r_t4 = rt.next()
                    kb.v("dve", "tensor_tensor", [r_ps, r_cs], [r_t3], out=t3[:], in0=psv[:, :, 0], in1=s_, op=ALU.mult)
                    kb.v("dve", "tensor_tensor", [r_ps, r_cs], [r_t4], out=t4[:], in0=psv[:, :, 1], in1=c_, op=ALU.mult)
                    kb.v("pool", "tensor_tensor", [r_t3, r_t4], [r_o], out=ov[:, :, 1], in0=t3[:], in1=t4[:], op=ALU.add)
                    if ns < 4:
                        kb.dma("pool", qd[rows, ns * 512:(ns + 1) * 512], o_[:], [r_o], [r_qd])
                    else:
                        kb.dma("pool", kd[rows, (ns - 4) * 512:(ns - 3) * 512], o_[:], [r_o], [r_kd])
                elif ns < 16:
                    kb.act(o_[:], ps[:, 0:512], AF.Copy, [r_ps], [r_o])
                    kb.dma("pool", vd[rows, (ns - 8) * 512:(ns - 7) * 512], o_[:], [r_o], [r_vd])
                else:
                    kb.act(o_[:], ps[:, 0:512], AF.Silu, [r_ps], [r_o])
                    kb.dma("pool", sgd[rows, (ns - 16) * 512:(ns - 15) * 512], o_[:], [r_o], [r_sgd])

    gam = ret_gammas()
    Sv = A1[:].bitcast(F32).rearrange("p (h d v) -> p h d v", h=RH, d=2)
    Sbf = wpool.tiles[0][:].rearrange("p (h d v) -> p h d v", h=RH, d=2)
    qkT = wpool.tiles[1][:].rearrange("p (b s j c) -> p b s j c", b=2, s=2, j=16)
    r_S = [Res() for _ in range(RH)]
    r_Sbf = [Res() for _ in range(RH)]
    r_qkT = [Res(), Res()]
    qk_tiles = [t[:].bitcast(BF16) for t in xin.tiles]
    r_qk = xin.res
    kdec = Pool(kb, "kdec", [128, D_MODEL], BF16, 2)
    vh = Pool(kb, "vh", [128, DV_], BF16, 4)
    barrier_w = [r_A1, wpool.res[0], wpool.res[1]] + r_S + r_Sbf + r_qkT
    kb.v("dve", "memset", [], barrier_w, ap=A1[:].bitcast(F32), constant=0.0)
    if full:
        Mp = kb.sb("Mps", [128, RH, 128], F32); r_Mp = Res()
        kb.dma("sp", Mp[:], Mpd.rearrange("p (h n) -> p h n", h=RH), [], [r_Mp])
        DQ = kb.sb("DQs", [128, RH], F32); r_DQ = Res()
        kb.dma("sp", DQ[:], DQd, [], [r_DQ])
        coef = kb.sb("coefs", [128, 3 * RH], F32); r_coef = Res()
        kb.dma("sp", coef[:], coefd, [], [r_coef])
        sgh = Pool(kb, "sgh", [128, DV_], BF16, 3)
        oh = Pool(kb, "oh", [128, DV_], F32, 3)
        ogp = Pool(kb, "ogp", [128, DV_], BF16, 3)
        sT = Pool(kb, "sT", [128, 128], BF16, 3)
        for i in range(3):
            for h in range(RH):
                for d in range(2):
                    l_, r_l = oh.next()
                    kb.dma("sp", l_[:], Lprev[i, h, d], [r_L], [r_l])
                    kb.v("dve", "scalar_tensor_tensor", [r_l, r_coef, r_S[h]], [r_S[h]], out=Sv[:, h, d, :], in0=l_[:],
                         scalar=coef[:, i * RH + h:i * RH + h + 1], in1=Sv[:, h, d, :], op0=ALU.mult, op1=ALU.add)
        for h in range(RH):
            kb.act(Sbf[:, h, :, :], Sv[:, h, :, :], AF.Copy, [r_S[h]], [r_Sbf[h]])
    for c in range(NCH):
        rows = slice(c * 128, (c + 1) * 128)
        bi = c % 2
        qk = qk_tiles[bi]
        r_q = r_qk[bi]
        if full:
            kb.dma("sp", qk[:, 0:D_MODEL], qd[rows, :], [r_qd], [r_q])
        kb.dma("sp", qk[:, D_MODEL:2 * D_MODEL], kd[rows, :], [r_kd], [r_q])
        kd_, r_kdec = kdec.next()
        kb.v("dve", "tensor_tensor", [r_q, r_DK], [r_kdec], out=kd_[:], in0=qk[:, D_MODEL:2 * D_MODEL], in1=DK[:], op=ALU.mult)
        if full:
            for s in range(2):
                for half in range(2):
                    ps, r_ps = kb.bank()
                    psb = ps[:].bitcast(BF16)
                    for j in range(8):
                        col = s * D_MODEL + (half * 8 + j) * 128
                        kb.tr(psb[:, j * 128:(j + 1) * 128], qk[:, col:col + 128], kb.identb[:], [r_q, kb.r_ident], [r_ps])
                    src = psb[:, 0:1024].rearrange("p (j c) -> p j c", c=128)
                    dst = qkT[:, bi, s, half * 8:half * 8 + 8, :]
                    if half == 0:
                        kb.act(dst, src, AF.Copy, [r_ps], [r_qkT[bi]])
                    else:
                        kb.v("dve", "tensor_copy", [r_ps], [r_qkT[bi]], out=dst, in_=src)
        for h in range(RH):
            v_, r_v = vh.next()
            kb.dma("sp", v_[:], vd[rows, h * DV_:(h + 1) * DV_], [r_vd], [r_v])
            if full:
                g_, r_g = sgh.next()
                kb.dma("sp", g_[:], sgd[rows, h * DV_:(h + 1) * DV_], [r_sgd], [r_g])
                ps, r_ps = kb.bank()
                for d in range(2):
                    kb.mm(ps[:, 0:128], qkT[:, bi, 1, h * 2 + d, :], qkT[:, bi, 0, h * 2 + d, :], d == 0, d == 1, [r_qkT[bi]], [r_ps])
                st_, r_st = sT.next()
                kb.v("dve", "tensor_tensor", [r_ps, r_Mp], [r_st], out=st_[:], in0=ps[:, 0:128], in1=Mp[:, h, :], op=ALU.mult)
                po, r_po = kb.bank()
                kb.mm(po[:, 0:DV_], st_[:], v_[:], True, False, [r_st, r_v], [r_po])
                for d in range(2):
                    kb.mm(po[:, 0:DV_], qkT[:, bi, 0, h * 2 + d, :], Sbf[:, h, d, :], False, d == 1, [r_qkT[bi], r_Sbf[h]], [r_po])
                o_, r_o = oh.next()
                kb.act(o_[:], po[:, 0:DV_], AF.Copy, [r_po, r_DQ], [r_o], scale=DQ[:, h:h + 1])
                rstd, r_rs = emit_rstd(kb, o_[:], r_o, DV_)
                og_, r_og = ogp.next()
                kb.v("dve", "scalar_tensor_tensor", [r_o, r_rs, r_g], [r_og], out=og_[:], in0=o_[:], scalar=rstd, in1=g_[:],
                     op0=ALU.mult, op1=ALU.mult)
                kb.dma("pool", og[rows, h * DV_:(h + 1) * DV_], og_[:], [r_og], [r_og_d])
            for d in range(2):
                pS, r_pS = kb.bank()
                kb.mm(pS[:, 0:DV_], kd_[:, h * DK_ + d * 128:h * DK_ + (d + 1) * 128], v_[:], True, True, [r_kdec, r_v], [r_pS])
                kb.v("dve", "scalar_tensor_tensor", [r_pS, r_S[h]], [r_S[h]], out=Sv[:, h, d, :], in0=Sv[:, h, d, :],
                     scalar=float(gam[h] ** 128), in1=pS[:, 0:DV_], op0=ALU.mult, op1=ALU.add)
            if full:
                kb.act(Sbf[:, h, :, :], Sv[:, h, :, :], AF.Copy, [r_S[h]], [r_Sbf[h]])
    if not full:
        for h in range(RH):
            for d in range(2):
                kb.dma("pool", Lout[h, d], Sv[:, h, d, :], [r_S[h]], [r_L])
    kb.end()


def ret_host_layouts(w_in, g0):
    w = w_in.reshape(KC, 128, NS, 512)
    win = np.ascontiguousarray(w.transpose(2, 1, 0, 3)).reshape(NS, 128, KC * 512)
    G = np.ascontiguousarray(np.broadcast_to(g0[None, :], (128, D_MODEL)))
    return dict(win=win, G=G, ident=np.eye(128, dtype=np.float32))


def ret_const_tables(pos0, T, j):
    theta = (1.0 / (10000.0 ** np.linspace(0.0, 1.0, DK_ // 2, dtype=np.float32))).astype(np.float32)
    ang = (np.arange(pos0, pos0 + T, dtype=np.float32)[:, None] * theta[None, :]).astype(np.float32)
    cos = np.cos(ang).astype(np.float32)
    sin = np.sin(ang).astype(np.float32)
    cos2 = np.ascontiguousarray(np.concatenate([cos, cos], axis=1))
    sin2 = np.ascontiguousarray(np.concatenate([sin, sin], axis=1))
    gam = np.array(ret_gammas(), dtype=np.float64)
    lg = np.log1p(-(2.0 ** (-5.0 - np.arange(RH, dtype=np.float64))))
    m = np.arange(128, dtype=np.float64)
    ks = DK_ ** -0.5
    DK = np.exp((127.0 - m)[:, None] * lg[None, :]) * ks
    DK = np.ascontiguousarray(np.repeat(DK, DK_, axis=1)).astype(np.float32)
    DQ = np.exp((m + 1.0)[:, None] * lg[None, :]).astype(np.float32)
    Mp = np.exp(-(m + 1.0)[:, None, None] * lg[None, :, None]) * ks
    Mp = Mp * (m[None, None, :] >= m[:, None, None])
    Mp = np.ascontiguousarray(Mp.reshape(128, RH * 128)).astype(np.float32)
    coef = np.zeros((3, RH), np.float64)
    for i in range(3):
        if i < j:
            coef[i] = np.exp(T * (j - 1 - i) * lg)
    coef = np.ascontiguousarray(np.broadcast_to(coef.reshape(1, 3 * RH), (128, 3 * RH))).astype(np.float32)
    return dict(cos2=cos2, sin2=sin2, DK=DK, DQ=DQ, Mp=Mp, coef=coef)


GROUPS = [[0, 1, 2, 3], [4, 5, 6, 7]]
TPC = SEQ * BATCH // NCORES


def build_fused():
    kb = KB()
    T, S, D = TPC, SEQ, D_MODEL
    ident = kb.din("ident", [128, 128])
    xb = kb.din("xb", [S, D])
    xs = kb.din("xs", [T, D])
    sel_prev = kb.din("sel_prev", [128, 4])
    sel_own = kb.din("sel_own", [128, 4])
    fio = fox_inputs(kb)
    w0 = wo_inputs(kb, "wo0_", 16)
    f0 = ffn_inputs(kb, "f0_")
    rio = ret_inputs(kb, T)
    w1 = wo_inputs(kb, "wo1_", 32)
    f1 = ffn_inputs(kb, "f1_")
    out = kb.dout("out", [T, D])

    prep = Prep()
    pf0 = prep_ffn(kb, prep, "f0_", f0)
    pr = prep_ret(kb, prep, rio)
    pf1 = prep_ffn(kb, prep, "f1_", f1)
    att = kb.dtmp("att", [S, HPC * DH], BF16); r_att = Res()
    emit_fox(kb, S, fio, xb, ident, att, r_att, prep=prep)
    CR = 1024
    r_attg = Res()
    attg = []
    for i in range(S // CR):
        g_ = kb.dtmp("attg%d" % i, [4 * CR, HPC * DH], BF16)
        kb.collective("AllGather", GROUPS, att[i * CR:(i + 1) * CR, :], g_, [r_att], [r_attg], flush=(i == S // CR - 1))
        attg.append(g_.rearrange("(m t) c -> t m c", m=4))

    def src0(rows):
        res = []
        for jj in range(4):
            r0 = jj * T + rows.start
            res.append((attg[r0 // CR][r0 % CR:r0 % CR + 128], lambda c_: c_[:].rearrange("p (m c) -> p m c", m=4)))
        return res

    xm0 = kb.dtmp("xm0", [T, D]); r_xm0 = Res()
    emit_wo(kb, T, 16, w0, ident, src0, r_attg, sel_own, xs, Res(), xm0, r_xm0)
    halo0 = kb.dtmp("halo0", [8, D]); r_h0 = Res()
    kb.collective("AllGather", GROUPS, xm0[T - 2:T, :], halo0, [r_xm0], [r_h0])
    x1 = kb.dtmp("x1", [T, D]); r_x1 = Res()
    emit_ffn(kb, T, "f0_", f0, pf0, xm0, r_xm0, halo0, r_h0, sel_prev, ident, x1, r_x1)

    L = kb.dtmp("Lst", [RH, 2, 128, DV_]); r_L = Res()
    emit_ret(kb, T, False, rio, pr, ident, x1, r_x1, Lout=L, r_L=r_L)
    r_La = Res()
    Lall = []
    Lf = L.rearrange("h d p v -> (h d p) v")
    for i in range(RH // 2):
        g_ = kb.dtmp("Lall%d" % i, [4 * 512, DV_])
        kb.collective("AllGather", GROUPS, Lf[i * 512:(i + 1) * 512, :], g_, [r_L], [r_La], flush=(i == RH // 2 - 1))
        Lall.append(g_.rearrange("(i h d p) v -> i h d p v", i=4, h=2, d=2))

    class _LP:
        def __getitem__(self, idx):
            i, h, d = idx
            return Lall[h // 2][i, h % 2, d]

    og = kb.dtmp("og", [T, RH * DV_], BF16); r_og = Res()
    emit_ret(kb, T, True, rio, pr, ident, x1, r_x1, r_L=r_La, Lprev=_LP(), og=og, r_og_d=r_og)

    def src1(rows):
        return [(og[rows, :], lambda c_: c_[:])]

    xm1 = kb.dtmp("xm1", [T, D]); r_xm1 = Res()
    emit_wo(kb, T, 32, w1, ident, src1, r_og, sel_own, x1, r_x1, xm1, r_xm1)
    halo1 = kb.dtmp("halo1", [8, D]); r_h1 = Res()
    kb.collective("AllGather", GROUPS, xm1[T - 2:T, :], halo1, [r_xm1], [r_h1])
    emit_ffn(kb, T, "f1_", f1, pf1, xm1, r_xm1, halo1, r_h1, sel_prev, ident, out, Res())
    return kb.nc


_NC = []


def kernel(x, norm_g, fox_w_in, fox_b_f, fox_w_o, ret_w_in, ret_w_o,
           ffn_w_up, ffn_conv_w, ffn_conv_b, ffn_w_down):
    f = lambda a: np.ascontiguousarray(np.asarray(a, dtype=np.float32))
    x, norm_g = f(x), f(norm_g)
    fox_w_in, fox_b_f, fox_w_o = f(fox_w_in), f(fox_b_f), f(fox_w_o)
    ret_w_in, ret_w_o = f(ret_w_in), f(ret_w_o)
    ffn_w_up, ffn_conv_w, ffn_conv_b, ffn_w_down = f(ffn_w_up), f(ffn_conv_w), f(ffn_conv_b), f(ffn_w_down)
    if not _NC:
        _NC.append(build_fused())
    nc = _NC[0]
    T, G = TPC, NCORES // BATCH
    shared = {"ident": np.eye(128, dtype=np.float32)}
    for k_, v_ in wo_host_layout(fox_w_o[0], norm_g[0, 1]).items():
        if k_ != "ident":
            shared["wo0_" + k_] = v_
    for k_, v_ in wo_host_layout(ret_w_o[0], norm_g[1, 1]).items():
        if k_ != "ident":
            shared["wo1_" + k_] = v_
    for l, pfx in ((0, "f0_"), (1, "f1_")):
        lay = ffn_host_layouts(ffn_w_up[l], ffn_conv_w[l], ffn_conv_b[l], ffn_w_down[l], norm_g[l, 2], norm_g[l, 3])
        for k_, v_ in lay.items():
            if k_ != "ident":
                shared[pfx + k_] = v_
    rl = ret_host_layouts(ret_w_in[0], norm_g[1, 0])
    shared["r_win"] = rl["win"]
    shared["r_G"] = rl["G"]
    foxl = [fox_host_layouts(fox_w_in[0], fox_b_f[0], norm_g[0, 0], m) for m in range(G)]
    maps = []
    for c in range(NCORES):
        b, j = c // G, c % G
        m = dict(shared)
        for k_, v_ in foxl[j].items():
            if k_ != "ident":
                m[k_] = v_
        m.update(ret_const_tables(j * T, T, j))
        m["xb"] = x[b]
        m["xs"] = np.ascontiguousarray(x[b, j * T:(j + 1) * T])
        sp = np.zeros((128, 4), np.float32)
        so = np.zeros((128, 4), np.float32)
        if j > 0:
            sp[:, j - 1] = 1.0
        so[:, j] = 1.0
        m["sel_prev"] = sp
        m["sel_own"] = so
        maps.append(m)
    res = run_bass_kernel_spmd(nc, maps, core_ids=list(range(NCORES))).results
    out = np.concatenate([res[c]["out"] for c in range(NCORES)], axis=0).reshape(BATCH, SEQ, D_MODEL)
    return out.astype(np.float32)
```

```python
import contextlib
import numpy as np
import concourse.bass as bass
import concourse.mybir as mybir
from concourse.bass_utils import run_bass_kernel_spmd

F32 = mybir.dt.float32
BF16 = mybir.dt.bfloat16
AF = mybir.ActivationFunctionType
ALU = mybir.AluOpType
AX = mybir.AxisListType

D_MODEL = 2048
SEQ = 16384
BATCH = 2
D_FF = 5632
NORM_EPS = 1e-6
NCORES = 8

SAME_ENG_SYNC = True
SEM_GEN = 30000
SEM_DMA_GEN = 1500


class Res:
    __slots__ = ("name", "w", "rs")

    def __init__(self, name=""):
        self.name = name
        self.w = None
        self.rs = []


class _Op:
    __slots__ = ("eng", "fn", "deps", "ev", "dma", "inc")


class Sched:
    ENGS = ("pe", "act", "dve", "pool", "sp")

    def __init__(self, nc, ndma=10):
        self.nc = nc
        self.ops = {e: [] for e in self.ENGS}
        self.cnt = {e: 0 for e in self.ENGS}
        self.dman = {e: 0 for e in self.ENGS}
        self.dmahist = {e: [] for e in self.ENGS}
        self.ndma = ndma
        self.ncc = 0
        self.sems = {}
        self.semstack = contextlib.ExitStack()
        self.waited = {e: {} for e in self.ENGS}
        self.fin = {}

    def op(self, eng, fn, reads=(), writes=(), dma=False, inc=None):
        o = _Op()
        o.eng = eng
        o.fn = fn
        o.dma = dma
        o.inc = inc
        deps = []
        for r in reads:
            if r.w is not None:
                deps.append(r.w)
        for r in writes:
            if r.w is not None:
                deps.append(r.w)
            deps.extend(r.rs)
        if inc is not None:
            o.ev = ("cc", self.ncc, inc)
            self.ncc += 1
        elif dma:
            n = self.dman[eng]
            self.dman[eng] = n + 1
            rnd = n // self.ndma
            o.ev = ("d_" + eng, (n % self.ndma, rnd // SEM_DMA_GEN), 16 * (rnd % SEM_DMA_GEN + 1))
            if n >= self.ndma:
                deps.append(self.dmahist[eng][n - self.ndma])
            self.dmahist[eng].append(o)
        else:
            c = self.cnt[eng]
            self.cnt[eng] = c + 1
            o.ev = ("c_" + eng, c // SEM_GEN, c % SEM_GEN + 1)
        dd = []
        seen = set()
        for d in deps:
            if id(d) in seen:
                continue
            seen.add(id(d))
            if (not d.dma) and d.eng == eng:
                if eng == "pe" or not SAME_ENG_SYNC:
                    continue
            dd.append(d)
        o.deps = dd
        for r in reads:
            r.rs.append(o)
        for r in writes:
            r.w = o
            r.rs = []
        self.ops[eng].append(o)
        return o

    def flush(self):
        nc = self.nc
        for e in self.ENGS:
            for o in self.ops[e]:
                k = (o.ev[0], o.ev[1])
                if k not in self.sems:
                    self.sems[k] = self.semstack.enter_context(nc.semaphore("s%d" % len(self.sems)))
                self.fin[k] = max(self.fin.get(k, 0), o.ev[2])
        sems = self.sems
        fin = dict(self.fin)
        ops = self.ops
        self.ops = {e: [] for e in self.ENGS}
        with nc.Block() as block:
            def run(eng_name):
                def body(eng):
                    waited = self.waited[eng_name]
                    for o in ops[eng_name]:
                        for d in o.deps:
                            k = (d.ev[0], d.ev[1])
                            if waited.get(k, 0) < d.ev[2]:
                                eng.wait_ge(sems[k], d.ev[2])
                                waited[k] = d.ev[2]
                        ins = o.fn(eng)
                        ins.then_inc(sems[(o.ev[0], o.ev[1])], o.inc if o.inc is not None else (16 if o.dma else 1))
                    for k, v in fin.items():
                        if waited.get(k, 0) < v:
                            eng.wait_ge(sems[k], v)
                            waited[k] = v
                return body

            block.tensor(run("pe"))
            block.scalar(run("act"))
            block.vector(run("dve"))
            block.gpsimd(run("pool"))
            block.sync(run("sp"))


class Pool:
    def __init__(self, kb, name, shape, dt, n, views=None):
        if views is not None:
            self.tiles = list(views)
            n = len(views)
        else:
            self.tiles = [kb.sb("%s%d" % (name, i), shape, dt) for i in range(n)]
        self.res = [Res("%s%d" % (name, i)) for i in range(n)]
        self.i = 0

    def next(self):
        i = self.i % len(self.tiles)
        self.i += 1
        return self.tiles[i], self.res[i]


class KB:
    def __init__(self):
        self.nc = bass.Bass("TRN2", target_bir_lowering=False)
        self.S = Sched(self.nc)
        self.st = None
        self.phase = 0

    def begin(self, ident_in):
        self.phase += 1
        self.st = contextlib.ExitStack()
        self.ps = []
        self.psr = []
        self.psi = 0
        setup_common(self, ident_in)

    def end(self):
        self.S.flush()
        self.st.close()
        self.st = None

    def sb(self, name, shape, dt):
        return self.st.enter_context(self.nc.sbuf_tensor("p%d_%s" % (self.phase, name), list(shape), dt))

    def alloc_psum(self, n=8):
        for i in range(n):
            self.ps.append(self.st.enter_context(self.nc.psum_tensor("p%d_ps%d" % (self.phase, i), [128, 512], F32)))
            self.psr.append(Res("ps%d" % i))

    def bank(self, lo=0, hi=8):
        i = lo + self.psi % (hi - lo)
        self.psi += 1
        return self.ps[i], self.psr[i]

    def din(self, name, shape, dt=F32):
        return self.nc.dram_tensor(name, list(shape), dt, kind="ExternalInput").ap()

    def dout(self, name, shape, dt=F32):
        return self.nc.dram_tensor(name, list(shape), dt, kind="ExternalOutput").ap()

    def dtmp(self, name, shape, dt=F32):
        return self.nc.dram_tensor(name, list(shape), dt, kind="Internal").ap()

    def collective(self, kind, groups, in_ap, out_ap, r, w, flush=True):
        if not hasattr(self, "r_cc"):
            self.r_cc = Res("cc")
        self.S.op("pool", lambda e: e.collective_compute(kind, ALU.bypass, replica_groups=groups, ins=[in_ap], outs=[out_ap]),
                  r, list(w) + [self.r_cc], dma=True, inc=1)
        if flush:
            self.S.flush()

    def dma(self, q, out, in_, r, w):
        return self.S.op(q, lambda e: e.dma_start(out=out, in_=in_), r, w, dma=True)

    def act(self, out, in_, func, r, w, **kw):
        return self.S.op("act", lambda e: e.activation(out=out, in_=in_, func=func, **kw), r, w)

    def mm(self, out, lhsT, rhs, start, stop, r, w):
        return self.S.op("pe", lambda e: e.matmul(out, lhsT=lhsT, rhs=rhs, start=start, stop=stop,
                                                  skip_group_check=True), r, w)

    def tr(self, out, in_, ident, r, w):
        return self.S.op("pe", lambda e: e.transpose(out=out, in_=in_, identity=ident), r, w)

    def v(self, eng, meth, r, w, **kw):
        return self.S.op(eng, lambda e: getattr(e, meth)(**kw), r, w)


def emit_rstd(kb, x_ap, r_x, ncols, P=128):
    junk, r_j = kb.junk.next()
    st, r_s = kb.small.next()
    kb.act(junk[0:P, 0:ncols], x_ap, AF.Square, [r_x], [r_j])
    kb.v("dve", "reduce_sum", [r_j], [r_s], out=st[0:P, 0:1], in_=junk[0:P, 0:ncols], axis=AX.X)
    kb.v("dve", "tensor_scalar", [r_s], [r_s], out=st[0:P, 1:2], in0=st[0:P, 0:1], scalar1=1.0 / ncols,
         scalar2=NORM_EPS, op0=ALU.mult, op1=ALU.add)
    kb.act(st[0:P, 2:3], st[0:P, 1:2], AF.Sqrt, [r_s], [r_s])
    kb.v("dve", "reciprocal", [r_s], [r_s], out=st[0:P, 3:4], in_=st[0:P, 2:3])
    return st[0:P, 3:4], r_s


def emit_norm_T(kb, x_tile, r_x, G, r_G, hT, r_hT, col0, KC, cp_eng="act"):
    rstd, r_s = emit_rstd(kb, x_tile[:, 0:KC * 128], r_x, KC * 128)
    hn, r_hn = kb.hn.next()
    kb.v("dve", "scalar_tensor_tensor", [r_x, r_s, r_G], [r_hn], out=hn[:, 0:KC * 128], in0=x_tile[:, 0:KC * 128],
         scalar=rstd, in1=G[:, 0:KC * 128], op0=ALU.mult, op1=ALU.mult)
    emit_T(kb, hn, r_hn, hT, r_hT, col0, KC, cp_eng)


def emit_T(kb, hn, r_hn, hT, r_hT, col0, KC, cp_eng="act"):
    for k0 in range(0, KC, 8):
        n = min(8, KC - k0)
        ps, r_ps = kb.bank()
        psb = ps[:].bitcast(BF16)
        for j in range(n):
            kb.tr(psb[:, j * 128:(j + 1) * 128], hn[:, (k0 + j) * 128:(k0 + j + 1) * 128], kb.identb[:],
                  [r_hn, kb.r_ident], [r_ps])
        src = psb[:, 0:n * 128].rearrange("p (k c) -> p k c", c=128)
        dst = hT[:, k0:k0 + n, col0:col0 + 128]
        if cp_eng == "act":
            kb.act(dst, src, AF.Copy, [r_ps], [r_hT])
        else:
            kb.v(cp_eng, "tensor_copy", [r_ps], [r_hT], out=dst, in_=src)


def emit_post(kb, f_ap, r_f, x_ap, r_x, G, r_G, out_tile, r_out, ncols=D_MODEL):
    rstd, r_s = emit_rstd(kb, f_ap, r_f, ncols)
    kb.v("dve", "scalar_tensor_tensor", [r_f, r_s, r_G], [r_out], out=out_tile, in0=f_ap, scalar=rstd, in1=G[:, 0:ncols],
         op0=ALU.mult, op1=ALU.mult)
    kb.v("pool", "tensor_tensor", [r_out, r_x], [r_out], out=out_tile, in0=out_tile, in1=x_ap, op=ALU.add)


def setup_common(kb, ident_in):
    kb.alloc_psum(8)
    idf = kb.sb("idf", [128, 128], F32)
    kb.identb = kb.sb("identb", [128, 128], BF16)
    kb.r_ident = Res("ident")
    r_idf = Res("idf")
    kb.dma("sp", idf[:], ident_in, [], [r_idf])
    kb.identf = idf
    kb.r_identf = r_idf
    kb.v("dve", "tensor_copy", [r_idf], [kb.r_ident], out=kb.identb[:], in_=idf[:])
    kb.small = Pool(kb, "small", [128, 4], F32, 6)


_cast_rr = [0]


def emit_cast(kb, out, in_, r, w):
    i = _cast_rr[0] % 3
    _cast_rr[0] += 1
    if i == 0:
        kb.v("dve", "tensor_copy", r, w, out=out, in_=in_)
    elif i == 1:
        kb.act(out, in_, AF.Copy, r, w)
    else:
        kb.v("pool", "tensor_copy", r, w, out=out, in_=in_)


TB = 512
NPAIR = D_FF // 128
KC = D_MODEL // 128


def ffn_inputs(kb, pfx):
    return dict(wup=kb.din(pfx + "wup", [NPAIR, 128, 2 * KC * 128]), wdn=kb.din(pfx + "wdn", [4, 128, NPAIR * 512]),
                G2=kb.din(pfx + "G2", [128, D_MODEL]), G3=kb.din(pfx + "G3", [128, D_MODEL]),
                cw=kb.din(pfx + "cw", [128, 2 * NPAIR * 3]), cb=kb.din(pfx + "cb", [128, 2 * NPAIR]))


class Prep:
    def __init__(self):
        self.steps = []

    def add(self, src_ap, dst_ap, r_dst):
        self.steps.append((src_ap, dst_ap, r_dst))

    def run(self, kb, n, stf, stb):
        for _ in range(n):
            if not self.steps:
                return
            src_ap, dst_ap, r_dst = self.steps.pop(0)
            st_, r_st = stf.next()
            sb_, r_sb = stb.next()
            kb.dma("sp", st_[:], src_ap, [], [r_st])
            kb.v("pool", "tensor_copy", [r_st], [r_sb], out=sb_[:], in_=st_[:])
            kb.dma("pool", dst_ap, sb_[:], [r_sb], [r_dst])


def prep_ffn(kb, prep, pfx, io):
    wup, wdn = io["wup"], io["wdn"]
    wub = kb.dtmp(pfx + "wub", [NPAIR, 128, 2 * KC * 128], BF16)
    wdb = kb.dtmp(pfx + "wdb", [4, 128, NPAIR * 512], BF16)
    r_wub = [Res() for _ in range(NPAIR)]
    r_wdb = [Res() for _ in range(4)]
    for i in range(NPAIR):
        for hf in range(2):
            sl = slice(hf * KC * 128, (hf + 1) * KC * 128)
            prep.add(wup[i, :, sl], wub[i, :, sl], r_wub[i])
    for dq in range(4):
        for g in range(NPAIR * 512 // D_MODEL):
            sl = slice(g * D_MODEL, (g + 1) * D_MODEL)
            prep.add(wdn[dq, :, sl], wdb[dq, :, sl], r_wdb[dq])
    return dict(wub=wub, wdb=wdb, r_wub=r_wub, r_wdb=r_wdb)


def prep_ret(kb, prep, io):
    wind = io["win"]
    wib = kb.dtmp("wib", [NS, 128, KC * 512], BF16)
    r_wib = [Res() for _ in range(NS)]
    for ns in range(NS):
        for g in range(KC * 512 // D_MODEL):
            sl = slice(g * D_MODEL, (g + 1) * D_MODEL)
            prep.add(wind[ns, :, sl], wib[ns, :, sl], r_wib[ns])
    return dict(wib=wib, r_wib=r_wib)


def emit_ffn(kb, T, pfx, io, pw, xm, r_xm, halo_all, r_halo, sel4d, identd, xo, r_xo):
    NB = T // TB
    G2d, G3d, cwd, cbd = io["G2"], io["G3"], io["cw"], io["cb"]
    wub, wdb, r_wub, r_wdb = pw["wub"], pw["wdb"], pw["r_wub"], pw["r_wdb"]

    kb.begin(identd)
    kb.junk = Pool(kb, "junk", [128, D_MODEL], F32, 1)
    kb.hn = Pool(kb, "hn", [128, D_MODEL], BF16, 2)
    xin = Pool(kb, "xin", [128, D_MODEL], F32, 2)
    G2 = kb.sb("G2s", [128, D_MODEL], F32); r_G2 = Res()
    G3 = kb.sb("G3s", [128, D_MODEL], F32); r_G3 = Res()
    cw = kb.sb("cws", [128, 2, NPAIR, 3], F32); r_cw = Res()
    cb = kb.sb("cbs", [128, 2, NPAIR], F32); r_cb = Res()
    kb.dma("sp", G2[:], G2d, [], [r_G2])
    kb.dma("sp", G3[:], G3d, [], [r_G3])
    kb.dma("sp", cw[:], cwd.rearrange("p (h i t) -> p h i t", h=2, i=NPAIR), [], [r_cw])
    kb.dma("sp", cb[:], cbd.rearrange("p (h i) -> p h i", h=2), [], [r_cb])

    hT = kb.sb("hT", [128, KC, TB], BF16); r_hT = Res()
    gT = kb.sb("gT", [128, NPAIR, TB], BF16); r_gT = [Res() for _ in range(NPAIR)]
    ft = kb.sb("ft", [128, TB // 128, D_MODEL], F32); r_ft = [Res() for _ in range(TB // 128)]
    wupp = Pool(kb, "wupp", [128, 2, KC, 128], BF16, 2)
    wdp = Pool(kb, "wdp", [128, 4, 512], BF16, 3)
    ub = Pool(kb, "ub", [128, TB + 2], F32, 4)
    tmp = Pool(kb, "tmp", [128, TB], F32, 5)
    carry = kb.sb("carry", [128, 2 * NPAIR, 2], F32); r_carry = [Res() for _ in range(2 * NPAIR)]
    hTh = kb.sb("hTh", [128, KC, 128], BF16); r_hTh = Res()
    sel4 = kb.sb("sel4", [128, 4], F32); r_sel = Res()
    kb.dma("sp", sel4[:], sel4d, [], [r_sel])
    xt, r_xt = xin.next()
    kb.v("dve", "memset", [], [r_xt], ap=xt[:], constant=0.0)
    for i in range(4):
        ct, r_ct = ft[:, i, :], r_ft[i]
        kb.v("pool", "memset", [], [r_ct], ap=ct, constant=0.0)
        kb.dma("sp", ft[126:128, i, :], halo_all[2 * i:2 * i + 2, :], [r_halo], [r_ct])
        kb.v("dve", "scalar_tensor_tensor", [r_ct, r_sel, r_xt], [r_xt], out=xt[:], in0=ct, scalar=sel4[:, i:i + 1],
             in1=xt[:], op0=ALU.mult, op1=ALU.add)
    emit_norm_T(kb, xt, r_xt, G2, r_G2, hTh, r_hTh, 0, KC)
    psc, r_psc = kb.bank()
    for i in range(NPAIR):
        wt, r_wt = wupp.next()
        kb.dma("sp", wt[:], wub[i].rearrange("p (h k c) -> p h k c", h=2, k=KC), [r_wub[i]], [r_wt])
        for hf in range(2):
            ch = hf * NPAIR + i
            for k in range(KC):
                kb.mm(psc[:, ch * 2:ch * 2 + 2], wt[:, hf, k, :], hTh[:, k, 126:128], k == 0, k == KC - 1,
                      [r_wt, r_hTh], [r_psc])
    kb.v("dve", "tensor_copy", [r_psc], r_carry, out=carry[:].rearrange("p c t -> p (c t)"), in_=psc[:, 0:4 * NPAIR])

    for b in range(NB):
        t0 = b * TB
        for tt in range(TB // 128):
            xt, r_xt = xin.next()
            kb.dma("sp", xt[:], xm[t0 + tt * 128:t0 + (tt + 1) * 128, :], [r_xm], [r_xt])
            emit_norm_T(kb, xt, r_xt, G2, r_G2, hT, r_hT, tt * 128, KC)
        for i in range(NPAIR):
            wt, r_wt = wupp.next()
            kb.dma("sp", wt[:], wub[i].rearrange("p (h k c) -> p h k c", h=2, k=KC), [r_wub[i]], [r_wt])
            cv = []
            for hf in range(2):
                ch = hf * NPAIR + i
                ps, r_ps = kb.bank()
                for k in range(KC):
                    kb.mm(ps[:, 0:TB], wt[:, hf, k, :], hT[:, k, :], k == 0, k == KC - 1, [r_wt, r_hT], [r_ps])
                u, r_u = ub.next()
                kb.act(u[:, 2:TB + 2], ps[:, 0:TB], AF.Copy, [r_ps], [r_u])
                kb.v("dve", "tensor_copy", [r_carry[ch]], [r_u], out=u[:, 0:2], in_=carry[:, ch, :])
                kb.v("dve", "tensor_copy", [r_u], [r_carry[ch]], out=carry[:, ch, :], in_=u[:, TB:TB + 2])
                t1, r_t1 = tmp.next()
                kb.act(t1[:], u[:, 2:TB + 2], AF.Identity, [r_u, r_cw, r_cb], [r_t1], scale=cw[:, hf, i, 2:3],
                       bias=cb[:, hf, i:i + 1])
                kb.v("dve", "scalar_tensor_tensor", [r_u, r_t1, r_cw], [r_t1], out=t1[:], in0=u[:, 1:TB + 1],
                     scalar=cw[:, hf, i, 1:2], in1=t1[:], op0=ALU.mult, op1=ALU.add)
                kb.v("dve", "scalar_tensor_tensor", [r_u, r_t1, r_cw], [r_t1], out=t1[:], in0=u[:, 0:TB],
                     scalar=cw[:, hf, i, 0:1], in1=t1[:], op0=ALU.mult, op1=ALU.add)
                cv.append((t1, r_t1))
            (ta, r_ta), (tb_, r_tb) = cv
            sa, r_sa = tmp.next()
            kb.act(sa[:], ta[:], AF.Silu, [r_ta], [r_sa])
            kb.v("dve", "tensor_tensor", [r_sa, r_tb], [r_gT[i]], out=gT[:, i, :], in0=sa[:], in1=tb_[:], op=ALU.mult)
        NTT = TB // 128
        for dq in range(4):
            banks = [kb.bank() for _ in range(NTT)]
            for g in range(NPAIR // 4):
                wd, r_wd = wdp.next()
                kb.dma("sp", wd[:], wdb[dq, :, g * 2048:(g + 1) * 2048].rearrange("p (f c) -> p f c", c=512),
                       [r_wdb[dq]], [r_wd])
                for j in range(4):
                    fc = g * 4 + j
                    for tt in range(NTT):
                        kb.mm(banks[tt][0][:, 0:512], gT[:, fc, tt * 128:(tt + 1) * 128], wd[:, j, :], fc == 0,
                              fc == NPAIR - 1, [r_gT[fc], r_wd], [banks[tt][1]])
            for tt in range(NTT):
                kb.act(ft[:, tt, dq * 512:(dq + 1) * 512], banks[tt][0][:, 0:512], AF.Copy, [banks[tt][1]], [r_ft[tt]])
        for tt in range(NTT):
            xt, r_xt = xin.next()
            rows = slice(t0 + tt * 128, t0 + (tt + 1) * 128)
            kb.dma("sp", xt[:], xm[rows, :], [r_xm], [r_xt])
            emit_post(kb, ft[:, tt, :], r_ft[tt], xt[:], r_xt, G3, r_G3, ft[:, tt, :], r_ft[tt])
            kb.dma("pool", xo[rows, :], ft[:, tt, :], [r_ft[tt]], [r_xo])
    kb.end()


def ffn_host_layouts(w_up, conv_w, conv_b, w_down, g2, g3):
    D, F = D_MODEL, D_FF
    w = w_up.reshape(KC, 128, 2, NPAIR, 128)
    wup = np.ascontiguousarray(w.transpose(3, 1, 2, 0, 4)).reshape(NPAIR, 128, 2 * KC * 128)
    w = w_down.reshape(NPAIR, 128, 4, 512)
    wdn = np.ascontiguousarray(w.transpose(2, 1, 0, 3)).reshape(4, 128, NPAIR * 512)
    c = conv_w.reshape(3, 2, NPAIR, 128)
    cw = np.ascontiguousarray(c.transpose(3, 1, 2, 0)).reshape(128, 2 * NPAIR * 3)
    c = conv_b.reshape(2, NPAIR, 128)
    cb = np.ascontiguousarray(c.transpose(2, 0, 1)).reshape(128, 2 * NPAIR)
    G2 = np.ascontiguousarray(np.broadcast_to(g2[None, :], (128, D)))
    G3 = np.ascontiguousarray(np.broadcast_to(g3[None, :], (128, D)))
    return dict(wup=wup, wdn=wdn, cw=cw, cb=cb, G2=G2, G3=G3, ident=np.eye(128, dtype=np.float32))


DH = 128
HPC = 4
QB = 512


def fox_inputs(kb):
    return dict(wqk=kb.din("wqk", [128, KC * 8 * 128]), wv=kb.din("wv", [128, KC * 512]), wf=kb.din("wf", [128, KC * HPC]),
                bf=kb.din("bf", [HPC, 1]), G0=kb.din("G0", [128, D_MODEL]), negmask=kb.din("negmask", [128, 128]))


def emit_fox(kb, S, io, xb, identd, att, r_att, prep=None):
    NB = S // QB
    NKB = S // 128
    wqkd, wvd, wfd, bfd, G0d, nmd = io["wqk"], io["wv"], io["wf"], io["bf"], io["G0"], io["negmask"]
    qTd = kb.dtmp("qTd", [HPC, 128, S], BF16)
    kTd = kb.dtmp("kTd", [HPC, 128, S], BF16)
    V1d = kb.dtmp("V1d", [HPC, S, DH + 1], BF16)
    Fsd = kb.dtmp("Fsd", [3, HPC, S], BF16)
    r_qTd = [Res() for _ in range(HPC)]
    r_kTd = [Res() for _ in range(HPC)]
    r_V1d = [Res() for _ in range(HPC)]
    r_Fsd = Res()

    kb.begin(identd)
    kb.junk = Pool(kb, "junk", [128, D_MODEL], F32, 1)
    kb.hn = Pool(kb, "hn", [128, D_MODEL], BF16, 2)
    A2 = kb.sb("A2", [128, 16384], BF16)
    xin = Pool(kb, "xin", None, None, 2, views=[A2[:, 8192:12288].bitcast(F32), A2[:, 12288:16384].bitcast(F32)])
    G0 = kb.sb("G0s", [128, D_MODEL], F32); r_G0 = Res()
    kb.dma("sp", G0[:], G0d, [], [r_G0])
    nmf = kb.sb("nmf", [128, 128], F32); r_nmf = Res()
    negmask = kb.sb("negmask_b", [128, 128], BF16); r_nm = Res()
    kb.dma("sp", nmf[:], nmd, [], [r_nmf])
    kb.v("dve", "tensor_copy", [r_nmf], [r_nm], out=negmask[:], in_=nmf[:])
    bft = kb.sb("bft", [HPC, 2], F32); r_bf = Res()
    kb.dma("sp", bft[:, 0:1], bfd, [], [r_bf])
    kb.v("dve", "tensor_scalar", [r_bf], [r_bf], out=bft[:, 1:2], in0=bft[:, 0:1], scalar1=-1.0, scalar2=None, op0=ALU.mult)
    ones4 = kb.sb("ones4", [HPC, QB], F32); r_ones = Res()
    kb.v("dve", "memset", [], [r_ones], ap=ones4[:], constant=1.0)

    wbig = kb.sb("wbig", [128, KC * 1024 + KC * 512], BF16); r_w = Res()
    wqk = wbig[:, 0:KC * 1024].rearrange("p (k c) -> p k c", k=KC)
    wv = wbig[:, KC * 1024:KC * 1536].rearrange("p (k c) -> p k c", k=KC)
    wf = kb.sb("wf_s", [128, KC, HPC], BF16)
    wqk_flat = wbig[:, 0:KC * 1024]
    for g in range(KC * 1024 // 2048):
        st_, r_st = xin.next()
        kb.dma("sp", st_[:], wqkd[:, g * 2048:(g + 1) * 2048], [], [r_st])
        emit_cast(kb, wqk_flat[:, g * 2048:(g + 1) * 2048], st_[:], [r_st], [r_w])
    wv_flat = wbig[:, KC * 1024:KC * 1536]
    for g in range(KC * 512 // 2048):
        st_, r_st = xin.next()
        kb.dma("sp", st_[:], wvd[:, g * 2048:(g + 1) * 2048], [], [r_st])
        emit_cast(kb, wv_flat[:, g * 2048:(g + 1) * 2048], st_[:], [r_st], [r_w])
    st_, r_st = xin.next()
    kb.dma("sp", st_[:, 0:KC * HPC], wfd, [], [r_st])
    kb.v("dve", "tensor_copy", [r_st], [r_w], out=wf[:].rearrange("p k c -> p (k c)"), in_=st_[:, 0:KC * HPC])

    hT = A2[:, 0:8192].rearrange("p (k c) -> p k c", k=KC); r_hT = Res()
    stq = Pool(kb, "stq", [128, QB], BF16, 4)
    vst = Pool(kb, "vst", [128, HPC, DH + 1], BF16, 3)
    for t_, r_ in zip(vst.tiles, vst.res):
        kb.v("dve", "memset", [], [r_], ap=t_[:], constant=1.0)
    fe = Pool(kb, "fe", [HPC, QB], F32, 1)
    Fb = Pool(kb, "Fb", [HPC, QB], F32, 2)
    fr = Pool(kb, "fr", [HPC, QB], F32, 3)
    fsb = Pool(kb, "fsb", [HPC, QB], BF16, 3)
    scale = float(DH) ** -0.5
    FK = kb.sb("FK", [128, NKB, HPC], F32); r_FK = Res()

    prevF = None
    for b in range(NB):
        t0 = b * QB
        for tt in range(QB // 128):
            xt, r_xt = xin.next()
            kb.dma("sp", xt[:], xb[t0 + tt * 128:t0 + (tt + 1) * 128, :], [], [r_xt])
            emit_norm_T(kb, xt, r_xt, G0, r_G0, hT, r_hT, tt * 128, KC)
        for j in range(8):
            ps, r_ps = kb.bank()
            for k in range(KC):
                kb.mm(ps[:, 0:QB], wqk[:, k, j * 128:(j + 1) * 128], hT[:, k, :], k == 0, k == KC - 1, [r_w, r_hT], [r_ps])
            s_, r_s = stq.next()
            h = j % HPC
            if j < HPC:
                kb.act(s_[:], ps[:, 0:QB], AF.Copy, [r_ps], [r_s], scale=scale)
                kb.dma("pool", qTd[h, :, t0:t0 + QB], s_[:], [r_s], [r_qTd[h]])
            else:
                kb.v("dve", "tensor_copy", [r_ps], [r_s], out=s_[:], in_=ps[:, 0:QB])
                kb.dma("pool", kTd[h, :, t0:t0 + QB], s_[:], [r_s], [r_kTd[h]])
        for tt in range(QB // 128):
            ps, r_ps = kb.bank()
            for k in range(KC):
                kb.mm(ps[:, 0:512], hT[:, k, tt * 128:(tt + 1) * 128], wv[:, k, :], k == 0, k == KC - 1, [r_w, r_hT], [r_ps])
            v_, r_v = vst.next()
            src = ps[:, 0:512].rearrange("p (h c) -> p h c", h=HPC)
            if tt % 2 == 0:
                kb.act(v_[:, :, 0:DH], src, AF.Copy, [r_ps], [r_v])
            else:
                kb.v("dve", "tensor_copy", [r_ps], [r_v], out=v_[:, :, 0:DH], in_=src)
            rows = slice(t0 + tt * 128, t0 + (tt + 1) * 128)
            kb.dma("pool", V1d.rearrange("h t c -> t h c")[rows, :, :], v_[:], [r_v], r_V1d)
        ps, r_ps = kb.bank()
        for k in range(KC):
            kb.mm(ps[0:HPC, 0:QB], wf[:, k, :], hT[:, k, :], k == 0, k == KC - 1, [r_w, r_hT], [r_ps])
        e_, r_e = fe.next()
        kb.act(e_[:], ps[0:HPC, 0:QB], AF.Exp, [r_ps, r_bf], [r_e], scale=-1.0, bias=bft[:, 1:2])
        kb.act(e_[:], e_[:], AF.Ln, [r_e], [r_e], bias=1.0)
        F_, r_F = Fb.next()
        init = 0.0 if prevF is None else prevF[0][:, QB - 1:QB]
        rd = [r_e, r_ones] + ([] if prevF is None else [prevF[1]])
        kb.v("dve", "tensor_tensor_scan", rd, [r_F], out=F_[:], data0=ones4[:], data1=e_[:], initial=init,
             op0=ALU.mult, op1=ALU.subtract)
        prevF = (F_, r_F)
        ps, r_ps = kb.bank()
        for tt in range(QB // 128):
            kb.tr(ps[:, tt * HPC:(tt + 1) * HPC], F_[:, tt * 128:(tt + 1) * 128], kb.identf[0:HPC, 0:HPC], [r_F, kb.r_identf], [r_ps])
        kb.act(FK[:, b * 4:(b + 1) * 4, :], ps[:, 0:4 * HPC].rearrange("p (t h) -> p t h", h=HPC), AF.Copy, [r_ps], [r_FK], scale=-1.0)
        cur, r_cur = F_, r_F
        for i in range(3):
            fb_, r_fb = fsb.next()
            kb.v("dve", "tensor_copy", [r_cur], [r_fb], out=fb_[:], in_=cur[:])
            kb.dma("pool", Fsd[i, :, t0:t0 + QB], fb_[:], [r_fb], [r_Fsd])
            if i < 2:
                ff, r_ff = fr.next()
                kb.v("dve", "tensor_copy", [r_fb], [r_ff], out=ff[:], in_=fb_[:])
                nr, r_nr = fr.next()
                kb.v("dve", "tensor_tensor", [r_cur, r_ff], [r_nr], out=nr[:], in0=cur[:], in1=ff[:], op=ALU.subtract)
                cur, r_cur = nr, r_nr

    assert S <= 16384
    kT = A2[:, 0:S]; r_kT = Res()
    kt_first = [r_hT, xin.res[0], xin.res[1]]
    V1 = kb.sb("V1", [128, NKB, DH + 1], BF16); r_V1 = Res()
    KF = wbig[0:6, 0:S]; r_KF = Res()
    qTb = Pool(kb, "qTb", [128, QB], BF16, 2)
    QFb = Pool(kb, "QFb", [6, QB], BF16, 2)
    for t_, r_ in zip(QFb.tiles, QFb.res):
        kb.v("dve", "memset", [], [r_], ap=t_[:], constant=-1.0)
    PT = Pool(kb, "PT", [128, QB], BF16, 6)
    ost = Pool(kb, "ost", [128, 4, DH], BF16, 2)
    frow = Pool(kb, "frow", [128, QB], F32, 2)
    stmp = Pool(kb, "stmp", [128, QB], F32, 4)
    ones3 = kb.sb("ones3", [3, 128], BF16); r_o3 = Res()
    kb.v("dve", "memset", [], [r_o3], ap=ones3[:], constant=1.0)
    if prep is not None:
        pstf = Pool(kb, "pstf", None, None, 2, views=[kb.junk.tiles[0], kb.sb("pstf1", [128, D_MODEL], F32)])
        pstf.res[0] = kb.junk.res[0]
        pstb = kb.hn
        per_q = -(-len(prep.steps) // max(1, (HPC * NB * 3) // 4))
    first = True
    for h in range(HPC):
        nsp = 4
        for i in range(nsp):
            cs = slice(i * S // nsp, (i + 1) * S // nsp)
            kb.dma("sp", kT[:, cs], kTd[h, :, cs], [r_kTd[h]], [r_kT] + kt_first)
            kt_first = []
        nvp = max(1, NKB // 8)
        for i in range(nvp):
            ks = slice(i * NKB // nvp, (i + 1) * NKB // nvp)
            kb.dma("sp", V1[:, ks, :], V1d[h].rearrange("(n p) c -> p n c", p=128)[:, ks, :], r_V1d, [r_V1])
        first = False
        for Q in range(NB):
            q_, r_q = qTb.next()
            kb.dma("sp", q_[:], qTd[h, :, Q * QB:(Q + 1) * QB], [r_qTd[h]], [r_q])
            qf, r_qf = QFb.next()
            kb.dma("sp", qf[0:3, :], Fsd[:, h, Q * QB:(Q + 1) * QB], [r_Fsd], [r_qf])
            if prep is not None:
                prep.run(kb, per_q, pstf, pstb)
            psF, r_psF = kb.bank(0, 4)
            kb.mm(psF[:, 0:QB], ones3[:], qf[0:3, :], True, True, [r_o3, r_qf], [r_psF])
            fr_, r_fr = frow.next()
            kb.v("dve", "tensor_copy", [r_psF], [r_fr], out=fr_[:], in_=psF[:, 0:QB])
            acc = [(kb.ps[4 + j], kb.psr[4 + j]) for j in range(4)]
            nkb = 4 * Q + 4
            LAG = 3
            pend = []
            for kbi in range(nkb + LAG):
                if kbi < nkb:
                    j0 = max(0, kbi - 4 * Q)
                    c0 = j0 * 128
                    ps, r_ps = kb.bank(0, 4)
                    diag = kbi >= 4 * Q
                    kb.mm(ps[:, c0:QB], kT[:, kbi * 128:(kbi + 1) * 128], q_[:, c0:QB], True, not diag, [r_kT, r_q], [r_ps])
                    if diag:
                        kb.mm(ps[:, c0:c0 + 128], kb.identb[:], negmask[:], False, True, [kb.r_ident, r_nm], [r_ps])
                    t_, r_t = stmp.next()
                    kb.v("dve", "tensor_tensor", [r_ps, r_fr], [r_t], out=t_[:, c0:QB], in0=ps[:, c0:QB], in1=fr_[:, c0:QB], op=ALU.add)
                    p_, r_p = PT.next()
                    kb.act(p_[:, c0:QB], t_[:, c0:QB], AF.Exp, [r_t, r_FK], [r_p], bias=FK[:, kbi, h:h + 1])
                    pend.append((kbi, j0, p_, r_p))
                if kbi >= LAG or kbi >= nkb:
                    if pend and (len(pend) > LAG or kbi >= nkb):
                        pk, pj0, pp, r_pp = pend.pop(0)
                        for j in range(pj0, 4):
                            kb.mm(acc[j][0][:, 0:DH + 1], pp[:, j * 128:(j + 1) * 128], V1[:, pk, :], pk == 0, pk == 4 * Q + j,
                                  [r_pp, r_V1], [acc[j][1]])
            assert not pend
            o_, r_o = ost.next()
            for j in range(4):
                sm, r_sm = kb.small.next()
                kb.v("dve", "reciprocal", [acc[j][1]], [r_sm], out=sm[:, 0:1], in_=acc[j][0][:, DH:DH + 1])
                kb.act(o_[:, j, :], acc[j][0][:, 0:DH], AF.Copy, [acc[j][1], r_sm], [r_o], scale=sm[:, 0:1])
            kb.dma("pool", att.rearrange("(n p) c -> p n c", p=128)[:, Q * 4:Q * 4 + 4, h * DH:(h + 1) * DH], o_[:], [r_o], [r_att])
    if prep is not None:
        prep.run(kb, len(prep.steps), pstf, pstb)
    kb.end()


def fox_host_layouts(w_in, b_f, g0, m):
    D = D_MODEL
    cols_q = w_in[:, (HPC * m) * DH:(HPC * m + HPC) * DH]
    cols_k = w_in[:, D + (HPC * m) * DH:D + (HPC * m + HPC) * DH]
    wqk = np.concatenate([cols_q, cols_k], axis=1).reshape(KC, 128, 8 * 128)
    wqk = np.ascontiguousarray(wqk.transpose(1, 0, 2)).reshape(128, KC * 1024)
    wv = w_in[:, 2 * D + HPC * m * DH:2 * D + (HPC * m + HPC) * DH].reshape(KC, 128, 512)
    wv = np.ascontiguousarray(wv.transpose(1, 0, 2)).reshape(128, KC * 512)
    wf = w_in[:, 3 * D + HPC * m:3 * D + HPC * m + HPC].reshape(KC, 128, HPC)
    wf = np.ascontiguousarray(wf.transpose(1, 0, 2)).reshape(128, KC * HPC)
    bf = np.ascontiguousarray(b_f[HPC * m:HPC * m + HPC].reshape(HPC, 1))
    G0 = np.ascontiguousarray(np.broadcast_to(g0[None, :], (128, D)))
    idx = np.arange(128)
    negmask = np.where(idx[:, None] <= idx[None, :], 0.0, -30000.0).astype(np.float32)
    return dict(wqk=wqk, wv=wv, wf=wf, bf=bf, G0=G0, ident=np.eye(128, dtype=np.float32), negmask=negmask)


def wo_inputs(kb, pfx, KCI):
    return dict(w=kb.din(pfx + "w", [128, KCI * D_MODEL]), G=kb.din(pfx + "G", [128, D_MODEL]))


def emit_wo(kb, T, KCI, io, identd, src_fn, r_src, sel4d, xr, r_xr, xo, r_xo):
    TBW = 512 if KCI <= 16 else 128
    NB = T // TBW
    wd, Gd = io["w"], io["G"]
    kb.begin(identd)
    kb.junk = Pool(kb, "junk", [128, D_MODEL], F32, 1)
    xin = Pool(kb, "xin", [128, D_MODEL], F32, 2)
    G = kb.sb("Gs", [128, D_MODEL], F32); r_G = Res()
    kb.dma("sp", G[:], Gd, [], [r_G])
    sel4 = kb.sb("sel4", [128, 4], F32); r_sel = Res()
    kb.dma("sp", sel4[:], sel4d, [], [r_sel])
    w = kb.sb("wres", [128, KCI, D_MODEL], BF16); r_w = Res()
    for k in range(KCI):
        st_, r_st = xin.next()
        kb.dma("sp", st_[:], wd[:, k * D_MODEL:(k + 1) * D_MODEL], [], [r_st])
        emit_cast(kb, w[:, k, :], st_[:], [r_st], [r_w])
    aTb = kb.sb("aTb", [128, KCI, TBW], BF16); r_a = Res()
    cand = Pool(kb, "cand", [128, KCI * 128], BF16, 2)
    hn = Pool(kb, "hnw", [128, KCI * 128], BF16, 2) if KCI <= 16 else None
    ft = Pool(kb, "ft", [128, D_MODEL], F32, 2)
    for b in range(NB):
        t0 = b * TBW
        for tt in range(TBW // 128):
            rows = slice(t0 + tt * 128, t0 + (tt + 1) * 128)
            srcs = src_fn(rows)
            if len(srcs) > 1:
                h_, r_h = hn.next()
            for i, (sap, view) in enumerate(srcs):
                c_, r_c = cand.next()
                kb.dma("sp", view(c_), sap, [r_src], [r_c])
                if len(srcs) == 1:
                    h_, r_h = c_, r_c
                elif i == 0:
                    kb.v("dve", "tensor_scalar", [r_c, r_sel], [r_h], out=h_[:], in0=c_[:], scalar1=sel4[:, 0:1], scalar2=None,
                         op0=ALU.mult)
                else:
                    kb.v("dve", "scalar_tensor_tensor", [r_c, r_sel, r_h], [r_h], out=h_[:], in0=c_[:], scalar=sel4[:, i:i + 1],
                         in1=h_[:], op0=ALU.mult, op1=ALU.add)
            emit_T(kb, h_, r_h, aTb, r_a, tt * 128, KCI)
        for tt in range(TBW // 128):
            f_, r_f = ft.next()
            for nq in range(4):
                ps, r_ps = kb.bank()
                for k in range(KCI):
                    kb.mm(ps[:, 0:512], aTb[:, k, tt * 128:(tt + 1) * 128], w[:, k, nq * 512:(nq + 1) * 512], k == 0, k == KCI - 1,
                          [r_a, r_w], [r_ps])
                if nq % 2 == 0:
                    kb.act(f_[:, nq * 512:(nq + 1) * 512], ps[:, 0:512], AF.Copy, [r_ps], [r_f])
                else:
                    kb.v("dve", "tensor_copy", [r_ps], [r_f], out=f_[:, nq * 512:(nq + 1) * 512], in_=ps[:, 0:512])
            xt, r_xt = xin.next()
            rows = slice(t0 + tt * 128, t0 + (tt + 1) * 128)
            kb.dma("sp", xt[:], xr[rows, :], [r_xr], [r_xt])
            emit_post(kb, f_[:], r_f, xt[:], r_xt, G, r_G, f_[:], r_f)
            kb.dma("pool", xo[rows, :], f_[:], [r_f], [r_xo])
    kb.end()


def wo_host_layout(w, g):
    KCI = w.shape[0] // 128
    wl = np.ascontiguousarray(w.reshape(KCI, 128, D_MODEL).transpose(1, 0, 2)).reshape(128, KCI * D_MODEL)
    G = np.ascontiguousarray(np.broadcast_to(g[None, :], (128, D_MODEL)))
    return dict(w=wl, G=G, ident=np.eye(128, dtype=np.float32))


RH = 8
DK_ = 256
DV_ = 512
RB = 1024
NS = 24


def ret_gammas():
    return [1.0 - 2.0 ** (-5.0 - h) for h in range(RH)]


def ret_inputs(kb, T):
    return dict(win=kb.din("r_win", [NS, 128, KC * 512]), G=kb.din("r_G", [128, D_MODEL]), cos2=kb.din("cos2", [T, 256]),
                sin2=kb.din("sin2", [T, 256]), DK=kb.din("DK", [128, D_MODEL]), Mp=kb.din("Mp", [128, RH * 128]),
                DQ=kb.din("DQ", [128, RH]), coef=kb.din("coef", [128, 3 * RH]))


def emit_ret(kb, T, full, io, pw, identd, xr, r_xr, Lout=None, r_L=None, Lprev=None, og=None, r_og_d=None):
    NBLK = T // RB
    NCH = T // 128
    wind, Gd, cosd, sind, DKd = io["win"], io["G"], io["cos2"], io["sin2"], io["DK"]
    Mpd, DQd, coefd = io["Mp"], io["DQ"], io["coef"]
    sfx = "f" if full else "s"
    wib, r_wib = pw["wib"], pw["r_wib"]
    qd = kb.dtmp("qd" + sfx, [T, D_MODEL], BF16)
    kd = kb.dtmp("kd" + sfx, [T, D_MODEL], BF16)
    vd = kb.dtmp("vd" + sfx, [T, 2 * D_MODEL], BF16)
    sgd = kb.dtmp("sgd" + sfx, [T, 2 * D_MODEL], BF16)
    r_qd, r_kd, r_vd, r_sgd = Res(), Res(), Res(), Res()
    slices = list(range(NS)) if full else list(range(4, 16))

    kb.begin(identd)
    kb.junk = Pool(kb, "junk", [128, D_MODEL], F32, 1)
    kb.hn = Pool(kb, "hn", [128, D_MODEL], BF16, 2)
    xin = Pool(kb, "xin", [128, D_MODEL], F32, 2)
    G = kb.sb("Gs", [128, D_MODEL], F32); r_G = Res()
    kb.dma("sp", G[:], Gd, [], [r_G])
    DK = kb.sb("DKs", [128, D_MODEL], F32); r_DK = Res()
    kb.dma("sp", DK[:], DKd, [], [r_DK])
    A1 = kb.sb("A1", [128, KC * RB], BF16); r_A1 = Res()
    hT = A1[:].rearrange("p (k c) -> p k c", k=KC)
    wpool = Pool(kb, "wsl", [128, KC * 512], BF16, 2)
    cs = kb.sb("cs", [128, 2, RB // 128, 256], F32); r_cs = Res()
    rt = Pool(kb, "rt", [128, 256], F32, 6)
    so = Pool(kb, "so", [128, 512], BF16, 4)

    for b in range(NBLK):
        t0 = b * RB
        for tt in range(RB // 128):
            xt, r_xt = xin.next()
            kb.dma("sp", xt[:], xr[t0 + tt * 128:t0 + (tt + 1) * 128, :], [r_xr], [r_xt])
            emit_norm_T(kb, xt, r_xt, G, r_G, hT, r_A1, tt * 128, KC)
        kb.dma("sp", cs[:, 0, :, :], cosd[t0:t0 + RB, :].rearrange("(n p) c -> p n c", p=128), [], [r_cs])
        kb.dma("sp", cs[:, 1, :, :], sind[t0:t0 + RB, :].rearrange("(n p) c -> p n c", p=128), [], [r_cs])
        for ns in slices:
            wt, r_wt = wpool.next()
            wv_ = wt[:].rearrange("p (k c) -> p k c", k=KC)
            kb.dma("sp", wt[:], wib[ns], [r_wib[ns]], [r_wt])
            for tt in range(RB // 128):
                ps, r_ps = kb.bank()
                for k in range(KC):
                    kb.mm(ps[:, 0:512], hT[:, k, tt * 128:(tt + 1) * 128], wv_[:, k, :], k == 0, k == KC - 1, [r_A1, r_wt], [r_ps])
                o_, r_o = so.next()
                rows = slice(t0 + tt * 128, t0 + (tt + 1) * 128)
                if ns < 8:
                    psv = ps[:, 0:512].rearrange("p (i t) -> p i t", t=2)
                    ov = o_[:].rearrange("p (i t) -> p i t", t=2)
                    c_ = cs[:, 0, tt, :]
                    s_ = cs[:, 1, tt, :]
                    t1, r_t1 = rt.next(); t2, r_t2 = rt.next()
                    kb.v("dve", "tensor_tensor", [r_ps, r_cs], [r_t1], out=t1[:], in0=psv[:, :, 0], in1=c_, op=ALU.mult)
                    kb.v("dve", "tensor_tensor", [r_ps, r_cs], [r_t2], out=t2[:], in0=psv[:, :, 1], in1=s_, op=ALU.mult)
                    kb.v("pool", "tensor_tensor", [r_t1, r_t2], [r_o], out=ov[:, :, 0], in0=t1[:], in1=t2[:], op=ALU.subtract)
                    t3, r_t3 = rt.next(); t4, r_t4 = rt.next()
                    kb.v("dve", "tensor_tensor", [r_ps, r_cs], [r_t3], out=t3[:], in0=psv[:, :, 0], in1=s_, op=ALU.mult)
                    kb.v("dve", "tensor_tensor", [r_ps, r_cs], [r_t4], out=t4[:], in0=psv[:, :, 1], in1=c_, op=ALU.mult)
                    kb.v("pool", "tensor_tensor", [r_t3, r_t4], [r_o], out=ov[:, :, 1], in0=t3[:], in1=t4[:], op=ALU.add)
                    if ns < 4:
                        kb.dma("pool", qd[rows, ns * 512:(ns + 1) * 512], o_[:], [r_o], [r_qd])
                    else:
                        kb.dma("pool", kd[rows, (ns - 4) * 512:(ns - 3) * 512], o_[:], [r_o], [r_kd])
                elif ns < 16:
                    kb.act(o_[:], ps[:, 0:512], AF.Copy, [r_ps], [r_o])
                    kb.dma("pool", vd[rows, (ns - 8) * 512:(ns - 7) * 512], o_[:], [r_o], [r_vd])
                else:
                    kb.act(o_[:], ps[:, 0:512], AF.Silu, [r_ps], [r_o])
                    kb.dma("pool", sgd[rows, (ns - 16) * 512:(ns - 15) * 512], o_[:], [r_o], [r_sgd])

    gam = ret_gammas()
    Sv = A1[:].bitcast(F32).rearrange("p (h d v) -> p h d v", h=RH, d=2)
    Sbf = wpool.tiles[0][:].rearrange("p (h d v) -> p h d v", h=RH, d=2)
    qkT = wpool.tiles[1][:].rearrange("p (b s j c) -> p b s j c", b=2, s=2, j=16)
    r_S = [Res() for _ in range(RH)]
    r_Sbf = [Res() for _ in range(RH)]
    r_qkT = [Res(), Res()]
    qk_tiles = [t[:].bitcast(BF16) for t in xin.tiles]
    r_qk = xin.res
    kdec = Pool(kb, "kdec", [128, D_MODEL], BF16, 2)
    vh = Pool(kb, "vh", [128, DV_], BF16, 4)
    barrier_w = [r_A1, wpool.res[0], wpool.res[1]] + r_S + r_Sbf + r_qkT
    kb.v("dve", "memset", [], barrier_w, ap=A1[:].bitcast(F32), constant=0.0)
    if full:
        Mp = kb.sb("Mps", [128, RH, 128], F32); r_Mp = Res()
        kb.dma("sp", Mp[:], Mpd.rearrange("p (h n) -> p h n", h=RH), [], [r_Mp])
        DQ = kb.sb("DQs", [128, RH], F32); r_DQ = Res()
        kb.dma("sp", DQ[:], DQd, [], [r_DQ])
        coef = kb.sb("coefs", [128, 3 * RH], F32); r_coef = Res()
        kb.dma("sp", coef[:], coefd, [], [r_coef])
        sgh = Pool(kb, "sgh", [128, DV_], BF16, 3)
        oh = Pool(kb, "oh", [128, DV_], F32, 3)
        ogp = Pool(kb, "ogp", [128, DV_], BF16, 3)
        sT = Pool(kb, "sT", [128, 128], BF16, 3)
        for i in range(3):
            for h in range(RH):
                for d in range(2):
                    l_, r_l = oh.next()
                    kb.dma("sp", l_[:], Lprev[i, h, d], [r_L], [r_l])
                    kb.v("dve", "scalar_tensor_tensor", [r_l, r_coef, r_S[h]], [r_S[h]], out=Sv[:, h, d, :], in0=l_[:],
                         scalar=coef[:, i * RH + h:i * RH + h + 1], in1=Sv[:, h, d, :], op0=ALU.mult, op1=ALU.add)
        for h in range(RH):
            kb.act(Sbf[:, h, :, :], Sv[:, h, :, :], AF.Copy, [r_S[h]], [r_Sbf[h]])
    for c in range(NCH):
        rows = slice(c * 128, (c + 1) * 128)
        bi = c % 2
        qk = qk_tiles[bi]
        r_q = r_qk[bi]
        if full:
            kb.dma("sp", qk[:, 0:D_MODEL], qd[rows, :], [r_qd], [r_q])
        kb.dma("sp", qk[:, D_MODEL:2 * D_MODEL], kd[rows, :], [r_kd], [r_q])
        kd_, r_kdec = kdec.next()
        kb.v("dve", "tensor_tensor", [r_q, r_DK], [r_kdec], out=kd_[:], in0=qk[:, D_MODEL:2 * D_MODEL], in1=DK[:], op=ALU.mult)
        if full:
            for s in range(2):
                for half in range(2):
                    ps, r_ps = kb.bank()
                    psb = ps[:].bitcast(BF16)
                    for j in range(8):
                        col = s * D_MODEL + (half * 8 + j) * 128
                        kb.tr(psb[:, j * 128:(j + 1) * 128], qk[:, col:col + 128], kb.identb[:], [r_q, kb.r_ident], [r_ps])
                    src = psb[:, 0:1024].rearrange("p (j c) -> p j c", c=128)
                    dst = qkT[:, bi, s, half * 8:half * 8 + 8, :]
                    if half == 0:
                        kb.act(dst, src, AF.Copy, [r_ps], [r_qkT[bi]])
                    else:
                        kb.v("dve", "tensor_copy", [r_ps], [r_qkT[bi]], out=dst, in_=src)
        for h in range(RH):
            v_, r_v = vh.next()
            kb.dma("sp", v_[:], vd[rows, h * DV_:(h + 1) * DV_], [r_vd], [r_v])
            if full:
                g_, r_g = sgh.next()
                kb.dma("sp", g_[:], sgd[rows, h * DV_:(h + 1) * DV_], [r_sgd], [r_g])
                ps, r_ps = kb.bank()
                for d in range(2):
                    kb.mm(ps[:, 0:128], qkT[:, bi, 1, h * 2 + d, :], qkT[:, bi, 0, h * 2 + d, :], d == 0, d == 1, [r_qkT[bi]], [r_ps])
                st_, r_st = sT.next()
                kb.v("dve", "tensor_tensor", [r_ps, r_Mp], [r_st], out=st_[:], in0=ps[:, 0:128], in1=Mp[:, h, :], op=ALU.mult)
                po, r_po = kb.bank()
                kb.mm(po[:, 0:DV_], st_[:], v_[:], True, False, [r_st, r_v], [r_po])
                for d in range(2):
                    kb.mm(po[:, 0:DV_], qkT[:, bi, 0, h * 2 + d, :], Sbf[:, h, d, :], False, d == 1, [r_qkT[bi], r_Sbf[h]], [r_po])
                o_, r_o = oh.next()
                kb.act(o_[:], po[:, 0:DV_], AF.Copy, [r_po, r_DQ], [r_o], scale=DQ[:, h:h + 1])
                rstd, r_rs = emit_rstd(kb, o_[:], r_o, DV_)
                og_, r_og = ogp.next()
                kb.v("dve", "scalar_tensor_tensor", [r_o, r_rs, r_g], [r_og], out=og_[:], in0=o_[:], scalar=rstd, in1=g_[:],
                     op0=ALU.mult, op1=ALU.mult)
                kb.dma("pool", og[rows, h * DV_:(h + 1) * DV_], og_[:], [r_og], [r_og_d])
            for d in range(2):
                pS, r_pS = kb.bank()
                kb.mm(pS[:, 0:DV_], kd_[:, h * DK_ + d * 128:h * DK_ + (d + 1) * 128], v_[:], True, True, [r_kdec, r_v], [r_pS])
                kb.v("dve", "scalar_tensor_tensor", [r_pS, r_S[h]], [r_S[h]], out=Sv[:, h, d, :], in0=Sv[:, h, d, :],
                     scalar=float(gam[h] ** 128), in1=pS[:, 0:DV_], op0=ALU.mult, op1=ALU.add)
            if full:
                kb.act(Sbf[:, h, :, :], Sv[:, h, :, :], AF.Copy, [r_S[h]], [r_Sbf[h]])
    if not full:
        for h in range(RH):
            for d in range(2):
                kb.dma("pool", Lout[h, d], Sv[:, h, d, :], [r_S[h]], [r_L])
    kb.end()


def ret_host_layouts(w_in, g0):
    w = w_in.reshape(KC, 128, NS, 512)
    win = np.ascontiguousarray(w.transpose(2, 1, 0, 3)).reshape(NS, 128, KC * 512)
    G = np.ascontiguousarray(np.broadcast_to(g0[None, :], (128, D_MODEL)))
    return dict(win=win, G=G, ident=np.eye(128, dtype=np.float32))


def ret_const_tables(pos0, T, j):
    theta = (1.0 / (10000.0 ** np.linspace(0.0, 1.0, DK_ // 2, dtype=np.float32))).astype(np.float32)
    ang = (np.arange(pos0, pos0 + T, dtype=np.float32)[:, None] * theta[None, :]).astype(np.float32)
    cos = np.cos(ang).astype(np.float32)
    sin = np.sin(ang).astype(np.float32)
    cos2 = np.ascontiguousarray(np.concatenate([cos, cos], axis=1))
    sin2 = np.ascontiguousarray(np.concatenate([sin, sin], axis=1))
    gam = np.array(ret_gammas(), dtype=np.float64)
    lg = np.log1p(-(2.0 ** (-5.0 - np.arange(RH, dtype=np.float64))))
    m = np.arange(128, dtype=np.float64)
    ks = DK_ ** -0.5
    DK = np.exp((127.0 - m)[:, None] * lg[None, :]) * ks
    DK = np.ascontiguousarray(np.repeat(DK, DK_, axis=1)).astype(np.float32)
    DQ = np.exp((m + 1.0)[:, None] * lg[None, :]).astype(np.float32)
    Mp = np.exp(-(m + 1.0)[:, None, None] * lg[None, :, None]) * ks
    Mp = Mp * (m[None, None, :] >= m[:, None, None])
    Mp = np.ascontiguousarray(Mp.reshape(128, RH * 128)).astype(np.float32)
    coef = np.zeros((3, RH), np.float64)
    for i in range(3):
        if i < j:
            coef[i] = np.exp(T * (j - 1 - i) * lg)
    coef = np.ascontiguousarray(np.broadcast_to(coef.reshape(1, 3 * RH), (128, 3 * RH))).astype(np.float32)
    return dict(cos2=cos2, sin2=sin2, DK=DK, DQ=DQ, Mp=Mp, coef=coef)


GROUPS = [[0, 1, 2, 3], [4, 5, 6, 7]]
TPC = SEQ * BATCH // NCORES


def build_fused():
    kb = KB()
    T, S, D = TPC, SEQ, D_MODEL
    ident = kb.din("ident", [128, 128])
    xb = kb.din("xb", [S, D])
    xs = kb.din("xs", [T, D])
    sel_prev = kb.din("sel_prev", [128, 4])
    sel_own = kb.din("sel_own", [128, 4])
    fio = fox_inputs(kb)
    w0 = wo_inputs(kb, "wo0_", 16)
    f0 = ffn_inputs(kb, "f0_")
    rio = ret_inputs(kb, T)
    w1 = wo_inputs(kb, "wo1_", 32)
    f1 = ffn_inputs(kb, "f1_")
    out = kb.dout("out", [T, D])

    prep = Prep()
    pf0 = prep_ffn(kb, prep, "f0_", f0)
    pr = prep_ret(kb, prep, rio)
    pf1 = prep_ffn(kb, prep, "f1_", f1)
    att = kb.dtmp("att", [S, HPC * DH], BF16); r_att = Res()
    emit_fox(kb, S, fio, xb, ident, att, r_att, prep=prep)
    CR = 1024
    r_attg = Res()
    attg = []
    for i in range(S // CR):
        g_ = kb.dtmp("attg%d" % i, [4 * CR, HPC * DH], BF16)
        kb.collective("AllGather", GROUPS, att[i * CR:(i + 1) * CR, :], g_, [r_att], [r_attg], flush=(i == S // CR - 1))
        attg.append(g_.rearrange("(m t) c -> t m c", m=4))

    def src0(rows):
        res = []
        for jj in range(4):
            r0 = jj * T + rows.start
            res.append((attg[r0 // CR][r0 % CR:r0 % CR + 128], lambda c_: c_[:].rearrange("p (m c) -> p m c", m=4)))
        return res

    xm0 = kb.dtmp("xm0", [T, D]); r_xm0 = Res()
    emit_wo(kb, T, 16, w0, ident, src0, r_attg, sel_own, xs, Res(), xm0, r_xm0)
    halo0 = kb.dtmp("halo0", [8, D]); r_h0 = Res()
    kb.collective("AllGather", GROUPS, xm0[T - 2:T, :], halo0, [r_xm0], [r_h0])
    x1 = kb.dtmp("x1", [T, D]); r_x1 = Res()
    emit_ffn(kb, T, "f0_", f0, pf0, xm0, r_xm0, halo0, r_h0, sel_prev, ident, x1, r_x1)

    L = kb.dtmp("Lst", [RH, 2, 128, DV_]); r_L = Res()
    emit_ret(kb, T, False, rio, pr, ident, x1, r_x1, Lout=L, r_L=r_L)
    r_La = Res()
    Lall = []
    Lf = L.rearrange("h d p v -> (h d p) v")
    for i in range(RH // 2):
        g_ = kb.dtmp("Lall%d" % i, [4 * 512, DV_])
        kb.collective("AllGather", GROUPS, Lf[i * 512:(i + 1) * 512, :], g_, [r_L], [r_La], flush=(i == RH // 2 - 1))
        Lall.append(g_.rearrange("(i h d p) v -> i h d p v", i=4, h=2, d=2))

    class _LP:
        def __getitem__(self, idx):
            i, h, d = idx
            return Lall[h // 2][i, h % 2, d]

    og = kb.dtmp("og", [T, RH * DV_], BF16); r_og = Res()
    emit_ret(kb, T, True, rio, pr, ident, x1, r_x1, r_L=r_La, Lprev=_LP(), og=og, r_og_d=r_og)

    def src1(rows):
        return [(og[rows, :], lambda c_: c_[:])]

    xm1 = kb.dtmp("xm1", [T, D]); r_xm1 = Res()
    emit_wo(kb, T, 32, w1, ident, src1, r_og, sel_own, x1, r_x1, xm1, r_xm1)
    halo1 = kb.dtmp("halo1", [8, D]); r_h1 = Res()
    kb.collective("AllGather", GROUPS, xm1[T - 2:T, :], halo1, [r_xm1], [r_h1])
    emit_ffn(kb, T, "f1_", f1, pf1, xm1, r_xm1, halo1, r_h1, sel_prev, ident, out, Res())
    return kb.nc


_NC = []


def kernel(x, norm_g, fox_w_in, fox_b_f, fox_w_o, ret_w_in, ret_w_o,
           ffn_w_up, ffn_conv_w, ffn_conv_b, ffn_w_down):
    f = lambda a: np.ascontiguousarray(np.asarray(a, dtype=np.float32))
    x, norm_g = f(x), f(norm_g)
    fox_w_in, fox_b_f, fox_w_o = f(fox_w_in), f(fox_b_f), f(fox_w_o)
    ret_w_in, ret_w_o = f(ret_w_in), f(ret_w_o)
    ffn_w_up, ffn_conv_w, ffn_conv_b, ffn_w_down = f(ffn_w_up), f(ffn_conv_w), f(ffn_conv_b), f(ffn_w_down)
    if not _NC:
        _NC.append(build_fused())
    nc = _NC[0]
    T, G = TPC, NCORES // BATCH
    shared = {"ident": np.eye(128, dtype=np.float32)}
    for k_, v_ in wo_host_layout(fox_w_o[0], norm_g[0, 1]).items():
        if k_ != "ident":
            shared["wo0_" + k_] = v_
    for k_, v_ in wo_host_layout(ret_w_o[0], norm_g[1, 1]).items():
        if k_ != "ident":
            shared["wo1_" + k_] = v_
    for l, pfx in ((0, "f0_"), (1, "f1_")):
        lay = ffn_host_layouts(ffn_w_up[l], ffn_conv_w[l], ffn_conv_b[l], ffn_w_down[l], norm_g[l, 2], norm_g[l, 3])
        for k_, v_ in lay.items():
            if k_ != "ident":
                shared[pfx + k_] = v_
    rl = ret_host_layouts(ret_w_in[0], norm_g[1, 0])
    shared["r_win"] = rl["win"]
    shared["r_G"] = rl["G"]
    foxl = [fox_host_layouts(fox_w_in[0], fox_b_f[0], norm_g[0, 0], m) for m in range(G)]
    maps = []
    for c in range(NCORES):
        b, j = c // G, c % G
        m = dict(shared)
        for k_, v_ in foxl[j].items():
            if k_ != "ident":
                m[k_] = v_
        m.update(ret_const_tables(j * T, T, j))
        m["xb"] = x[b]
        m["xs"] = np.ascontiguousarray(x[b, j * T:(j + 1) * T])
        sp = np.zeros((128, 4), np.float32)
        so = np.zeros((128, 4), np.float32)
        if j > 0:
            sp[:, j - 1] = 1.0
        so[:, j] = 1.0
        m["sel_prev"] = sp
        m["sel_own"] = so
        maps.append(m)
    res = run_bass_kernel_spmd(nc, maps, core_ids=list(range(NCORES))).results
    out = np.concatenate([res[c]["out"] for c in range(NCORES)], axis=0).reshape(BATCH, SEQ, D_MODEL)
    return out.astype(np.float32)
```

```python
import contextlib
import numpy as np
import concourse.bass as bass
import concourse.mybir as mybir
from concourse.bass_utils import run_bass_kernel_spmd

F32 = mybir.dt.float32
BF16 = mybir.dt.bfloat16
AF = mybir.ActivationFunctionType
ALU = mybir.AluOpType
AX = mybir.AxisListType

D_MODEL = 2048
SEQ = 16384
BATCH = 2
D_FF = 5632
NORM_EPS = 1e-6
NCORES = 8

SAME_ENG_SYNC = True
SEM_GEN = 30000
SEM_DMA_GEN = 1500


class Res:
    __slots__ = ("name", "w", "rs")

    def __init__(self, name=""):
        self.name = name
        self.w = None
        self.rs = []


class _Op:
    __slots__ = ("eng", "fn", "deps", "ev", "dma", "inc")


class Sched:
    ENGS = ("pe", "act", "dve", "pool", "sp")

    def __init__(self, nc, ndma=10):
        self.nc = nc
        self.ops = {e: [] for e in self.ENGS}
        self.cnt = {e: 0 for e in self.ENGS}
        self.dman = {e: 0 for e in self.ENGS}
        self.dmahist = {e: [] for e in self.ENGS}
        self.ndma = ndma
        self.ncc = 0
        self.sems = {}
        self.semstack = contextlib.ExitStack()
        self.waited = {e: {} for e in self.ENGS}
        self.fin = {}

    def op(self, eng, fn, reads=(), writes=(), dma=False, inc=None):
        o = _Op()
        o.eng = eng
        o.fn = fn
        o.dma = dma
        o.inc = inc
        deps = []
        for r in reads:
            if r.w is not None:
                deps.append(r.w)
        for r in writes:
            if r.w is not None:
                deps.append(r.w)
            deps.extend(r.rs)
        if inc is not None:
            o.ev = ("cc", self.ncc, inc)
            self.ncc += 1
        elif dma:
            n = self.dman[eng]
            self.dman[eng] = n + 1
            rnd = n // self.ndma
            o.ev = ("d_" + eng, (n % self.ndma, rnd // SEM_DMA_GEN), 16 * (rnd % SEM_DMA_GEN + 1))
            if n >= self.ndma:
                deps.append(self.dmahist[eng][n - self.ndma])
            self.dmahist[eng].append(o)
        else:
            c = self.cnt[eng]
            self.cnt[eng] = c + 1
            o.ev = ("c_" + eng, c // SEM_GEN, c % SEM_GEN + 1)
        dd = []
        seen = set()
        for d in deps:
            if id(d) in seen:
                continue
            seen.add(id(d))
            if (not d.dma) and d.eng == eng:
                if eng == "pe" or not SAME_ENG_SYNC:
                    continue
            dd.append(d)
        o.deps = dd
        for r in reads:
            r.rs.append(o)
        for r in writes:
            r.w = o
            r.rs = []
        self.ops[eng].append(o)
        return o

    def flush(self):
        nc = self.nc
        for e in self.ENGS:
            for o in self.ops[e]:
                k = (o.ev[0], o.ev[1])
                if k not in self.sems:
                    self.sems[k] = self.semstack.enter_context(nc.semaphore("s%d" % len(self.sems)))
                self.fin[k] = max(self.fin.get(k, 0), o.ev[2])
        sems = self.sems
        fin = dict(self.fin)
        ops = self.ops
        self.ops = {e: [] for e in self.ENGS}
        with nc.Block() as block:
            def run(eng_name):
                def body(eng):
                    waited = self.waited[eng_name]
                    for o in ops[eng_name]:
                        for d in o.deps:
                            k = (d.ev[0], d.ev[1])
                            if waited.get(k, 0) < d.ev[2]:
                                eng.wait_ge(sems[k], d.ev[2])
                                waited[k] = d.ev[2]
                        ins = o.fn(eng)
                        ins.then_inc(sems[(o.ev[0], o.ev[1])], o.inc if o.inc is not None else (16 if o.dma else 1))
                    for k, v in fin.items():
                        if waited.get(k, 0) < v:
                            eng.wait_ge(sems[k], v)
                            waited[k] = v
                return body

            block.tensor(run("pe"))
            block.scalar(run("act"))
            block.vector(run("dve"))
            block.gpsimd(run("pool"))
            block.sync(run("sp"))


class Pool:
    def __init__(self, kb, name, shape, dt, n, views=None):
        if views is not None:
            self.tiles = list(views)
            n = len(views)
        else:
            self.tiles = [kb.sb("%s%d" % (name, i), shape, dt) for i in range(n)]
        self.res = [Res("%s%d" % (name, i)) for i in range(n)]
        self.i = 0

    def next(self):
        i = self.i % len(self.tiles)
        self.i += 1
        return self.tiles[i], self.res[i]


class KB:
    def __init__(self):
        self.nc = bass.Bass("TRN2", target_bir_lowering=False)
        self.S = Sched(self.nc)
        self.st = None
        self.phase = 0

    def begin(self, ident_in):
        self.phase += 1
        self.st = contextlib.ExitStack()
        self.ps = []
        self.psr = []
        self.psi = 0
        setup_common(self, ident_in)

    def end(self):
        self.S.flush()
        self.st.close()
        self.st = None

    def sb(self, name, shape, dt):
        return self.st.enter_context(self.nc.sbuf_tensor("p%d_%s" % (self.phase, name), list(shape), dt))

    def alloc_psum(self, n=8):
        for i in range(n):
            self.ps.append(self.st.enter_context(self.nc.psum_tensor("p%d_ps%d" % (self.phase, i), [128, 512], F32)))
            self.psr.append(Res("ps%d" % i))

    def bank(self, lo=0, hi=8):
        i = lo + self.psi % (hi - lo)
        self.psi += 1
        return self.ps[i], self.psr[i]

    def din(self, name, shape, dt=F32):
        return self.nc.dram_tensor(name, list(shape), dt, kind="ExternalInput").ap()

    def dout(self, name, shape, dt=F32):
        return self.nc.dram_tensor(name, list(shape), dt, kind="ExternalOutput").ap()

    def dtmp(self, name, shape, dt=F32):
        return self.nc.dram_tensor(name, list(shape), dt, kind="Internal").ap()

    def collective(self, kind, groups, in_ap, out_ap, r, w, flush=True):
        if not hasattr(self, "r_cc"):
            self.r_cc = Res("cc")
        self.S.op("pool", lambda e: e.collective_compute(kind, ALU.bypass, replica_groups=groups, ins=[in_ap], outs=[out_ap]),
                  r, list(w) + [self.r_cc], dma=True, inc=1)
        if flush:
            self.S.flush()

    def dma(self, q, out, in_, r, w):
        return self.S.op(q, lambda e: e.dma_start(out=out, in_=in_), r, w, dma=True)

    def act(self, out, in_, func, r, w, **kw):
        return self.S.op("act", lambda e: e.activation(out=out, in_=in_, func=func, **kw), r, w)

    def mm(self, out, lhsT, rhs, start, stop, r, w):
        return self.S.op("pe", lambda e: e.matmul(out, lhsT=lhsT, rhs=rhs, start=start, stop=stop,
                                                  skip_group_check=True), r, w)

    def tr(self, out, in_, ident, r, w):
        return self.S.op("pe", lambda e: e.transpose(out=out, in_=in_, identity=ident), r, w)

    def v(self, eng, meth, r, w, **kw):
        return self.S.op(eng, lambda e: getattr(e, meth)(**kw), r, w)


def emit_rstd(kb, x_ap, r_x, ncols, P=128):
    junk, r_j = kb.junk.next()
    st, r_s = kb.small.next()
    kb.act(junk[0:P, 0:ncols], x_ap, AF.Square, [r_x], [r_j])
    kb.v("dve", "reduce_sum", [r_j], [r_s], out=st[0:P, 0:1], in_=junk[0:P, 0:ncols], axis=AX.X)
    kb.v("dve", "tensor_scalar", [r_s], [r_s], out=st[0:P, 1:2], in0=st[0:P, 0:1], scalar1=1.0 / ncols,
         scalar2=NORM_EPS, op0=ALU.mult, op1=ALU.add)
    kb.act(st[0:P, 2:3], st[0:P, 1:2], AF.Sqrt, [r_s], [r_s])
    kb.v("dve", "reciprocal", [r_s], [r_s], out=st[0:P, 3:4], in_=st[0:P, 2:3])
    return st[0:P, 3:4], r_s


def emit_norm_T(kb, x_tile, r_x, G, r_G, hT, r_hT, col0, KC, cp_eng="act"):
    rstd, r_s = emit_rstd(kb, x_tile[:, 0:KC * 128], r_x, KC * 128)
    hn, r_hn = kb.hn.next()
    kb.v("dve", "scalar_tensor_tensor", [r_x, r_s, r_G], [r_hn], out=hn[:, 0:KC * 128], in0=x_tile[:, 0:KC * 128],
         scalar=rstd, in1=G[:, 0:KC * 128], op0=ALU.mult, op1=ALU.mult)
    emit_T(kb, hn, r_hn, hT, r_hT, col0, KC, cp_eng)


def emit_T(kb, hn, r_hn, hT, r_hT, col0, KC, cp_eng="act"):
    for k0 in range(0, KC, 8):
        n = min(8, KC - k0)
        ps, r_ps = kb.bank()
        psb = ps[:].bitcast(BF16)
        for j in range(n):
            kb.tr(psb[:, j * 128:(j + 1) * 128], hn[:, (k0 + j) * 128:(k0 + j + 1) * 128], kb.identb[:],
                  [r_hn, kb.r_ident], [r_ps])
        src = psb[:, 0:n * 128].rearrange("p (k c) -> p k c", c=128)
        dst = hT[:, k0:k0 + n, col0:col0 + 128]
        if cp_eng == "act":
            kb.act(dst, src, AF.Copy, [r_ps], [r_hT])
        else:
            kb.v(cp_eng, "tensor_copy", [r_ps], [r_hT], out=dst, in_=src)


def emit_post(kb, f_ap, r_f, x_ap, r_x, G, r_G, out_tile, r_out, ncols=D_MODEL):
    rstd, r_s = emit_rstd(kb, f_ap, r_f, ncols)
    kb.v("dve", "scalar_tensor_tensor", [r_f, r_s, r_G], [r_out], out=out_tile, in0=f_ap, scalar=rstd, in1=G[:, 0:ncols],
         op0=ALU.mult, op1=ALU.mult)
    kb.v("pool", "tensor_tensor", [r_out, r_x], [r_out], out=out_tile, in0=out_tile, in1=x_ap, op=ALU.add)


def setup_common(kb, ident_in):
    kb.alloc_psum(8)
    idf = kb.sb("idf", [128, 128], F32)
    kb.identb = kb.sb("identb", [128, 128], BF16)
    kb.r_ident = Res("ident")
    r_idf = Res("idf")
    kb.dma("sp", idf[:], ident_in, [], [r_idf])
    kb.identf = idf
    kb.r_identf = r_idf
    kb.v("dve", "tensor_copy", [r_idf], [kb.r_ident], out=kb.identb[:], in_=idf[:])
    kb.small = Pool(kb, "small", [128, 4], F32, 6)


_cast_rr = [0]


def emit_cast(kb, out, in_, r, w):
    i = _cast_rr[0] % 3
    _cast_rr[0] += 1
    if i == 0:
        kb.v("dve", "tensor_copy", r, w, out=out, in_=in_)
    elif i == 1:
        kb.act(out, in_, AF.Copy, r, w)
    else:
        kb.v("pool", "tensor_copy", r, w, out=out, in_=in_)


TB = 512
NPAIR = D_FF // 128
KC = D_MODEL // 128


def ffn_inputs(kb, pfx):
    return dict(wup=kb.din(pfx + "wup", [NPAIR, 128, 2 * KC * 128]), wdn=kb.din(pfx + "wdn", [4, 128, NPAIR * 512]),
                G2=kb.din(pfx + "G2", [128, D_MODEL]), G3=kb.din(pfx + "G3", [128, D_MODEL]),
                cw=kb.din(pfx + "cw", [128, 2 * NPAIR * 3]), cb=kb.din(pfx + "cb", [128, 2 * NPAIR]))


class Prep:
    def __init__(self):
        self.steps = []

    def add(self, src_ap, dst_ap, r_dst):
        self.steps.append((src_ap, dst_ap, r_dst))

    def run(self, kb, n, stf, stb):
        for _ in range(n):
            if not self.steps:
                return
            src_ap, dst_ap, r_dst = self.steps.pop(0)
            st_, r_st = stf.next()
            sb_, r_sb = stb.next()
            kb.dma("sp", st_[:], src_ap, [], [r_st])
            kb.v("pool", "tensor_copy", [r_st], [r_sb], out=sb_[:], in_=st_[:])
            kb.dma("pool", dst_ap, sb_[:], [r_sb], [r_dst])


def prep_ffn(kb, prep, pfx, io):
    wup, wdn = io["wup"], io["wdn"]
    wub = kb.dtmp(pfx + "wub", [NPAIR, 128, 2 * KC * 128], BF16)
    wdb = kb.dtmp(pfx + "wdb", [4, 128, NPAIR * 512], BF16)
    r_wub = [Res() for _ in range(NPAIR)]
    r_wdb = [Res() for _ in range(4)]
    for i in range(NPAIR):
        for hf in range(2):
            sl = slice(hf * KC * 128, (hf + 1) * KC * 128)
            prep.add(wup[i, :, sl], wub[i, :, sl], r_wub[i])
    for dq in range(4):
        for g in range(NPAIR * 512 // D_MODEL):
            sl = slice(g * D_MODEL, (g + 1) * D_MODEL)
            prep.add(wdn[dq, :, sl], wdb[dq, :, sl], r_wdb[dq])
    return dict(wub=wub, wdb=wdb, r_wub=r_wub, r_wdb=r_wdb)


def prep_ret(kb, prep, io):
    wind = io["win"]
    wib = kb.dtmp("wib", [NS, 128, KC * 512], BF16)
    r_wib = [Res() for _ in range(NS)]
    for ns in range(NS):
        for g in range(KC * 512 // D_MODEL):
            sl = slice(g * D_MODEL, (g + 1) * D_MODEL)
            prep.add(wind[ns, :, sl], wib[ns, :, sl], r_wib[ns])
    return dict(wib=wib, r_wib=r_wib)


def emit_ffn(kb, T, pfx, io, pw, xm, r_xm, halo_all, r_halo, sel4d, identd, xo, r_xo):
    NB = T // TB
    G2d, G3d, cwd, cbd = io["G2"], io["G3"], io["cw"], io["cb"]
    wub, wdb, r_wub, r_wdb = pw["wub"], pw["wdb"], pw["r_wub"], pw["r_wdb"]

    kb.begin(identd)
    kb.junk = Pool(kb, "junk", [128, D_MODEL], F32, 1)
    kb.hn = Pool(kb, "hn", [128, D_MODEL], BF16, 2)
    xin = Pool(kb, "xin", [128, D_MODEL], F32, 2)
    G2 = kb.sb("G2s", [128, D_MODEL], F32); r_G2 = Res()
    G3 = kb.sb("G3s", [128, D_MODEL], F32); r_G3 = Res()
    cw = kb.sb("cws", [128, 2, NPAIR, 3], F32); r_cw = Res()
    cb = kb.sb("cbs", [128, 2, NPAIR], F32); r_cb = Res()
    kb.dma("sp", G2[:], G2d, [], [r_G2])
    kb.dma("sp", G3[:], G3d, [], [r_G3])
    kb.dma("sp", cw[:], cwd.rearrange("p (h i t) -> p h i t", h=2, i=NPAIR), [], [r_cw])
    kb.dma("sp", cb[:], cbd.rearrange("p (h i) -> p h i", h=2), [], [r_cb])

    hT = kb.sb("hT", [128, KC, TB], BF16); r_hT = Res()
    gT = kb.sb("gT", [128, NPAIR, TB], BF16); r_gT = [Res() for _ in range(NPAIR)]
    ft = kb.sb("ft", [128, TB // 128, D_MODEL], F32); r_ft = [Res() for _ in range(TB // 128)]
    wupp = Pool(kb, "wupp", [128, 2, KC, 128], BF16, 2)
    wdp = Pool(kb, "wdp", [128, 4, 512], BF16, 3)
    ub = Pool(kb, "ub", [128, TB + 2], F32, 4)
    tmp = Pool(kb, "tmp", [128, TB], F32, 5)
    carry = kb.sb("carry", [128, 2 * NPAIR, 2], F32); r_carry = [Res() for _ in range(2 * NPAIR)]
    hTh = kb.sb("hTh", [128, KC, 128], BF16); r_hTh = Res()
    sel4 = kb.sb("sel4", [128, 4], F32); r_sel = Res()
    kb.dma("sp", sel4[:], sel4d, [], [r_sel])
    xt, r_xt = xin.next()
    kb.v("dve", "memset", [], [r_xt], ap=xt[:], constant=0.0)
    for i in range(4):
        ct, r_ct = ft[:, i, :], r_ft[i]
        kb.v("pool", "memset", [], [r_ct], ap=ct, constant=0.0)
        kb.dma("sp", ft[126:128, i, :], halo_all[2 * i:2 * i + 2, :], [r_halo], [r_ct])
        kb.v("dve", "scalar_tensor_tensor", [r_ct, r_sel, r_xt], [r_xt], out=xt[:], in0=ct, scalar=sel4[:, i:i + 1],
             in1=xt[:], op0=ALU.mult, op1=ALU.add)
    emit_norm_T(kb, xt, r_xt, G2, r_G2, hTh, r_hTh, 0, KC)
    psc, r_psc = kb.bank()
    for i in range(NPAIR):
        wt, r_wt = wupp.next()
        kb.dma("sp", wt[:], wub[i].rearrange("p (h k c) -> p h k c", h=2, k=KC), [r_wub[i]], [r_wt])
        for hf in range(2):
            ch = hf * NPAIR + i
            for k in range(KC):
                kb.mm(psc[:, ch * 2:ch * 2 + 2], wt[:, hf, k, :], hTh[:, k, 126:128], k == 0, k == KC - 1,
                      [r_wt, r_hTh], [r_psc])
    kb.v("dve", "tensor_copy", [r_psc], r_carry, out=carry[:].rearrange("p c t -> p (c t)"), in_=psc[:, 0:4 * NPAIR])

    for b in range(NB):
        t0 = b * TB
        for tt in range(TB // 128):
            xt, r_xt = xin.next()
            kb.dma("sp", xt[:], xm[t0 + tt * 128:t0 + (tt + 1) * 128, :], [r_xm], [r_xt])
            emit_norm_T(kb, xt, r_xt, G2, r_G2, hT, r_hT, tt * 128, KC)
        for i in range(NPAIR):
            wt, r_wt = wupp.next()
            kb.dma("sp", wt[:], wub[i].rearrange("p (h k c) -> p h k c", h=2, k=KC), [r_wub[i]], [r_wt])
            cv = []
            for hf in range(2):
                ch = hf * NPAIR + i
                ps, r_ps = kb.bank()
                for k in range(KC):
                    kb.mm(ps[:, 0:TB], wt[:, hf, k, :], hT[:, k, :], k == 0, k == KC - 1, [r_wt, r_hT], [r_ps])
                u, r_u = ub.next()
                kb.act(u[:, 2:TB + 2], ps[:, 0:TB], AF.Copy, [r_ps], [r_u])
                kb.v("dve", "tensor_copy", [r_carry[ch]], [r_u], out=u[:, 0:2], in_=carry[:, ch, :])
                kb.v("dve", "tensor_copy", [r_u], [r_carry[ch]], out=carry[:, ch, :], in_=u[:, TB:TB + 2])
                t1, r_t1 = tmp.next()
                kb.act(t1[:], u[:, 2:TB + 2], AF.Identity, [r_u, r_cw, r_cb], [r_t1], scale=cw[:, hf, i, 2:3],
                       bias=cb[:, hf, i:i + 1])
                kb.v("dve", "scalar_tensor_tensor", [r_u, r_t1, r_cw], [r_t1], out=t1[:], in0=u[:, 1:TB + 1],
                     scalar=cw[:, hf, i, 1:2], in1=t1[:], op0=ALU.mult, op1=ALU.add)
                kb.v("dve", "scalar_tensor_tensor", [r_u, r_t1, r_cw], [r_t1], out=t1[:], in0=u[:, 0:TB],
                     scalar=cw[:, hf, i, 0:1], in1=t1[:], op0=ALU.mult, op1=ALU.add)
                cv.append((t1, r_t1))
            (ta, r_ta), (tb_, r_tb) = cv
            sa, r_sa = tmp.next()
            kb.act(sa[:], ta[:], AF.Silu, [r_ta], [r_sa])
            kb.v("dve", "tensor_tensor", [r_sa, r_tb], [r_gT[i]], out=gT[:, i, :], in0=sa[:], in1=tb_[:], op=ALU.mult)
        NTT = TB // 128
        for dq in range(4):
            banks = [kb.bank() for _ in range(NTT)]
            for g in range(NPAIR // 4):
                wd, r_wd = wdp.next()
                kb.dma("sp", wd[:], wdb[dq, :, g * 2048:(g + 1) * 2048].rearrange("p (f c) -> p f c", c=512),
                       [r_wdb[dq]], [r_wd])
                for j in range(4):
                    fc = g * 4 + j
                    for tt in range(NTT):
                        kb.mm(banks[tt][0][:, 0:512], gT[:, fc, tt * 128:(tt + 1) * 128], wd[:, j, :], fc == 0,
                              fc == NPAIR - 1, [r_gT[fc], r_wd], [banks[tt][1]])
            for tt in range(NTT):
                kb.act(ft[:, tt, dq * 512:(dq + 1) * 512], banks[tt][0][:, 0:512], AF.Copy, [banks[tt][1]], [r_ft[tt]])
        for tt in range(NTT):
            xt, r_xt = xin.next()
            rows = slice(t0 + tt * 128, t0 + (tt + 1) * 128)
            kb.dma("sp", xt[:], xm[rows, :], [r_xm], [r_xt])
            emit_post(kb, ft[:, tt, :], r_ft[tt], xt[:], r_xt, G3, r_G3, ft[:, tt, :], r_ft[tt])
            kb.dma("pool", xo[rows, :], ft[:, tt, :], [r_ft[tt]], [r_xo])
    kb.end()


def ffn_host_layouts(w_up, conv_w, conv_b, w_down, g2, g3):
    D, F = D_MODEL, D_FF
    w = w_up.reshape(KC, 128, 2, NPAIR, 128)
    wup = np.ascontiguousarray(w.transpose(3, 1, 2, 0, 4)).reshape(NPAIR, 128, 2 * KC * 128)
    w = w_down.reshape(NPAIR, 128, 4, 512)
    wdn = np.ascontiguousarray(w.transpose(2, 1, 0, 3)).reshape(4, 128, NPAIR * 512)
    c = conv_w.reshape(3, 2, NPAIR, 128)
    cw = np.ascontiguousarray(c.transpose(3, 1, 2, 0)).reshape(128, 2 * NPAIR * 3)
    c = conv_b.reshape(2, NPAIR, 128)
    cb = np.ascontiguousarray(c.transpose(2, 0, 1)).reshape(128, 2 * NPAIR)
    G2 = np.ascontiguousarray(np.broadcast_to(g2[None, :], (128, D)))
    G3 = np.ascontiguousarray(np.broadcast_to(g3[None, :], (128, D)))
    return dict(wup=wup, wdn=wdn, cw=cw, cb=cb, G2=G2, G3=G3, ident=np.eye(128, dtype=np.float32))


DH = 128
HPC = 4
QB = 512


def fox_inputs(kb):
    return dict(wqk=kb.din("wqk", [128, KC * 8 * 128]), wv=kb.din("wv", [128, KC * 512]), wf=kb.din("wf", [128, KC * HPC]),
                bf=kb.din("bf", [HPC, 1]), G0=kb.din("G0", [128, D_MODEL]), negmask=kb.din("negmask", [128, 128]))


def emit_fox(kb, S, io, xb, identd, att, r_att, prep=None):
    NB = S // QB
    NKB = S // 128
    wqkd, wvd, wfd, bfd, G0d, nmd = io["wqk"], io["wv"], io["wf"], io["bf"], io["G0"], io["negmask"]
    qTd = kb.dtmp("qTd", [HPC, 128, S], BF16)
    kTd = kb.dtmp("kTd", [HPC, 128, S], BF16)
    V1d = kb.dtmp("V1d", [HPC, S, DH + 1], BF16)
    Fsd = kb.dtmp("Fsd", [3, HPC, S], BF16)
    r_qTd = [Res() for _ in range(HPC)]
    r_kTd = [Res() for _ in range(HPC)]
    r_V1d = [Res() for _ in range(HPC)]
    r_Fsd = Res()

    kb.begin(identd)
    kb.junk = Pool(kb, "junk", [128, D_MODEL], F32, 1)
    kb.hn = Pool(kb, "hn", [128, D_MODEL], BF16, 2)
    A2 = kb.sb("A2", [128, 16384], BF16)
    xin = Pool(kb, "xin", None, None, 2, views=[A2[:, 8192:12288].bitcast(F32), A2[:, 12288:16384].bitcast(F32)])
    G0 = kb.sb("G0s", [128, D_MODEL], F32); r_G0 = Res()
    kb.dma("sp", G0[:], G0d, [], [r_G0])
    nmf = kb.sb("nmf", [128, 128], F32); r_nmf = Res()
    negmask = kb.sb("negmask_b", [128, 128], BF16); r_nm = Res()
    kb.dma("sp", nmf[:], nmd, [], [r_nmf])
    kb.v("dve", "tensor_copy", [r_nmf], [r_nm], out=negmask[:], in_=nmf[:])
    bft = kb.sb("bft", [HPC, 2], F32); r_bf = Res()
    kb.dma("sp", bft[:, 0:1], bfd, [], [r_bf])
    kb.v("dve", "tensor_scalar", [r_bf], [r_bf], out=bft[:, 1:2], in0=bft[:, 0:1], scalar1=-1.0, scalar2=None, op0=ALU.mult)
    ones4 = kb.sb("ones4", [HPC, QB], F32); r_ones = Res()
    kb.v("dve", "memset", [], [r_ones], ap=ones4[:], constant=1.0)

    wbig = kb.sb("wbig", [128, KC * 1024 + KC * 512], BF16); r_w = Res()
    wqk = wbig[:, 0:KC * 1024].rearrange("p (k c) -> p k c", k=KC)
    wv = wbig[:, KC * 1024:KC * 1536].rearrange("p (k c) -> p k c", k=KC)
    wf = kb.sb("wf_s", [128, KC, HPC], BF16)
    wqk_flat = wbig[:, 0:KC * 1024]
    for g in range(KC * 1024 // 2048):
        st_, r_st = xin.next()
        kb.dma("sp", st_[:], wqkd[:, g * 2048:(g + 1) * 2048], [], [r_st])
        emit_cast(kb, wqk_flat[:, g * 2048:(g + 1) * 2048], st_[:], [r_st], [r_w])
    wv_flat = wbig[:, KC * 1024:KC * 1536]
    for g in range(KC * 512 // 2048):
        st_, r_st = xin.next()
        kb.dma("sp", st_[:], wvd[:, g * 2048:(g + 1) * 2048], [], [r_st])
        emit_cast(kb, wv_flat[:, g * 2048:(g + 1) * 2048], st_[:], [r_st], [r_w])
    st_, r_st = xin.next()
    kb.dma("sp", st_[:, 0:KC * HPC], wfd, [], [r_st])
    kb.v("dve", "tensor_copy", [r_st], [r_w], out=wf[:].rearrange("p k c -> p (k c)"), in_=st_[:, 0:KC * HPC])

    hT = A2[:, 0:8192].rearrange("p (k c) -> p k c", k=KC); r_hT = Res()
    stq = Pool(kb, "stq", [128, QB], BF16, 4)
    vst = Pool(kb, "vst", [128, HPC, DH + 1], BF16, 3)
    for t_, r_ in zip(vst.tiles, vst.res):
        kb.v("dve", "memset", [], [r_], ap=t_[:], constant=1.0)
    fe = Pool(kb, "fe", [HPC, QB], F32, 1)
    Fb = Pool(kb, "Fb", [HPC, QB], F32, 2)
    fr = Pool(kb, "fr", [HPC, QB], F32, 3)
    fsb = Pool(kb, "fsb", [HPC, QB], BF16, 3)
    scale = float(DH) ** -0.5
    FK = kb.sb("FK", [128, NKB, HPC], F32); r_FK = Res()

    prevF = None
    for b in range(NB):
        t0 = b * QB
        for tt in range(QB // 128):
            xt, r_xt = xin.next()
            kb.dma("sp", xt[:], xb[t0 + tt * 128:t0 + (tt + 1) * 128, :], [], [r_xt])
            emit_norm_T(kb, xt, r_xt, G0, r_G0, hT, r_hT, tt * 128, KC)
        for j in range(8):
            ps, r_ps = kb.bank()
            for k in range(KC):
                kb.mm(ps[:, 0:QB], wqk[:, k, j * 128:(j + 1) * 128], hT[:, k, :], k == 0, k == KC - 1, [r_w, r_hT], [r_ps])
            s_, r_s = stq.next()
            h = j % HPC
            if j < HPC:
                kb.act(s_[:], ps[:, 0:QB], AF.Copy, [r_ps], [r_s], scale=scale)
                kb.dma("pool", qTd[h, :, t0:t0 + QB], s_[:], [r_s], [r_qTd[h]])
            else:
                kb.v("dve", "tensor_copy", [r_ps], [r_s], out=s_[:], in_=ps[:, 0:QB])
                kb.dma("pool", kTd[h, :, t0:t0 + QB], s_[:], [r_s], [r_kTd[h]])
        for tt in range(QB // 128):
            ps, r_ps = kb.bank()
            for k in range(KC):
                kb.mm(ps[:, 0:512], hT[:, k, tt * 128:(tt + 1) * 128], wv[:, k, :], k == 0, k == KC - 1, [r_w, r_hT], [r_ps])
            v_, r_v = vst.next()
            src = ps[:, 0:512].rearrange("p (h c) -> p h c", h=HPC)
            if tt % 2 == 0:
                kb.act(v_[:, :, 0:DH], src, AF.Copy, [r_ps], [r_v])
            else:
                kb.v("dve", "tensor_copy", [r_ps], [r_v], out=v_[:, :, 0:DH], in_=src)
            rows = slice(t0 + tt * 128, t0 + (tt + 1) * 128)
            kb.dma("pool", V1d.rearrange("h t c -> t h c")[rows, :, :], v_[:], [r_v], r_V1d)
        ps, r_ps = kb.bank()
        for k in range(KC):
            kb.mm(ps[0:HPC, 0:QB], wf[:, k, :], hT[:, k, :], k == 0, k == KC - 1, [r_w, r_hT], [r_ps])
        e_, r_e = fe.next()
        kb.act(e_[:], ps[0:HPC, 0:QB], AF.Exp, [r_ps, r_bf], [r_e], scale=-1.0, bias=bft[:, 1:2])
        kb.act(e_[:], e_[:], AF.Ln, [r_e], [r_e], bias=1.0)
        F_, r_F = Fb.next()
        init = 0.0 if prevF is None else prevF[0][:, QB - 1:QB]
        rd = [r_e, r_ones] + ([] if prevF is None else [prevF[1]])
        kb.v("dve", "tensor_tensor_scan", rd, [r_F], out=F_[:], data0=ones4[:], data1=e_[:], initial=init,
             op0=ALU.mult, op1=ALU.subtract)
        prevF = (F_, r_F)
        ps, r_ps = kb.bank()
        for tt in range(QB // 128):
            kb.tr(ps[:, tt * HPC:(tt + 1) * HPC], F_[:, tt * 128:(tt + 1) * 128], kb.identf[0:HPC, 0:HPC], [r_F, kb.r_identf], [r_ps])
        kb.act(FK[:, b * 4:(b + 1) * 4, :], ps[:, 0:4 * HPC].rearrange("p (t h) -> p t h", h=HPC), AF.Copy, [r_ps], [r_FK], scale=-1.0)
        cur, r_cur = F_, r_F
        for i in range(3):
            fb_, r_fb = fsb.next()
            kb.v("dve", "tensor_copy", [r_cur], [r_fb], out=fb_[:], in_=cur[:])
            kb.dma("pool", Fsd[i, :, t0:t0 + QB], fb_[:], [r_fb], [r_Fsd])
            if i < 2:
                ff, r_ff = fr.next()
                kb.v("dve", "tensor_copy", [r_fb], [r_ff], out=ff[:], in_=fb_[:])
                nr, r_nr = fr.next()
                kb.v("dve", "tensor_tensor", [r_cur, r_ff], [r_nr], out=nr[:], in0=cur[:], in1=ff[:], op=ALU.subtract)
                cur, r_cur = nr, r_nr

    assert S <= 16384
    kT = A2[:, 0:S]; r_kT = Res()
    kt_first = [r_hT, xin.res[0], xin.res[1]]
    V1 = kb.sb("V1", [128, NKB, DH + 1], BF16); r_V1 = Res()
    KF = wbig[0:6, 0:S]; r_KF = Res()
    qTb = Pool(kb, "qTb", [128, QB], BF16, 2)
    QFb = Pool(kb, "QFb", [6, QB], BF16, 2)
    for t_, r_ in zip(QFb.tiles, QFb.res):
        kb.v("dve", "memset", [], [r_], ap=t_[:], constant=-1.0)
    PT = Pool(kb, "PT", [128, QB], BF16, 6)
    ost = Pool(kb, "ost", [128, 4, DH], BF16, 2)
    frow = Pool(kb, "frow", [128, QB], F32, 2)
    stmp = Pool(kb, "stmp", [128, QB], F32, 4)
    ones3 = kb.sb("ones3", [3, 128], BF16); r_o3 = Res()
    kb.v("dve", "memset", [], [r_o3], ap=ones3[:], constant=1.0)
    if prep is not None:
        pstf = Pool(kb, "pstf", None, None, 2, views=[kb.junk.tiles[0], kb.sb("pstf1", [128, D_MODEL], F32)])
        pstf.res[0] = kb.junk.res[0]
        pstb = kb.hn
        per_q = -(-len(prep.steps) // max(1, (HPC * NB * 3) // 4))
    first = True
    for h in range(HPC):
        nsp = 4
        for i in range(nsp):
            cs = slice(i * S // nsp, (i + 1) * S // nsp)
            kb.dma("sp", kT[:, cs], kTd[h, :, cs], [r_kTd[h]], [r_kT] + kt_first)
            kt_first = []
        nvp = max(1, NKB // 8)
        for i in range(nvp):
            ks = slice(i * NKB // nvp, (i + 1) * NKB // nvp)
            kb.dma("sp", V1[:, ks, :], V1d[h].rearrange("(n p) c -> p n c", p=128)[:, ks, :], r_V1d, [r_V1])
        first = False
        for Q in range(NB):
            q_, r_q = qTb.next()
            kb.dma("sp", q_[:], qTd[h, :, Q * QB:(Q + 1) * QB], [r_qTd[h]], [r_q])
            qf, r_qf = QFb.next()
            kb.dma("sp", qf[0:3, :], Fsd[:, h, Q * QB:(Q + 1) * QB], [r_Fsd], [r_qf])
            if prep is not None:
                prep.run(kb, per_q, pstf, pstb)
            psF, r_psF = kb.bank(0, 4)
            kb.mm(psF[:, 0:QB], ones3[:], qf[0:3, :], True, True, [r_o3, r_qf], [r_psF])
            fr_, r_fr = frow.next()
            kb.v("dve", "tensor_copy", [r_psF], [r_fr], out=fr_[:], in_=psF[:, 0:QB])
            acc = [(kb.ps[4 + j], kb.psr[4 + j]) for j in range(4)]
            nkb = 4 * Q + 4
            LAG = 3
            pend = []
            for kbi in range(nkb + LAG):
                if kbi < nkb:
                    j0 = max(0, kbi - 4 * Q)
                    c0 = j0 * 128
                    ps, r_ps = kb.bank(0, 4)
                    diag = kbi >= 4 * Q
                    kb.mm(ps[:, c0:QB], kT[:, kbi * 128:(kbi + 1) * 128], q_[:, c0:QB], True, not diag, [r_kT, r_q], [r_ps])
                    if diag:
                        kb.mm(ps[:, c0:c0 + 128], kb.identb[:], negmask[:], False, True, [kb.r_ident, r_nm], [r_ps])
                    t_, r_t = stmp.next()
                    kb.v("dve", "tensor_tensor", [r_ps, r_fr], [r_t], out=t_[:, c0:QB], in0=ps[:, c0:QB], in1=fr_[:, c0:QB], op=ALU.add)
                    p_, r_p = PT.next()
                    kb.act(p_[:, c0:QB], t_[:, c0:QB], AF.Exp, [r_t, r_FK], [r_p], bias=FK[:, kbi, h:h + 1])
                    pend.append((kbi, j0, p_, r_p))
                if kbi >= LAG or kbi >= nkb:
                    if pend and (len(pend) > LAG or kbi >= nkb):
                        pk, pj0, pp, r_pp = pend.pop(0)
                        for j in range(pj0, 4):
                            kb.mm(acc[j][0][:, 0:DH + 1], pp[:, j * 128:(j + 1) * 128], V1[:, pk, :], pk == 0, pk == 4 * Q + j,
                                  [r_pp, r_V1], [acc[j][1]])
            assert not pend
            o_, r_o = ost.next()
            for j in range(4):
                sm, r_sm = kb.small.next()
                kb.v("dve", "reciprocal", [acc[j][1]], [r_sm], out=sm[:, 0:1], in_=acc[j][0][:, DH:DH + 1])
                kb.act(o_[:, j, :], acc[j][0][:, 0:DH], AF.Copy, [acc[j][1], r_sm], [r_o], scale=sm[:, 0:1])
            kb.dma("pool", att.rearrange("(n p) c -> p n c", p=128)[:, Q * 4:Q * 4 + 4, h * DH:(h + 1) * DH], o_[:], [r_o], [r_att])
    if prep is not None:
        prep.run(kb, len(prep.steps), pstf, pstb)
    kb.end()


def fox_host_layouts(w_in, b_f, g0, m):
    D = D_MODEL
    cols_q = w_in[:, (HPC * m) * DH:(HPC * m + HPC) * DH]
    cols_k = w_in[:, D + (HPC * m) * DH:D + (HPC * m + HPC) * DH]
    wqk = np.concatenate([cols_q, cols_k], axis=1).reshape(KC, 128, 8 * 128)
    wqk = np.ascontiguousarray(wqk.transpose(1, 0, 2)).reshape(128, KC * 1024)
    wv = w_in[:, 2 * D + HPC * m * DH:2 * D + (HPC * m + HPC) * DH].reshape(KC, 128, 512)
    wv = np.ascontiguousarray(wv.transpose(1, 0, 2)).reshape(128, KC * 512)
    wf = w_in[:, 3 * D + HPC * m:3 * D + HPC * m + HPC].reshape(KC, 128, HPC)
    wf = np.ascontiguousarray(wf.transpose(1, 0, 2)).reshape(128, KC * HPC)
    bf = np.ascontiguousarray(b_f[HPC * m:HPC * m + HPC].reshape(HPC, 1))
    G0 = np.ascontiguousarray(np.broadcast_to(g0[None, :], (128, D)))
    idx = np.arange(128)
    negmask = np.where(idx[:, None] <= idx[None, :], 0.0, -30000.0).astype(np.float32)
    return dict(wqk=wqk, wv=wv, wf=wf, bf=bf, G0=G0, ident=np.eye(128, dtype=np.float32), negmask=negmask)


def wo_inputs(kb, pfx, KCI):
    return dict(w=kb.din(pfx + "w", [128, KCI * D_MODEL]), G=kb.din(pfx + "G", [128, D_MODEL]))


def emit_wo(kb, T, KCI, io, identd, src_fn, r_src, sel4d, xr, r_xr, xo, r_xo):
    TBW = 512 if KCI <= 16 else 128
    NB = T // TBW
    wd, Gd = io["w"], io["G"]
    kb.begin(identd)
    kb.junk = Pool(kb, "junk", [128, D_MODEL], F32, 1)
    xin = Pool(kb, "xin", [128, D_MODEL], F32, 2)
    G = kb.sb("Gs", [128, D_MODEL], F32); r_G = Res()
    kb.dma("sp", G[:], Gd, [], [r_G])
    sel4 = kb.sb("sel4", [128, 4], F32); r_sel = Res()
    kb.dma("sp", sel4[:], sel4d, [], [r_sel])
    w = kb.sb("wres", [128, KCI, D_MODEL], BF16); r_w = Res()
    for k in range(KCI):
        st_, r_st = xin.next()
        kb.dma("sp", st_[:], wd[:, k * D_MODEL:(k + 1) * D_MODEL], [], [r_st])
        emit_cast(kb, w[:, k, :], st_[:], [r_st], [r_w])
    aTbp = Pool(kb, "aTb", [128, KCI, TBW], BF16, 2 if KCI <= 16 else 1)
    cand = Pool(kb, "cand", [128, KCI * 128], BF16, 2)
    hn = Pool(kb, "hnw", [128, KCI * 128], BF16, 2) if KCI <= 16 else None
    ft = Pool(kb, "ft", [128, D_MODEL], F32, 2)
    for b in range(NB):
        t0 = b * TBW
        aTb, r_a = aTbp.next()
        for tt in range(TBW // 128):
            rows = slice(t0 + tt * 128, t0 + (tt + 1) * 128)
            srcs = src_fn(rows)
            if len(srcs) > 1:
                h_, r_h = hn.next()
            for i, (sap, view) in enumerate(srcs):
                c_, r_c = cand.next()
                kb.dma("sp", view(c_), sap, [r_src], [r_c])
                if len(srcs) == 1:
                    h_, r_h = c_, r_c
                elif i == 0:
                    kb.v("dve", "tensor_scalar", [r_c, r_sel], [r_h], out=h_[:], in0=c_[:], scalar1=sel4[:, 0:1], scalar2=None,
                         op0=ALU.mult)
                else:
                    kb.v("dve", "scalar_tensor_tensor", [r_c, r_sel, r_h], [r_h], out=h_[:], in0=c_[:], scalar=sel4[:, i:i + 1],
                         in1=h_[:], op0=ALU.mult, op1=ALU.add)
            emit_T(kb, h_, r_h, aTb, r_a, tt * 128, KCI)
        for tt in range(TBW // 128):
            f_, r_f = ft.next()
            for nq in range(4):
                ps, r_ps = kb.bank()
                for k in range(KCI):
                    kb.mm(ps[:, 0:512], aTb[:, k, tt * 128:(tt + 1) * 128], w[:, k, nq * 512:(nq + 1) * 512], k == 0, k == KCI - 1,
                          [r_a, r_w], [r_ps])
                if nq % 2 == 0:
                    kb.act(f_[:, nq * 512:(nq + 1) * 512], ps[:, 0:512], AF.Copy, [r_ps], [r_f])
                else:
                    kb.v("dve", "tensor_copy", [r_ps], [r_f], out=f_[:, nq * 512:(nq + 1) * 512], in_=ps[:, 0:512])
            xt, r_xt = xin.next()
            rows = slice(t0 + tt * 128, t0 + (tt + 1) * 128)
            kb.dma("sp", xt[:], xr[rows, :], [r_xr], [r_xt])
            emit_post(kb, f_[:], r_f, xt[:], r_xt, G, r_G, f_[:], r_f)
            kb.dma("pool", xo[rows, :], f_[:], [r_f], [r_xo])
    kb.end()


def wo_host_layout(w, g):
    KCI = w.shape[0] // 128
    wl = np.ascontiguousarray(w.reshape(KCI, 128, D_MODEL).transpose(1, 0, 2)).reshape(128, KCI * D_MODEL)
    G = np.ascontiguousarray(np.broadcast_to(g[None, :], (128, D_MODEL)))
    return dict(w=wl, G=G, ident=np.eye(128, dtype=np.float32))


RH = 8
DK_ = 256
DV_ = 512
RB = 1024
NS = 24


def ret_gammas():
    return [1.0 - 2.0 ** (-5.0 - h) for h in range(RH)]


def ret_inputs(kb, T):
    return dict(win=kb.din("r_win", [NS, 128, KC * 512]), G=kb.din("r_G", [128, D_MODEL]), cos2=kb.din("cos2", [T, 256]),
                sin2=kb.din("sin2", [T, 256]), DK=kb.din("DK", [128, D_MODEL]), Mp=kb.din("Mp", [128, RH * 128]),
                DQ=kb.din("DQ", [128, RH]), coef=kb.din("coef", [128, 3 * RH]))


def emit_ret(kb, T, full, io, pw, identd, xr, r_xr, Lout=None, r_L=None, Lprev=None, og=None, r_og_d=None):
    NBLK = T // RB
    NCH = T // 128
    wind, Gd, cosd, sind, DKd = io["win"], io["G"], io["cos2"], io["sin2"], io["DK"]
    Mpd, DQd, coefd = io["Mp"], io["DQ"], io["coef"]
    sfx = "f" if full else "s"
    wib, r_wib = pw["wib"], pw["r_wib"]
    qd = kb.dtmp("qd" + sfx, [T, D_MODEL], BF16)
    kd = kb.dtmp("kd" + sfx, [T, D_MODEL], BF16)
    vd = kb.dtmp("vd" + sfx, [T, 2 * D_MODEL], BF16)
    sgd = kb.dtmp("sgd" + sfx, [T, 2 * D_MODEL], BF16)
    r_qd, r_kd, r_vd, r_sgd = Res(), Res(), Res(), Res()
    slices = list(range(NS)) if full else list(range(4, 16))

    kb.begin(identd)
    kb.junk = Pool(kb, "junk", [128, D_MODEL], F32, 1)
    kb.hn = Pool(kb, "hn", [128, D_MODEL], BF16, 2)
    xin = Pool(kb, "xin", [128, D_MODEL], F32, 2)
    G = kb.sb("Gs", [128, D_MODEL], F32); r_G = Res()
    kb.dma("sp", G[:], Gd, [], [r_G])
    DK = kb.sb("DKs", [128, D_MODEL], F32); r_DK = Res()
    kb.dma("sp", DK[:], DKd, [], [r_DK])
    A1 = kb.sb("A1", [128, KC * RB], BF16); r_A1 = Res()
    hT = A1[:].rearrange("p (k c) -> p k c", k=KC)
    wpool = Pool(kb, "wsl", [128, KC * 512], BF16, 2)
    cs = kb.sb("cs", [128, 2, RB // 128, 256], F32); r_cs = Res()
    rt = Pool(kb, "rt", [128, 256], F32, 6)
    so = Pool(kb, "so", [128, 512], BF16, 4)

    for b in range(NBLK):
        t0 = b * RB
        for tt in range(RB // 128):
            xt, r_xt = xin.next()
            kb.dma("sp", xt[:], xr[t0 + tt * 128:t0 + (tt + 1) * 128, :], [r_xr], [r_xt])
            emit_norm_T(kb, xt, r_xt, G, r_G, hT, r_A1, tt * 128, KC)
        kb.dma("sp", cs[:, 0, :, :], cosd[t0:t0 + RB, :].rearrange("(n p) c -> p n c", p=128), [], [r_cs])
        kb.dma("sp", cs[:, 1, :, :], sind[t0:t0 + RB, :].rearrange("(n p) c -> p n c", p=128), [], [r_cs])
        for ns in slices:
            wt, r_wt = wpool.next()
            wv_ = wt[:].rearrange("p (k c) -> p k c", k=KC)
            kb.dma("sp", wt[:], wib[ns], [r_wib[ns]], [r_wt])
            for tt in range(RB // 128):
                ps, r_ps = kb.bank()
                for k in range(KC):
                    kb.mm(ps[:, 0:512], hT[:, k, tt * 128:(tt + 1) * 128], wv_[:, k, :], k == 0, k == KC - 1, [r_A1, r_wt], [r_ps])
                o_, r_o = so.next()
                rows = slice(t0 + tt * 128, t0 + (tt + 1) * 128)
                if ns < 8:
                    psv = ps[:, 0:512].rearrange("p (i t) -> p i t", t=2)
                    ov = o_[:].rearrange("p (i t) -> p i t", t=2)
                    c_ = cs[:, 0, tt, :]
                    s_ = cs[:, 1, tt, :]
                    t1, r_t1 = rt.next(); t2, r_t2 = rt.next()
                    kb.v("dve", "tensor_tensor", [r_ps, r_cs], [r_t1], out=t1[:], in0=psv[:, :, 0], in1=c_, op=ALU.mult)
                    kb.v("dve", "tensor_tensor", [r_ps, r_cs], [r_t2], out=t2[:], in0=psv[:, :, 1], in1=s_, op=ALU.mult)
                    kb.v("pool", "tensor_tensor", [r_t1, r_t2], [r_o], out=ov[:, :, 0], in0=t1[:], in1=t2[:], op=ALU.subtract)
                    t3, r_t3 = rt.next(); t4, r_t4 = rt.next()
                    kb.v("dve", "tensor_tensor", [r_ps, r_cs], [r_t3], out=t3[:], in0=psv[:, :, 0], in1=s_, op=ALU.mult)
                    kb.v("dve", "tensor_tensor", [r_ps, r_cs], [r_t4], out=t4[:], in0=psv[:, :, 1], in1=c_, op=ALU.mult)
                    kb.v("pool", "tensor_tensor", [r_t3, r_t4], [r_o], out=ov[:, :, 1], in0=t3[:], in1=t4[:], op=ALU.add)
                    if ns < 4:
                        kb.dma("pool", qd[rows, ns * 512:(ns + 1) * 512], o_[:], [r_o], [r_qd])
                    else:
                        kb.dma("pool", kd[rows, (ns - 4) * 512:(ns - 3) * 512], o_[:], [r_o], [r_kd])
                elif ns < 16:
                    kb.act(o_[:], ps[:, 0:512], AF.Copy, [r_ps], [r_o])
                    kb.dma("pool", vd[rows, (ns - 8) * 512:(ns - 7) * 512], o_[:], [r_o], [r_vd])
                else:
                    kb.act(o_[:], ps[:, 0:512], AF.Silu, [r_ps], [r_o])
                    kb.dma("pool", sgd[rows, (ns - 16) * 512:(ns - 15) * 512], o_[:], [r_o], [r_sgd])

    gam = ret_gammas()
    Sv = A1[:].bitcast(F32).rearrange("p (h d v) -> p h d v", h=RH, d=2)
    Sbf = wpool.tiles[0][:].rearrange("p (h d v) -> p h d v", h=RH, d=2)
    qkT = wpool.tiles[1][:].rearrange("p (b s j c) -> p b s j c", b=2, s=2, j=16)
    r_S = [Res() for _ in range(RH)]
    r_Sbf = [Res() for _ in range(RH)]
    r_qkT = [Res(), Res()]
    qk_tiles = [t[:].bitcast(BF16) for t in xin.tiles]
    r_qk = xin.res
    kdec = Pool(kb, "kdec", [128, D_MODEL], BF16, 2)
    vh = Pool(kb, "vh", [128, DV_], BF16, 4)
    barrier_w = [r_A1, wpool.res[0], wpool.res[1]] + r_S + r_Sbf + r_qkT
    kb.v("dve", "memset", [], barrier_w, ap=A1[:].bitcast(F32), constant=0.0)
    if full:
        Mp = kb.sb("Mps", [128, RH, 128], F32); r_Mp = Res()
        kb.dma("sp", Mp[:], Mpd.rearrange("p (h n) -> p h n", h=RH), [], [r_Mp])
        DQ = kb.sb("DQs", [128, RH], F32); r_DQ = Res()
        kb.dma("sp", DQ[:], DQd, [], [r_DQ])
        coef = kb.sb("coefs", [128, 3 * RH], F32); r_coef = Res()
        kb.dma("sp", coef[:], coefd, [], [r_coef])
        sgh = Pool(kb, "sgh", [128, DV_], BF16, 3)
        oh = Pool(kb, "oh", [128, DV_], F32, 3)
        ogp = Pool(kb, "ogp", [128, DV_], BF16, 3)
        sT = Pool(kb, "sT", [128, 128], BF16, 2 * RH)
        for i in range(3):
            for h in range(RH):
                for d in range(2):
                    l_, r_l = oh.next()
                    kb.dma("sp", l_[:], Lprev[i, h, d], [r_L], [r_l])
                    kb.v("dve", "scalar_tensor_tensor", [r_l, r_coef, r_S[h]], [r_S[h]], out=Sv[:, h, d, :], in0=l_[:],
                         scalar=coef[:, i * RH + h:i * RH + h + 1], in1=Sv[:, h, d, :], op0=ALU.mult, op1=ALU.add)
        for h in range(RH):
            kb.act(Sbf[:, h, :, :], Sv[:, h, :, :], AF.Copy, [r_S[h]], [r_Sbf[h]])
    for c in range(NCH):
        rows = slice(c * 128, (c + 1) * 128)
        bi = c % 2
        qk = qk_tiles[bi]
        r_q = r_qk[bi]
        if full:
            kb.dma("sp", qk[:, 0:D_MODEL], qd[rows, :], [r_qd], [r_q])
        kb.dma("sp", qk[:, D_MODEL:2 * D_MODEL], kd[rows, :], [r_kd], [r_q])
        kd_, r_kdec = kdec.next()
        kb.v("dve", "tensor_tensor", [r_q, r_DK], [r_kdec], out=kd_[:], in0=qk[:, D_MODEL:2 * D_MODEL], in1=DK[:], op=ALU.mult)
        if full:
            for s in range(2):
                for half in range(2):
                    ps, r_ps = kb.bank()
                    psb = ps[:].bitcast(BF16)
                    for j in range(8):
                        col = s * D_MODEL + (half * 8 + j) * 128
                        kb.tr(psb[:, j * 128:(j + 1) * 128], qk[:, col:col + 128], kb.identb[:], [r_q, kb.r_ident], [r_ps])
                    src = psb[:, 0:1024].rearrange("p (j c) -> p j c", c=128)
                    dst = qkT[:, bi, s, half * 8:half * 8 + 8, :]
                    if half == 0:
                        kb.act(dst, src, AF.Copy, [r_ps], [r_qkT[bi]])
                    else:
                        kb.v("dve", "tensor_copy", [r_ps], [r_qkT[bi]], out=dst, in_=src)
        sts = []
        if full:
            for h in range(RH):
                ps, r_ps = kb.bank()
                for d in range(2):
                    kb.mm(ps[:, 0:128], qkT[:, bi, 1, h * 2 + d, :], qkT[:, bi, 0, h * 2 + d, :], d == 0, d == 1, [r_qkT[bi]], [r_ps])
                st_, r_st = sT.next()
                kb.v("dve", "tensor_tensor", [r_ps, r_Mp], [r_st], out=st_[:], in0=ps[:, 0:128], in1=Mp[:, h, :], op=ALU.mult)
                sts.append((st_, r_st))
        for h in range(RH):
            v_, r_v = vh.next()
            kb.dma("sp", v_[:], vd[rows, h * DV_:(h + 1) * DV_], [r_vd], [r_v])
            if full:
                g_, r_g = sgh.next()
                kb.dma("sp", g_[:], sgd[rows, h * DV_:(h + 1) * DV_], [r_sgd], [r_g])
                st_, r_st = sts[h]
                po, r_po = kb.bank()
                kb.mm(po[:, 0:DV_], st_[:], v_[:], True, False, [r_st, r_v], [r_po])
                for d in range(2):
                    kb.mm(po[:, 0:DV_], qkT[:, bi, 0, h * 2 + d, :], Sbf[:, h, d, :], False, d == 1, [r_qkT[bi], r_Sbf[h]], [r_po])
                o_, r_o = oh.next()
                kb.act(o_[:], po[:, 0:DV_], AF.Copy, [r_po, r_DQ], [r_o], scale=DQ[:, h:h + 1])
                rstd, r_rs = emit_rstd(kb, o_[:], r_o, DV_)
                og_, r_og = ogp.next()
                kb.v("dve", "scalar_tensor_tensor", [r_o, r_rs, r_g], [r_og], out=og_[:], in0=o_[:], scalar=rstd, in1=g_[:],
                     op0=ALU.mult, op1=ALU.mult)
                kb.dma("pool", og[rows, h * DV_:(h + 1) * DV_], og_[:], [r_og], [r_og_d])
            for d in range(2):
                pS, r_pS = kb.bank()
                kb.mm(pS[:, 0:DV_], kd_[:, h * DK_ + d * 128:h * DK_ + (d + 1) * 128], v_[:], True, True, [r_kdec, r_v], [r_pS])
                kb.v("dve", "scalar_tensor_tensor", [r_pS, r_S[h]], [r_S[h]], out=Sv[:, h, d, :], in0=Sv[:, h, d, :],
                     scalar=float(gam[h] ** 128), in1=pS[:, 0:DV_], op0=ALU.mult, op1=ALU.add)
            if full:
                kb.act(Sbf[:, h, :, :], Sv[:, h, :, :], AF.Copy, [r_S[h]], [r_Sbf[h]])
    if not full:
        for h in range(RH):
            for d in range(2):
                kb.dma("pool", Lout[h, d], Sv[:, h, d, :], [r_S[h]], [r_L])
    kb.end()


def ret_host_layouts(w_in, g0):
    w = w_in.reshape(KC, 128, NS, 512)
    win = np.ascontiguousarray(w.transpose(2, 1, 0, 3)).reshape(NS, 128, KC * 512)
    G = np.ascontiguousarray(np.broadcast_to(g0[None, :], (128, D_MODEL)))
    return dict(win=win, G=G, ident=np.eye(128, dtype=np.float32))


def ret_const_tables(pos0, T, j):
    theta = (1.0 / (10000.0 ** np.linspace(0.0, 1.0, DK_ // 2, dtype=np.float32))).astype(np.float32)
    ang = (np.arange(pos0, pos0 + T, dtype=np.float32)[:, None] * theta[None, :]).astype(np.float32)
    cos = np.cos(ang).astype(np.float32)
    sin = np.sin(ang).astype(np.float32)
    cos2 = np.ascontiguousarray(np.concatenate([cos, cos], axis=1))
    sin2 = np.ascontiguousarray(np.concatenate([sin, sin], axis=1))
    gam = np.array(ret_gammas(), dtype=np.float64)
    lg = np.log1p(-(2.0 ** (-5.0 - np.arange(RH, dtype=np.float64))))
    m = np.arange(128, dtype=np.float64)
    ks = DK_ ** -0.5
    DK = np.exp((127.0 - m)[:, None] * lg[None, :]) * ks
    DK = np.ascontiguousarray(np.repeat(DK, DK_, axis=1)).astype(np.float32)
    DQ = np.exp((m + 1.0)[:, None] * lg[None, :]).astype(np.float32)
    Mp = np.exp(-(m + 1.0)[:, None, None] * lg[None, :, None]) * ks
    Mp = Mp * (m[None, None, :] >= m[:, None, None])
    Mp = np.ascontiguousarray(Mp.reshape(128, RH * 128)).astype(np.float32)
    coef = np.zeros((3, RH), np.float64)
    for i in range(3):
        if i < j:
            coef[i] = np.exp(T * (j - 1 - i) * lg)
    coef = np.ascontiguousarray(np.broadcast_to(coef.reshape(1, 3 * RH), (128, 3 * RH))).astype(np.float32)
    return dict(cos2=cos2, sin2=sin2, DK=DK, DQ=DQ, Mp=Mp, coef=coef)


GROUPS = [[0, 1, 2, 3], [4, 5, 6, 7]]
TPC = SEQ * BATCH // NCORES


def build_fused():
    kb = KB()
    T, S, D = TPC, SEQ, D_MODEL
    ident = kb.din("ident", [128, 128])
    xb = kb.din("xb", [S, D])
    xs = kb.din("xs", [T, D])
    sel_prev = kb.din("sel_prev", [128, 4])
    sel_own = kb.din("sel_own", [128, 4])
    fio = fox_inputs(kb)
    w0 = wo_inputs(kb, "wo0_", 16)
    f0 = ffn_inputs(kb, "f0_")
    rio = ret_inputs(kb, T)
    w1 = wo_inputs(kb, "wo1_", 32)
    f1 = ffn_inputs(kb, "f1_")
    out = kb.dout("out", [T, D])

    prep = Prep()
    pf0 = prep_ffn(kb, prep, "f0_", f0)
    pr = prep_ret(kb, prep, rio)
    pf1 = prep_ffn(kb, prep, "f1_", f1)
    att = kb.dtmp("att", [S, HPC * DH], BF16); r_att = Res()
    emit_fox(kb, S, fio, xb, ident, att, r_att, prep=prep)
    CR = 1024
    r_attg = Res()
    attg = []
    for i in range(S // CR):
        g_ = kb.dtmp("attg%d" % i, [4 * CR, HPC * DH], BF16)
        kb.collective("AllGather", GROUPS, att[i * CR:(i + 1) * CR, :], g_, [r_att], [r_attg], flush=(i == S // CR - 1))
        attg.append(g_.rearrange("(m t) c -> t m c", m=4))

    def src0(rows):
        res = []
        for jj in range(4):
            r0 = jj * T + rows.start
            res.append((attg[r0 // CR][r0 % CR:r0 % CR + 128], lambda c_: c_[:].rearrange("p (m c) -> p m c", m=4)))
        return res

    xm0 = kb.dtmp("xm0", [T, D]); r_xm0 = Res()
    emit_wo(kb, T, 16, w0, ident, src0, r_attg, sel_own, xs, Res(), xm0, r_xm0)
    halo0 = kb.dtmp("halo0", [8, D]); r_h0 = Res()
    kb.collective("AllGather", GROUPS, xm0[T - 2:T, :], halo0, [r_xm0], [r_h0])
    x1 = kb.dtmp("x1", [T, D]); r_x1 = Res()
    emit_ffn(kb, T, "f0_", f0, pf0, xm0, r_xm0, halo0, r_h0, sel_prev, ident, x1, r_x1)

    L = kb.dtmp("Lst", [RH, 2, 128, DV_]); r_L = Res()
    emit_ret(kb, T, False, rio, pr, ident, x1, r_x1, Lout=L, r_L=r_L)
    r_La = Res()
    Lall = []
    Lf = L.rearrange("h d p v -> (h d p) v")
    for i in range(RH // 2):
        g_ = kb.dtmp("Lall%d" % i, [4 * 512, DV_])
        kb.collective("AllGather", GROUPS, Lf[i * 512:(i + 1) * 512, :], g_, [r_L], [r_La], flush=(i == RH // 2 - 1))
        Lall.append(g_.rearrange("(i h d p) v -> i h d p v", i=4, h=2, d=2))

    class _LP:
        def __getitem__(self, idx):
            i, h, d = idx
            return Lall[h // 2][i, h % 2, d]

    og = kb.dtmp("og", [T, RH * DV_], BF16); r_og = Res()
    emit_ret(kb, T, True, rio, pr, ident, x1, r_x1, r_L=r_La, Lprev=_LP(), og=og, r_og_d=r_og)

    def src1(rows):
        return [(og[rows, :], lambda c_: c_[:])]

    xm1 = kb.dtmp("xm1", [T, D]); r_xm1 = Res()
    emit_wo(kb, T, 32, w1, ident, src1, r_og, sel_own, x1, r_x1, xm1, r_xm1)
    halo1 = kb.dtmp("halo1", [8, D]); r_h1 = Res()
    kb.collective("AllGather", GROUPS, xm1[T - 2:T, :], halo1, [r_xm1], [r_h1])
    emit_ffn(kb, T, "f1_", f1, pf1, xm1, r_xm1, halo1, r_h1, sel_prev, ident, out, Res())
    return kb.nc


_NC = []


def kernel(x, norm_g, fox_w_in, fox_b_f, fox_w_o, ret_w_in, ret_w_o,
           ffn_w_up, ffn_conv_w, ffn_conv_b, ffn_w_down):
    f = lambda a: np.ascontiguousarray(np.asarray(a, dtype=np.float32))
    x, norm_g = f(x), f(norm_g)
    fox_w_in, fox_b_f, fox_w_o = f(fox_w_in), f(fox_b_f), f(fox_w_o)
    ret_w_in, ret_w_o = f(ret_w_in), f(ret_w_o)
    ffn_w_up, ffn_conv_w, ffn_conv_b, ffn_w_down = f(ffn_w_up), f(ffn_conv_w), f(ffn_conv_b), f(ffn_w_down)
    if not _NC:
        _NC.append(build_fused())
    nc = _NC[0]
    T, G = TPC, NCORES // BATCH
    shared = {"ident": np.eye(128, dtype=np.float32)}
    for k_, v_ in wo_host_layout(fox_w_o[0], norm_g[0, 1]).items():
        if k_ != "ident":
            shared["wo0_" + k_] = v_
    for k_, v_ in wo_host_layout(ret_w_o[0], norm_g[1, 1]).items():
        if k_ != "ident":
            shared["wo1_" + k_] = v_
    for l, pfx in ((0, "f0_"), (1, "f1_")):
        lay = ffn_host_layouts(ffn_w_up[l], ffn_conv_w[l], ffn_conv_b[l], ffn_w_down[l], norm_g[l, 2], norm_g[l, 3])
        for k_, v_ in lay.items():
            if k_ != "ident":
                shared[pfx + k_] = v_
    rl = ret_host_layouts(ret_w_in[0], norm_g[1, 0])
    shared["r_win"] = rl["win"]
    shared["r_G"] = rl["G"]
    foxl = [fox_host_layouts(fox_w_in[0], fox_b_f[0], norm_g[0, 0], m) for m in range(G)]
    maps = []
    for c in range(NCORES):
        b, j = c // G, c % G
        m = dict(shared)
        for k_, v_ in foxl[j].items():
            if k_ != "ident":
                m[k_] = v_
        m.update(ret_const_tables(j * T, T, j))
        m["xb"] = x[b]
        m["xs"] = np.ascontiguousarray(x[b, j * T:(j + 1) * T])
        sp = np.zeros((128, 4), np.float32)
        so = np.zeros((128, 4), np.float32)
        if j > 0:
            sp[:, j - 1] = 1.0
        so[:, j] = 1.0
        m["sel_prev"] = sp
        m["sel_own"] = so
        maps.append(m)
    res = run_bass_kernel_spmd(nc, maps, core_ids=list(range(NCORES))).results
    out = np.concatenate([res[c]["out"] for c in range(NCORES)], axis=0).reshape(BATCH, SEQ, D_MODEL)
    return out.astype(np.float32)
```

```python
import contextlib
import numpy as np
import concourse.bass as bass
import concourse.mybir as mybir
from concourse.bass_utils import run_bass_kernel_spmd

F32 = mybir.dt.float32
BF16 = mybir.dt.bfloat16
AF = mybir.ActivationFunctionType
ALU = mybir.AluOpType
AX = mybir.AxisListType

D_MODEL = 2048
SEQ = 16384
BATCH = 2
D_FF = 5632
NORM_EPS = 1e-6
NCORES = 8

SAME_ENG_SYNC = True
SEM_GEN = 30000
SEM_DMA_GEN = 1500


class Res:
    __slots__ = ("name", "w", "rs")

    def __init__(self, name=""):
        self.name = name
        self.w = None
        self.rs = []


class _Op:
    __slots__ = ("eng", "fn", "deps", "ev", "dma", "inc")


class Sched:
    ENGS = ("pe", "act", "dve", "pool", "sp")

    def __init__(self, nc, ndma=10):
        self.nc = nc
        self.ops = {e: [] for e in self.ENGS}
        self.cnt = {e: 0 for e in self.ENGS}
        self.dman = {e: 0 for e in self.ENGS}
        self.dmahist = {e: [] for e in self.ENGS}
        self.ndma = ndma
        self.ncc = 0
        self.sems = {}
        self.semstack = contextlib.ExitStack()
        self.waited = {e: {} for e in self.ENGS}
        self.fin = {}

    def op(self, eng, fn, reads=(), writes=(), dma=False, inc=None):
        o = _Op()
        o.eng = eng
        o.fn = fn
        o.dma = dma
        o.inc = inc
        deps = []
        for r in reads:
            if r.w is not None:
                deps.append(r.w)
        for r in writes:
            if r.w is not None:
                deps.append(r.w)
            deps.extend(r.rs)
        if inc is not None:
            o.ev = ("cc", self.ncc, inc)
            self.ncc += 1
        elif dma:
            n = self.dman[eng]
            self.dman[eng] = n + 1
            rnd = n // self.ndma
            o.ev = ("d_" + eng, (n % self.ndma, rnd // SEM_DMA_GEN), 16 * (rnd % SEM_DMA_GEN + 1))
            if n >= self.ndma:
                deps.append(self.dmahist[eng][n - self.ndma])
            self.dmahist[eng].append(o)
        else:
            c = self.cnt[eng]
            self.cnt[eng] = c + 1
            o.ev = ("c_" + eng, c // SEM_GEN, c % SEM_GEN + 1)
        dd = []
        seen = set()
        for d in deps:
            if id(d) in seen:
                continue
            seen.add(id(d))
            if (not d.dma) and d.eng == eng:
                if eng == "pe" or not SAME_ENG_SYNC:
                    continue
            dd.append(d)
        o.deps = dd
        for r in reads:
            r.rs.append(o)
        for r in writes:
            r.w = o
            r.rs = []
        self.ops[eng].append(o)
        return o

    def flush(self):
        nc = self.nc
        for e in self.ENGS:
            for o in self.ops[e]:
                k = (o.ev[0], o.ev[1])
                if k not in self.sems:
                    self.sems[k] = self.semstack.enter_context(nc.semaphore("s%d" % len(self.sems)))
                self.fin[k] = max(self.fin.get(k, 0), o.ev[2])
        sems = self.sems
        fin = dict(self.fin)
        ops = self.ops
        self.ops = {e: [] for e in self.ENGS}
        with nc.Block() as block:
            def run(eng_name):
                def body(eng):
                    waited = self.waited[eng_name]
                    for o in ops[eng_name]:
                        for d in o.deps:
                            k = (d.ev[0], d.ev[1])
                            if waited.get(k, 0) < d.ev[2]:
                                eng.wait_ge(sems[k], d.ev[2])
                                waited[k] = d.ev[2]
                        ins = o.fn(eng)
                        ins.then_inc(sems[(o.ev[0], o.ev[1])], o.inc if o.inc is not None else (16 if o.dma else 1))
                    for k, v in fin.items():
                        if waited.get(k, 0) < v:
                            eng.wait_ge(sems[k], v)
                            waited[k] = v
                return body

            block.tensor(run("pe"))
            block.scalar(run("act"))
            block.vector(run("dve"))
            block.gpsimd(run("pool"))
            block.sync(run("sp"))


class Pool:
    def __init__(self, kb, name, shape, dt, n, views=None):
        if views is not None:
            self.tiles = list(views)
            n = len(views)
        else:
            self.tiles = [kb.sb("%s%d" % (name, i), shape, dt) for i in range(n)]
        self.res = [Res("%s%d" % (name, i)) for i in range(n)]
        self.i = 0

    def next(self):
        i = self.i % len(self.tiles)
        self.i += 1
        return self.tiles[i], self.res[i]


class KB:
    def __init__(self):
        self.nc = bass.Bass("TRN2", target_bir_lowering=False)
        self.S = Sched(self.nc)
        self.st = None
        self.phase = 0

    def begin(self, ident_in):
        self.phase += 1
        self.st = contextlib.ExitStack()
        self.ps = []
        self.psr = []
        self.psi = 0
        setup_common(self, ident_in)

    def end(self):
        self.S.flush()
        self.st.close()
        self.st = None

    def sb(self, name, shape, dt):
        return self.st.enter_context(self.nc.sbuf_tensor("p%d_%s" % (self.phase, name), list(shape), dt))

    def alloc_psum(self, n=8):
        for i in range(n):
            self.ps.append(self.st.enter_context(self.nc.psum_tensor("p%d_ps%d" % (self.phase, i), [128, 512], F32)))
            self.psr.append(Res("ps%d" % i))

    def bank(self, lo=0, hi=8):
        i = lo + self.psi % (hi - lo)
        self.psi += 1
        return self.ps[i], self.psr[i]

    def din(self, name, shape, dt=F32):
        return self.nc.dram_tensor(name, list(shape), dt, kind="ExternalInput").ap()

    def dout(self, name, shape, dt=F32):
        return self.nc.dram_tensor(name, list(shape), dt, kind="ExternalOutput").ap()

    def dtmp(self, name, shape, dt=F32):
        return self.nc.dram_tensor(name, list(shape), dt, kind="Internal").ap()

    def collective(self, kind, groups, in_ap, out_ap, r, w, flush=True):
        if not hasattr(self, "r_cc"):
            self.r_cc = Res("cc")
        self.S.op("pool", lambda e: e.collective_compute(kind, ALU.bypass, replica_groups=groups, ins=[in_ap], outs=[out_ap]),
                  r, list(w) + [self.r_cc], dma=True, inc=1)
        if flush:
            self.S.flush()

    def dma(self, q, out, in_, r, w):
        return self.S.op(q, lambda e: e.dma_start(out=out, in_=in_), r, w, dma=True)

    def act(self, out, in_, func, r, w, **kw):
        return self.S.op("act", lambda e: e.activation(out=out, in_=in_, func=func, **kw), r, w)

    def mm(self, out, lhsT, rhs, start, stop, r, w):
        return self.S.op("pe", lambda e: e.matmul(out, lhsT=lhsT, rhs=rhs, start=start, stop=stop,
                                                  skip_group_check=True), r, w)

    def tr(self, out, in_, ident, r, w):
        return self.S.op("pe", lambda e: e.transpose(out=out, in_=in_, identity=ident), r, w)

    def v(self, eng, meth, r, w, **kw):
        return self.S.op(eng, lambda e: getattr(e, meth)(**kw), r, w)


def emit_rstd(kb, x_ap, r_x, ncols, P=128):
    junk, r_j = kb.junk.next()
    st, r_s = kb.small.next()
    kb.act(junk[0:P, 0:ncols], x_ap, AF.Square, [r_x], [r_j])
    kb.v("dve", "reduce_sum", [r_j], [r_s], out=st[0:P, 0:1], in_=junk[0:P, 0:ncols], axis=AX.X)
    kb.v("dve", "tensor_scalar", [r_s], [r_s], out=st[0:P, 1:2], in0=st[0:P, 0:1], scalar1=1.0 / ncols,
         scalar2=NORM_EPS, op0=ALU.mult, op1=ALU.add)
    kb.act(st[0:P, 2:3], st[0:P, 1:2], AF.Sqrt, [r_s], [r_s])
    kb.v("dve", "reciprocal", [r_s], [r_s], out=st[0:P, 3:4], in_=st[0:P, 2:3])
    return st[0:P, 3:4], r_s


def emit_norm_T(kb, x_tile, r_x, G, r_G, hT, r_hT, col0, KC, cp_eng="act"):
    rstd, r_s = emit_rstd(kb, x_tile[:, 0:KC * 128], r_x, KC * 128)
    hn, r_hn = kb.hn.next()
    kb.v("dve", "scalar_tensor_tensor", [r_x, r_s, r_G], [r_hn], out=hn[:, 0:KC * 128], in0=x_tile[:, 0:KC * 128],
         scalar=rstd, in1=G[:, 0:KC * 128], op0=ALU.mult, op1=ALU.mult)
    emit_T(kb, hn, r_hn, hT, r_hT, col0, KC, cp_eng)


def emit_T(kb, hn, r_hn, hT, r_hT, col0, KC, cp_eng="act"):
    for k0 in range(0, KC, 8):
        n = min(8, KC - k0)
        ps, r_ps = kb.bank()
        psb = ps[:].bitcast(BF16)
        for j in range(n):
            kb.tr(psb[:, j * 128:(j + 1) * 128], hn[:, (k0 + j) * 128:(k0 + j + 1) * 128], kb.identb[:],
                  [r_hn, kb.r_ident], [r_ps])
        src = psb[:, 0:n * 128].rearrange("p (k c) -> p k c", c=128)
        dst = hT[:, k0:k0 + n, col0:col0 + 128]
        if cp_eng == "act":
            kb.act(dst, src, AF.Copy, [r_ps], [r_hT])
        else:
            kb.v(cp_eng, "tensor_copy", [r_ps], [r_hT], out=dst, in_=src)


def emit_post(kb, f_ap, r_f, x_ap, r_x, G, r_G, out_tile, r_out, ncols=D_MODEL):
    rstd, r_s = emit_rstd(kb, f_ap, r_f, ncols)
    kb.v("dve", "scalar_tensor_tensor", [r_f, r_s, r_G], [r_out], out=out_tile, in0=f_ap, scalar=rstd, in1=G[:, 0:ncols],
         op0=ALU.mult, op1=ALU.mult)
    kb.v("pool", "tensor_tensor", [r_out, r_x], [r_out], out=out_tile, in0=out_tile, in1=x_ap, op=ALU.add)


def setup_common(kb, ident_in):
    kb.alloc_psum(8)
    idf = kb.sb("idf", [128, 128], F32)
    kb.identb = kb.sb("identb", [128, 128], BF16)
    kb.r_ident = Res("ident")
    r_idf = Res("idf")
    kb.dma("sp", idf[:], ident_in, [], [r_idf])
    kb.identf = idf
    kb.r_identf = r_idf
    kb.v("dve", "tensor_copy", [r_idf], [kb.r_ident], out=kb.identb[:], in_=idf[:])
    kb.small = Pool(kb, "small", [128, 4], F32, 12)


_cast_rr = [0]


def emit_cast(kb, out, in_, r, w):
    i = _cast_rr[0] % 3
    _cast_rr[0] += 1
    if i == 0:
        kb.v("dve", "tensor_copy", r, w, out=out, in_=in_)
    elif i == 1:
        kb.act(out, in_, AF.Copy, r, w)
    else:
        kb.v("pool", "tensor_copy", r, w, out=out, in_=in_)


TB = 512
NPAIR = D_FF // 128
KC = D_MODEL // 128


def ffn_inputs(kb, pfx):
    return dict(wup=kb.din(pfx + "wup", [NPAIR, 128, 2 * KC * 128]), wdn=kb.din(pfx + "wdn", [4, 128, NPAIR * 512]),
                G2=kb.din(pfx + "G2", [128, D_MODEL]), G3=kb.din(pfx + "G3", [128, D_MODEL]),
                cw=kb.din(pfx + "cw", [128, 2 * NPAIR * 3]), cb=kb.din(pfx + "cb", [128, 2 * NPAIR]))


class Prep:
    def __init__(self):
        self.steps = []

    def add(self, src_ap, dst_ap, r_dst):
        self.steps.append((src_ap, dst_ap, r_dst))

    def run(self, kb, n, stf, stb):
        for _ in range(n):
            if not self.steps:
                return
            src_ap, dst_ap, r_dst = self.steps.pop(0)
            st_, r_st = stf.next()
            sb_, r_sb = stb.next()
            kb.dma("sp", st_[:], src_ap, [], [r_st])
            kb.v("pool", "tensor_copy", [r_st], [r_sb], out=sb_[:], in_=st_[:])
            kb.dma("pool", dst_ap, sb_[:], [r_sb], [r_dst])


def prep_ffn(kb, prep, pfx, io):
    wup, wdn = io["wup"], io["wdn"]
    wub = kb.dtmp(pfx + "wub", [NPAIR, 128, 2 * KC * 128], BF16)
    wdb = kb.dtmp(pfx + "wdb", [4, 128, NPAIR * 512], BF16)
    r_wub = [Res() for _ in range(NPAIR)]
    r_wdb = [Res() for _ in range(4)]
    for i in range(NPAIR):
        for hf in range(2):
            sl = slice(hf * KC * 128, (hf + 1) * KC * 128)
            prep.add(wup[i, :, sl], wub[i, :, sl], r_wub[i])
    for dq in range(4):
        for g in range(NPAIR * 512 // D_MODEL):
            sl = slice(g * D_MODEL, (g + 1) * D_MODEL)
            prep.add(wdn[dq, :, sl], wdb[dq, :, sl], r_wdb[dq])
    return dict(wub=wub, wdb=wdb, r_wub=r_wub, r_wdb=r_wdb)


def prep_ret(kb, prep, io):
    wind = io["win"]
    wib = kb.dtmp("wib", [NS, 128, KC * 512], BF16)
    r_wib = [Res() for _ in range(NS)]
    for ns in range(NS):
        for g in range(KC * 512 // D_MODEL):
            sl = slice(g * D_MODEL, (g + 1) * D_MODEL)
            prep.add(wind[ns, :, sl], wib[ns, :, sl], r_wib[ns])
    return dict(wib=wib, r_wib=r_wib)


def emit_ffn(kb, T, pfx, io, pw, xm, r_xm, halo_all, r_halo, sel4d, identd, xo, r_xo):
    NB = T // TB
    G2d, G3d, cwd, cbd = io["G2"], io["G3"], io["cw"], io["cb"]
    wub, wdb, r_wub, r_wdb = pw["wub"], pw["wdb"], pw["r_wub"], pw["r_wdb"]

    kb.begin(identd)
    kb.junk = Pool(kb, "junk", [128, D_MODEL], F32, 1)
    kb.hn = Pool(kb, "hn", [128, D_MODEL], BF16, 2)
    xin = Pool(kb, "xin", [128, D_MODEL], F32, 2)
    G2 = kb.sb("G2s", [128, D_MODEL], F32); r_G2 = Res()
    G3 = kb.sb("G3s", [128, D_MODEL], F32); r_G3 = Res()
    cw = kb.sb("cws", [128, 2, NPAIR, 3], F32); r_cw = Res()
    cb = kb.sb("cbs", [128, 2, NPAIR], F32); r_cb = Res()
    kb.dma("sp", G2[:], G2d, [], [r_G2])
    kb.dma("sp", G3[:], G3d, [], [r_G3])
    kb.dma("sp", cw[:], cwd.rearrange("p (h i t) -> p h i t", h=2, i=NPAIR), [], [r_cw])
    kb.dma("sp", cb[:], cbd.rearrange("p (h i) -> p h i", h=2), [], [r_cb])

    hT = kb.sb("hT", [128, KC, TB], BF16); r_hT = Res()
    gT = kb.sb("gT", [128, NPAIR, TB], BF16); r_gT = [Res() for _ in range(NPAIR)]
    ft = kb.sb("ft", [128, TB // 128, D_MODEL], F32); r_ft = [Res() for _ in range(TB // 128)]
    wupp = Pool(kb, "wupp", [128, 2, KC, 128], BF16, 2)
    wdp = Pool(kb, "wdp", [128, 4, 512], BF16, 3)
    ub = Pool(kb, "ub", [128, TB + 2], F32, 4)
    tmp = Pool(kb, "tmp", [128, TB], F32, 5)
    carry = kb.sb("carry", [128, 2 * NPAIR, 2], F32); r_carry = [Res() for _ in range(2 * NPAIR)]
    hTh = kb.sb("hTh", [128, KC, 128], BF16); r_hTh = Res()
    sel4 = kb.sb("sel4", [128, 4], F32); r_sel = Res()
    kb.dma("sp", sel4[:], sel4d, [], [r_sel])
    xt, r_xt = xin.next()
    kb.v("dve", "memset", [], [r_xt], ap=xt[:], constant=0.0)
    for i in range(4):
        ct, r_ct = ft[:, i, :], r_ft[i]
        kb.v("pool", "memset", [], [r_ct], ap=ct, constant=0.0)
        kb.dma("sp", ft[126:128, i, :], halo_all[2 * i:2 * i + 2, :], [r_halo], [r_ct])
        kb.v("dve", "scalar_tensor_tensor", [r_ct, r_sel, r_xt], [r_xt], out=xt[:], in0=ct, scalar=sel4[:, i:i + 1],
             in1=xt[:], op0=ALU.mult, op1=ALU.add)
    emit_norm_T(kb, xt, r_xt, G2, r_G2, hTh, r_hTh, 0, KC)
    psc, r_psc = kb.bank()
    for i in range(NPAIR):
        wt, r_wt = wupp.next()
        kb.dma("sp", wt[:], wub[i].rearrange("p (h k c) -> p h k c", h=2, k=KC), [r_wub[i]], [r_wt])
        for hf in range(2):
            ch = hf * NPAIR + i
            for k in range(KC):
                kb.mm(psc[:, ch * 2:ch * 2 + 2], wt[:, hf, k, :], hTh[:, k, 126:128], k == 0, k == KC - 1,
                      [r_wt, r_hTh], [r_psc])
    kb.v("dve", "tensor_copy", [r_psc], r_carry, out=carry[:].rearrange("p c t -> p (c t)"), in_=psc[:, 0:4 * NPAIR])

    for b in range(NB):
        t0 = b * TB
        for tt in range(TB // 128):
            xt, r_xt = xin.next()
            kb.dma("sp", xt[:], xm[t0 + tt * 128:t0 + (tt + 1) * 128, :], [r_xm], [r_xt])
            emit_norm_T(kb, xt, r_xt, G2, r_G2, hT, r_hT, tt * 128, KC)
        for i in range(NPAIR):
            wt, r_wt = wupp.next()
            kb.dma("sp", wt[:], wub[i].rearrange("p (h k c) -> p h k c", h=2, k=KC), [r_wub[i]], [r_wt])
            cv = []
            for hf in range(2):
                ch = hf * NPAIR + i
                ps, r_ps = kb.bank()
                for k in range(KC):
                    kb.mm(ps[:, 0:TB], wt[:, hf, k, :], hT[:, k, :], k == 0, k == KC - 1, [r_wt, r_hT], [r_ps])
                u, r_u = ub.next()
                kb.act(u[:, 2:TB + 2], ps[:, 0:TB], AF.Copy, [r_ps], [r_u])
                kb.v("dve", "tensor_copy", [r_carry[ch]], [r_u], out=u[:, 0:2], in_=carry[:, ch, :])
                kb.v("dve", "tensor_copy", [r_u], [r_carry[ch]], out=carry[:, ch, :], in_=u[:, TB:TB + 2])
                t1, r_t1 = tmp.next()
                kb.act(t1[:], u[:, 2:TB + 2], AF.Identity, [r_u, r_cw, r_cb], [r_t1], scale=cw[:, hf, i, 2:3],
                       bias=cb[:, hf, i:i + 1])
                kb.v("dve", "scalar_tensor_tensor", [r_u, r_t1, r_cw], [r_t1], out=t1[:], in0=u[:, 1:TB + 1],
                     scalar=cw[:, hf, i, 1:2], in1=t1[:], op0=ALU.mult, op1=ALU.add)
                kb.v("dve", "scalar_tensor_tensor", [r_u, r_t1, r_cw], [r_t1], out=t1[:], in0=u[:, 0:TB],
                     scalar=cw[:, hf, i, 0:1], in1=t1[:], op0=ALU.mult, op1=ALU.add)
                cv.append((t1, r_t1))
            (ta, r_ta), (tb_, r_tb) = cv
            sa, r_sa = tmp.next()
            kb.act(sa[:], ta[:], AF.Silu, [r_ta], [r_sa])
            kb.v("dve", "tensor_tensor", [r_sa, r_tb], [r_gT[i]], out=gT[:, i, :], in0=sa[:], in1=tb_[:], op=ALU.mult)
        NTT = TB // 128
        for dq in range(4):
            banks = [kb.bank() for _ in range(NTT)]
            for g in range(NPAIR // 4):
                wd, r_wd = wdp.next()
                kb.dma("sp", wd[:], wdb[dq, :, g * 2048:(g + 1) * 2048].rearrange("p (f c) -> p f c", c=512),
                       [r_wdb[dq]], [r_wd])
                for j in range(4):
                    fc = g * 4 + j
                    for tt in range(NTT):
                        kb.mm(banks[tt][0][:, 0:512], gT[:, fc, tt * 128:(tt + 1) * 128], wd[:, j, :], fc == 0,
                              fc == NPAIR - 1, [r_gT[fc], r_wd], [banks[tt][1]])
            for tt in range(NTT):
                kb.act(ft[:, tt, dq * 512:(dq + 1) * 512], banks[tt][0][:, 0:512], AF.Copy, [banks[tt][1]], [r_ft[tt]])
        for tt in range(NTT):
            xt, r_xt = xin.next()
            rows = slice(t0 + tt * 128, t0 + (tt + 1) * 128)
            kb.dma("sp", xt[:], xm[rows, :], [r_xm], [r_xt])
            emit_post(kb, ft[:, tt, :], r_ft[tt], xt[:], r_xt, G3, r_G3, ft[:, tt, :], r_ft[tt])
            kb.dma("pool", xo[rows, :], ft[:, tt, :], [r_ft[tt]], [r_xo])
    kb.end()


def ffn_host_layouts(w_up, conv_w, conv_b, w_down, g2, g3):
    D, F = D_MODEL, D_FF
    w = w_up.reshape(KC, 128, 2, NPAIR, 128)
    wup = np.ascontiguousarray(w.transpose(3, 1, 2, 0, 4)).reshape(NPAIR, 128, 2 * KC * 128)
    w = w_down.reshape(NPAIR, 128, 4, 512)
    wdn = np.ascontiguousarray(w.transpose(2, 1, 0, 3)).reshape(4, 128, NPAIR * 512)
    c = conv_w.reshape(3, 2, NPAIR, 128)
    cw = np.ascontiguousarray(c.transpose(3, 1, 2, 0)).reshape(128, 2 * NPAIR * 3)
    c = conv_b.reshape(2, NPAIR, 128)
    cb = np.ascontiguousarray(c.transpose(2, 0, 1)).reshape(128, 2 * NPAIR)
    G2 = np.ascontiguousarray(np.broadcast_to(g2[None, :], (128, D)))
    G3 = np.ascontiguousarray(np.broadcast_to(g3[None, :], (128, D)))
    return dict(wup=wup, wdn=wdn, cw=cw, cb=cb, G2=G2, G3=G3, ident=np.eye(128, dtype=np.float32))


DH = 128
HPC = 4
QB = 512


def fox_inputs(kb):
    return dict(wqk=kb.din("wqk", [128, KC * 8 * 128]), wv=kb.din("wv", [128, KC * 512]), wf=kb.din("wf", [128, KC * HPC]),
                bf=kb.din("bf", [HPC, 1]), G0=kb.din("G0", [128, D_MODEL]), negmask=kb.din("negmask", [128, 128]))


def emit_fox(kb, S, io, xb, identd, att, r_att, prep=None):
    NB = S // QB
    NKB = S // 128
    wqkd, wvd, wfd, bfd, G0d, nmd = io["wqk"], io["wv"], io["wf"], io["bf"], io["G0"], io["negmask"]
    qTd = kb.dtmp("qTd", [HPC, 128, S], BF16)
    kTd = kb.dtmp("kTd", [HPC, 128, S], BF16)
    V1d = kb.dtmp("V1d", [HPC, S, DH + 1], BF16)
    Fsd = kb.dtmp("Fsd", [3, HPC, S], BF16)
    r_qTd = [Res() for _ in range(HPC)]
    r_kTd = [Res() for _ in range(HPC)]
    r_V1d = [Res() for _ in range(HPC)]
    r_Fsd = Res()

    kb.begin(identd)
    kb.junk = Pool(kb, "junk", [128, D_MODEL], F32, 1)
    kb.hn = Pool(kb, "hn", [128, D_MODEL], BF16, 2)
    A2 = kb.sb("A2", [128, 16384], BF16)
    xin = Pool(kb, "xin", None, None, 2, views=[A2[:, 8192:12288].bitcast(F32), A2[:, 12288:16384].bitcast(F32)])
    G0 = kb.sb("G0s", [128, D_MODEL], F32); r_G0 = Res()
    kb.dma("sp", G0[:], G0d, [], [r_G0])
    nmf = kb.sb("nmf", [128, 128], F32); r_nmf = Res()
    negmask = kb.sb("negmask_b", [128, 128], BF16); r_nm = Res()
    kb.dma("sp", nmf[:], nmd, [], [r_nmf])
    kb.v("dve", "tensor_copy", [r_nmf], [r_nm], out=negmask[:], in_=nmf[:])
    bft = kb.sb("bft", [HPC, 2], F32); r_bf = Res()
    kb.dma("sp", bft[:, 0:1], bfd, [], [r_bf])
    kb.v("dve", "tensor_scalar", [r_bf], [r_bf], out=bft[:, 1:2], in0=bft[:, 0:1], scalar1=-1.0, scalar2=None, op0=ALU.mult)
    ones4 = kb.sb("ones4", [HPC, QB], F32); r_ones = Res()
    kb.v("dve", "memset", [], [r_ones], ap=ones4[:], constant=1.0)

    wbig = kb.sb("wbig", [128, KC * 1024 + KC * 512], BF16); r_w = Res()
    wqk = wbig[:, 0:KC * 1024].rearrange("p (k c) -> p k c", k=KC)
    wv = wbig[:, KC * 1024:KC * 1536].rearrange("p (k c) -> p k c", k=KC)
    wf = kb.sb("wf_s", [128, KC, HPC], BF16)
    wqk_flat = wbig[:, 0:KC * 1024]
    for g in range(KC * 1024 // 2048):
        st_, r_st = xin.next()
        kb.dma("sp", st_[:], wqkd[:, g * 2048:(g + 1) * 2048], [], [r_st])
        emit_cast(kb, wqk_flat[:, g * 2048:(g + 1) * 2048], st_[:], [r_st], [r_w])
    wv_flat = wbig[:, KC * 1024:KC * 1536]
    for g in range(KC * 512 // 2048):
        st_, r_st = xin.next()
        kb.dma("sp", st_[:], wvd[:, g * 2048:(g + 1) * 2048], [], [r_st])
        emit_cast(kb, wv_flat[:, g * 2048:(g + 1) * 2048], st_[:], [r_st], [r_w])
    st_, r_st = xin.next()
    kb.dma("sp", st_[:, 0:KC * HPC], wfd, [], [r_st])
    kb.v("dve", "tensor_copy", [r_st], [r_w], out=wf[:].rearrange("p k c -> p (k c)"), in_=st_[:, 0:KC * HPC])

    hT = A2[:, 0:8192].rearrange("p (k c) -> p k c", k=KC); r_hT = Res()
    stq = Pool(kb, "stq", [128, QB], BF16, 4)
    vst = Pool(kb, "vst", [128, HPC, DH + 1], BF16, 3)
    for t_, r_ in zip(vst.tiles, vst.res):
        kb.v("dve", "memset", [], [r_], ap=t_[:], constant=1.0)
    fe = Pool(kb, "fe", [HPC, QB], F32, 1)
    Fb = Pool(kb, "Fb", [HPC, QB], F32, 2)
    fr = Pool(kb, "fr", [HPC, QB], F32, 3)
    fsb = Pool(kb, "fsb", [HPC, QB], BF16, 3)
    scale = float(DH) ** -0.5
    FK = kb.sb("FK", [128, NKB, HPC], F32); r_FK = Res()

    prevF = None
    for b in range(NB):
        t0 = b * QB
        for tt in range(QB // 128):
            xt, r_xt = xin.next()
            kb.dma("sp", xt[:], xb[t0 + tt * 128:t0 + (tt + 1) * 128, :], [], [r_xt])
            emit_norm_T(kb, xt, r_xt, G0, r_G0, hT, r_hT, tt * 128, KC)
        for j in range(8):
            ps, r_ps = kb.bank()
            for k in range(KC):
                kb.mm(ps[:, 0:QB], wqk[:, k, j * 128:(j + 1) * 128], hT[:, k, :], k == 0, k == KC - 1, [r_w, r_hT], [r_ps])
            s_, r_s = stq.next()
            h = j % HPC
            if j < HPC:
                kb.act(s_[:], ps[:, 0:QB], AF.Copy, [r_ps], [r_s], scale=scale)
                kb.dma("pool", qTd[h, :, t0:t0 + QB], s_[:], [r_s], [r_qTd[h]])
            else:
                kb.v("dve", "tensor_copy", [r_ps], [r_s], out=s_[:], in_=ps[:, 0:QB])
                kb.dma("pool", kTd[h, :, t0:t0 + QB], s_[:], [r_s], [r_kTd[h]])
        for tt in range(QB // 128):
            ps, r_ps = kb.bank()
            for k in range(KC):
                kb.mm(ps[:, 0:512], hT[:, k, tt * 128:(tt + 1) * 128], wv[:, k, :], k == 0, k == KC - 1, [r_w, r_hT], [r_ps])
            v_, r_v = vst.next()
            src = ps[:, 0:512].rearrange("p (h c) -> p h c", h=HPC)
            if tt % 2 == 0:
                kb.act(v_[:, :, 0:DH], src, AF.Copy, [r_ps], [r_v])
            else:
                kb.v("dve", "tensor_copy", [r_ps], [r_v], out=v_[:, :, 0:DH], in_=src)
            rows = slice(t0 + tt * 128, t0 + (tt + 1) * 128)
            kb.dma("pool", V1d.rearrange("h t c -> t h c")[rows, :, :], v_[:], [r_v], r_V1d)
        ps, r_ps = kb.bank()
        for k in range(KC):
            kb.mm(ps[0:HPC, 0:QB], wf[:, k, :], hT[:, k, :], k == 0, k == KC - 1, [r_w, r_hT], [r_ps])
        e_, r_e = fe.next()
        kb.act(e_[:], ps[0:HPC, 0:QB], AF.Exp, [r_ps, r_bf], [r_e], scale=-1.0, bias=bft[:, 1:2])
        kb.act(e_[:], e_[:], AF.Ln, [r_e], [r_e], bias=1.0)
        F_, r_F = Fb.next()
        init = 0.0 if prevF is None else prevF[0][:, QB - 1:QB]
        rd = [r_e, r_ones] + ([] if prevF is None else [prevF[1]])
        kb.v("dve", "tensor_tensor_scan", rd, [r_F], out=F_[:], data0=ones4[:], data1=e_[:], initial=init,
             op0=ALU.mult, op1=ALU.subtract)
        prevF = (F_, r_F)
        ps, r_ps = kb.bank()
        for tt in range(QB // 128):
            kb.tr(ps[:, tt * HPC:(tt + 1) * HPC], F_[:, tt * 128:(tt + 1) * 128], kb.identf[0:HPC, 0:HPC], [r_F, kb.r_identf], [r_ps])
        kb.act(FK[:, b * 4:(b + 1) * 4, :], ps[:, 0:4 * HPC].rearrange("p (t h) -> p t h", h=HPC), AF.Copy, [r_ps], [r_FK], scale=-1.0)
        cur, r_cur = F_, r_F
        for i in range(3):
            fb_, r_fb = fsb.next()
            kb.v("dve", "tensor_copy", [r_cur], [r_fb], out=fb_[:], in_=cur[:])
            kb.dma("pool", Fsd[i, :, t0:t0 + QB], fb_[:], [r_fb], [r_Fsd])
            if i < 2:
                ff, r_ff = fr.next()
                kb.v("dve", "tensor_copy", [r_fb], [r_ff], out=ff[:], in_=fb_[:])
                nr, r_nr = fr.next()
                kb.v("dve", "tensor_tensor", [r_cur, r_ff], [r_nr], out=nr[:], in0=cur[:], in1=ff[:], op=ALU.subtract)
                cur, r_cur = nr, r_nr

    assert S <= 16384
    kT = A2[:, 0:S]; r_kT = Res()
    kt_first = [r_hT, xin.res[0], xin.res[1]]
    V1 = kb.sb("V1", [128, NKB, DH + 1], BF16); r_V1 = Res()
    KF = wbig[0:6, 0:S]; r_KF = Res()
    qTb = Pool(kb, "qTb", [128, QB], BF16, 2)
    QFb = Pool(kb, "QFb", [6, QB], BF16, 2)
    for t_, r_ in zip(QFb.tiles, QFb.res):
        kb.v("dve", "memset", [], [r_], ap=t_[:], constant=-1.0)
    PT = Pool(kb, "PT", [128, QB], BF16, 6)
    ost = Pool(kb, "ost", [128, 4, DH], BF16, 2)
    frow = Pool(kb, "frow", [128, QB], F32, 2)
    stmp = Pool(kb, "stmp", [128, QB], F32, 4)
    ones3 = kb.sb("ones3", [3, 128], BF16); r_o3 = Res()
    kb.v("dve", "memset", [], [r_o3], ap=ones3[:], constant=1.0)
    if prep is not None:
        pstf = Pool(kb, "pstf", None, None, 2, views=[kb.junk.tiles[0], kb.sb("pstf1", [128, D_MODEL], F32)])
        pstf.res[0] = kb.junk.res[0]
        pstb = kb.hn
        per_q = -(-len(prep.steps) // max(1, (HPC * NB * 3) // 4))
    first = True
    for h in range(HPC):
        nsp = 4
        for i in range(nsp):
            cs = slice(i * S // nsp, (i + 1) * S // nsp)
            kb.dma("sp", kT[:, cs], kTd[h, :, cs], [r_kTd[h]], [r_kT] + kt_first)
            kt_first = []
        nvp = max(1, NKB // 8)
        for i in range(nvp):
            ks = slice(i * NKB // nvp, (i + 1) * NKB // nvp)
            kb.dma("sp", V1[:, ks, :], V1d[h].rearrange("(n p) c -> p n c", p=128)[:, ks, :], r_V1d, [r_V1])
        first = False
        for Q in range(NB):
            q_, r_q = qTb.next()
            kb.dma("sp", q_[:], qTd[h, :, Q * QB:(Q + 1) * QB], [r_qTd[h]], [r_q])
            qf, r_qf = QFb.next()
            kb.dma("sp", qf[0:3, :], Fsd[:, h, Q * QB:(Q + 1) * QB], [r_Fsd], [r_qf])
            if prep is not None:
                prep.run(kb, per_q, pstf, pstb)
            psF, r_psF = kb.bank(0, 4)
            kb.mm(psF[:, 0:QB], ones3[:], qf[0:3, :], True, True, [r_o3, r_qf], [r_psF])
            fr_, r_fr = frow.next()
            kb.v("dve", "tensor_copy", [r_psF], [r_fr], out=fr_[:], in_=psF[:, 0:QB])
            acc = [(kb.ps[4 + j], kb.psr[4 + j]) for j in range(4)]
            nkb = 4 * Q + 4
            LAG = 3
            pend = []
            for kbi in range(nkb + LAG):
                if kbi < nkb:
                    j0 = max(0, kbi - 4 * Q)
                    c0 = j0 * 128
                    ps, r_ps = kb.bank(0, 4)
                    diag = kbi >= 4 * Q
                    kb.mm(ps[:, c0:QB], kT[:, kbi * 128:(kbi + 1) * 128], q_[:, c0:QB], True, not diag, [r_kT, r_q], [r_ps])
                    if diag:
                        kb.mm(ps[:, c0:c0 + 128], kb.identb[:], negmask[:], False, True, [kb.r_ident, r_nm], [r_ps])
                    t_, r_t = stmp.next()
                    kb.v("dve", "tensor_tensor", [r_ps, r_fr], [r_t], out=t_[:, c0:QB], in0=ps[:, c0:QB], in1=fr_[:, c0:QB], op=ALU.add)
                    p_, r_p = PT.next()
                    kb.act(p_[:, c0:QB], t_[:, c0:QB], AF.Exp, [r_t, r_FK], [r_p], bias=FK[:, kbi, h:h + 1])
                    pend.append((kbi, j0, p_, r_p))
                if kbi >= LAG or kbi >= nkb:
                    if pend and (len(pend) > LAG or kbi >= nkb):
                        pk, pj0, pp, r_pp = pend.pop(0)
                        for j in range(pj0, 4):
                            kb.mm(acc[j][0][:, 0:DH + 1], pp[:, j * 128:(j + 1) * 128], V1[:, pk, :], pk == 0, pk == 4 * Q + j,
                                  [r_pp, r_V1], [acc[j][1]])
            assert not pend
            o_, r_o = ost.next()
            for j in range(4):
                sm, r_sm = kb.small.next()
                kb.v("dve", "reciprocal", [acc[j][1]], [r_sm], out=sm[:, 0:1], in_=acc[j][0][:, DH:DH + 1])
                kb.act(o_[:, j, :], acc[j][0][:, 0:DH], AF.Copy, [acc[j][1], r_sm], [r_o], scale=sm[:, 0:1])
            kb.dma("pool", att.rearrange("(n p) c -> p n c", p=128)[:, Q * 4:Q * 4 + 4, h * DH:(h + 1) * DH], o_[:], [r_o], [r_att])
    if prep is not None:
        prep.run(kb, len(prep.steps), pstf, pstb)
    kb.end()


def fox_host_layouts(w_in, b_f, g0, m):
    D = D_MODEL
    cols_q = w_in[:, (HPC * m) * DH:(HPC * m + HPC) * DH]
    cols_k = w_in[:, D + (HPC * m) * DH:D + (HPC * m + HPC) * DH]
    wqk = np.concatenate([cols_q, cols_k], axis=1).reshape(KC, 128, 8 * 128)
    wqk = np.ascontiguousarray(wqk.transpose(1, 0, 2)).reshape(128, KC * 1024)
    wv = w_in[:, 2 * D + HPC * m * DH:2 * D + (HPC * m + HPC) * DH].reshape(KC, 128, 512)
    wv = np.ascontiguousarray(wv.transpose(1, 0, 2)).reshape(128, KC * 512)
    wf = w_in[:, 3 * D + HPC * m:3 * D + HPC * m + HPC].reshape(KC, 128, HPC)
    wf = np.ascontiguousarray(wf.transpose(1, 0, 2)).reshape(128, KC * HPC)
    bf = np.ascontiguousarray(b_f[HPC * m:HPC * m + HPC].reshape(HPC, 1))
    G0 = np.ascontiguousarray(np.broadcast_to(g0[None, :], (128, D)))
    idx = np.arange(128)
    negmask = np.where(idx[:, None] <= idx[None, :], 0.0, -30000.0).astype(np.float32)
    return dict(wqk=wqk, wv=wv, wf=wf, bf=bf, G0=G0, ident=np.eye(128, dtype=np.float32), negmask=negmask)


def wo_inputs(kb, pfx, KCI):
    return dict(w=kb.din(pfx + "w", [128, KCI * D_MODEL]), G=kb.din(pfx + "G", [128, D_MODEL]))


def emit_wo(kb, T, KCI, io, identd, src_fn, r_src, sel4d, xr, r_xr, xo, r_xo):
    TBW = 512 if KCI <= 16 else 128
    NB = T // TBW
    wd, Gd = io["w"], io["G"]
    kb.begin(identd)
    kb.junk = Pool(kb, "junk", [128, D_MODEL], F32, 1)
    xin = Pool(kb, "xin", [128, D_MODEL], F32, 2)
    G = kb.sb("Gs", [128, D_MODEL], F32); r_G = Res()
    kb.dma("sp", G[:], Gd, [], [r_G])
    sel4 = kb.sb("sel4", [128, 4], F32); r_sel = Res()
    kb.dma("sp", sel4[:], sel4d, [], [r_sel])
    w = kb.sb("wres", [128, KCI, D_MODEL], BF16); r_w = Res()
    for k in range(KCI):
        st_, r_st = xin.next()
        kb.dma("sp", st_[:], wd[:, k * D_MODEL:(k + 1) * D_MODEL], [], [r_st])
        emit_cast(kb, w[:, k, :], st_[:], [r_st], [r_w])
    aTbp = Pool(kb, "aTb", [128, KCI, TBW], BF16, 2 if KCI <= 16 else 1)
    cand = Pool(kb, "cand", [128, KCI * 128], BF16, 4 if KCI <= 16 else 2)
    hn = Pool(kb, "hnw", [128, KCI * 128], BF16, 2) if KCI <= 16 else None
    ft = Pool(kb, "ft", [128, D_MODEL], F32, 2)
    for b in range(NB):
        t0 = b * TBW
        aTb, r_a = aTbp.next()
        for tt in range(TBW // 128):
            rows = slice(t0 + tt * 128, t0 + (tt + 1) * 128)
            srcs = src_fn(rows)
            if len(srcs) > 1:
                h_, r_h = hn.next()
            for i, (sap, view) in enumerate(srcs):
                c_, r_c = cand.next()
                kb.dma("sp", view(c_), sap, [r_src], [r_c])
                if len(srcs) == 1:
                    h_, r_h = c_, r_c
                elif i == 0:
                    kb.v("dve", "tensor_scalar", [r_c, r_sel], [r_h], out=h_[:], in0=c_[:], scalar1=sel4[:, 0:1], scalar2=None,
                         op0=ALU.mult)
                else:
                    kb.v("dve", "scalar_tensor_tensor", [r_c, r_sel, r_h], [r_h], out=h_[:], in0=c_[:], scalar=sel4[:, i:i + 1],
                         in1=h_[:], op0=ALU.mult, op1=ALU.add)
            emit_T(kb, h_, r_h, aTb, r_a, tt * 128, KCI)
        for tt in range(TBW // 128):
            f_, r_f = ft.next()
            for nq in range(4):
                ps, r_ps = kb.bank()
                for k in range(KCI):
                    kb.mm(ps[:, 0:512], aTb[:, k, tt * 128:(tt + 1) * 128], w[:, k, nq * 512:(nq + 1) * 512], k == 0, k == KCI - 1,
                          [r_a, r_w], [r_ps])
                if nq % 2 == 0:
                    kb.act(f_[:, nq * 512:(nq + 1) * 512], ps[:, 0:512], AF.Copy, [r_ps], [r_f])
                else:
                    kb.v("dve", "tensor_copy", [r_ps], [r_f], out=f_[:, nq * 512:(nq + 1) * 512], in_=ps[:, 0:512])
            xt, r_xt = xin.next()
            rows = slice(t0 + tt * 128, t0 + (tt + 1) * 128)
            kb.dma("sp", xt[:], xr[rows, :], [r_xr], [r_xt])
            emit_post(kb, f_[:], r_f, xt[:], r_xt, G, r_G, f_[:], r_f)
            kb.dma("pool", xo[rows, :], f_[:], [r_f], [r_xo])
    kb.end()


def wo_host_layout(w, g):
    KCI = w.shape[0] // 128
    wl = np.ascontiguousarray(w.reshape(KCI, 128, D_MODEL).transpose(1, 0, 2)).reshape(128, KCI * D_MODEL)
    G = np.ascontiguousarray(np.broadcast_to(g[None, :], (128, D_MODEL)))
    return dict(w=wl, G=G, ident=np.eye(128, dtype=np.float32))


RH = 8
DK_ = 256
DV_ = 512
RB = 1024
NS = 24


def ret_gammas():
    return [1.0 - 2.0 ** (-5.0 - h) for h in range(RH)]


def ret_inputs(kb, T):
    return dict(win=kb.din("r_win", [NS, 128, KC * 512]), G=kb.din("r_G", [128, D_MODEL]), cos2=kb.din("cos2", [T, 256]),
                sin2=kb.din("sin2", [T, 256]), DK=kb.din("DK", [128, D_MODEL]), Mp=kb.din("Mp", [128, RH * 128]),
                DQ=kb.din("DQ", [128, RH]), coef=kb.din("coef", [128, 3 * RH]))


def emit_ret(kb, T, full, io, pw, identd, xr, r_xr, Lout=None, r_L=None, Lprev=None, og=None, r_og_d=None):
    NBLK = T // RB
    NCH = T // 128
    wind, Gd, cosd, sind, DKd = io["win"], io["G"], io["cos2"], io["sin2"], io["DK"]
    Mpd, DQd, coefd = io["Mp"], io["DQ"], io["coef"]
    sfx = "f" if full else "s"
    wib, r_wib = pw["wib"], pw["r_wib"]
    qd = kb.dtmp("qd" + sfx, [T, D_MODEL], BF16)
    kd = kb.dtmp("kd" + sfx, [T, D_MODEL], BF16)
    vd = kb.dtmp("vd" + sfx, [T, 2 * D_MODEL], BF16)
    sgd = kb.dtmp("sgd" + sfx, [T, 2 * D_MODEL], BF16)
    r_qd, r_kd, r_vd, r_sgd = Res(), Res(), Res(), Res()
    slices = list(range(NS)) if full else list(range(4, 16))

    kb.begin(identd)
    kb.junk = Pool(kb, "junk", [128, D_MODEL], F32, 1)
    kb.hn = Pool(kb, "hn", [128, D_MODEL], BF16, 2)
    xin = Pool(kb, "xin", [128, D_MODEL], F32, 2)
    G = kb.sb("Gs", [128, D_MODEL], F32); r_G = Res()
    kb.dma("sp", G[:], Gd, [], [r_G])
    DK = kb.sb("DKs", [128, D_MODEL], F32); r_DK = Res()
    kb.dma("sp", DK[:], DKd, [], [r_DK])
    A1 = kb.sb("A1", [128, KC * RB], BF16); r_A1 = Res()
    hT = A1[:].rearrange("p (k c) -> p k c", k=KC)
    wpool = Pool(kb, "wsl", [128, KC * 512], BF16, 2)
    cs = kb.sb("cs", [128, 2, RB // 128, 256], F32); r_cs = Res()
    rt = Pool(kb, "rt", [128, 256], F32, 6)
    so = Pool(kb, "so", [128, 512], BF16, 4)

    for b in range(NBLK):
        t0 = b * RB
        for tt in range(RB // 128):
            xt, r_xt = xin.next()
            kb.dma("sp", xt[:], xr[t0 + tt * 128:t0 + (tt + 1) * 128, :], [r_xr], [r_xt])
            emit_norm_T(kb, xt, r_xt, G, r_G, hT, r_A1, tt * 128, KC)
        kb.dma("sp", cs[:, 0, :, :], cosd[t0:t0 + RB, :].rearrange("(n p) c -> p n c", p=128), [], [r_cs])
        kb.dma("sp", cs[:, 1, :, :], sind[t0:t0 + RB, :].rearrange("(n p) c -> p n c", p=128), [], [r_cs])
        for ns in slices:
            wt, r_wt = wpool.next()
            wv_ = wt[:].rearrange("p (k c) -> p k c", k=KC)
            kb.dma("sp", wt[:], wib[ns], [r_wib[ns]], [r_wt])
            for tt in range(RB // 128):
                ps, r_ps = kb.bank()
                for k in range(KC):
                    kb.mm(ps[:, 0:512], hT[:, k, tt * 128:(tt + 1) * 128], wv_[:, k, :], k == 0, k == KC - 1, [r_A1, r_wt], [r_ps])
                o_, r_o = so.next()
                rows = slice(t0 + tt * 128, t0 + (tt + 1) * 128)
                if ns < 8:
                    psv = ps[:, 0:512].rearrange("p (i t) -> p i t", t=2)
                    ov = o_[:].rearrange("p (i t) -> p i t", t=2)
                    c_ = cs[:, 0, tt, :]
                    s_ = cs[:, 1, tt, :]
                    t1, r_t1 = rt.next(); t2, r_t2 = rt.next()
                    kb.v("dve", "tensor_tensor", [r_ps, r_cs], [r_t1], out=t1[:], in0=psv[:, :, 0], in1=c_, op=ALU.mult)
                    kb.v("dve", "tensor_tensor", [r_ps, r_cs], [r_t2], out=t2[:], in0=psv[:, :, 1], in1=s_, op=ALU.mult)
                    kb.v("pool", "tensor_tensor", [r_t1, r_t2], [r_o], out=ov[:, :, 0], in0=t1[:], in1=t2[:], op=ALU.subtract)
                    t3, r_t3 = rt.next(); t4, r_t4 = rt.next()
                    kb.v("dve", "tensor_tensor", [r_ps, r_cs], [r_t3], out=t3[:], in0=psv[:, :, 0], in1=s_, op=ALU.mult)
                    kb.v("dve", "tensor_tensor", [r_ps, r_cs], [r_t4], out=t4[:], in0=psv[:, :, 1], in1=c_, op=ALU.mult)
                    kb.v("pool", "tensor_tensor", [r_t3, r_t4], [r_o], out=ov[:, :, 1], in0=t3[:], in1=t4[:], op=ALU.add)
                    if ns < 4:
                        kb.dma("pool", qd[rows, ns * 512:(ns + 1) * 512], o_[:], [r_o], [r_qd])
                    else:
                        kb.dma("pool", kd[rows, (ns - 4) * 512:(ns - 3) * 512], o_[:], [r_o], [r_kd])
                elif ns < 16:
                    kb.act(o_[:], ps[:, 0:512], AF.Copy, [r_ps], [r_o])
                    kb.dma("pool", vd[rows, (ns - 8) * 512:(ns - 7) * 512], o_[:], [r_o], [r_vd])
                else:
                    kb.act(o_[:], ps[:, 0:512], AF.Silu, [r_ps], [r_o])
                    kb.dma("pool", sgd[rows, (ns - 16) * 512:(ns - 15) * 512], o_[:], [r_o], [r_sgd])

    gam = ret_gammas()
    Sv = A1[:].bitcast(F32).rearrange("p (h d v) -> p h d v", h=RH, d=2)
    Sbf = wpool.tiles[0][:].rearrange("p (h d v) -> p h d v", h=RH, d=2)
    qkT = wpool.tiles[1][:].rearrange("p (b s j c) -> p b s j c", b=2, s=2, j=16)
    r_S = [Res() for _ in range(RH)]
    r_Sbf = [Res() for _ in range(RH)]
    r_qkT = [Res(), Res()]
    qk_tiles = [t[:].bitcast(BF16) for t in xin.tiles]
    r_qk = xin.res
    kdec = Pool(kb, "kdec", [128, D_MODEL], BF16, 2)
    vh = Pool(kb, "vh", [128, DV_], BF16, 4)
    barrier_w = [r_A1, wpool.res[0], wpool.res[1], kb.junk.res[0]] + r_S + r_Sbf + r_qkT
    kb.v("dve", "memset", [], barrier_w, ap=A1[:].bitcast(F32), constant=0.0)
    if full:
        Mp = kb.sb("Mps", [128, RH, 128], F32); r_Mp = Res()
        kb.dma("sp", Mp[:], Mpd.rearrange("p (h n) -> p h n", h=RH), [], [r_Mp])
        DQ = kb.sb("DQs", [128, RH], F32); r_DQ = Res()
        kb.dma("sp", DQ[:], DQd, [], [r_DQ])
        coef = kb.sb("coefs", [128, 3 * RH], F32); r_coef = Res()
        kb.dma("sp", coef[:], coefd, [], [r_coef])
        sgh = Pool(kb, "sgh", [128, DV_], BF16, 6)
        oh = Pool(kb, "oh", [128, DV_], F32, 6)
        junk4 = [kb.junk.tiles[0][:, i * DV_:(i + 1) * DV_] for i in range(4)]
        r_junk4 = [Res() for _ in range(4)]
        ogp = Pool(kb, "ogp", [128, DV_], BF16, 3)
        sT = Pool(kb, "sT", [128, 128], BF16, 2 * RH)
        for i in range(3):
            for h in range(RH):
                for d in range(2):
                    l_, r_l = oh.next()
                    kb.dma("sp", l_[:], Lprev[i, h, d], [r_L], [r_l])
                    kb.v("dve", "scalar_tensor_tensor", [r_l, r_coef, r_S[h]], [r_S[h]], out=Sv[:, h, d, :], in0=l_[:],
                         scalar=coef[:, i * RH + h:i * RH + h + 1], in1=Sv[:, h, d, :], op0=ALU.mult, op1=ALU.add)
        for h in range(RH):
            kb.act(Sbf[:, h, :, :], Sv[:, h, :, :], AF.Copy, [r_S[h]], [r_Sbf[h]])
    for c in range(NCH):
        rows = slice(c * 128, (c + 1) * 128)
        bi = c % 2
        qk = qk_tiles[bi]
        r_q = r_qk[bi]
        if full:
            kb.dma("sp", qk[:, 0:D_MODEL], qd[rows, :], [r_qd], [r_q])
        kb.dma("sp", qk[:, D_MODEL:2 * D_MODEL], kd[rows, :], [r_kd], [r_q])
        kd_, r_kdec = kdec.next()
        kb.v("dve", "tensor_tensor", [r_q, r_DK], [r_kdec], out=kd_[:], in0=qk[:, D_MODEL:2 * D_MODEL], in1=DK[:], op=ALU.mult)
        if full:
            for s in range(2):
                for half in range(2):
                    ps, r_ps = kb.bank()
                    psb = ps[:].bitcast(BF16)
                    for j in range(8):
                        col = s * D_MODEL + (half * 8 + j) * 128
                        kb.tr(psb[:, j * 128:(j + 1) * 128], qk[:, col:col + 128], kb.identb[:], [r_q, kb.r_ident], [r_ps])
                    src = psb[:, 0:1024].rearrange("p (j c) -> p j c", c=128)
                    dst = qkT[:, bi, s, half * 8:half * 8 + 8, :]
                    if half == 0:
                        kb.act(dst, src, AF.Copy, [r_ps], [r_qkT[bi]])
                    else:
                        kb.v("dve", "tensor_copy", [r_ps], [r_qkT[bi]], out=dst, in_=src)
        sts = []
        if full:
            for h in range(RH):
                ps, r_ps = kb.bank()
                for d in range(2):
                    kb.mm(ps[:, 0:128], qkT[:, bi, 1, h * 2 + d, :], qkT[:, bi, 0, h * 2 + d, :], d == 0, d == 1, [r_qkT[bi]], [r_ps])
                st_, r_st = sT.next()
                kb.v("dve", "tensor_tensor", [r_ps, r_Mp], [r_st], out=st_[:], in0=ps[:, 0:128], in1=Mp[:, h, :], op=ALU.mult)
                sts.append((st_, r_st))
        pending = []
        for h in range(RH):
            v_, r_v = vh.next()
            kb.dma("sp", v_[:], vd[rows, h * DV_:(h + 1) * DV_], [r_vd], [r_v])
            if full:
                g_, r_g = sgh.next()
                kb.dma("sp", g_[:], sgd[rows, h * DV_:(h + 1) * DV_], [r_sgd], [r_g])
                st_, r_st = sts[h]
                po, r_po = kb.bank()
                kb.mm(po[:, 0:DV_], st_[:], v_[:], True, False, [r_st, r_v], [r_po])
                for d in range(2):
                    kb.mm(po[:, 0:DV_], qkT[:, bi, 0, h * 2 + d, :], Sbf[:, h, d, :], False, d == 1, [r_qkT[bi], r_Sbf[h]], [r_po])
                o_, r_o = oh.next()
                kb.act(o_[:], po[:, 0:DV_], AF.Copy, [r_po, r_DQ], [r_o], scale=DQ[:, h:h + 1])
                jk, r_jk = junk4[h % 4], r_junk4[h % 4]
                kb.act(jk, o_[:], AF.Square, [r_o], [r_jk])
                pending.append((h, o_, r_o, g_, r_g, jk, r_jk))
            for d in range(2):
                pS, r_pS = kb.bank()
                kb.mm(pS[:, 0:DV_], kd_[:, h * DK_ + d * 128:h * DK_ + (d + 1) * 128], v_[:], True, True, [r_kdec, r_v], [r_pS])
                kb.v("dve", "scalar_tensor_tensor", [r_pS, r_S[h]], [r_S[h]], out=Sv[:, h, d, :], in0=Sv[:, h, d, :],
                     scalar=float(gam[h] ** 128), in1=pS[:, 0:DV_], op0=ALU.mult, op1=ALU.add)
            if full:
                kb.act(Sbf[:, h, :, :], Sv[:, h, :, :], AF.Copy, [r_S[h]], [r_Sbf[h]])
            if full and len(pending) == 4:
                sms = []
                for (hh, o_, r_o, g_, r_g, jk, r_jk) in pending:
                    sm, r_sm = kb.small.next()
                    kb.v("dve", "reduce_sum", [r_jk], [r_sm], out=sm[:, 0:1], in_=jk, axis=AX.X)
                    kb.v("dve", "tensor_scalar", [r_sm], [r_sm], out=sm[:, 1:2], in0=sm[:, 0:1], scalar1=1.0 / DV_,
                         scalar2=NORM_EPS, op0=ALU.mult, op1=ALU.add)
                    sms.append((sm, r_sm))
                for (sm, r_sm) in sms:
                    kb.act(sm[:, 2:3], sm[:, 1:2], AF.Sqrt, [r_sm], [r_sm])
                for (hh, o_, r_o, g_, r_g, jk, r_jk), (sm, r_sm) in zip(pending, sms):
                    kb.v("dve", "reciprocal", [r_sm], [r_sm], out=sm[:, 3:4], in_=sm[:, 2:3])
                    og_, r_og = ogp.next()
                    kb.v("dve", "scalar_tensor_tensor", [r_o, r_sm, r_g], [r_og], out=og_[:], in0=o_[:], scalar=sm[:, 3:4], in1=g_[:],
                         op0=ALU.mult, op1=ALU.mult)
                    kb.dma("pool", og[rows, hh * DV_:(hh + 1) * DV_], og_[:], [r_og], [r_og_d])
                pending = []
    if not full:
        for h in range(RH):
            for d in range(2):
                kb.dma("pool", Lout[h, d], Sv[:, h, d, :], [r_S[h]], [r_L])
    kb.end()


def ret_host_layouts(w_in, g0):
    w = w_in.reshape(KC, 128, NS, 512)
    win = np.ascontiguousarray(w.transpose(2, 1, 0, 3)).reshape(NS, 128, KC * 512)
    G = np.ascontiguousarray(np.broadcast_to(g0[None, :], (128, D_MODEL)))
    return dict(win=win, G=G, ident=np.eye(128, dtype=np.float32))


def ret_const_tables(pos0, T, j):
    theta = (1.0 / (10000.0 ** np.linspace(0.0, 1.0, DK_ // 2, dtype=np.float32))).astype(np.float32)
    ang = (np.arange(pos0, pos0 + T, dtype=np.float32)[:, None] * theta[None, :]).astype(np.float32)
    cos = np.cos(ang).astype(np.float32)
    sin = np.sin(ang).astype(np.float32)
    cos2 = np.ascontiguousarray(np.concatenate([cos, cos], axis=1))
    sin2 = np.ascontiguousarray(np.concatenate([sin, sin], axis=1))
    gam = np.array(ret_gammas(), dtype=np.float64)
    lg = np.log1p(-(2.0 ** (-5.0 - np.arange(RH, dtype=np.float64))))
    m = np.arange(128, dtype=np.float64)
    ks = DK_ ** -0.5
    DK = np.exp((127.0 - m)[:, None] * lg[None, :]) * ks
    DK = np.ascontiguousarray(np.repeat(DK, DK_, axis=1)).astype(np.float32)
    DQ = np.exp((m + 1.0)[:, None] * lg[None, :]).astype(np.float32)
    Mp = np.exp(-(m + 1.0)[:, None, None] * lg[None, :, None]) * ks
    Mp = Mp * (m[None, None, :] >= m[:, None, None])
    Mp = np.ascontiguousarray(Mp.reshape(128, RH * 128)).astype(np.float32)
    coef = np.zeros((3, RH), np.float64)
    for i in range(3):
        if i < j:
            coef[i] = np.exp(T * (j - 1 - i) * lg)
    coef = np.ascontiguousarray(np.broadcast_to(coef.reshape(1, 3 * RH), (128, 3 * RH))).astype(np.float32)
    return dict(cos2=cos2, sin2=sin2, DK=DK, DQ=DQ, Mp=Mp, coef=coef)


GROUPS = [[0, 1, 2, 3], [4, 5, 6, 7]]
TPC = SEQ * BATCH // NCORES


def build_fused():
    kb = KB()
    T, S, D = TPC, SEQ, D_MODEL
    ident = kb.din("ident", [128, 128])
    xb = kb.din("xb", [S, D])
    xs = kb.din("xs", [T, D])
    sel_prev = kb.din("sel_prev", [128, 4])
    sel_own = kb.din("sel_own", [128, 4])
    fio = fox_inputs(kb)
    w0 = wo_inputs(kb, "wo0_", 16)
    f0 = ffn_inputs(kb, "f0_")
    rio = ret_inputs(kb, T)
    w1 = wo_inputs(kb, "wo1_", 32)
    f1 = ffn_inputs(kb, "f1_")
    out = kb.dout("out", [T, D])

    prep = Prep()
    pf0 = prep_ffn(kb, prep, "f0_", f0)
    pr = prep_ret(kb, prep, rio)
    pf1 = prep_ffn(kb, prep, "f1_", f1)
    att = kb.dtmp("att", [S, HPC * DH], BF16); r_att = Res()
    emit_fox(kb, S, fio, xb, ident, att, r_att, prep=prep)
    CR = 1024
    r_attg = Res()
    attg = []
    for i in range(S // CR):
        g_ = kb.dtmp("attg%d" % i, [4 * CR, HPC * DH], BF16)
        kb.collective("AllGather", GROUPS, att[i * CR:(i + 1) * CR, :], g_, [r_att], [r_attg], flush=(i == S // CR - 1))
        attg.append(g_.rearrange("(m t) c -> t m c", m=4))

    def src0(rows):
        res = []
        for jj in range(4):
            r0 = jj * T + rows.start
            res.append((attg[r0 // CR][r0 % CR:r0 % CR + 128], lambda c_: c_[:].rearrange("p (m c) -> p m c", m=4)))
        return res

    xm0 = kb.dtmp("xm0", [T, D]); r_xm0 = Res()
    emit_wo(kb, T, 16, w0, ident, src0, r_attg, sel_own, xs, Res(), xm0, r_xm0)
    halo0 = kb.dtmp("halo0", [8, D]); r_h0 = Res()
    kb.collective("AllGather", GROUPS, xm0[T - 2:T, :], halo0, [r_xm0], [r_h0])
    x1 = kb.dtmp("x1", [T, D]); r_x1 = Res()
    emit_ffn(kb, T, "f0_", f0, pf0, xm0, r_xm0, halo0, r_h0, sel_prev, ident, x1, r_x1)

    L = kb.dtmp("Lst", [RH, 2, 128, DV_]); r_L = Res()
    emit_ret(kb, T, False, rio, pr, ident, x1, r_x1, Lout=L, r_L=r_L)
    r_La = Res()
    Lall = []
    Lf = L.rearrange("h d p v -> (h d p) v")
    for i in range(RH // 2):
        g_ = kb.dtmp("Lall%d" % i, [4 * 512, DV_])
        kb.collective("AllGather", GROUPS, Lf[i * 512:(i + 1) * 512, :], g_, [r_L], [r_La], flush=(i == RH // 2 - 1))
        Lall.append(g_.rearrange("(i h d p) v -> i h d p v", i=4, h=2, d=2))

    class _LP:
        def __getitem__(self, idx):
            i, h, d = idx
            return Lall[h // 2][i, h % 2, d]

    og = kb.dtmp("og", [T, RH * DV_], BF16); r_og = Res()
    emit_ret(kb, T, True, rio, pr, ident, x1, r_x1, r_L=r_La, Lprev=_LP(), og=og, r_og_d=r_og)

    def src1(rows):
        return [(og[rows, :], lambda c_: c_[:])]

    xm1 = kb.dtmp("xm1", [T, D]); r_xm1 = Res()
    emit_wo(kb, T, 32, w1, ident, src1, r_og, sel_own, x1, r_x1, xm1, r_xm1)
    halo1 = kb.dtmp("halo1", [8, D]); r_h1 = Res()
    kb.collective("AllGather", GROUPS, xm1[T - 2:T, :], halo1, [r_xm1], [r_h1])
    emit_ffn(kb, T, "f1_", f1, pf1, xm1, r_xm1, halo1, r_h1, sel_prev, ident, out, Res())
    return kb.nc


_NC = []


def kernel(x, norm_g, fox_w_in, fox_b_f, fox_w_o, ret_w_in, ret_w_o,
           ffn_w_up, ffn_conv_w, ffn_conv_b, ffn_w_down):
    f = lambda a: np.ascontiguousarray(np.asarray(a, dtype=np.float32))
    x, norm_g = f(x), f(norm_g)
    fox_w_in, fox_b_f, fox_w_o = f(fox_w_in), f(fox_b_f), f(fox_w_o)
    ret_w_in, ret_w_o = f(ret_w_in), f(ret_w_o)
    ffn_w_up, ffn_conv_w, ffn_conv_b, ffn_w_down = f(ffn_w_up), f(ffn_conv_w), f(ffn_conv_b), f(ffn_w_down)
    if not _NC:
        _NC.append(build_fused())
    nc = _NC[0]
    T, G = TPC, NCORES // BATCH
    shared = {"ident": np.eye(128, dtype=np.float32)}
    for k_, v_ in wo_host_layout(fox_w_o[0], norm_g[0, 1]).items():
        if k_ != "ident":
            shared["wo0_" + k_] = v_
    for k_, v_ in wo_host_layout(ret_w_o[0], norm_g[1, 1]).items():
        if k_ != "ident":
            shared["wo1_" + k_] = v_
    for l, pfx in ((0, "f0_"), (1, "f1_")):
        lay = ffn_host_layouts(ffn_w_up[l], ffn_conv_w[l], ffn_conv_b[l], ffn_w_down[l], norm_g[l, 2], norm_g[l, 3])
        for k_, v_ in lay.items():
            if k_ != "ident":
                shared[pfx + k_] = v_
    rl = ret_host_layouts(ret_w_in[0], norm_g[1, 0])
    shared["r_win"] = rl["win"]
    shared["r_G"] = rl["G"]
    foxl = [fox_host_layouts(fox_w_in[0], fox_b_f[0], norm_g[0, 0], m) for m in range(G)]
    maps = []
    for c in range(NCORES):
        b, j = c // G, c % G
        m = dict(shared)
        for k_, v_ in foxl[j].items():
            if k_ != "ident":
                m[k_] = v_
        m.update(ret_const_tables(j * T, T, j))
        m["xb"] = x[b]
        m["xs"] = np.ascontiguousarray(x[b, j * T:(j + 1) * T])
        sp = np.zeros((128, 4), np.float32)
        so = np.zeros((128, 4), np.float32)
        if j > 0:
            sp[:, j - 1] = 1.0
        so[:, j] = 1.0
        m["sel_prev"] = sp
        m["sel_own"] = so
        maps.append(m)
    res = run_bass_kernel_spmd(nc, maps, core_ids=list(range(NCORES))).results
    out = np.concatenate([res[c]["out"] for c in range(NCORES)], axis=0).reshape(BATCH, SEQ, D_MODEL)
    return out.astype(np.float32)
```

```python
import contextlib
import numpy as np
import concourse.bass as bass
import concourse.mybir as mybir
from concourse.bass_utils import run_bass_kernel_spmd

F32 = mybir.dt.float32
BF16 = mybir.dt.bfloat16
AF = mybir.ActivationFunctionType
ALU = mybir.AluOpType
AX = mybir.AxisListType

D_MODEL = 2048
SEQ = 16384
BATCH = 2
D_FF = 5632
NORM_EPS = 1e-6
NCORES = 8

SAME_ENG_SYNC = True
SEM_GEN = 30000
SEM_DMA_GEN = 1500


class Res:
    __slots__ = ("name", "w", "rs")

    def __init__(self, name=""):
        self.name = name
        self.w = None
        self.rs = []


class _Op:
    __slots__ = ("eng", "fn", "deps", "ev", "dma", "inc")


class Sched:
    ENGS = ("pe", "act", "dve", "pool", "sp")

    def __init__(self, nc, ndma=10):
        self.nc = nc
        self.ops = {e: [] for e in self.ENGS}
        self.cnt = {e: 0 for e in self.ENGS}
        self.dman = {e: 0 for e in self.ENGS}
        self.dmahist = {e: [] for e in self.ENGS}
        self.ndma = ndma
        self.ncc = 0
        self.sems = {}
        self.semstack = contextlib.ExitStack()
        self.waited = {e: {} for e in self.ENGS}
        self.fin = {}

    def op(self, eng, fn, reads=(), writes=(), dma=False, inc=None):
        o = _Op()
        o.eng = eng
        o.fn = fn
        o.dma = dma
        o.inc = inc
        deps = []
        for r in reads:
            if r.w is not None:
                deps.append(r.w)
        for r in writes:
            if r.w is not None:
                deps.append(r.w)
            deps.extend(r.rs)
        if inc is not None:
            o.ev = ("cc", self.ncc, inc)
            self.ncc += 1
        elif dma:
            n = self.dman[eng]
            self.dman[eng] = n + 1
            rnd = n // self.ndma
            o.ev = ("d_" + eng, (n % self.ndma, rnd // SEM_DMA_GEN), 16 * (rnd % SEM_DMA_GEN + 1))
            if n >= self.ndma:
                deps.append(self.dmahist[eng][n - self.ndma])
            self.dmahist[eng].append(o)
        else:
            c = self.cnt[eng]
            self.cnt[eng] = c + 1
            o.ev = ("c_" + eng, c // SEM_GEN, c % SEM_GEN + 1)
        dd = []
        seen = set()
        for d in deps:
            if id(d) in seen:
                continue
            seen.add(id(d))
            if (not d.dma) and d.eng == eng:
                if eng == "pe" or not SAME_ENG_SYNC:
                    continue
            dd.append(d)
        o.deps = dd
        for r in reads:
            r.rs.append(o)
        for r in writes:
            r.w = o
            r.rs = []
        self.ops[eng].append(o)
        return o

    def flush(self):
        nc = self.nc
        for e in self.ENGS:
            for o in self.ops[e]:
                k = (o.ev[0], o.ev[1])
                if k not in self.sems:
                    self.sems[k] = self.semstack.enter_context(nc.semaphore("s%d" % len(self.sems)))
                self.fin[k] = max(self.fin.get(k, 0), o.ev[2])
        sems = self.sems
        fin = dict(self.fin)
        ops = self.ops
        self.ops = {e: [] for e in self.ENGS}
        with nc.Block() as block:
            def run(eng_name):
                def body(eng):
                    waited = self.waited[eng_name]
                    for o in ops[eng_name]:
                        for d in o.deps:
                            k = (d.ev[0], d.ev[1])
                            if waited.get(k, 0) < d.ev[2]:
                                eng.wait_ge(sems[k], d.ev[2])
                                waited[k] = d.ev[2]
                        ins = o.fn(eng)
                        ins.then_inc(sems[(o.ev[0], o.ev[1])], o.inc if o.inc is not None else (16 if o.dma else 1))
                    for k, v in fin.items():
                        if waited.get(k, 0) < v:
                            eng.wait_ge(sems[k], v)
                            waited[k] = v
                return body

            block.tensor(run("pe"))
            block.scalar(run("act"))
            block.vector(run("dve"))
            block.gpsimd(run("pool"))
            block.sync(run("sp"))


class Pool:
    def __init__(self, kb, name, shape, dt, n, views=None):
        if views is not None:
            self.tiles = list(views)
            n = len(views)
        else:
            self.tiles = [kb.sb("%s%d" % (name, i), shape, dt) for i in range(n)]
        self.res = [Res("%s%d" % (name, i)) for i in range(n)]
        self.i = 0

    def next(self):
        i = self.i % len(self.tiles)
        self.i += 1
        return self.tiles[i], self.res[i]


class KB:
    def __init__(self):
        self.nc = bass.Bass("TRN2", target_bir_lowering=False)
        self.S = Sched(self.nc)
        self.st = None
        self.phase = 0

    def begin(self, ident_in):
        self.phase += 1
        self.st = contextlib.ExitStack()
        self.ps = []
        self.psr = []
        self.psi = 0
        setup_common(self, ident_in)

    def end(self):
        self.S.flush()
        self.st.close()
        self.st = None

    def sb(self, name, shape, dt):
        return self.st.enter_context(self.nc.sbuf_tensor("p%d_%s" % (self.phase, name), list(shape), dt))

    def alloc_psum(self, n=8):
        for i in range(n):
            self.ps.append(self.st.enter_context(self.nc.psum_tensor("p%d_ps%d" % (self.phase, i), [128, 512], F32)))
            self.psr.append(Res("ps%d" % i))

    def bank(self, lo=0, hi=8):
        i = lo + self.psi % (hi - lo)
        self.psi += 1
        return self.ps[i], self.psr[i]

    def din(self, name, shape, dt=F32):
        return self.nc.dram_tensor(name, list(shape), dt, kind="ExternalInput").ap()

    def dout(self, name, shape, dt=F32):
        return self.nc.dram_tensor(name, list(shape), dt, kind="ExternalOutput").ap()

    def dtmp(self, name, shape, dt=F32):
        return self.nc.dram_tensor(name, list(shape), dt, kind="Internal").ap()

    def collective(self, kind, groups, in_ap, out_ap, r, w, flush=True):
        if not hasattr(self, "r_cc"):
            self.r_cc = Res("cc")
        self.S.op("pool", lambda e: e.collective_compute(kind, ALU.bypass, replica_groups=groups, ins=[in_ap], outs=[out_ap]),
                  r, list(w) + [self.r_cc], dma=True, inc=1)
        if flush:
            self.S.flush()

    def dma(self, q, out, in_, r, w):
        return self.S.op(q, lambda e: e.dma_start(out=out, in_=in_), r, w, dma=True)

    def act(self, out, in_, func, r, w, **kw):
        return self.S.op("act", lambda e: e.activation(out=out, in_=in_, func=func, **kw), r, w)

    def mm(self, out, lhsT, rhs, start, stop, r, w):
        return self.S.op("pe", lambda e: e.matmul(out, lhsT=lhsT, rhs=rhs, start=start, stop=stop,
                                                  skip_group_check=True), r, w)

    def tr(self, out, in_, ident, r, w):
        return self.S.op("pe", lambda e: e.transpose(out=out, in_=in_, identity=ident), r, w)

    def v(self, eng, meth, r, w, **kw):
        return self.S.op(eng, lambda e: getattr(e, meth)(**kw), r, w)


def emit_rstd(kb, x_ap, r_x, ncols, P=128):
    junk, r_j = kb.junk.next()
    st, r_s = kb.small.next()
    kb.act(junk[0:P, 0:ncols], x_ap, AF.Square, [r_x], [r_j])
    kb.v("dve", "reduce_sum", [r_j], [r_s], out=st[0:P, 0:1], in_=junk[0:P, 0:ncols], axis=AX.X)
    kb.v("dve", "tensor_scalar", [r_s], [r_s], out=st[0:P, 1:2], in0=st[0:P, 0:1], scalar1=1.0 / ncols,
         scalar2=NORM_EPS, op0=ALU.mult, op1=ALU.add)
    kb.act(st[0:P, 2:3], st[0:P, 1:2], AF.Sqrt, [r_s], [r_s])
    kb.v("dve", "reciprocal", [r_s], [r_s], out=st[0:P, 3:4], in_=st[0:P, 2:3])
    return st[0:P, 3:4], r_s


def emit_norm_T(kb, x_tile, r_x, G, r_G, hT, r_hT, col0, KC, cp_eng="act"):
    rstd, r_s = emit_rstd(kb, x_tile[:, 0:KC * 128], r_x, KC * 128)
    hn, r_hn = kb.hn.next()
    kb.v("dve", "scalar_tensor_tensor", [r_x, r_s, r_G], [r_hn], out=hn[:, 0:KC * 128], in0=x_tile[:, 0:KC * 128],
         scalar=rstd, in1=G[:, 0:KC * 128], op0=ALU.mult, op1=ALU.mult)
    emit_T(kb, hn, r_hn, hT, r_hT, col0, KC, cp_eng)


def emit_T(kb, hn, r_hn, hT, r_hT, col0, KC, cp_eng="act"):
    for k0 in range(0, KC, 8):
        n = min(8, KC - k0)
        ps, r_ps = kb.bank()
        psb = ps[:].bitcast(BF16)
        for j in range(n):
            kb.tr(psb[:, j * 128:(j + 1) * 128], hn[:, (k0 + j) * 128:(k0 + j + 1) * 128], kb.identb[:],
                  [r_hn, kb.r_ident], [r_ps])
        src = psb[:, 0:n * 128].rearrange("p (k c) -> p k c", c=128)
        dst = hT[:, k0:k0 + n, col0:col0 + 128]
        if cp_eng == "act":
            kb.act(dst, src, AF.Copy, [r_ps], [r_hT])
        else:
            kb.v(cp_eng, "tensor_copy", [r_ps], [r_hT], out=dst, in_=src)


def emit_post(kb, f_ap, r_f, x_ap, r_x, G, r_G, out_tile, r_out, ncols=D_MODEL):
    rstd, r_s = emit_rstd(kb, f_ap, r_f, ncols)
    kb.v("dve", "scalar_tensor_tensor", [r_f, r_s, r_G], [r_out], out=out_tile, in0=f_ap, scalar=rstd, in1=G[:, 0:ncols],
         op0=ALU.mult, op1=ALU.mult)
    kb.v("pool", "tensor_tensor", [r_out, r_x], [r_out], out=out_tile, in0=out_tile, in1=x_ap, op=ALU.add)


def setup_common(kb, ident_in):
    kb.alloc_psum(8)
    idf = kb.sb("idf", [128, 128], F32)
    kb.identb = kb.sb("identb", [128, 128], BF16)
    kb.r_ident = Res("ident")
    r_idf = Res("idf")
    kb.dma("sp", idf[:], ident_in, [], [r_idf])
    kb.identf = idf
    kb.r_identf = r_idf
    kb.v("dve", "tensor_copy", [r_idf], [kb.r_ident], out=kb.identb[:], in_=idf[:])
    kb.small = Pool(kb, "small", [128, 4], F32, 12)


_cast_rr = [0]


def emit_cast(kb, out, in_, r, w):
    i = _cast_rr[0] % 3
    _cast_rr[0] += 1
    if i == 0:
        kb.v("dve", "tensor_copy", r, w, out=out, in_=in_)
    elif i == 1:
        kb.act(out, in_, AF.Copy, r, w)
    else:
        kb.v("pool", "tensor_copy", r, w, out=out, in_=in_)


TB = 512
NPAIR = D_FF // 128
KC = D_MODEL // 128


def ffn_inputs(kb, pfx):
    return dict(wup=kb.din(pfx + "wup", [NPAIR, 128, 2 * KC * 128]), wdn=kb.din(pfx + "wdn", [4, 128, NPAIR * 512]),
                G2=kb.din(pfx + "G2", [128, D_MODEL]), G3=kb.din(pfx + "G3", [128, D_MODEL]),
                cw=kb.din(pfx + "cw", [128, 2 * NPAIR * 3]), cb=kb.din(pfx + "cb", [128, 2 * NPAIR]))


class Prep:
    def __init__(self):
        self.steps = []

    def add(self, src_ap, dst_ap, r_dst):
        self.steps.append((src_ap, dst_ap, r_dst))

    def run(self, kb, n, stf, stb):
        for _ in range(n):
            if not self.steps:
                return
            src_ap, dst_ap, r_dst = self.steps.pop(0)
            st_, r_st = stf.next()
            sb_, r_sb = stb.next()
            kb.dma("sp", st_[:], src_ap, [], [r_st])
            kb.v("pool", "tensor_copy", [r_st], [r_sb], out=sb_[:], in_=st_[:])
            kb.dma("pool", dst_ap, sb_[:], [r_sb], [r_dst])


def prep_ffn(kb, prep, pfx, io):
    wup, wdn = io["wup"], io["wdn"]
    wub = kb.dtmp(pfx + "wub", [NPAIR, 128, 2 * KC * 128], BF16)
    wdb = kb.dtmp(pfx + "wdb", [4, 128, NPAIR * 512], BF16)
    r_wub = [Res() for _ in range(NPAIR)]
    r_wdb = [Res() for _ in range(4)]
    for i in range(NPAIR):
        for hf in range(2):
            sl = slice(hf * KC * 128, (hf + 1) * KC * 128)
            prep.add(wup[i, :, sl], wub[i, :, sl], r_wub[i])
    for dq in range(4):
        for g in range(NPAIR * 512 // D_MODEL):
            sl = slice(g * D_MODEL, (g + 1) * D_MODEL)
            prep.add(wdn[dq, :, sl], wdb[dq, :, sl], r_wdb[dq])
    return dict(wub=wub, wdb=wdb, r_wub=r_wub, r_wdb=r_wdb)


def prep_ret(kb, prep, io):
    wind = io["win"]
    wib = kb.dtmp("wib", [NS, 128, KC * 512], BF16)
    r_wib = [Res() for _ in range(NS)]
    for ns in range(NS):
        for g in range(KC * 512 // D_MODEL):
            sl = slice(g * D_MODEL, (g + 1) * D_MODEL)
            prep.add(wind[ns, :, sl], wib[ns, :, sl], r_wib[ns])
    return dict(wib=wib, r_wib=r_wib)


def emit_ffn(kb, T, pfx, io, pw, xm, r_xm, halo_all, r_halo, sel4d, identd, xo, r_xo):
    NB = T // TB
    G2d, G3d, cwd, cbd = io["G2"], io["G3"], io["cw"], io["cb"]
    wub, wdb, r_wub, r_wdb = pw["wub"], pw["wdb"], pw["r_wub"], pw["r_wdb"]

    kb.begin(identd)
    kb.junk = Pool(kb, "junk", [128, D_MODEL], F32, 1)
    kb.hn = Pool(kb, "hn", [128, D_MODEL], BF16, 2)
    xin = Pool(kb, "xin", [128, D_MODEL], F32, 2)
    G2 = kb.sb("G2s", [128, D_MODEL], F32); r_G2 = Res()
    G3 = kb.sb("G3s", [128, D_MODEL], F32); r_G3 = Res()
    cw = kb.sb("cws", [128, 2, NPAIR, 3], F32); r_cw = Res()
    cb = kb.sb("cbs", [128, 2, NPAIR], F32); r_cb = Res()
    kb.dma("sp", G2[:], G2d, [], [r_G2])
    kb.dma("sp", G3[:], G3d, [], [r_G3])
    kb.dma("sp", cw[:], cwd.rearrange("p (h i t) -> p h i t", h=2, i=NPAIR), [], [r_cw])
    kb.dma("sp", cb[:], cbd.rearrange("p (h i) -> p h i", h=2), [], [r_cb])

    hT = kb.sb("hT", [128, KC, TB], BF16); r_hT = Res()
    gT = kb.sb("gT", [128, NPAIR, TB], BF16); r_gT = [Res() for _ in range(NPAIR)]
    ft = kb.sb("ft", [128, TB // 128, D_MODEL], F32); r_ft = [Res() for _ in range(TB // 128)]
    wupp = Pool(kb, "wupp", [128, 2, KC, 128], BF16, 3)
    wdp = Pool(kb, "wdp", [128, 4, 512], BF16, 3)
    ub = Pool(kb, "ub", [128, TB + 2], F32, 4)
    tmp = Pool(kb, "tmp", [128, TB], F32, 5)
    carry = kb.sb("carry", [128, 2 * NPAIR, 2], F32); r_carry = [Res() for _ in range(2 * NPAIR)]
    hTh = kb.sb("hTh", [128, KC, 128], BF16); r_hTh = Res()
    sel4 = kb.sb("sel4", [128, 4], F32); r_sel = Res()
    kb.dma("sp", sel4[:], sel4d, [], [r_sel])
    xt, r_xt = xin.next()
    kb.v("dve", "memset", [], [r_xt], ap=xt[:], constant=0.0)
    for i in range(4):
        ct, r_ct = ft[:, i, :], r_ft[i]
        kb.v("pool", "memset", [], [r_ct], ap=ct, constant=0.0)
        kb.dma("sp", ft[126:128, i, :], halo_all[2 * i:2 * i + 2, :], [r_halo], [r_ct])
        kb.v("dve", "scalar_tensor_tensor", [r_ct, r_sel, r_xt], [r_xt], out=xt[:], in0=ct, scalar=sel4[:, i:i + 1],
             in1=xt[:], op0=ALU.mult, op1=ALU.add)
    emit_norm_T(kb, xt, r_xt, G2, r_G2, hTh, r_hTh, 0, KC)
    psc, r_psc = kb.bank()
    for i in range(NPAIR):
        wt, r_wt = wupp.next()
        kb.dma("sp", wt[:], wub[i].rearrange("p (h k c) -> p h k c", h=2, k=KC), [r_wub[i]], [r_wt])
        for hf in range(2):
            ch = hf * NPAIR + i
            for k in range(KC):
                kb.mm(psc[:, ch * 2:ch * 2 + 2], wt[:, hf, k, :], hTh[:, k, 126:128], k == 0, k == KC - 1,
                      [r_wt, r_hTh], [r_psc])
    kb.v("dve", "tensor_copy", [r_psc], r_carry, out=carry[:].rearrange("p c t -> p (c t)"), in_=psc[:, 0:4 * NPAIR])

    for b in range(NB):
        t0 = b * TB
        for tt in range(TB // 128):
            xt, r_xt = xin.next()
            kb.dma("sp", xt[:], xm[t0 + tt * 128:t0 + (tt + 1) * 128, :], [r_xm], [r_xt])
            emit_norm_T(kb, xt, r_xt, G2, r_G2, hT, r_hT, tt * 128, KC)
        for i in range(NPAIR):
            wt, r_wt = wupp.next()
            kb.dma("sp", wt[:], wub[i].rearrange("p (h k c) -> p h k c", h=2, k=KC), [r_wub[i]], [r_wt])
            cv = []
            for hf in range(2):
                ch = hf * NPAIR + i
                ps, r_ps = kb.bank()
                for k in range(KC):
                    kb.mm(ps[:, 0:TB], wt[:, hf, k, :], hT[:, k, :], k == 0, k == KC - 1, [r_wt, r_hT], [r_ps])
                u, r_u = ub.next()
                kb.act(u[:, 2:TB + 2], ps[:, 0:TB], AF.Copy, [r_ps], [r_u])
                kb.v("dve", "tensor_copy", [r_carry[ch]], [r_u], out=u[:, 0:2], in_=carry[:, ch, :])
                kb.v("dve", "tensor_copy", [r_u], [r_carry[ch]], out=carry[:, ch, :], in_=u[:, TB:TB + 2])
                t1, r_t1 = tmp.next()
                kb.act(t1[:], u[:, 2:TB + 2], AF.Identity, [r_u, r_cw, r_cb], [r_t1], scale=cw[:, hf, i, 2:3],
                       bias=cb[:, hf, i:i + 1])
                kb.v("dve", "scalar_tensor_tensor", [r_u, r_t1, r_cw], [r_t1], out=t1[:], in0=u[:, 1:TB + 1],
                     scalar=cw[:, hf, i, 1:2], in1=t1[:], op0=ALU.mult, op1=ALU.add)
                kb.v("dve", "scalar_tensor_tensor", [r_u, r_t1, r_cw], [r_t1], out=t1[:], in0=u[:, 0:TB],
                     scalar=cw[:, hf, i, 0:1], in1=t1[:], op0=ALU.mult, op1=ALU.add)
                cv.append((t1, r_t1))
            (ta, r_ta), (tb_, r_tb) = cv
            sa, r_sa = tmp.next()
            kb.act(sa[:], ta[:], AF.Silu, [r_ta], [r_sa])
            kb.v("dve", "tensor_tensor", [r_sa, r_tb], [r_gT[i]], out=gT[:, i, :], in0=sa[:], in1=tb_[:], op=ALU.mult)
        NTT = TB // 128
        for dq in range(4):
            banks = [kb.bank() for _ in range(NTT)]
            for g in range(NPAIR // 4):
                wd, r_wd = wdp.next()
                kb.dma("sp", wd[:], wdb[dq, :, g * 2048:(g + 1) * 2048].rearrange("p (f c) -> p f c", c=512),
                       [r_wdb[dq]], [r_wd])
                for j in range(4):
                    fc = g * 4 + j
                    for tt in range(NTT):
                        kb.mm(banks[tt][0][:, 0:512], gT[:, fc, tt * 128:(tt + 1) * 128], wd[:, j, :], fc == 0,
                              fc == NPAIR - 1, [r_gT[fc], r_wd], [banks[tt][1]])
            for tt in range(NTT):
                kb.act(ft[:, tt, dq * 512:(dq + 1) * 512], banks[tt][0][:, 0:512], AF.Copy, [banks[tt][1]], [r_ft[tt]])
        for tt in range(NTT):
            xt, r_xt = xin.next()
            rows = slice(t0 + tt * 128, t0 + (tt + 1) * 128)
            kb.dma("sp", xt[:], xm[rows, :], [r_xm], [r_xt])
            emit_post(kb, ft[:, tt, :], r_ft[tt], xt[:], r_xt, G3, r_G3, ft[:, tt, :], r_ft[tt])
            kb.dma("pool", xo[rows, :], ft[:, tt, :], [r_ft[tt]], [r_xo])
    kb.end()


def ffn_host_layouts(w_up, conv_w, conv_b, w_down, g2, g3):
    D, F = D_MODEL, D_FF
    w = w_up.reshape(KC, 128, 2, NPAIR, 128)
    wup = np.ascontiguousarray(w.transpose(3, 1, 2, 0, 4)).reshape(NPAIR, 128, 2 * KC * 128)
    w = w_down.reshape(NPAIR, 128, 4, 512)
    wdn = np.ascontiguousarray(w.transpose(2, 1, 0, 3)).reshape(4, 128, NPAIR * 512)
    c = conv_w.reshape(3, 2, NPAIR, 128)
    cw = np.ascontiguousarray(c.transpose(3, 1, 2, 0)).reshape(128, 2 * NPAIR * 3)
    c = conv_b.reshape(2, NPAIR, 128)
    cb = np.ascontiguousarray(c.transpose(2, 0, 1)).reshape(128, 2 * NPAIR)
    G2 = np.ascontiguousarray(np.broadcast_to(g2[None, :], (128, D)))
    G3 = np.ascontiguousarray(np.broadcast_to(g3[None, :], (128, D)))
    return dict(wup=wup, wdn=wdn, cw=cw, cb=cb, G2=G2, G3=G3, ident=np.eye(128, dtype=np.float32))


DH = 128
HPC = 4
QB = 512


def fox_inputs(kb):
    return dict(wqk=kb.din("wqk", [128, KC * 8 * 128]), wv=kb.din("wv", [128, KC * 512]), wf=kb.din("wf", [128, KC * HPC]),
                bf=kb.din("bf", [HPC, 1]), G0=kb.din("G0", [128, D_MODEL]), negmask=kb.din("negmask", [128, 128]))


def emit_fox(kb, S, io, xb, identd, att, r_att, prep=None):
    NB = S // QB
    NKB = S // 128
    wqkd, wvd, wfd, bfd, G0d, nmd = io["wqk"], io["wv"], io["wf"], io["bf"], io["G0"], io["negmask"]
    qTd = kb.dtmp("qTd", [HPC, 128, S], BF16)
    kTd = kb.dtmp("kTd", [HPC, 128, S], BF16)
    V1d = kb.dtmp("V1d", [HPC, S, DH + 1], BF16)
    Fsd = kb.dtmp("Fsd", [3, HPC, S], BF16)
    r_qTd = [Res() for _ in range(HPC)]
    r_kTd = [Res() for _ in range(HPC)]
    r_V1d = [Res() for _ in range(HPC)]
    r_Fsd = Res()

    kb.begin(identd)
    kb.junk = Pool(kb, "junk", [128, D_MODEL], F32, 1)
    kb.hn = Pool(kb, "hn", [128, D_MODEL], BF16, 2)
    A2 = kb.sb("A2", [128, 16384], BF16)
    xin = Pool(kb, "xin", None, None, 2, views=[A2[:, 8192:12288].bitcast(F32), A2[:, 12288:16384].bitcast(F32)])
    G0 = kb.sb("G0s", [128, D_MODEL], F32); r_G0 = Res()
    kb.dma("sp", G0[:], G0d, [], [r_G0])
    nmf = kb.sb("nmf", [128, 128], F32); r_nmf = Res()
    negmask = kb.sb("negmask_b", [128, 128], BF16); r_nm = Res()
    kb.dma("sp", nmf[:], nmd, [], [r_nmf])
    kb.v("dve", "tensor_copy", [r_nmf], [r_nm], out=negmask[:], in_=nmf[:])
    bft = kb.sb("bft", [HPC, 2], F32); r_bf = Res()
    kb.dma("sp", bft[:, 0:1], bfd, [], [r_bf])
    kb.v("dve", "tensor_scalar", [r_bf], [r_bf], out=bft[:, 1:2], in0=bft[:, 0:1], scalar1=-1.0, scalar2=None, op0=ALU.mult)
    ones4 = kb.sb("ones4", [HPC, QB], F32); r_ones = Res()
    kb.v("dve", "memset", [], [r_ones], ap=ones4[:], constant=1.0)

    wbig = kb.sb("wbig", [128, KC * 1024 + KC * 512], BF16); r_w = Res()
    wqk = wbig[:, 0:KC * 1024].rearrange("p (k c) -> p k c", k=KC)
    wv = wbig[:, KC * 1024:KC * 1536].rearrange("p (k c) -> p k c", k=KC)
    wf = kb.sb("wf_s", [128, KC, HPC], BF16)
    wqk_flat = wbig[:, 0:KC * 1024]
    for g in range(KC * 1024 // 2048):
        st_, r_st = xin.next()
        kb.dma("sp", st_[:], wqkd[:, g * 2048:(g + 1) * 2048], [], [r_st])
        emit_cast(kb, wqk_flat[:, g * 2048:(g + 1) * 2048], st_[:], [r_st], [r_w])
    wv_flat = wbig[:, KC * 1024:KC * 1536]
    for g in range(KC * 512 // 2048):
        st_, r_st = xin.next()
        kb.dma("sp", st_[:], wvd[:, g * 2048:(g + 1) * 2048], [], [r_st])
        emit_cast(kb, wv_flat[:, g * 2048:(g + 1) * 2048], st_[:], [r_st], [r_w])
    st_, r_st = xin.next()
    kb.dma("sp", st_[:, 0:KC * HPC], wfd, [], [r_st])
    kb.v("dve", "tensor_copy", [r_st], [r_w], out=wf[:].rearrange("p k c -> p (k c)"), in_=st_[:, 0:KC * HPC])

    hT = A2[:, 0:8192].rearrange("p (k c) -> p k c", k=KC); r_hT = Res()
    stq = Pool(kb, "stq", [128, QB], BF16, 4)
    vst = Pool(kb, "vst", [128, HPC, DH + 1], BF16, 3)
    for t_, r_ in zip(vst.tiles, vst.res):
        kb.v("dve", "memset", [], [r_], ap=t_[:], constant=1.0)
    fe = Pool(kb, "fe", [HPC, QB], F32, 1)
    Fb = Pool(kb, "Fb", [HPC, QB], F32, 2)
    fr = Pool(kb, "fr", [HPC, QB], F32, 3)
    fsb = Pool(kb, "fsb", [HPC, QB], BF16, 3)
    scale = float(DH) ** -0.5
    FK = kb.sb("FK", [128, NKB, HPC], F32); r_FK = Res()

    prevF = None
    for b in range(NB):
        t0 = b * QB
        for tt in range(QB // 128):
            xt, r_xt = xin.next()
            kb.dma("sp", xt[:], xb[t0 + tt * 128:t0 + (tt + 1) * 128, :], [], [r_xt])
            emit_norm_T(kb, xt, r_xt, G0, r_G0, hT, r_hT, tt * 128, KC)
        for j in range(8):
            ps, r_ps = kb.bank()
            for k in range(KC):
                kb.mm(ps[:, 0:QB], wqk[:, k, j * 128:(j + 1) * 128], hT[:, k, :], k == 0, k == KC - 1, [r_w, r_hT], [r_ps])
            s_, r_s = stq.next()
            h = j % HPC
            if j < HPC:
                kb.act(s_[:], ps[:, 0:QB], AF.Copy, [r_ps], [r_s], scale=scale)
                kb.dma("pool", qTd[h, :, t0:t0 + QB], s_[:], [r_s], [r_qTd[h]])
            else:
                kb.v("dve", "tensor_copy", [r_ps], [r_s], out=s_[:], in_=ps[:, 0:QB])
                kb.dma("pool", kTd[h, :, t0:t0 + QB], s_[:], [r_s], [r_kTd[h]])
        for tt in range(QB // 128):
            ps, r_ps = kb.bank()
            for k in range(KC):
                kb.mm(ps[:, 0:512], hT[:, k, tt * 128:(tt + 1) * 128], wv[:, k, :], k == 0, k == KC - 1, [r_w, r_hT], [r_ps])
            v_, r_v = vst.next()
            src = ps[:, 0:512].rearrange("p (h c) -> p h c", h=HPC)
            if tt % 2 == 0:
                kb.act(v_[:, :, 0:DH], src, AF.Copy, [r_ps], [r_v])
            else:
                kb.v("dve", "tensor_copy", [r_ps], [r_v], out=v_[:, :, 0:DH], in_=src)
            rows = slice(t0 + tt * 128, t0 + (tt + 1) * 128)
            kb.dma("pool", V1d.rearrange("h t c -> t h c")[rows, :, :], v_[:], [r_v], r_V1d)
        ps, r_ps = kb.bank()
        for k in range(KC):
            kb.mm(ps[0:HPC, 0:QB], wf[:, k, :], hT[:, k, :], k == 0, k == KC - 1, [r_w, r_hT], [r_ps])
        e_, r_e = fe.next()
        kb.act(e_[:], ps[0:HPC, 0:QB], AF.Exp, [r_ps, r_bf], [r_e], scale=-1.0, bias=bft[:, 1:2])
        kb.act(e_[:], e_[:], AF.Ln, [r_e], [r_e], bias=1.0)
        F_, r_F = Fb.next()
        init = 0.0 if prevF is None else prevF[0][:, QB - 1:QB]
        rd = [r_e, r_ones] + ([] if prevF is None else [prevF[1]])
        kb.v("dve", "tensor_tensor_scan", rd, [r_F], out=F_[:], data0=ones4[:], data1=e_[:], initial=init,
             op0=ALU.mult, op1=ALU.subtract)
        prevF = (F_, r_F)
        ps, r_ps = kb.bank()
        for tt in range(QB // 128):
            kb.tr(ps[:, tt * HPC:(tt + 1) * HPC], F_[:, tt * 128:(tt + 1) * 128], kb.identf[0:HPC, 0:HPC], [r_F, kb.r_identf], [r_ps])
        kb.act(FK[:, b * 4:(b + 1) * 4, :], ps[:, 0:4 * HPC].rearrange("p (t h) -> p t h", h=HPC), AF.Copy, [r_ps], [r_FK], scale=-1.0)
        cur, r_cur = F_, r_F
        for i in range(3):
            fb_, r_fb = fsb.next()
            kb.v("dve", "tensor_copy", [r_cur], [r_fb], out=fb_[:], in_=cur[:])
            kb.dma("pool", Fsd[i, :, t0:t0 + QB], fb_[:], [r_fb], [r_Fsd])
            if i < 2:
                ff, r_ff = fr.next()
                kb.v("dve", "tensor_copy", [r_fb], [r_ff], out=ff[:], in_=fb_[:])
                nr, r_nr = fr.next()
                kb.v("dve", "tensor_tensor", [r_cur, r_ff], [r_nr], out=nr[:], in0=cur[:], in1=ff[:], op=ALU.subtract)
                cur, r_cur = nr, r_nr

    assert S <= 16384
    kT = A2[:, 0:S]; r_kT = Res()
    kt_first = [r_hT, xin.res[0], xin.res[1]]
    V1 = kb.sb("V1", [128, NKB, DH + 1], BF16); r_V1 = Res()
    KF = wbig[0:6, 0:S]; r_KF = Res()
    qTb = Pool(kb, "qTb", [128, QB], BF16, 2)
    QFb = Pool(kb, "QFb", [6, QB], BF16, 2)
    for t_, r_ in zip(QFb.tiles, QFb.res):
        kb.v("dve", "memset", [], [r_], ap=t_[:], constant=-1.0)
    PT = Pool(kb, "PT", [128, QB], BF16, 6)
    ost = Pool(kb, "ost", [128, 4, DH], BF16, 2)
    frow = Pool(kb, "frow", [128, QB], F32, 2)
    stmp = Pool(kb, "stmp", [128, QB], F32, 4)
    ones3 = kb.sb("ones3", [3, 128], BF16); r_o3 = Res()
    kb.v("dve", "memset", [], [r_o3], ap=ones3[:], constant=1.0)
    if prep is not None:
        pstf = Pool(kb, "pstf", None, None, 2, views=[kb.junk.tiles[0], kb.sb("pstf1", [128, D_MODEL], F32)])
        pstf.res[0] = kb.junk.res[0]
        pstb = kb.hn
        per_q = -(-len(prep.steps) // max(1, (HPC * NB * 3) // 4))
    first = True
    for h in range(HPC):
        nsp = 4
        for i in range(nsp):
            cs = slice(i * S // nsp, (i + 1) * S // nsp)
            kb.dma("sp", kT[:, cs], kTd[h, :, cs], [r_kTd[h]], [r_kT] + kt_first)
            kt_first = []
        nvp = max(1, NKB // 8)
        for i in range(nvp):
            ks = slice(i * NKB // nvp, (i + 1) * NKB // nvp)
            kb.dma("sp", V1[:, ks, :], V1d[h].rearrange("(n p) c -> p n c", p=128)[:, ks, :], r_V1d, [r_V1])
        first = False
        for Q in range(NB):
            q_, r_q = qTb.next()
            kb.dma("sp", q_[:], qTd[h, :, Q * QB:(Q + 1) * QB], [r_qTd[h]], [r_q])
            qf, r_qf = QFb.next()
            kb.dma("sp", qf[0:3, :], Fsd[:, h, Q * QB:(Q + 1) * QB], [r_Fsd], [r_qf])
            if prep is not None:
                prep.run(kb, per_q, pstf, pstb)
            psF, r_psF = kb.bank(0, 4)
            kb.mm(psF[:, 0:QB], ones3[:], qf[0:3, :], True, True, [r_o3, r_qf], [r_psF])
            fr_, r_fr = frow.next()
            kb.v("dve", "tensor_copy", [r_psF], [r_fr], out=fr_[:], in_=psF[:, 0:QB])
            acc = [(kb.ps[4 + j], kb.psr[4 + j]) for j in range(4)]
            nkb = 4 * Q + 4
            LAG = 3
            pend = []
            for kbi in range(nkb + LAG):
                if kbi < nkb:
                    j0 = max(0, kbi - 4 * Q)
                    c0 = j0 * 128
                    ps, r_ps = kb.bank(0, 4)
                    diag = kbi >= 4 * Q
                    kb.mm(ps[:, c0:QB], kT[:, kbi * 128:(kbi + 1) * 128], q_[:, c0:QB], True, not diag, [r_kT, r_q], [r_ps])
                    if diag:
                        kb.mm(ps[:, c0:c0 + 128], kb.identb[:], negmask[:], False, True, [kb.r_ident, r_nm], [r_ps])
                    t_, r_t = stmp.next()
                    kb.v("dve", "tensor_tensor", [r_ps, r_fr], [r_t], out=t_[:, c0:QB], in0=ps[:, c0:QB], in1=fr_[:, c0:QB], op=ALU.add)
                    p_, r_p = PT.next()
                    kb.act(p_[:, c0:QB], t_[:, c0:QB], AF.Exp, [r_t, r_FK], [r_p], bias=FK[:, kbi, h:h + 1])
                    pend.append((kbi, j0, p_, r_p))
                if kbi >= LAG or kbi >= nkb:
                    if pend and (len(pend) > LAG or kbi >= nkb):
                        pk, pj0, pp, r_pp = pend.pop(0)
                        for j in range(pj0, 4):
                            kb.mm(acc[j][0][:, 0:DH + 1], pp[:, j * 128:(j + 1) * 128], V1[:, pk, :], pk == 0, pk == 4 * Q + j,
                                  [r_pp, r_V1], [acc[j][1]])
            assert not pend
            o_, r_o = ost.next()
            for j in range(4):
                sm, r_sm = kb.small.next()
                kb.v("dve", "reciprocal", [acc[j][1]], [r_sm], out=sm[:, 0:1], in_=acc[j][0][:, DH:DH + 1])
                kb.act(o_[:, j, :], acc[j][0][:, 0:DH], AF.Copy, [acc[j][1], r_sm], [r_o], scale=sm[:, 0:1])
            kb.dma("pool", att.rearrange("(n p) c -> p n c", p=128)[:, Q * 4:Q * 4 + 4, h * DH:(h + 1) * DH], o_[:], [r_o], [r_att])
    if prep is not None:
        prep.run(kb, len(prep.steps), pstf, pstb)
    kb.end()


def fox_host_layouts(w_in, b_f, g0, m):
    D = D_MODEL
    cols_q = w_in[:, (HPC * m) * DH:(HPC * m + HPC) * DH]
    cols_k = w_in[:, D + (HPC * m) * DH:D + (HPC * m + HPC) * DH]
    wqk = np.concatenate([cols_q, cols_k], axis=1).reshape(KC, 128, 8 * 128)
    wqk = np.ascontiguousarray(wqk.transpose(1, 0, 2)).reshape(128, KC * 1024)
    wv = w_in[:, 2 * D + HPC * m * DH:2 * D + (HPC * m + HPC) * DH].reshape(KC, 128, 512)
    wv = np.ascontiguousarray(wv.transpose(1, 0, 2)).reshape(128, KC * 512)
    wf = w_in[:, 3 * D + HPC * m:3 * D + HPC * m + HPC].reshape(KC, 128, HPC)
    wf = np.ascontiguousarray(wf.transpose(1, 0, 2)).reshape(128, KC * HPC)
    bf = np.ascontiguousarray(b_f[HPC * m:HPC * m + HPC].reshape(HPC, 1))
    G0 = np.ascontiguousarray(np.broadcast_to(g0[None, :], (128, D)))
    idx = np.arange(128)
    negmask = np.where(idx[:, None] <= idx[None, :], 0.0, -30000.0).astype(np.float32)
    return dict(wqk=wqk, wv=wv, wf=wf, bf=bf, G0=G0, ident=np.eye(128, dtype=np.float32), negmask=negmask)


def wo_inputs(kb, pfx, KCI):
    return dict(w=kb.din(pfx + "w", [128, KCI * D_MODEL]), G=kb.din(pfx + "G", [128, D_MODEL]))


def emit_wo(kb, T, KCI, io, identd, src_fn, r_src, sel4d, xr, r_xr, xo, r_xo):
    TBW = 512 if KCI <= 16 else 128
    NB = T // TBW
    wd, Gd = io["w"], io["G"]
    kb.begin(identd)
    kb.junk = Pool(kb, "junk", [128, D_MODEL], F32, 1)
    xin = Pool(kb, "xin", [128, D_MODEL], F32, 2)
    G = kb.sb("Gs", [128, D_MODEL], F32); r_G = Res()
    kb.dma("sp", G[:], Gd, [], [r_G])
    sel4 = kb.sb("sel4", [128, 4], F32); r_sel = Res()
    kb.dma("sp", sel4[:], sel4d, [], [r_sel])
    w = kb.sb("wres", [128, KCI, D_MODEL], BF16); r_w = Res()
    for k in range(KCI):
        st_, r_st = xin.next()
        kb.dma("sp", st_[:], wd[:, k * D_MODEL:(k + 1) * D_MODEL], [], [r_st])
        emit_cast(kb, w[:, k, :], st_[:], [r_st], [r_w])
    aTbp = Pool(kb, "aTb", [128, KCI, TBW], BF16, 2 if KCI <= 16 else 1)
    cand = Pool(kb, "cand", [128, KCI * 128], BF16, 4 if KCI <= 16 else 2)
    hn = Pool(kb, "hnw", [128, KCI * 128], BF16, 2) if KCI <= 16 else None
    ft = Pool(kb, "ft", [128, D_MODEL], F32, 2)
    for b in range(NB):
        t0 = b * TBW
        aTb, r_a = aTbp.next()
        for tt in range(TBW // 128):
            rows = slice(t0 + tt * 128, t0 + (tt + 1) * 128)
            srcs = src_fn(rows)
            if len(srcs) > 1:
                h_, r_h = hn.next()
            for i, (sap, view) in enumerate(srcs):
                c_, r_c = cand.next()
                kb.dma("sp", view(c_), sap, [r_src], [r_c])
                if len(srcs) == 1:
                    h_, r_h = c_, r_c
                elif i == 0:
                    kb.v("dve", "tensor_scalar", [r_c, r_sel], [r_h], out=h_[:], in0=c_[:], scalar1=sel4[:, 0:1], scalar2=None,
                         op0=ALU.mult)
                else:
                    kb.v("dve", "scalar_tensor_tensor", [r_c, r_sel, r_h], [r_h], out=h_[:], in0=c_[:], scalar=sel4[:, i:i + 1],
                         in1=h_[:], op0=ALU.mult, op1=ALU.add)
            emit_T(kb, h_, r_h, aTb, r_a, tt * 128, KCI)
        for tt in range(TBW // 128):
            f_, r_f = ft.next()
            for nq in range(4):
                ps, r_ps = kb.bank()
                for k in range(KCI):
                    kb.mm(ps[:, 0:512], aTb[:, k, tt * 128:(tt + 1) * 128], w[:, k, nq * 512:(nq + 1) * 512], k == 0, k == KCI - 1,
                          [r_a, r_w], [r_ps])
                if nq % 2 == 0:
                    kb.act(f_[:, nq * 512:(nq + 1) * 512], ps[:, 0:512], AF.Copy, [r_ps], [r_f])
                else:
                    kb.v("dve", "tensor_copy", [r_ps], [r_f], out=f_[:, nq * 512:(nq + 1) * 512], in_=ps[:, 0:512])
            xt, r_xt = xin.next()
            rows = slice(t0 + tt * 128, t0 + (tt + 1) * 128)
            kb.dma("sp", xt[:], xr[rows, :], [r_xr], [r_xt])
            emit_post(kb, f_[:], r_f, xt[:], r_xt, G, r_G, f_[:], r_f)
            kb.dma("pool", xo[rows, :], f_[:], [r_f], [r_xo])
    kb.end()


def wo_host_layout(w, g):
    KCI = w.shape[0] // 128
    wl = np.ascontiguousarray(w.reshape(KCI, 128, D_MODEL).transpose(1, 0, 2)).reshape(128, KCI * D_MODEL)
    G = np.ascontiguousarray(np.broadcast_to(g[None, :], (128, D_MODEL)))
    return dict(w=wl, G=G, ident=np.eye(128, dtype=np.float32))


RH = 8
DK_ = 256
DV_ = 512
RB = 1024
NS = 24


def ret_gammas():
    return [1.0 - 2.0 ** (-5.0 - h) for h in range(RH)]


def ret_inputs(kb, T):
    return dict(win=kb.din("r_win", [NS, 128, KC * 512]), G=kb.din("r_G", [128, D_MODEL]), cos2=kb.din("cos2", [T, 256]),
                sin2=kb.din("sin2", [T, 256]), DK=kb.din("DK", [128, D_MODEL]), Mp=kb.din("Mp", [128, RH * 128]),
                DQ=kb.din("DQ", [128, RH]), coef=kb.din("coef", [128, 3 * RH]))


def emit_ret(kb, T, full, io, pw, identd, xr, r_xr, Lout=None, r_L=None, Lprev=None, og=None, r_og_d=None):
    NBLK = T // RB
    NCH = T // 128
    wind, Gd, cosd, sind, DKd = io["win"], io["G"], io["cos2"], io["sin2"], io["DK"]
    Mpd, DQd, coefd = io["Mp"], io["DQ"], io["coef"]
    sfx = "f" if full else "s"
    wib, r_wib = pw["wib"], pw["r_wib"]
    qd = kb.dtmp("qd" + sfx, [T, D_MODEL], BF16)
    kd = kb.dtmp("kd" + sfx, [T, D_MODEL], BF16)
    vd = kb.dtmp("vd" + sfx, [T, 2 * D_MODEL], BF16)
    sgd = kb.dtmp("sgd" + sfx, [T, 2 * D_MODEL], BF16)
    r_qd, r_kd, r_vd, r_sgd = Res(), Res(), Res(), Res()
    slices = list(range(NS)) if full else list(range(4, 16))

    kb.begin(identd)
    kb.junk = Pool(kb, "junk", [128, D_MODEL], F32, 1)
    kb.hn = Pool(kb, "hn", [128, D_MODEL], BF16, 2)
    xin = Pool(kb, "xin", [128, D_MODEL], F32, 2)
    G = kb.sb("Gs", [128, D_MODEL], F32); r_G = Res()
    kb.dma("sp", G[:], Gd, [], [r_G])
    DK = kb.sb("DKs", [128, D_MODEL], F32); r_DK = Res()
    kb.dma("sp", DK[:], DKd, [], [r_DK])
    A1 = kb.sb("A1", [128, KC * RB], BF16); r_A1 = Res()
    hT = A1[:].rearrange("p (k c) -> p k c", k=KC)
    wpool = Pool(kb, "wsl", [128, KC * 512], BF16, 2)
    cs = kb.sb("cs", [128, 2, RB // 128, 256], F32); r_cs = Res()
    rt = Pool(kb, "rt", [128, 256], F32, 6)
    so = Pool(kb, "so", [128, 512], BF16, 4)

    for b in range(NBLK):
        t0 = b * RB
        for tt in range(RB // 128):
            xt, r_xt = xin.next()
            kb.dma("sp", xt[:], xr[t0 + tt * 128:t0 + (tt + 1) * 128, :], [r_xr], [r_xt])
            emit_norm_T(kb, xt, r_xt, G, r_G, hT, r_A1, tt * 128, KC)
        kb.dma("sp", cs[:, 0, :, :], cosd[t0:t0 + RB, :].rearrange("(n p) c -> p n c", p=128), [], [r_cs])
        kb.dma("sp", cs[:, 1, :, :], sind[t0:t0 + RB, :].rearrange("(n p) c -> p n c", p=128), [], [r_cs])
        for ns in slices:
            wt, r_wt = wpool.next()
            wv_ = wt[:].rearrange("p (k c) -> p k c", k=KC)
            kb.dma("sp", wt[:], wib[ns], [r_wib[ns]], [r_wt])
            for tt in range(RB // 128):
                ps, r_ps = kb.bank()
                for k in range(KC):
                    kb.mm(ps[:, 0:512], hT[:, k, tt * 128:(tt + 1) * 128], wv_[:, k, :], k == 0, k == KC - 1, [r_A1, r_wt], [r_ps])
                o_, r_o = so.next()
                rows = slice(t0 + tt * 128, t0 + (tt + 1) * 128)
                if ns < 8:
                    psv = ps[:, 0:512].rearrange("p (i t) -> p i t", t=2)
                    ov = o_[:].rearrange("p (i t) -> p i t", t=2)
                    c_ = cs[:, 0, tt, :]
                    s_ = cs[:, 1, tt, :]
                    t1, r_t1 = rt.next(); t2, r_t2 = rt.next()
                    kb.v("dve", "tensor_tensor", [r_ps, r_cs], [r_t1], out=t1[:], in0=psv[:, :, 0], in1=c_, op=ALU.mult)
                    kb.v("dve", "tensor_tensor", [r_ps, r_cs], [r_t2], out=t2[:], in0=psv[:, :, 1], in1=s_, op=ALU.mult)
                    kb.v("pool", "tensor_tensor", [r_t1, r_t2], [r_o], out=ov[:, :, 0], in0=t1[:], in1=t2[:], op=ALU.subtract)
                    t3, r_t3 = rt.next(); t4, r_t4 = rt.next()
                    kb.v("dve", "tensor_tensor", [r_ps, r_cs], [r_t3], out=t3[:], in0=psv[:, :, 0], in1=s_, op=ALU.mult)
                    kb.v("dve", "tensor_tensor", [r_ps, r_cs], [r_t4], out=t4[:], in0=psv[:, :, 1], in1=c_, op=ALU.mult)
                    kb.v("pool", "tensor_tensor", [r_t3, r_t4], [r_o], out=ov[:, :, 1], in0=t3[:], in1=t4[:], op=ALU.add)
                    if ns < 4:
                        kb.dma("pool", qd[rows, ns * 512:(ns + 1) * 512], o_[:], [r_o], [r_qd])
                    else:
                        kb.dma("pool", kd[rows, (ns - 4) * 512:(ns - 3) * 512], o_[:], [r_o], [r_kd])
                elif ns < 16:
                    kb.act(o_[:], ps[:, 0:512], AF.Copy, [r_ps], [r_o])
                    kb.dma("pool", vd[rows, (ns - 8) * 512:(ns - 7) * 512], o_[:], [r_o], [r_vd])
                else:
                    kb.act(o_[:], ps[:, 0:512], AF.Silu, [r_ps], [r_o])
                    kb.dma("pool", sgd[rows, (ns - 16) * 512:(ns - 15) * 512], o_[:], [r_o], [r_sgd])

    gam = ret_gammas()
    Sv = A1[:].bitcast(F32).rearrange("p (h d v) -> p h d v", h=RH, d=2)
    Sbf = wpool.tiles[0][:].rearrange("p (h d v) -> p h d v", h=RH, d=2)
    qkT = wpool.tiles[1][:].rearrange("p (b s j c) -> p b s j c", b=2, s=2, j=16)
    r_S = [Res() for _ in range(RH)]
    r_Sbf = [Res() for _ in range(RH)]
    r_qkT = [Res(), Res()]
    qk_tiles = [t[:].bitcast(BF16) for t in xin.tiles]
    r_qk = xin.res
    kdec = Pool(kb, "kdec", [128, D_MODEL], BF16, 2)
    vh = Pool(kb, "vh", [128, DV_], BF16, 8)
    barrier_w = [r_A1, wpool.res[0], wpool.res[1], kb.junk.res[0]] + r_S + r_Sbf + r_qkT
    kb.v("dve", "memset", [], barrier_w, ap=A1[:].bitcast(F32), constant=0.0)
    if full:
        Mp = kb.sb("Mps", [128, RH, 128], F32); r_Mp = Res()
        kb.dma("sp", Mp[:], Mpd.rearrange("p (h n) -> p h n", h=RH), [], [r_Mp])
        DQ = kb.sb("DQs", [128, RH], F32); r_DQ = Res()
        kb.dma("sp", DQ[:], DQd, [], [r_DQ])
        coef = kb.sb("coefs", [128, 3 * RH], F32); r_coef = Res()
        kb.dma("sp", coef[:], coefd, [], [r_coef])
        sgh = Pool(kb, "sgh", [128, DV_], BF16, 6)
        oh = Pool(kb, "oh", [128, DV_], F32, 6)
        junk4 = [kb.junk.tiles[0][:, i * DV_:(i + 1) * DV_] for i in range(4)]
        r_junk4 = [Res() for _ in range(4)]
        ogp = Pool(kb, "ogp", [128, DV_], BF16, 3)
        sT = Pool(kb, "sT", [128, 128], BF16, 2 * RH)
        for i in range(3):
            for h in range(RH):
                for d in range(2):
                    l_, r_l = oh.next()
                    kb.dma("sp", l_[:], Lprev[i, h, d], [r_L], [r_l])
                    kb.v("dve", "scalar_tensor_tensor", [r_l, r_coef, r_S[h]], [r_S[h]], out=Sv[:, h, d, :], in0=l_[:],
                         scalar=coef[:, i * RH + h:i * RH + h + 1], in1=Sv[:, h, d, :], op0=ALU.mult, op1=ALU.add)
        for h in range(RH):
            kb.act(Sbf[:, h, :, :], Sv[:, h, :, :], AF.Copy, [r_S[h]], [r_Sbf[h]])
    for c in range(NCH):
        rows = slice(c * 128, (c + 1) * 128)
        bi = c % 2
        qk = qk_tiles[bi]
        r_q = r_qk[bi]
        if full:
            kb.dma("sp", qk[:, 0:D_MODEL], qd[rows, :], [r_qd], [r_q])
        kb.dma("sp", qk[:, D_MODEL:2 * D_MODEL], kd[rows, :], [r_kd], [r_q])
        kd_, r_kdec = kdec.next()
        kb.v("dve", "tensor_tensor", [r_q, r_DK], [r_kdec], out=kd_[:], in0=qk[:, D_MODEL:2 * D_MODEL], in1=DK[:], op=ALU.mult)
        if full:
            for s in range(2):
                for half in range(2):
                    ps, r_ps = kb.bank()
                    psb = ps[:].bitcast(BF16)
                    for j in range(8):
                        col = s * D_MODEL + (half * 8 + j) * 128
                        kb.tr(psb[:, j * 128:(j + 1) * 128], qk[:, col:col + 128], kb.identb[:], [r_q, kb.r_ident], [r_ps])
                    src = psb[:, 0:1024].rearrange("p (j c) -> p j c", c=128)
                    dst = qkT[:, bi, s, half * 8:half * 8 + 8, :]
                    if half == 0:
                        kb.act(dst, src, AF.Copy, [r_ps], [r_qkT[bi]])
                    else:
                        kb.v("dve", "tensor_copy", [r_ps], [r_qkT[bi]], out=dst, in_=src)
        sts = []
        if full:
            for h in range(RH):
                ps, r_ps = kb.bank()
                for d in range(2):
                    kb.mm(ps[:, 0:128], qkT[:, bi, 1, h * 2 + d, :], qkT[:, bi, 0, h * 2 + d, :], d == 0, d == 1, [r_qkT[bi]], [r_ps])
                st_, r_st = sT.next()
                kb.v("dve", "tensor_tensor", [r_ps, r_Mp], [r_st], out=st_[:], in0=ps[:, 0:128], in1=Mp[:, h, :], op=ALU.mult)
                sts.append((st_, r_st))
        pending = []
        for h in range(RH):
            v_, r_v = vh.next()
            kb.dma("sp", v_[:], vd[rows, h * DV_:(h + 1) * DV_], [r_vd], [r_v])
            if full:
                g_, r_g = sgh.next()
                kb.dma("sp", g_[:], sgd[rows, h * DV_:(h + 1) * DV_], [r_sgd], [r_g])
                st_, r_st = sts[h]
                po, r_po = kb.bank()
                kb.mm(po[:, 0:DV_], st_[:], v_[:], True, False, [r_st, r_v], [r_po])
                for d in range(2):
                    kb.mm(po[:, 0:DV_], qkT[:, bi, 0, h * 2 + d, :], Sbf[:, h, d, :], False, d == 1, [r_qkT[bi], r_Sbf[h]], [r_po])
                o_, r_o = oh.next()
                kb.act(o_[:], po[:, 0:DV_], AF.Copy, [r_po, r_DQ], [r_o], scale=DQ[:, h:h + 1])
                jk, r_jk = junk4[h % 4], r_junk4[h % 4]
                kb.act(jk, o_[:], AF.Square, [r_o], [r_jk])
                pending.append((h, o_, r_o, g_, r_g, jk, r_jk))
            for d in range(2):
                pS, r_pS = kb.bank()
                kb.mm(pS[:, 0:DV_], kd_[:, h * DK_ + d * 128:h * DK_ + (d + 1) * 128], v_[:], True, True, [r_kdec, r_v], [r_pS])
                kb.v("dve", "scalar_tensor_tensor", [r_pS, r_S[h]], [r_S[h]], out=Sv[:, h, d, :], in0=Sv[:, h, d, :],
                     scalar=float(gam[h] ** 128), in1=pS[:, 0:DV_], op0=ALU.mult, op1=ALU.add)
            if full:
                kb.act(Sbf[:, h, :, :], Sv[:, h, :, :], AF.Copy, [r_S[h]], [r_Sbf[h]])
            if full and len(pending) == 4:
                sms = []
                for (hh, o_, r_o, g_, r_g, jk, r_jk) in pending:
                    sm, r_sm = kb.small.next()
                    kb.v("dve", "reduce_sum", [r_jk], [r_sm], out=sm[:, 0:1], in_=jk, axis=AX.X)
                    kb.v("dve", "tensor_scalar", [r_sm], [r_sm], out=sm[:, 1:2], in0=sm[:, 0:1], scalar1=1.0 / DV_,
                         scalar2=NORM_EPS, op0=ALU.mult, op1=ALU.add)
                    sms.append((sm, r_sm))
                for (sm, r_sm) in sms:
                    kb.act(sm[:, 2:3], sm[:, 1:2], AF.Sqrt, [r_sm], [r_sm])
                for (hh, o_, r_o, g_, r_g, jk, r_jk), (sm, r_sm) in zip(pending, sms):
                    kb.v("dve", "reciprocal", [r_sm], [r_sm], out=sm[:, 3:4], in_=sm[:, 2:3])
                    og_, r_og = ogp.next()
                    kb.v("dve", "scalar_tensor_tensor", [r_o, r_sm, r_g], [r_og], out=og_[:], in0=o_[:], scalar=sm[:, 3:4], in1=g_[:],
                         op0=ALU.mult, op1=ALU.mult)
                    kb.dma("pool", og[rows, hh * DV_:(hh + 1) * DV_], og_[:], [r_og], [r_og_d])
                pending = []
    if not full:
        for h in range(RH):
            for d in range(2):
                kb.dma("pool", Lout[h, d], Sv[:, h, d, :], [r_S[h]], [r_L])
    kb.end()


def ret_host_layouts(w_in, g0):
    w = w_in.reshape(KC, 128, NS, 512)
    win = np.ascontiguousarray(w.transpose(2, 1, 0, 3)).reshape(NS, 128, KC * 512)
    G = np.ascontiguousarray(np.broadcast_to(g0[None, :], (128, D_MODEL)))
    return dict(win=win, G=G, ident=np.eye(128, dtype=np.float32))


def ret_const_tables(pos0, T, j):
    theta = (1.0 / (10000.0 ** np.linspace(0.0, 1.0, DK_ // 2, dtype=np.float32))).astype(np.float32)
    ang = (np.arange(pos0, pos0 + T, dtype=np.float32)[:, None] * theta[None, :]).astype(np.float32)
    cos = np.cos(ang).astype(np.float32)
    sin = np.sin(ang).astype(np.float32)
    cos2 = np.ascontiguousarray(np.concatenate([cos, cos], axis=1))
    sin2 = np.ascontiguousarray(np.concatenate([sin, sin], axis=1))
    gam = np.array(ret_gammas(), dtype=np.float64)
    lg = np.log1p(-(2.0 ** (-5.0 - np.arange(RH, dtype=np.float64))))
    m = np.arange(128, dtype=np.float64)
    ks = DK_ ** -0.5
    DK = np.exp((127.0 - m)[:, None] * lg[None, :]) * ks
    DK = np.ascontiguousarray(np.repeat(DK, DK_, axis=1)).astype(np.float32)
    DQ = np.exp((m + 1.0)[:, None] * lg[None, :]).astype(np.float32)
    Mp = np.exp(-(m + 1.0)[:, None, None] * lg[None, :, None]) * ks
    Mp = Mp * (m[None, None, :] >= m[:, None, None])
    Mp = np.ascontiguousarray(Mp.reshape(128, RH * 128)).astype(np.float32)
    coef = np.zeros((3, RH), np.float64)
    for i in range(3):
        if i < j:
            coef[i] = np.exp(T * (j - 1 - i) * lg)
    coef = np.ascontiguousarray(np.broadcast_to(coef.reshape(1, 3 * RH), (128, 3 * RH))).astype(np.float32)
    return dict(cos2=cos2, sin2=sin2, DK=DK, DQ=DQ, Mp=Mp, coef=coef)


GROUPS = [[0, 1, 2, 3], [4, 5, 6, 7]]
TPC = SEQ * BATCH // NCORES


def build_fused():
    kb = KB()
    T, S, D = TPC, SEQ, D_MODEL
    ident = kb.din("ident", [128, 128])
    xb = kb.din("xb", [S, D])
    xs = kb.din("xs", [T, D])
    sel_prev = kb.din("sel_prev", [128, 4])
    sel_own = kb.din("sel_own", [128, 4])
    fio = fox_inputs(kb)
    w0 = wo_inputs(kb, "wo0_", 16)
    f0 = ffn_inputs(kb, "f0_")
    rio = ret_inputs(kb, T)
    w1 = wo_inputs(kb, "wo1_", 32)
    f1 = ffn_inputs(kb, "f1_")
    out = kb.dout("out", [T, D])

    prep = Prep()
    pf0 = prep_ffn(kb, prep, "f0_", f0)
    pr = prep_ret(kb, prep, rio)
    pf1 = prep_ffn(kb, prep, "f1_", f1)
    att = kb.dtmp("att", [S, HPC * DH], BF16); r_att = Res()
    emit_fox(kb, S, fio, xb, ident, att, r_att, prep=prep)
    CR = 1024
    r_attg = Res()
    attg = []
    for i in range(S // CR):
        g_ = kb.dtmp("attg%d" % i, [4 * CR, HPC * DH], BF16)
        kb.collective("AllGather", GROUPS, att[i * CR:(i + 1) * CR, :], g_, [r_att], [r_attg], flush=(i == S // CR - 1))
        attg.append(g_.rearrange("(m t) c -> t m c", m=4))

    def src0(rows):
        res = []
        for jj in range(4):
            r0 = jj * T + rows.start
            res.append((attg[r0 // CR][r0 % CR:r0 % CR + 128], lambda c_: c_[:].rearrange("p (m c) -> p m c", m=4)))
        return res

    xm0 = kb.dtmp("xm0", [T, D]); r_xm0 = Res()
    emit_wo(kb, T, 16, w0, ident, src0, r_attg, sel_own, xs, Res(), xm0, r_xm0)
    halo0 = kb.dtmp("halo0", [8, D]); r_h0 = Res()
    kb.collective("AllGather", GROUPS, xm0[T - 2:T, :], halo0, [r_xm0], [r_h0])
    x1 = kb.dtmp("x1", [T, D]); r_x1 = Res()
    emit_ffn(kb, T, "f0_", f0, pf0, xm0, r_xm0, halo0, r_h0, sel_prev, ident, x1, r_x1)

    L = kb.dtmp("Lst", [RH, 2, 128, DV_]); r_L = Res()
    emit_ret(kb, T, False, rio, pr, ident, x1, r_x1, Lout=L, r_L=r_L)
    r_La = Res()
    Lall = []
    Lf = L.rearrange("h d p v -> (h d p) v")
    for i in range(RH // 2):
        g_ = kb.dtmp("Lall%d" % i, [4 * 512, DV_])
        kb.collective("AllGather", GROUPS, Lf[i * 512:(i + 1) * 512, :], g_, [r_L], [r_La], flush=(i == RH // 2 - 1))
        Lall.append(g_.rearrange("(i h d p) v -> i h d p v", i=4, h=2, d=2))

    class _LP:
        def __getitem__(self, idx):
            i, h, d = idx
            return Lall[h // 2][i, h % 2, d]

    og = kb.dtmp("og", [T, RH * DV_], BF16); r_og = Res()
    emit_ret(kb, T, True, rio, pr, ident, x1, r_x1, r_L=r_La, Lprev=_LP(), og=og, r_og_d=r_og)

    def src1(rows):
        return [(og[rows, :], lambda c_: c_[:])]

    xm1 = kb.dtmp("xm1", [T, D]); r_xm1 = Res()
    emit_wo(kb, T, 32, w1, ident, src1, r_og, sel_own, x1, r_x1, xm1, r_xm1)
    halo1 = kb.dtmp("halo1", [8, D]); r_h1 = Res()
    kb.collective("AllGather", GROUPS, xm1[T - 2:T, :], halo1, [r_xm1], [r_h1])
    emit_ffn(kb, T, "f1_", f1, pf1, xm1, r_xm1, halo1, r_h1, sel_prev, ident, out, Res())
    return kb.nc


_NC = []


def kernel(x, norm_g, fox_w_in, fox_b_f, fox_w_o, ret_w_in, ret_w_o,
           ffn_w_up, ffn_conv_w, ffn_conv_b, ffn_w_down):
    f = lambda a: np.ascontiguousarray(np.asarray(a, dtype=np.float32))
    x, norm_g = f(x), f(norm_g)
    fox_w_in, fox_b_f, fox_w_o = f(fox_w_in), f(fox_b_f), f(fox_w_o)
    ret_w_in, ret_w_o = f(ret_w_in), f(ret_w_o)
    ffn_w_up, ffn_conv_w, ffn_conv_b, ffn_w_down = f(ffn_w_up), f(ffn_conv_w), f(ffn_conv_b), f(ffn_w_down)
    if not _NC:
        _NC.append(build_fused())
    nc = _NC[0]
    T, G = TPC, NCORES // BATCH
    shared = {"ident": np.eye(128, dtype=np.float32)}
    for k_, v_ in wo_host_layout(fox_w_o[0], norm_g[0, 1]).items():
        if k_ != "ident":
            shared["wo0_" + k_] = v_
    for k_, v_ in wo_host_layout(ret_w_o[0], norm_g[1, 1]).items():
        if k_ != "ident":
            shared["wo1_" + k_] = v_
    for l, pfx in ((0, "f0_"), (1, "f1_")):
        lay = ffn_host_layouts(ffn_w_up[l], ffn_conv_w[l], ffn_conv_b[l], ffn_w_down[l], norm_g[l, 2], norm_g[l, 3])
        for k_, v_ in lay.items():
            if k_ != "ident":
                shared[pfx + k_] = v_
    rl = ret_host_layouts(ret_w_in[0], norm_g[1, 0])
    shared["r_win"] = rl["win"]
    shared["r_G"] = rl["G"]
    foxl = [fox_host_layouts(fox_w_in[0], fox_b_f[0], norm_g[0, 0], m) for m in range(G)]
    maps = []
    for c in range(NCORES):
        b, j = c // G, c % G
        m = dict(shared)
        for k_, v_ in foxl[j].items():
            if k_ != "ident":
                m[k_] = v_
        m.update(ret_const_tables(j * T, T, j))
        m["xb"] = x[b]
        m["xs"] = np.ascontiguousarray(x[b, j * T:(j + 1) * T])
        sp = np.zeros((128, 4), np.float32)
        so = np.zeros((128, 4), np.float32)
        if j > 0:
            sp[:, j - 1] = 1.0
        so[:, j] = 1.0
        m["sel_prev"] = sp
        m["sel_own"] = so
        maps.append(m)
    res = run_bass_kernel_spmd(nc, maps, core_ids=list(range(NCORES))).results
    out = np.concatenate([res[c]["out"] for c in range(NCORES)], axis=0).reshape(BATCH, SEQ, D_MODEL)
    return out.astype(np.float32)
```
